# Optimizing a Trainium2 kernel written in Bass

```python
import math
import jax, jax.numpy as jnp
from jax import lax
import numpy as np

D_MODEL = 2048
BATCH = 4
SEQ = 4096
DEPTH = 2

N_MIXERS = 2
HEAD_DIM = 128
N_HEADS = D_MODEL // (2 * HEAD_DIM)
V_DIM = 2 * HEAD_DIM
QKV_DIM = 2 * N_HEADS * HEAD_DIM * 2 + N_HEADS * V_DIM
ROT_DIM = HEAD_DIM // 4
ROPE_THETA = 500000.0
Q_BLOCK = 128
SSM_GROUP = 16
N_GROUPS = D_MODEL // SSM_GROUP
SSM_STATE = 64
SSM_CHUNK = 128
D_FF = 256 * ((8 * D_MODEL // 3 + 255) // 256)
N_SUBLAYERS = 3
N_ATTN_LAYERS = (DEPTH + 1) // 2
N_SSM_LAYERS = DEPTH // 2
NORM_EPS = 1e-6
MACARON_WEIGHT = 0.5

kernel_name = "hybrid_diffattn_s5_macaron_adaln"


def rms_norm(x, g):
    xf = x.astype(jnp.float32)
    y = xf * lax.rsqrt(jnp.mean(xf * xf, axis=-1, keepdims=True) + NORM_EPS)
    return (y * g.astype(jnp.float32)).astype(x.dtype)


def rope_tables(positions):
    inv_freq = ROPE_THETA ** (-jnp.arange(0, ROT_DIM, 2, dtype=jnp.float32) / ROT_DIM)
    ang = positions.astype(jnp.float32)[..., None] * inv_freq
    return jnp.cos(ang), jnp.sin(ang)


def apply_partial_rope(t, cos, sin):
    cos = cos[:, None, None]
    sin = sin[:, None, None]
    tf = t.astype(jnp.float32)
    half = ROT_DIM // 2
    r1 = tf[..., :half]
    r2 = tf[..., half:ROT_DIM]
    out = jnp.concatenate([r1 * cos - r2 * sin, r2 * cos + r1 * sin, tf[..., ROT_DIM:]], axis=-1)
    return out.astype(t.dtype)


def swiglu(h, w_in, w_out):
    a, b = jnp.split(h @ w_in, 2, axis=-1)
    return (jax.nn.silu(a) * b) @ w_out


def diff_attention(h, w_in, w_out, q_g, k_g, lam_vec, subln_g, cos, sin, lambda_init):
    bsz, seq, _ = h.shape
    qkv = h @ w_in
    qd = 2 * N_HEADS * HEAD_DIM
    q, k, v = jnp.split(qkv, [qd, 2 * qd], axis=-1)
    q = q.reshape(bsz, seq, N_HEADS, 2, HEAD_DIM).transpose(0, 2, 3, 1, 4)
    k = k.reshape(bsz, seq, N_HEADS, 2, HEAD_DIM).transpose(0, 2, 3, 1, 4)
    v = v.reshape(bsz, seq, N_HEADS, V_DIM).transpose(0, 2, 1, 3)
    q = apply_partial_rope(rms_norm(q, q_g), cos, sin)
    k = apply_partial_rope(rms_norm(k, k_g), cos, sin)
    lv = lam_vec.astype(jnp.float32)
    lam = jnp.exp(jnp.sum(lv[0] * lv[1])) - jnp.exp(jnp.sum(lv[2] * lv[3])) + lambda_init
    scale = HEAD_DIM ** -0.5
    n_blocks = seq // Q_BLOCK
    q_blocks = q.reshape(bsz, N_HEADS, 2, n_blocks, Q_BLOCK, HEAD_DIM).transpose(3, 0, 1, 2, 4, 5)
    k_pos = jnp.arange(seq)

    def one_block(args):
        q_blk, blk = args
        s = jnp.einsum('bhcqd,bhckd->bhcqk', q_blk, k, preferred_element_type=jnp.float32) * scale
        q_pos = blk * Q_BLOCK + jnp.arange(Q_BLOCK)
        mask = k_pos[None, :] <= q_pos[:, None]
        p = jax.nn.softmax(jnp.where(mask, s, -jnp.inf), axis=-1)
        attn = p[:, :, 0] - lam * p[:, :, 1]
        return jnp.einsum('bhqk,bhkv->bhqv', attn.astype(v.dtype), v)

    o = lax.map(one_block, (q_blocks, jnp.arange(n_blocks)))
    o = o.transpose(1, 2, 0, 3, 4).reshape(bsz, N_HEADS, seq, V_DIM)
    o = rms_norm(o, subln_g) * (1.0 - lambda_init)
    o = o.transpose(0, 2, 1, 3).reshape(bsz, seq, N_HEADS * V_DIM)
    return o @ w_out


def _ssm_combine(e1, e2):
    a1r, a1i, b1r, b1i = e1
    a2r, a2i, b2r, b2i = e2
    return (a2r * a1r - a2i * a1i,
            a2r * a1i + a2i * a1r,
            a2r * b1r - a2i * b1i + b2r,
            a2r * b1i + a2i * b1r + b2i)


def s5_glu_mixer(h, a_re, a_im, log_step, b_re, b_im, c_re, c_im, d_skip, w_glu):
    bsz, seq, _ = h.shape
    f32 = jnp.float32
    u = h.astype(f32).reshape(bsz, seq, N_GROUPS, SSM_GROUP)
    ar = a_re.astype(f32)
    ai = a_im.astype(f32)
    dt = jnp.exp(log_step.astype(f32))[:, None]
    mag = jnp.exp(dt * ar)
    abar_re = mag * jnp.cos(dt * ai)
    abar_im = mag * jnp.sin(dt * ai)
    den = ar * ar + ai * ai
    num_re = abar_re - 1.0
    coef_re = (num_re * ar + abar_im * ai) / den
    coef_im = (abar_im * ar - num_re * ai) / den
    br = b_re.astype(f32)
    bi = b_im.astype(f32)
    bbar_re = coef_re[..., None] * br - coef_im[..., None] * bi
    bbar_im = coef_re[..., None] * bi + coef_im[..., None] * br
    cr = c_re.astype(f32)
    ci = c_im.astype(f32)
    n_chunks = seq // SSM_CHUNK
    u_chunks = u.reshape(bsz, n_chunks, SSM_CHUNK, N_GROUPS, SSM_GROUP).transpose(1, 2, 0, 3, 4)
    a_shape = (SSM_CHUNK, bsz, N_GROUPS, SSM_STATE)
    abar_re_b = jnp.broadcast_to(abar_re, a_shape)
    abar_im_b = jnp.broadcast_to(abar_im, a_shape)

    def chunk_step(carry, u_c):
        h_re, h_im = carry
        bu_re = jnp.einsum('tbgp,gnp->tbgn', u_c, bbar_re)
        bu_im = jnp.einsum('tbgp,gnp->tbgn', u_c, bbar_im)
        p_re, p_im, l_re, l_im = lax.associative_scan(
            _ssm_combine, (abar_re_b, abar_im_b, bu_re, bu_im), axis=0)
        s_re = l_re + p_re * h_re - p_im * h_im
        s_im = l_im + p_re * h_im + p_im * h_re
        y_c = jnp.einsum('tbgn,gpn->tbgp', s_re, cr) - jnp.einsum('tbgn,gpn->tbgp', s_im, ci)
        return (s_re[-1], s_im[-1]), y_c

    zeros = jnp.zeros((bsz, N_GROUPS, SSM_STATE), f32)
    _, y = lax.scan(chunk_step, (zeros, zeros), u_chunks)
    y = y.transpose(2, 0, 1, 3, 4).reshape(bsz, seq, D_MODEL)
    y = y + d_skip.astype(f32) * h.astype(f32)
    z = jax.nn.gelu(y).astype(h.dtype)
    val, gate = jnp.split(z @ w_glu, 2, axis=-1)
    return val * jax.nn.sigmoid(gate)


def setup_inputs(seed: int = 0) -> dict:
    key = jax.random.key(seed)
    ks = jax.random.split(key, 24)
    f32 = jnp.float32
    D, F = D_MODEL, D_FF
    nrm = lambda k, shape, s: jax.random.normal(k, shape, f32) * s
    x = jax.random.normal(ks[0], (BATCH, SEQ, D), f32)
    c = jax.random.normal(ks[1], (BATCH, D), f32)
    offsets = jax.random.randint(ks[2], (BATCH, 1), 0, 1024, dtype=jnp.int32)
    positions = offsets + jnp.arange(SEQ, dtype=jnp.int32)[None, :]
    norm_g = 1.0 + nrm(ks[3], (DEPTH, N_SUBLAYERS, D), 0.02)
    ada_w = nrm(ks[4], (DEPTH, D, N_SUBLAYERS * 3 * D), 0.5 * D ** -0.5)
    ada_b = nrm(ks[5], (DEPTH, N_SUBLAYERS * 3 * D), 0.02)
    ffn_w_in = nrm(ks[6], (DEPTH, 2, D, 2 * F), D ** -0.5)
    ffn_w_out = nrm(ks[7], (DEPTH, 2, F, D), F ** -0.5)
    attn_w_in = nrm(ks[8], (N_ATTN_LAYERS, D, QKV_DIM), D ** -0.5)
    attn_w_out = nrm(ks[9], (N_ATTN_LAYERS, N_HEADS * V_DIM, D), (N_HEADS * V_DIM) ** -0.5)
    attn_q_norm = 1.0 + nrm(ks[10], (N_ATTN_LAYERS, HEAD_DIM), 0.02)
    attn_k_norm = 1.0 + nrm(ks[11], (N_ATTN_LAYERS, HEAD_DIM), 0.02)
    attn_lambda = nrm(ks[12], (N_ATTN_LAYERS, 4, HEAD_DIM), 0.1)
    attn_subln = 1.0 + nrm(ks[13], (N_ATTN_LAYERS, V_DIM), 0.02)
    ssm_a_re = -0.5 + nrm(ks[14], (N_SSM_LAYERS, N_GROUPS, SSM_STATE), 0.01)
    ssm_a_im = jnp.broadcast_to(math.pi * jnp.arange(SSM_STATE, dtype=f32),
                                (N_SSM_LAYERS, N_GROUPS, SSM_STATE))
    ssm_log_step = jax.random.uniform(ks[15], (N_SSM_LAYERS, N_GROUPS), f32,
                                      math.log(1e-3), math.log(1e-1))
    b_s = (0.5 / SSM_GROUP) ** 0.5
    c_s = (0.5 / SSM_STATE) ** 0.5
    ssm_b_re = nrm(ks[16], (N_SSM_LAYERS, N_GROUPS, SSM_STATE, SSM_GROUP), b_s)
    ssm_b_im = nrm(ks[17], (N_SSM_LAYERS, N_GROUPS, SSM_STATE, SSM_GROUP), b_s)
    ssm_c_re = nrm(ks[18], (N_SSM_LAYERS, N_GROUPS, SSM_GROUP, SSM_STATE), c_s)
    ssm_c_im = nrm(ks[19], (N_SSM_LAYERS, N_GROUPS, SSM_GROUP, SSM_STATE), c_s)
    ssm_d = nrm(ks[20], (N_SSM_LAYERS, D), 1.0)
    ssm_w_glu = nrm(ks[21], (N_SSM_LAYERS, D, 2 * D), D ** -0.5)
    return {"x": x, "c": c, "positions": positions, "norm_g": norm_g,
            "ada_w": ada_w, "ada_b": ada_b, "ffn_w_in": ffn_w_in, "ffn_w_out": ffn_w_out,
            "attn_w_in": attn_w_in, "attn_w_out": attn_w_out, "attn_q_norm": attn_q_norm,
            "attn_k_norm": attn_k_norm, "attn_lambda": attn_lambda, "attn_subln": attn_subln,
            "ssm_a_re": ssm_a_re, "ssm_a_im": ssm_a_im, "ssm_log_step": ssm_log_step,
            "ssm_b_re": ssm_b_re, "ssm_b_im": ssm_b_im, "ssm_c_re": ssm_c_re,
            "ssm_c_im": ssm_c_im, "ssm_d": ssm_d, "ssm_w_glu": ssm_w_glu}


def reference(x, c, positions, norm_g, ada_w, ada_b, ffn_w_in, ffn_w_out,
              attn_w_in, attn_w_out, attn_q_norm, attn_k_norm, attn_lambda, attn_subln,
              ssm_a_re, ssm_a_im, ssm_log_step, ssm_b_re, ssm_b_im, ssm_c_re,
              ssm_c_im, ssm_d, ssm_w_glu):
    bsz = x.shape[0]
    cos, sin = rope_tables(positions)
    cond = jax.nn.silu(c)
    for i in range(DEPTH):
        mod = (cond @ ada_w[i] + ada_b[i]).reshape(bsz, N_SUBLAYERS, 3, D_MODEL)
        shift = mod[:, :, 0][:, :, None, :]
        scale = mod[:, :, 1][:, :, None, :]
        gate = mod[:, :, 2][:, :, None, :]

        h = rms_norm(x, norm_g[i, 0]) * (1.0 + scale[:, 0]) + shift[:, 0]
        x = x + MACARON_WEIGHT * gate[:, 0] * swiglu(h, ffn_w_in[i, 0], ffn_w_out[i, 0])

        h = rms_norm(x, norm_g[i, 1]) * (1.0 + scale[:, 1]) + shift[:, 1]
        j = i // N_MIXERS
        if i % N_MIXERS == 0:
            lambda_init = 0.8 - 0.6 * math.exp(-0.3 * i)
            m = diff_attention(h, attn_w_in[j], attn_w_out[j], attn_q_norm[j], attn_k_norm[j],
                               attn_lambda[j], attn_subln[j], cos, sin, lambda_init)
        else:
            m = s5_glu_mixer(h, ssm_a_re[j], ssm_a_im[j], ssm_log_step[j], ssm_b_re[j],
                             ssm_b_im[j], ssm_c_re[j], ssm_c_im[j], ssm_d[j], ssm_w_glu[j])
        x = x + gate[:, 1] * m

        h = rms_norm(x, norm_g[i, 2]) * (1.0 + scale[:, 2]) + shift[:, 2]
        x = x + MACARON_WEIGHT * gate[:, 2] * swiglu(h, ffn_w_in[i, 1], ffn_w_out[i, 1])
    return x
```

```python
import math
from contextlib import ExitStack

import numpy as np
import ml_dtypes

import concourse.bass as bass
import concourse.mybir as mybir
from concourse.bass_utils import run_bass_kernel_spmd

F32 = mybir.dt.float32
BF16 = mybir.dt.bfloat16
I32 = mybir.dt.int32
ALU = mybir.AluOpType
AF = mybir.ActivationFunctionType
AX = mybir.AxisListType

D = 2048
KC = 16
SEQ = 4096
DFF = 5632
FC = 44
NT = 512
EPS = 1e-6
NH = 8
HD = 128
N_CORES = 8
TWO_PI = 2.0 * math.pi

SELF_SYNC = True
ALL_PHASES = ['ffn00', 'att0', 'ffn01', 'ffn10', 'ssm1', 'ffn11']


class Buf:
    __slots__ = ("w", "r", "name")

    def __init__(self, name=""):
        self.w = None
        self.r = []
        self.name = name


class Prog:
    NDS = {"sp": 12, "pool": 6, "act": 4}

    def __init__(self, nc, es):
        self.nc = nc
        self.engs = {"pe": nc.tensor, "act": nc.scalar, "dve": nc.vector, "pool": nc.gpsimd, "sp": nc.sync}
        self.sem = {e: es.enter_context(nc.semaphore("s_" + e)) for e in ("pe", "act", "dve", "pool")}
        self.cnt = {e: 0 for e in self.sem}
        self.ops = {e: [] for e in self.engs}
        self.seen = {e: {} for e in self.engs}
        self.dsem = {q: [es.enter_context(nc.semaphore("d_%s%d" % (q, i))) for i in range(n)]
                     for q, n in self.NDS.items()}
        self.duse = {q: [0] * n for q, n in self.NDS.items()}
        self.dnext = {q: 0 for q in self.NDS}
        self.pending = {e: [] for e in self.engs}
        self.ccsem = es.enter_context(nc.semaphore("s_cc"))
        self.cccnt = 0

    def _semof(self, key):
        if isinstance(key, tuple):
            return self.dsem[key[1]][key[2]]
        if key == "cc":
            return self.ccsem
        return self.sem[key]

    def _deps(self, e, reads, writes):
        toks = []
        for b in reads:
            if b.w is not None:
                toks.append(b.w)
        for b in writes:
            if b.w is not None:
                toks.append(b.w)
            toks.extend(b.r)
        waits = {}
        for k, v in toks:
            if k == e and (e == "pe" or not SELF_SYNC):
                continue
            if self.seen[e].get(k, 0) >= v:
                continue
            if waits.get(k, 0) < v:
                waits[k] = v
        for k, v in waits.items():
            self.seen[e][k] = v
        return list(waits.items())

    def op(self, e, fn, reads=(), writes=()):
        waits = self.pending[e] + self._deps(e, reads, writes)
        self.pending[e] = []
        self.cnt[e] += 1
        tok = (e, self.cnt[e])
        self.ops[e].append((waits, fn, self.sem[e], 1))
        for b in reads:
            b.r.append(tok)
        for b in writes:
            b.w = tok
            b.r = []

    def dma(self, q, out, in_, reads=(), writes=(), **kw):
        i = self.dnext[q]
        self.dnext[q] = (i + 1) % self.NDS[q]
        waits = self.pending[q] + self._deps(q, reads, writes)
        self.pending[q] = []
        key = ("d", q, i)
        prev = self.duse[q][i]
        if prev > 0 and self.seen[q].get(key, 0) < prev:
            waits.append((key, prev))
            self.seen[q][key] = prev
        self.duse[q][i] = prev + 16
        tok = (key, prev + 16)
        self.ops[q].append((waits, lambda eng: eng.dma_start(out=out, in_=in_, **kw), self.dsem[q][i], 16))
        for b in reads:
            b.r.append(tok)
        for b in writes:
            b.w = tok
            b.r = []

    def barrier(self):
        allw = []
        for q, n in self.NDS.items():
            for i in range(n):
                if self.duse[q][i] > 0:
                    allw.append((("d", q, i), self.duse[q][i]))
        for e in self.sem:
            if self.cnt[e] > 0:
                allw.append((e, self.cnt[e]))
        if self.cccnt > 0:
            allw.append(("cc", self.cccnt))
        for e in self.engs:
            lst = []
            for k, v in allw:
                if k == e:
                    continue
                if self.seen[e].get(k, 0) >= v:
                    continue
                self.seen[e][k] = v
                lst.append((k, v))
            self.pending[e].extend(lst)

    def coll(self, kind, groups, in_h, out_h, reads=(), writes=()):
        q = "pool"
        waits = self.pending[q] + self._deps(q, reads, writes)
        self.pending[q] = []
        self.cccnt += 1
        tok = ("cc", self.cccnt)

        def fn(eng):
            return eng.collective_compute(kind, ALU.bypass, replica_groups=groups,
                                          ins=[in_h.ap().opt()], outs=[out_h.ap().opt()])
        self.ops[q].append((waits, fn, self.ccsem, None))
        for b in reads:
            b.r.append(tok)
        for b in writes:
            b.w = tok
            b.r = []

    def emit(self):
        nc = self.nc
        fin = []
        for q, n in self.NDS.items():
            for i in range(n):
                if self.duse[q][i] > 0:
                    fin.append((("d", q, i), self.duse[q][i]))
        for e in self.sem:
            if self.cnt[e] > 0:
                fin.append((e, self.cnt[e]))
        if self.cccnt > 0:
            fin.append(("cc", self.cccnt))

        def replay(name, eng):
            for waits, fn, sem, inc in self.ops[name]:
                for k, v in waits:
                    eng.wait_ge(self._semof(k), v)
                if inc is None:
                    fn(eng).then_inc(sem)
                else:
                    fn(eng).then_inc(sem, inc)
            if name == "sp":
                for k, v in fin:
                    eng.wait_ge(self._semof(k), v)

        with nc.Block() as block:
            @block.sync
            def _(e):
                replay("sp", e)

            @block.scalar
            def _(e):
                replay("act", e)

            @block.vector
            def _(e):
                replay("dve", e)

            @block.gpsimd
            def _(e):
                replay("pool", e)

            @block.tensor
            def _(e):
                replay("pe", e)


class Tile:
    def __init__(self, t, nsub=1, name=""):
        self.t = t
        self.b = [Buf("%s[%d]" % (name, i)) for i in range(nsub)]


class K:
    pass


def build(T=SEQ, phases=None, dbg=None, pair=False, groups=None):
    nc = bass.Bass("TRN2", target_bir_lowering=False)
    TL = T // 2 if pair else T
    NTT = TL // NT
    NTG = T // NT
    NHL = NH // 2 if pair else NH
    if groups is None:
        groups = [[2 * i, 2 * i + 1] for i in range(N_CORES // 2)]
    es = ExitStack()
    with es:
        P = Prog(nc, es)

        def dram_in(name, shape, dt=F32):
            return nc.dram_tensor(name, list(shape), dt, kind="ExternalInput").ap()

        def dram_scr(name, shape, dt=F32):
            return nc.dram_tensor(name, list(shape), dt, kind="Internal").ap()

        cur = [es]
        uid = [0]

        def sb(name, shape, dt=F32, nsub=1):
            uid[0] += 1
            t = cur[0].enter_context(nc.sbuf_tensor("sb%d_%s" % (uid[0], name), list(shape), dt))
            return Tile(t, nsub, name)

        class Scope:
            def __enter__(self_):
                self_.st = ExitStack()
                self_.st.__enter__()
                self_.prev = cur[0]
                cur[0] = self_.st
                return self_

            def __exit__(self_, *a):
                P.barrier()
                cur[0] = self_.prev
                return self_.st.__exit__(*a)

        def ps(name, shape, dt=F32):
            t = es.enter_context(nc.psum_tensor("ps_" + name, list(shape), dt))
            return Tile(t, 1, name)

        class CG:
            def __init__(self, name, rows, cols, dt):
                self.i_h = nc.dram_tensor("cg_in_" + name, [rows, cols], dt)
                self.o_h = nc.dram_tensor("cg_out_" + name, [2 * rows, cols], dt)
                self.i = self.i_h.ap()
                self.o = self.o_h.ap()
                self.ib = Buf("cgi_" + name)
                self.ob = Buf("cgo_" + name)

            def go(self):
                P.coll("AllGather", groups, self.i_h, self.o_h, reads=[self.ib], writes=[self.ob])

        x_in = dram_in("x", [TL, D])
        c_in = dram_in("c", [16, 128])
        pos_in = dram_in("pos", [32, T], I32)
        normg_in = dram_in("norm_g", [96, 128])
        if phases is None:
            phases = ALL_PHASES
        layers = sorted(set(int(ph[3]) for ph in phases))
        NJ = 72 if pair else 144
        adaw_in = {l: dram_in("ada_w%d" % l, [D, NJ * 128]) for l in layers}
        adab_in = dram_in("ada_b", [2, NJ, 128])
        ffnwi_in = {}
        ffnwo_in = {}
        for ph in phases:
            if ph.startswith("ffn"):
                ffnwi_in[ph[3:]] = dram_in("ffn_w_in" + ph[3:], [D, 2 * DFF])
                ffnwo_in[ph[3:]] = dram_in("ffn_w_out" + ph[3:], [DFF, D])
        ident_in = dram_in("ident", [128, 128])
        if any(ph.startswith("ssm") for ph in phases):
            pm_in = dram_in("pm", [128, 2])
            ssmd_in = dram_in("ssm_dT", [128, 16])
            iota_in = dram_in("iota", [128, NT])
            NGP = 32 if pair else 64
            NKL = NGP // 4
            apair_in = dram_in("a_pair", [128, 3, NGP])
            afeat_in = dram_in("a_feat", [128, 3, NKL * 64])
            bfeat_in = dram_in("b_feat", [128, 2, NKL * 64])
            cpair_in = dram_in("c_pair", [128, 2, NGP, 16])
            wglu_in = dram_in("w_glu", [D, 2 * D])
            if pair:
                cg_u = [CG("u%d" % i, D, NT, BF16) for i in range(NTT)]
                cg_y = [CG("y%d" % i, NKL * 128, NT, F32) for i in range(NTG)]
                uT_d = dram_scr("um_scr", [NKL * 128, T], BF16)
            else:
                uT_d = dram_scr("uT_scr", [D, T], BF16)
            if pair:
                y_d = None
            elif dbg == "y":
                y_d = nc.dram_tensor("dbg_y", [D, T], F32, kind="ExternalOutput").ap()
            else:
                y_d = dram_scr("y_scr", [D, T])
            u_buf = Buf("u")
            y_buf = Buf("y")
        if any(ph.startswith("att") for ph in phases):
            attnwi_in = dram_in("attn_w_in", [D, 3 * NHL * 256])
            attnwo_in = dram_in("attn_w_out", [D, D])
            ropec_in = dram_in("ropec", [32, 4])
            rotm_in = dram_in("rotm", [32, 32])
            qkg_in = dram_in("qkg", [128, 2])
            subg_in = dram_in("subg", [128, 256])
            qkgrep_in = dram_in("qkgrep", [128, 2, 128])
            lamrep_in = dram_in("lamrep", [128, 4, 128])
            mask_in = dram_in("maskc", [128, 2, 256], BF16)
            qkT_d = dram_scr("qkT_scr", [4 * NHL, 128, T], BF16)
            V_d = dram_scr("V_scr", [T, NHL * 256], BF16)
            OTC = 1024
            if pair:
                cg_h = [CG("hT%d" % i, D, NT, BF16) for i in range(NTT)]
                cg_o = [CG("oT%d" % i, NHL * 256, OTC, BF16) for i in range(T // OTC)]
            else:
                oT_d = dram_scr("oT_scr", [D, T], BF16)
            qk_buf = Buf("qk")
            v_buf = Buf("v")
            oT_buf = Buf("oT")
        out_d = nc.dram_tensor("out", [TL, D], F32, kind="ExternalOutput").ap()
        xT_d = dram_scr("xT_scr", [D, TL])
        if pair:
            rankf_in = dram_in("rankf", [128, 2])

        xT_buf = Buf("xT_d")
        out_buf = Buf("out_d")
        in_buf = Buf("inputs")

        ident = sb("ident", [128, 128])
        identb = sb("identb", [128, 128], BF16)
        onesb = sb("onesb", [128, 128], BF16)
        vecrows = sb("vecrows", [128, 128])
        normgT = sb("normgT", [128, 96])
        condT = sb("condT", [128, 16, 2])
        adabT = [sb("adabT%d" % l, [128, 144]) for l in range(2)]
        modT = [sb("modT%d" % l, [128, 144]) for l in range(2)]
        modA = [sb("modA%d" % l, [128, 3, 16]) for l in range(2)]
        modG = [sb("modG%d" % l, [128, 3, 16]) for l in range(2)]

        rankf = sb("rankf", [128, 2], F32)
        if pair:
            P.dma("sp", rankf.t[:, :], rankf_in[:, :], reads=[in_buf], writes=rankf.b)

        def select2(dst_ap, c0_ap, c1_ap, tmp_ap, reads, writes, eng="dve"):
            P.op(eng, lambda e: e.tensor_scalar(tmp_ap, c0_ap, rankf.t[:, 0:1], None, ALU.mult),
                 reads=list(reads) + rankf.b, writes=list(writes))
            P.op(eng, lambda e: e.scalar_tensor_tensor(dst_ap, c1_ap, rankf.t[:, 1:2], tmp_ap, ALU.mult, ALU.add),
                 reads=list(reads) + rankf.b, writes=list(writes))

        pbank = [ps("pb%d" % i, [128, 512]) for i in range(7)]
        pbf = ps("pbf", [128, 1024], BF16)

        P.dma("sp", ident.t[:], ident_in[:, :], reads=[in_buf], writes=ident.b)
        P.op("dve", lambda e: e.tensor_copy(identb.t[:], ident.t[:]), reads=ident.b, writes=identb.b)
        P.op("dve", lambda e: e.memset(onesb.t[:], 1.0), writes=onesb.b)

        def transpose_rows(src_ap_dram, nrows, dst_ap, dst_bufs, bank):
            P.dma("sp", vecrows.t[0:nrows, :], src_ap_dram, reads=[in_buf], writes=vecrows.b)
            P.op("pe", lambda e: e.transpose(bank.t[:, 0:nrows], vecrows.t[0:nrows, :], ident.t[0:nrows, 0:nrows]),
                 reads=vecrows.b + ident.b, writes=bank.b)
            P.op("dve", lambda e: e.tensor_copy(dst_ap, bank.t[:, 0:nrows]), reads=bank.b, writes=dst_bufs)

        transpose_rows(normg_in[:, :], 96, normgT.t[:, :], normgT.b, pbank[0])
        transpose_rows(c_in[:, :], 16, condT.t[:, :, 0], condT.b, pbank[1])
        P.op("act", lambda e: e.activation(condT.t[:, :, 0], condT.t[:, :, 0], AF.Silu), reads=condT.b, writes=condT.b)
        P.op("act", lambda e: e.activation(condT.t[:, :, 1], condT.t[:, :, 0], AF.Copy), reads=condT.b, writes=condT.b)
        for l in range(2):
            if pair:
                transpose_rows(adab_in[l, 0:NJ, :], NJ, adabT[l].t[:, 0:NJ], adabT[l].b, pbank[2])
            else:
                transpose_rows(adab_in[l, 0:128, :], 128, adabT[l].t[:, 0:128], adabT[l].b, pbank[2])
                transpose_rows(adab_in[l, 128:144, :], 16, adabT[l].t[:, 128:144], adabT[l].b, pbank[3])

        CB = 512
        sc_ada = Scope()
        sc_ada.__enter__()
        adaw = [sb("adaw%d" % i, [128, 16, CB]) for i in range(2)]
        nblk = NJ * 128 // CB
        if pair:
            cg_mod = CG("mod", 128, 2 * NJ, F32)
            modh = sb("modh", [128, 2, NJ])
        for l in layers:
            bank = pbank[4 + l]
            for blk in range(nblk):
                wt = adaw[blk % 2]
                src = adaw_in[l][:, blk * CB:(blk + 1) * CB].rearrange("(kc p) c -> p kc c", p=128)
                P.dma("sp", wt.t[:, 0:8, :], src[:, 0:8, :], reads=[in_buf], writes=wt.b)
                P.dma("act", wt.t[:, 8:16, :], src[:, 8:16, :], reads=[in_buf], writes=wt.b)
                for jj in range(CB // 128):
                    j = blk * (CB // 128) + jj
                    for kc in range(KC):
                        P.op("pe", lambda e, wt=wt, jj=jj, kc=kc, j=j, bank=bank: e.matmul(
                            bank.t[:, 2 * j:2 * j + 2], wt.t[:, kc, jj * 128:(jj + 1) * 128], condT.t[:, kc, :],
                            start=(kc == 0), stop=(kc == KC - 1)),
                            reads=wt.b + condT.b, writes=bank.b)
            if pair:
                P.op("dve", lambda e, l=l, bank=bank: e.tensor_tensor(
                    modh.t[:, l, :], bank.t[:, 0:2 * NJ].rearrange("p (j two) -> p j two", two=2)[:, :, 0], adabT[l].t[:, 0:NJ], ALU.add),
                    reads=bank.b + adabT[l].b, writes=modh.b)
                continue
            P.op("dve", lambda e, l=l, bank=bank: e.tensor_tensor(
                modT[l].t[:, :], bank.t[:, 0:288].rearrange("p (j two) -> p j two", two=2)[:, :, 0], adabT[l].t[:, :], ALU.add),
                reads=bank.b + adabT[l].b, writes=modT[l].b)
        if pair:
            P.dma("sp", cg_mod.i[:, :], modh.t[:, :, :].rearrange("p l j -> p (l j)"), reads=modh.b, writes=[cg_mod.ib])
            cg_mod.go()
            for l in layers:
                for r_ in range(2):
                    P.dma(("sp", "act")[r_], modT[l].t[:, r_ * NJ:(r_ + 1) * NJ], cg_mod.o[r_ * 128:(r_ + 1) * 128, l * NJ:(l + 1) * NJ],
                          reads=[cg_mod.ob], writes=modT[l].b)
        for l in layers:
            for s in range(3):
                sh = (s * 3 + 0) * 16
                sc = (s * 3 + 1) * 16
                ga = (s * 3 + 2) * 16
                P.op("dve", lambda e, l=l, s=s, sc=sc: e.scalar_tensor_tensor(
                    modA[l].t[:, s, :], modT[l].t[:, sc:sc + 16], 1.0, normgT.t[:, (l * 3 + s) * 16:(l * 3 + s) * 16 + 16],
                    ALU.add, ALU.mult), reads=modT[l].b + normgT.b, writes=modA[l].b)
                wgt = 1.0 if s == 1 else 0.5
                P.op("dve", lambda e, l=l, s=s, ga=ga, wgt=wgt: e.tensor_scalar(
                    modG[l].t[:, s, :], modT[l].t[:, ga:ga + 16], wgt, None, ALU.mult),
                    reads=modT[l].b, writes=modG[l].b)

        sc_ada.__exit__(None, None, None)

        xT = sb("xT", [128, KC, NT], F32)
        hT = sb("hT", [128, KC, NT], BF16)
        tmpf = [sb("tmpf%d" % i, [128, NT], F32) for i in range(3)]
        tmpb = [sb("tmpb%d" % i, [128, NT], BF16) for i in range(3)]
        rstd = sb("rstd", [128, NT], F32)
        xtok = []
        wst = []
        wbf = []
        wost = []
        wobf = []
        gTh = [None]

        def alloc_ffn_tiles():
            gTh[0] = sb("gT", [128, FC, NT], BF16, nsub=FC)
            xtok[:] = [sb("xtok%d" % i, [128, D], F32) for i in range(2)]
            wst[:] = [sb("wst%d" % i, [128, KC, 128], F32) for i in range(2)]
            wbf[:] = [sb("wbf%d" % i, [128, KC, 128], BF16) for i in range(5)]
            wost[:] = [sb("wost%d" % i, [128, 22, 128], F32) for i in range(2)]
            wobf[:] = [sb("wobf%d" % i, [128, 22, 128], BF16) for i in range(3)]
        ctr = {"tmpf": 0, "tmpb": 0, "w": 0, "wb": 0, "wo": 0, "wob": 0, "pb": 0, "xtok": 0, "cast": 0, "ldq": 0}

        def rot(lst, key):
            i = ctr[key]
            ctr[key] = i + 1
            return lst[i % len(lst)]

        mm_banks = pbank[0:4]
        ssq_bank = pbank[4]
        tr_banks = pbank[5:7]

        def load_xT_tile(tt, from_input):
            t0 = tt * NT
            if not from_input:
                src = xT_d[:, t0:t0 + NT].rearrange("(kc p) t -> p kc t", p=128)
                P.dma("sp", xT.t[:, 0:8, :], src[:, 0:8, :], reads=[xT_buf], writes=xT.b)
                P.dma("act", xT.t[:, 8:16, :], src[:, 8:16, :], reads=[xT_buf], writes=xT.b)
                return
            for sub in range(NT // 128):
                xt = rot(xtok, "xtok")
                P.dma("sp", xt.t[:, :], x_in[t0 + sub * 128:t0 + (sub + 1) * 128, :], reads=[in_buf], writes=xt.b)
                for kc in range(KC):
                    bank = tr_banks[kc % 2]
                    P.op("pe", lambda e, xt=xt, kc=kc, bank=bank: e.transpose(
                        bank.t[:, 0:128], xt.t[:, kc * 128:(kc + 1) * 128], ident.t[:, :]),
                        reads=xt.b + ident.b, writes=bank.b)
                    eng = "dve" if kc % 2 == 0 else "act"
                    if eng == "dve":
                        P.op("dve", lambda e, kc=kc, sub=sub, bank=bank: e.tensor_copy(
                            xT.t[:, kc, sub * 128:(sub + 1) * 128], bank.t[:, 0:128]), reads=bank.b, writes=xT.b)
                    else:
                        P.op("act", lambda e, kc=kc, sub=sub, bank=bank: e.activation(
                            xT.t[:, kc, sub * 128:(sub + 1) * 128], bank.t[:, 0:128], AF.Copy), reads=bank.b, writes=xT.b)

        def store_xT_tile(tt, to_output):
            t0 = tt * NT
            if not to_output:
                dst = xT_d[:, t0:t0 + NT].rearrange("(kc p) t -> p kc t", p=128)
                P.dma("pool", dst[:, :, :], xT.t[:, :, :], reads=xT.b, writes=[xT_buf])
                return
            for sub in range(NT // 128):
                xt = rot(xtok, "xtok")
                for kc in range(KC):
                    bank = tr_banks[kc % 2]
                    P.op("pe", lambda e, kc=kc, sub=sub, bank=bank: e.transpose(
                        bank.t[:, 0:128], xT.t[:, kc, sub * 128:(sub + 1) * 128], ident.t[:, :]),
                        reads=xT.b + ident.b, writes=bank.b)
                    if kc % 2 == 0:
                        P.op("dve", lambda e, kc=kc, xt=xt, bank=bank: e.tensor_copy(
                            xt.t[:, kc * 128:(kc + 1) * 128], bank.t[:, 0:128]), reads=bank.b, writes=xt.b)
                    else:
                        P.op("act", lambda e, kc=kc, xt=xt, bank=bank: e.activation(
                            xt.t[:, kc * 128:(kc + 1) * 128], bank.t[:, 0:128], AF.Copy), reads=bank.b, writes=xt.b)
                P.dma("pool", out_d[t0 + sub * 128:t0 + (sub + 1) * 128, :], xt.t[:, :], reads=xt.b, writes=[out_buf])

        def norm_mod(l, s, h32=None, want_bf=True):
            for kc in range(KC):
                sq = rot(tmpb, "tmpb")
                P.op("act", lambda e, kc=kc, sq=sq: e.activation(sq.t[:, :], xT.t[:, kc, :], AF.Square),
                     reads=xT.b, writes=sq.b)
                P.op("pe", lambda e, kc=kc, sq=sq: e.matmul(ssq_bank.t[:, :], onesb.t[:, :], sq.t[:, :],
                                                           start=(kc == 0), stop=(kc == KC - 1)),
                     reads=sq.b + onesb.b, writes=ssq_bank.b)
            P.op("dve", lambda e: e.tensor_scalar(rstd.t[:, :], ssq_bank.t[:, :], 1.0 / D, EPS, ALU.mult, ALU.add),
                 reads=ssq_bank.b, writes=rstd.b)
            P.op("act", lambda e: e.activation(rstd.t[:, :], rstd.t[:, :], AF.Sqrt), reads=rstd.b, writes=rstd.b)
            P.op("dve", lambda e: e.reciprocal(rstd.t[:, :], rstd.t[:, :]), reads=rstd.b, writes=rstd.b)
            sh = (s * 3 + 0) * 16
            for kc in range(KC):
                tf = rot(tmpf, "tmpf")
                P.op("dve", lambda e, kc=kc, tf=tf: e.scalar_tensor_tensor(
                    tf.t[:, :], xT.t[:, kc, :], modA[l].t[:, s, kc:kc + 1], rstd.t[:, :], ALU.mult, ALU.mult),
                    reads=xT.b + modA[l].b + rstd.b, writes=tf.b)
                if h32 is not None:
                    P.op("act", lambda e, kc=kc, tf=tf: e.activation(
                        h32.t[:, kc, :], tf.t[:, :], AF.Identity, bias=modT[l].t[:, sh + kc:sh + kc + 1], scale=1.0),
                        reads=tf.b + modT[l].b, writes=h32.b)
                    if want_bf:
                        P.op("pool", lambda e, kc=kc: e.tensor_copy(hT.t[:, kc, :], h32.t[:, kc, :]), reads=h32.b, writes=hT.b)
                    continue
                P.op("act", lambda e, kc=kc, tf=tf: e.activation(
                    hT.t[:, kc, :], tf.t[:, :], AF.Identity, bias=modT[l].t[:, sh + kc:sh + kc + 1], scale=1.0),
                    reads=tf.b + modT[l].b, writes=hT.b)

        wcache = {}
        CAST_ENGS = ["act", "dve", "pool"]

        def cast_op(dst_ap, src_ap, reads, writes):
            e = CAST_ENGS[ctr["cast"] % 3]
            ctr["cast"] += 1
            if e == "act":
                P.op("act", lambda eng: eng.activation(dst_ap, src_ap, AF.Copy), reads=reads, writes=writes)
            else:
                P.op(e, lambda eng: eng.tensor_copy(dst_ap, src_ap), reads=reads, writes=writes)

        def ldq():
            q = ("sp", "pool")[ctr["ldq"] % 2]
            ctr["ldq"] += 1
            return q

        def load_w_chunk(w_ap, c0, wname=None):
            idx = c0 // 128
            wb = rot(wbf, "wb")
            bufs = None
            if wname is not None:
                if wname not in wcache:
                    wcache[wname] = (dram_scr("wc_" + wname, [w_ap.shape[1] // 128, 128, KC * 128], BF16), {})
                scr, bufs = wcache[wname]
            if bufs is not None and idx in bufs:
                P.dma(ldq(), wb.t[:, :, :], scr[idx].rearrange("p (k c) -> p k c", c=128), reads=[bufs[idx]], writes=wb.b)
                return wb
            st = rot(wst, "w")
            src = w_ap[:, c0:c0 + 128].rearrange("(kc p) c -> p kc c", p=128)
            P.dma("sp", st.t[:, :, :], src, reads=[in_buf], writes=st.b)
            cast_op(wb.t[:, :, :], st.t[:, :, :], st.b, wb.b)
            if bufs is not None:
                bufs[idx] = Buf("wc")
                P.dma("pool", scr[idx].rearrange("p (k c) -> p k c", c=128), wb.t[:, :, :], reads=wb.b, writes=[bufs[idx]])
            return wb

        def load_wo_half(w_out, dc, half, wname):
            wb = rot(wobf, "wob")
            if wname not in wcache:
                wcache[wname] = (dram_scr("wc_" + wname, [32, 128, 22 * 128], BF16), {})
            scr, bufs = wcache[wname]
            idx = dc * 2 + half
            if idx in bufs:
                P.dma(ldq(), wb.t[:, :, :], scr[idx].rearrange("p (k c) -> p k c", c=128), reads=[bufs[idx]], writes=wb.b)
                return wb
            st = rot(wost, "wo")
            src = w_out[half * 22 * 128:(half + 1) * 22 * 128, dc * 128:(dc + 1) * 128].rearrange("(j p) c -> p j c", p=128)
            P.dma("sp", st.t[:, :, :], src, reads=[in_buf], writes=st.b)
            cast_op(wb.t[:, :, :], st.t[:, :, :], st.b, wb.b)
            bufs[idx] = Buf("wc")
            P.dma("pool", scr[idx].rearrange("p (k c) -> p k c", c=128), wb.t[:, :, :], reads=wb.b, writes=[bufs[idx]])
            return wb

        def ffn(l, s, fi, first=False, last=False):
            w_in = ffnwi_in['%d%d' % (l, fi)]
            w_out = ffnwo_in['%d%d' % (l, fi)]
            sc = Scope()
            sc.__enter__()
            alloc_ffn_tiles()
            gT = gTh[0]
            for tt in range(NTT):
                load_xT_tile(tt, first)
                norm_mod(l, s)
                for j in range(FC):
                    wa = load_w_chunk(w_in, j * 128, "fi%d%d" % (l, fi))
                    wb_ = load_w_chunk(w_in, DFF + j * 128, "fi%d%d" % (l, fi))
                    pa = rot(mm_banks, "pb")
                    pb_ = rot(mm_banks, "pb")
                    for kc in range(KC):
                        P.op("pe", lambda e, wa=wa, kc=kc, pa=pa: e.matmul(
                            pa.t[:, :], wa.t[:, kc, :], hT.t[:, kc, :], start=(kc == 0), stop=(kc == KC - 1)),
                            reads=wa.b + hT.b, writes=pa.b)
                    for kc in range(KC):
                        P.op("pe", lambda e, wb_=wb_, kc=kc, pb_=pb_: e.matmul(
                            pb_.t[:, :], wb_.t[:, kc, :], hT.t[:, kc, :], start=(kc == 0), stop=(kc == KC - 1)),
                            reads=wb_.b + hT.b, writes=pb_.b)
                    tf = rot(tmpf, "tmpf")
                    P.op("act", lambda e, tf=tf, pa=pa: e.activation(tf.t[:, :], pa.t[:, :], AF.Silu),
                         reads=pa.b, writes=tf.b)
                    P.op("dve", lambda e, tf=tf, pb_=pb_, j=j: e.tensor_tensor(
                        gT.t[:, j, :], tf.t[:, :], pb_.t[:, :], ALU.mult),
                        reads=tf.b + pb_.b, writes=[gT.b[j]])
                for dc in range(KC):
                    po = rot(mm_banks, "pb")
                    for half in range(2):
                        wb = load_wo_half(w_out, dc, half, "fo%d%d" % (l, fi))
                        for jj in range(22):
                            j = half * 22 + jj
                            P.op("pe", lambda e, wb=wb, jj=jj, j=j, po=po: e.matmul(
                                po.t[:, :], wb.t[:, jj, :], gT.t[:, j, :], start=(j == 0), stop=(j == FC - 1)),
                                reads=wb.b + [gT.b[j]], writes=po.b)
                    P.op("dve", lambda e, dc=dc, po=po: e.scalar_tensor_tensor(
                        xT.t[:, dc, :], po.t[:, :], modG[l].t[:, s, dc:dc + 1], xT.t[:, dc, :], ALU.mult, ALU.add),
                        reads=po.b + modG[l].b + xT.b, writes=xT.b)
                store_xT_tile(tt, last)
            sc.__exit__(None, None, None)


        def attention(l, first=False, last=False):
            s = 1
            lambda_init = 0.8 - 0.6 * math.exp(-0.3 * l)
            scale = HD ** -0.5
            w_in = attnwi_in
            w_out = attnwo_in
            NQB = T // 256
            NKT = T // 128
            NQC = NHL * 2
            sc = Scope()
            sc.__enter__()
            if pair:
                scH = Scope()
                scH.__enter__()
                if first:
                    xtok[:] = [sb("xtok%d" % i, [128, D], F32) for i in range(2)]
                for tt in range(NTT):
                    load_xT_tile(tt, first)
                    if first:
                        store_xT_tile(tt, False)
                    norm_mod(l, s)
                    P.dma("pool", cg_h[tt].i[:, :].rearrange("(kc p) t -> p kc t", p=128), hT.t[:, :, :],
                          reads=hT.b, writes=[cg_h[tt].ib])
                    cg_h[tt].go()
                scH.__exit__(None, None, None)
            ropec = sb("ropec", [32, 4], F32)
            rotm = sb("rotm", [32, 32], F32)
            qkg = sb("qkg", [128, 2], F32)
            negmb = sb("negmb", [128, 1], F32)
            nlam = sb("nlam", [128, 1], F32)
            subg = sb("subg", [128, 256], F32)
            P.dma("sp", ropec.t[:, :], ropec_in[:, :], reads=[in_buf], writes=ropec.b)
            P.dma("sp", rotm.t[:, :], rotm_in[:, :], reads=[in_buf], writes=rotm.b)
            P.dma("sp", qkg.t[:, :], qkg_in[:, :], reads=[in_buf], writes=qkg.b)
            P.dma("sp", subg.t[:, :], subg_in[:, :], reads=[in_buf], writes=subg.b)
            P.op("dve", lambda e: e.tensor_scalar(subg.t[:, :], subg.t[:, :], 1.0 - lambda_init, None, ALU.mult),
                 reads=subg.b, writes=subg.b)
            scA = Scope()
            scA.__enter__()
            wst[:] = [sb("wst%d" % i, [128, KC, 128], F32) for i in range(2)]
            wbf[:] = [sb("wbf%d" % i, [128, KC, 128], BF16) for i in range(5)]
            cosT = sb("cosT", [32, T], F32)
            sinT = sb("sinT", [32, T], F32)
            if first:
                xtok[:] = [sb("xtok%d" % i, [128, D], F32) for i in range(2)]
            sc2 = Scope()
            sc2.__enter__()
            grep_ = sb("grep", [128, 2, 128], F32)
            lrep = sb("lrep", [128, 4, 128], F32)
            sm = sb("sm", [128, 8], F32)
            posi = sb("posi", [32, T], I32)
            ang = sb("ang", [32, T], F32)
            ang2 = sb("ang2", [32, T], F32)
            P.dma("sp", grep_.t[:, :, :], qkgrep_in[:, :, :], reads=[in_buf], writes=grep_.b)
            P.dma("sp", lrep.t[:, :, :], lamrep_in[:, :, :], reads=[in_buf], writes=lrep.b)
            P.dma("sp", posi.t[:, :], pos_in[:, :], reads=[in_buf], writes=posi.b)
            for i in range(2):
                P.op("dve", lambda e, i=i: e.reduce_max(sm.t[:, i:i + 1], grep_.t[:, i, :], AX.X, apply_absolute_value=True),
                     reads=grep_.b, writes=sm.b)
            P.op("dve", lambda e: e.tensor_tensor(sm.t[:, 2:3], sm.t[:, 0:1], sm.t[:, 1:2], ALU.mult), reads=sm.b, writes=sm.b)
            P.op("dve", lambda e: e.tensor_scalar(negmb.t[:, :], sm.t[:, 2:3], -(HD * scale), None, ALU.mult),
                 reads=sm.b, writes=negmb.b)
            for i in range(2):
                P.op("dve", lambda e, i=i: e.tensor_tensor(grep_.t[:, i, :], lrep.t[:, 2 * i, :], lrep.t[:, 2 * i + 1, :], ALU.mult),
                     reads=lrep.b + grep_.b, writes=grep_.b)
                P.op("dve", lambda e, i=i: e.reduce_sum(sm.t[:, 3 + i:4 + i], grep_.t[:, i, :], AX.X), reads=grep_.b, writes=sm.b)
                P.op("act", lambda e, i=i: e.activation(sm.t[:, 5 + i:6 + i], sm.t[:, 3 + i:4 + i], AF.Exp), reads=sm.b, writes=sm.b)
            P.op("dve", lambda e: e.scalar_tensor_tensor(nlam.t[:, :], sm.t[:, 6:7], -lambda_init, sm.t[:, 5:6], ALU.add, ALU.subtract),
                 reads=sm.b, writes=nlam.b)
            P.op("dve", lambda e: e.tensor_copy(ang.t[:, :], posi.t[:, :]), reads=posi.b, writes=ang.b)
            P.op("dve", lambda e: e.tensor_scalar(ang.t[:, :], ang.t[:, :], ropec.t[:, 0:1], None, ALU.mult),
                 reads=ang.b + ropec.b, writes=ang.b)
            P.op("dve", lambda e: e.tensor_scalar(ang2.t[:, :], ang.t[:, :], 1.0 / TWO_PI, None, ALU.mult), reads=ang.b, writes=ang2.b)
            P.op("dve", lambda e: e.tensor_copy(posi.t[:, :], ang2.t[:, :]), reads=ang2.b, writes=posi.b)
            P.op("dve", lambda e: e.tensor_copy(ang2.t[:, :], posi.t[:, :]), reads=posi.b, writes=ang2.b)
            P.op("dve", lambda e: e.scalar_tensor_tensor(ang.t[:, :], ang2.t[:, :], -TWO_PI, ang.t[:, :], ALU.mult, ALU.add),
                 reads=ang2.b + ang.b, writes=ang.b)

            def wrap_pi():
                P.op("dve", lambda e: e.tensor_scalar(ang2.t[:, :], ang.t[:, :], math.pi, TWO_PI, ALU.is_gt, ALU.mult),
                     reads=ang.b, writes=ang2.b)
                P.op("dve", lambda e: e.tensor_tensor(ang.t[:, :], ang.t[:, :], ang2.t[:, :], ALU.subtract),
                     reads=ang.b + ang2.b, writes=ang.b)
            wrap_pi()
            P.op("act", lambda e: e.activation(sinT.t[:, :], ang.t[:, :], AF.Sin, scale=ropec.t[:, 1:2]),
                 reads=ang.b + ropec.b, writes=sinT.b)
            P.op("dve", lambda e: e.tensor_scalar(ang.t[:, :], ang.t[:, :], 0.5 * math.pi, None, ALU.add), reads=ang.b + sinT.b, writes=ang.b)
            wrap_pi()
            P.op("act", lambda e: e.activation(cosT.t[:, :], ang.t[:, :], AF.Sin), reads=ang.b, writes=cosT.b)
            sc2.__exit__(None, None, None)

            qn = [sb("qn%d" % i, [128, NT], F32) for i in range(2)]
            rr = [sb("rr%d" % i, [128, NT], F32) for i in range(2)]
            rt1 = sb("rt1", [32, NT], F32)
            rt2 = sb("rt2", [32, NT], F32)
            qkb = [sb("qkb%d" % i, [128, NT], BF16) for i in range(3)]
            vtile = sb("vtile", [128, 4, NHL * 256], BF16)
            actr = {"qn": 0, "qkb": 0}
            qk_bufs = [Buf("qk%d" % i) for i in range(4 * NHL)]
            for tt in range(NTG):
                t0 = tt * NT
                if pair:
                    cgx = cg_h[tt % NTT]
                    src = cgx.o[(tt // NTT) * D:(tt // NTT + 1) * D, :].rearrange("(kc p) t -> p kc t", p=128)
                    P.dma("sp", hT.t[:, :, :], src, reads=[cgx.ob], writes=hT.b)
                else:
                    load_xT_tile(tt, first)
                    if first:
                        store_xT_tile(tt, False)
                    norm_mod(l, s)
                def stA(ch):
                    w_ = load_w_chunk(w_in, ch * 128, "awi")
                    p_ = rot(mm_banks, "pb")
                    for kc in range(KC):
                        P.op("pe", lambda e, w_=w_, kc=kc, p_=p_: e.matmul(
                            p_.t[:, :], w_.t[:, kc, :], hT.t[:, kc, :], start=(kc == 0), stop=(kc == KC - 1)),
                            reads=w_.b + hT.b, writes=p_.b)
                    return {"ch": ch, "p": p_}

                def stB(st):
                    ch, pq = st["ch"], st["p"]
                    if ch >= 2 * NQC:
                        vb = qkb[actr["qkb"] % 3]
                        actr["qkb"] += 1
                        P.op("act", lambda e, vb=vb, pq=pq: e.activation(vb.t[:, :], pq.t[:, :], AF.Copy), reads=pq.b, writes=vb.b)
                        for sub in range(4):
                            P.op("pe", lambda e, vb=vb, sub=sub: e.transpose(
                                pbf.t[:, sub * 128:(sub + 1) * 128], vb.t[:, sub * 128:(sub + 1) * 128], identb.t[:, :]),
                                reads=vb.b + identb.b, writes=pbf.b)
                        vc = ch - 2 * NQC
                        P.op("dve", lambda e, vc=vc: e.tensor_copy(
                            vtile.t[:, :, vc * 128:(vc + 1) * 128], pbf.t[:, 0:512].rearrange("p (s c) -> p s c", c=128)),
                            reads=pbf.b, writes=vtile.b)
                        return
                    sq = rot(tmpb, "tmpb")
                    P.op("act", lambda e, sq=sq, pq=pq: e.activation(sq.t[:, :], pq.t[:, :], AF.Square), reads=pq.b, writes=sq.b)
                    P.op("pe", lambda e, sq=sq: e.matmul(ssq_bank.t[:, :], onesb.t[:, :], sq.t[:, :], start=True, stop=True),
                         reads=sq.b + onesb.b, writes=ssq_bank.b)
                    r = rr[ch % 2]
                    q = qn[ch % 2]
                    st["q"] = q
                    P.op("dve", lambda e, r=r: e.tensor_scalar(r.t[:, :], ssq_bank.t[:, :], 1.0 / HD, EPS, ALU.mult, ALU.add),
                         reads=ssq_bank.b, writes=r.b)
                    P.op("act", lambda e, r=r: e.activation(r.t[:, :], r.t[:, :], AF.Sqrt), reads=r.b, writes=r.b)
                    P.op("dve", lambda e, r=r: e.reciprocal(r.t[:, :], r.t[:, :]), reads=r.b, writes=r.b)
                    gi = 0 if ch < NQC else 1
                    P.op("dve", lambda e, q=q, r=r, pq=pq, gi=gi: e.scalar_tensor_tensor(
                        q.t[:, :], pq.t[:, :], qkg.t[:, gi:gi + 1], r.t[:, :], ALU.mult, ALU.mult),
                        reads=pq.b + qkg.b + r.b, writes=q.b)
                    rb = tr_banks[ch % 2]
                    st["rb"] = rb
                    P.op("pe", lambda e, q=q, rb=rb: e.matmul(rb.t[0:32, :], rotm.t[:, :], q.t[0:32, :], start=True, stop=True),
                         reads=q.b + rotm.b, writes=rb.b)

                def stC(st):
                    ch = st["ch"]
                    if ch >= 2 * NQC:
                        return
                    q, rb = st["q"], st["rb"]
                    P.op("pool", lambda e, q=q, t0=t0: e.tensor_tensor(rt1.t[:, :], q.t[0:32, :], cosT.t[:, t0:t0 + NT], ALU.mult),
                         reads=q.b + cosT.b, writes=rt1.b)
                    P.op("dve", lambda e, rb=rb, t0=t0: e.tensor_tensor(rt2.t[:, :], rb.t[0:32, :], sinT.t[:, t0:t0 + NT], ALU.mult),
                         reads=rb.b + sinT.b, writes=rt2.b)
                    P.op("pool", lambda e, q=q: e.tensor_tensor(q.t[0:32, :], rt1.t[:, :], rt2.t[:, :], ALU.add),
                         reads=rt1.b + rt2.b, writes=q.b)
                    qb_ = qkb[actr["qkb"] % 3]
                    actr["qkb"] += 1
                    P.op("act", lambda e, q=q, qb_=qb_: e.activation(qb_.t[:, :], q.t[:, :], AF.Copy), reads=q.b, writes=qb_.b)
                    P.dma("act", qkT_d[ch, :, t0:t0 + NT], qb_.t[:, :], reads=qb_.b, writes=[qk_bufs[ch]])

                sts = []
                nch = 3 * NQC
                for i in range(nch + 2):
                    if i < nch:
                        sts.append(stA(i))
                    if 1 <= i <= nch:
                        stB(sts[i - 1])
                    if i >= 2:
                        stC(sts[i - 2])
                P.dma("act", V_d[t0:t0 + NT, :].rearrange("(s p) c -> p s c", p=128), vtile.t[:, :, :],
                      reads=vtile.b, writes=[v_buf])
            scA.__exit__(None, None, None)

            scB = Scope()
            scB.__enter__()
            qk_t = [[sb("qk_%d_%d" % (i, j), [128, T], BF16) for j in range(4)] for i in range(2)]
            v_t = [sb("v_%d" % i, [128, NKT, 257], BF16) for i in range(2)]
            maskt = sb("maskt", [128, 2, 256], BF16)
            pT = [sb("pT%d" % i, [128, 256], BF16) for i in range(5)]
            oacc = [[sb("oacc%d_%d" % (c, sub), [128, 257], F32) for sub in range(2)] for c in range(2)]
            sm2 = [sb("sm2_%d" % i, [128, 8], F32) for i in range(2)]
            ot = [sb("ot%d" % i, [128, 256], F32) for i in range(2)]
            osq = sb("osq", [128, 256], F32)
            onb = [sb("onb%d" % i, [128, 256], BF16) for i in range(2)]
            oTt = [sb("oTt%d" % i, [128, 2, 128], BF16) for i in range(2)]
            P.dma("sp", maskt.t[:, :, :], mask_in[:, :, :], reads=[in_buf], writes=maskt.b)
            for i in range(2):
                P.op("dve", lambda e, i=i: e.memset(v_t[i].t[:, :, 256:257], 1.0), writes=v_t[i].b)
            acc_banks = [[pbank[0], pbank[1]], [pbank[2], pbank[3]]]
            sc_banks = [pbank[4], pbank[5], pbank[6]]
            bctr = {"sc": 0, "pT": 0, "o": 0}
            from collections import deque
            LA = 2
            pendq = deque()

            def front(h, qb, c, kt, qkh):
                q0 = qb * 256
                sb_ = sc_banks[bctr["sc"] % 3]
                bctr["sc"] += 1
                P.op("pe", lambda e, sb_=sb_, kt=kt, c=c, q0=q0, qkh=qkh: e.matmul(
                    sb_.t[:, 0:256], qkh[2 + c].t[:, kt * 128:(kt + 1) * 128], qkh[c].t[:, q0:q0 + 256],
                    start=True, stop=True), reads=qkh[2 + c].b + qkh[c].b, writes=sb_.b)
                p_ = pT[bctr["pT"] % len(pT)]
                bctr["pT"] += 1
                P.op("act", lambda e, p_=p_, sb_=sb_: e.activation(
                    p_.t[:, :], sb_.t[:, 0:256], AF.Exp, bias=negmb.t[:, 0:1], scale=scale),
                    reads=sb_.b + negmb.b, writes=p_.b)
                if kt >= 2 * qb:
                    mi = kt - 2 * qb
                    P.op("dve", lambda e, p_=p_, mi=mi: e.tensor_tensor(p_.t[:, :], p_.t[:, :], maskt.t[:, mi, :], ALU.mult),
                         reads=p_.b + maskt.b, writes=p_.b)
                return p_

            def back(h, qb, c, kt, vh, p_):
                q0 = qb * 256
                nkt = 2 * qb + 2
                accs = acc_banks[c]
                for sub in range(2):
                    last_kt = 2 * qb + sub
                    if kt > last_kt:
                        continue
                    P.op("pe", lambda e, p_=p_, sub=sub, kt=kt, vh=vh, accs=accs, last_kt=last_kt: e.matmul(
                        accs[sub].t[:, 0:257], p_.t[:, sub * 128:(sub + 1) * 128], vh.t[:, kt, :],
                        start=(kt == 0), stop=(kt == last_kt)), reads=p_.b + vh.b, writes=accs[sub].b)
                if kt != nkt - 1:
                    return
                for sub in range(2):
                    if sub == 0:
                        P.op("act", lambda e, c=c, sub=sub, accs=accs: e.activation(
                            oacc[c][sub].t[:, :], accs[sub].t[:, 0:257], AF.Copy), reads=accs[sub].b, writes=oacc[c][sub].b)
                    else:
                        P.op("dve", lambda e, c=c, sub=sub, accs=accs: e.tensor_copy(
                            oacc[c][sub].t[:, :], accs[sub].t[:, 0:257]), reads=accs[sub].b, writes=oacc[c][sub].b)
                if c != 1:
                    return
                for sub in range(2):
                    k_ = bctr["o"] % 2
                    bctr["o"] += 1
                    sm_ = sm2[k_]
                    o_ = ot[k_]
                    on_ = onb[k_]
                    oT_ = oTt[k_]
                    o0 = oacc[0][sub]
                    o1 = oacc[1][sub]
                    P.op("dve", lambda e, sm_=sm_, o0=o0: e.reciprocal(sm_.t[:, 0:1], o0.t[:, 256:257]), reads=o0.b, writes=sm_.b)
                    P.op("dve", lambda e, sm_=sm_, o1=o1: e.reciprocal(sm_.t[:, 1:2], o1.t[:, 256:257]), reads=o1.b + sm_.b, writes=sm_.b)
                    P.op("dve", lambda e, sm_=sm_: e.tensor_tensor(sm_.t[:, 2:3], sm_.t[:, 1:2], nlam.t[:, 0:1], ALU.mult),
                         reads=sm_.b + nlam.b, writes=sm_.b)
                    P.op("dve", lambda e, sm_=sm_, o_=o_, o0=o0: e.tensor_scalar(
                        o_.t[:, :], o0.t[:, 0:256], sm_.t[:, 0:1], None, ALU.mult), reads=o0.b + sm_.b, writes=o_.b)
                    P.op("dve", lambda e, sm_=sm_, o_=o_, o1=o1: e.scalar_tensor_tensor(
                        o_.t[:, :], o1.t[:, 0:256], sm_.t[:, 2:3], o_.t[:, :], ALU.mult, ALU.add),
                        reads=o1.b + sm_.b + o_.b, writes=o_.b)
                    P.op("dve", lambda e, o_=o_: e.tensor_tensor(osq.t[:, :], o_.t[:, :], o_.t[:, :], ALU.mult), reads=o_.b, writes=osq.b)
                    P.op("dve", lambda e, sm_=sm_: e.reduce_sum(sm_.t[:, 3:4], osq.t[:, :], AX.X), reads=osq.b + sm_.b, writes=sm_.b)
                    P.op("dve", lambda e, sm_=sm_: e.tensor_scalar(sm_.t[:, 4:5], sm_.t[:, 3:4], 1.0 / 256, EPS, ALU.mult, ALU.add),
                         reads=sm_.b, writes=sm_.b)
                    P.op("act", lambda e, sm_=sm_: e.activation(sm_.t[:, 5:6], sm_.t[:, 4:5], AF.Sqrt), reads=sm_.b, writes=sm_.b)
                    P.op("dve", lambda e, sm_=sm_: e.reciprocal(sm_.t[:, 6:7], sm_.t[:, 5:6]), reads=sm_.b, writes=sm_.b)
                    P.op("dve", lambda e, sm_=sm_, o_=o_, on_=on_: e.scalar_tensor_tensor(
                        on_.t[:, :], o_.t[:, :], sm_.t[:, 6:7], subg.t[:, :], ALU.mult, ALU.mult),
                        reads=o_.b + sm_.b + subg.b, writes=on_.b)
                    for fc in range(2):
                        P.op("pe", lambda e, on_=on_, fc=fc: e.transpose(
                            pbf.t[:, fc * 128:(fc + 1) * 128], on_.t[:, fc * 128:(fc + 1) * 128], identb.t[:, :]),
                            reads=on_.b + identb.b, writes=pbf.b)
                    P.op("act", lambda e, oT_=oT_: e.activation(
                        oT_.t[:, :, :], pbf.t[:, 0:256].rearrange("p (f c) -> p f c", c=128), AF.Copy), reads=pbf.b, writes=oT_.b)
                    qs = q0 + sub * 128
                    if pair:
                        cgx = cg_o[qs // OTC]
                        P.dma("pool", cgx.i[h * 256:(h + 1) * 256, qs % OTC:qs % OTC + 128].rearrange("(f p) q -> p f q", p=128),
                              oT_.t[:, :, :], reads=oT_.b, writes=[cgx.ib])
                    else:
                        P.dma("pool", oT_d[h * 256:(h + 1) * 256, qs:qs + 128].rearrange("(f p) q -> p f q", p=128),
                              oT_.t[:, :, :], reads=oT_.b, writes=[oT_buf])

            for h in range(NHL):
                qkh = qk_t[h % 2]
                vh = v_t[h % 2]
                for c in range(2):
                    P.dma("sp", qkh[c].t[:, :], qkT_d[h * 2 + c, :, :], reads=[qk_bufs[h * 2 + c]], writes=qkh[c].b)
                    P.dma("sp", qkh[2 + c].t[:, :], qkT_d[NQC + h * 2 + c, :, :], reads=[qk_bufs[NQC + h * 2 + c]], writes=qkh[2 + c].b)
                P.dma("act", vh.t[:, :, 0:256], V_d[:, h * 256:(h + 1) * 256].rearrange("(kt p) c -> p kt c", p=128),
                      reads=[v_buf], writes=vh.b)
                for qb in range(NQB):
                    for c in range(2):
                        for kt in range(2 * qb + 2):
                            p_ = front(h, qb, c, kt, qkh)
                            pendq.append((h, qb, c, kt, vh, p_))
                            if len(pendq) > LA:
                                back(*pendq.popleft())
            while pendq:
                back(*pendq.popleft())
            scB.__exit__(None, None, None)

            if pair:
                for cgx in cg_o:
                    cgx.go()
            scC = Scope()
            scC.__enter__()
            if pair:
                ocand = [sb("ocand%d" % i, [128, KC, NT], BF16) for i in range(2)]
            wst[:] = [sb("wst%d" % i, [128, KC, 128], F32) for i in range(2)]
            wbf[:] = [sb("wbf%d" % i, [128, KC, 128], BF16) for i in range(5)]
            if last:
                xtok[:] = [sb("xtok%d" % i, [128, D], F32) for i in range(2)]
            for tt in range(NTT):
                t0 = tt * NT
                load_xT_tile(tt, False)
                if pair:
                    for r_ in range(2):
                        g0 = r_ * TL + t0
                        cgx = cg_o[g0 // OTC]
                        src = cgx.o[:, g0 % OTC:g0 % OTC + NT].rearrange("(kc p) t -> p kc t", p=128)
                        P.dma(("sp", "act")[r_], ocand[r_].t[:, :, :], src, reads=[cgx.ob], writes=ocand[r_].b)
                    select2(hT.t[:, :, :], ocand[0].t[:, :, :], ocand[1].t[:, :, :], ocand[0].t[:, :, :],
                            ocand[0].b + ocand[1].b, hT.b + ocand[0].b)
                else:
                    src = oT_d[:, t0:t0 + NT].rearrange("(kc p) t -> p kc t", p=128)
                    P.dma("sp", hT.t[:, :, :], src, reads=[oT_buf], writes=hT.b)
                for dc in range(KC):
                    wo = load_w_chunk(w_out, dc * 128, "awo")
                    po = rot(mm_banks, "pb")
                    for kc in range(KC):
                        P.op("pe", lambda e, wo=wo, kc=kc, po=po: e.matmul(
                            po.t[:, :], wo.t[:, kc, :], hT.t[:, kc, :], start=(kc == 0), stop=(kc == KC - 1)),
                            reads=wo.b + hT.b, writes=po.b)
                    P.op("dve", lambda e, dc=dc, po=po: e.scalar_tensor_tensor(
                        xT.t[:, dc, :], po.t[:, :], modG[l].t[:, s, dc:dc + 1], xT.t[:, dc, :], ALU.mult, ALU.add),
                        reads=po.b + modG[l].b + xT.b, writes=xT.b)
                store_xT_tile(tt, last)
            scC.__exit__(None, None, None)
            sc.__exit__(None, None, None)


        def ssm(l, first=False, last=False):
            s = 1
            sc = Scope()
            sc.__enter__()
            rho = sb("rho", [128, NGP], F32)
            fq = sb("fq", [128, NGP], F32)
            c512 = sb("c512", [128, NGP], F32)
            s512 = sb("s512", [128, NGP], F32)
            pm = sb("pm", [128, 2], F32)
            dT = sb("dT", [128, 16], F32)
            iota = sb("iota", [128, NT], F32)
            P.dma("sp", pm.t[:, :], pm_in[:, :], reads=[in_buf], writes=pm.b)
            P.dma("sp", dT.t[:, :], ssmd_in[:, :], reads=[in_buf], writes=dT.b)
            P.dma("sp", iota.t[:, :], iota_in[:, :], reads=[in_buf], writes=iota.b)
            scL = Scope()
            scL.__enter__()
            LB = [sb("LB%d" % i, [128, NGP, 128], BF16) for i in range(2)]
            LC = [sb("LC%d" % i, [128, NGP, 128], BF16) for i in range(2)]
            for i in range(2):
                P.op("pool", lambda e, i=i: e.memset(LB[i].t[:, :, :], 0.0), writes=LB[i].b)
                P.op("pool", lambda e, i=i: e.memset(LC[i].t[:, :, :], 0.0), writes=LC[i].b)

            def frac_wrap(dst, src, W, tmpi, tmpf_):
                P.op("dve", lambda e: e.tensor_copy(tmpi, src), reads=[gen_b], writes=[gen_b])
                P.op("dve", lambda e: e.tensor_copy(tmpf_, tmpi), reads=[gen_b], writes=[gen_b])
                P.op("dve", lambda e: e.tensor_tensor(dst, src, tmpf_, ALU.subtract), reads=[gen_b], writes=[gen_b])
                wrap_half(dst, tmpf_)

            def wrap_half(dst, tmpf_):
                P.op("dve", lambda e: e.tensor_scalar(tmpf_, dst, 0.5, None, ALU.is_gt), reads=[gen_b], writes=[gen_b])
                P.op("dve", lambda e: e.tensor_tensor(dst, dst, tmpf_, ALU.subtract), reads=[gen_b], writes=[gen_b])
                P.op("dve", lambda e: e.tensor_scalar(tmpf_, dst, -0.5, None, ALU.is_lt), reads=[gen_b], writes=[gen_b])
                P.op("dve", lambda e: e.tensor_tensor(dst, dst, tmpf_, ALU.add), reads=[gen_b], writes=[gen_b])

            gen_b = Buf("ssm_setup")

            def G(eng, fn):
                P.op(eng, fn, reads=[gen_b], writes=[gen_b])

            scS = Scope()
            scS.__enter__()
            ap_ = sb("a_pair", [128, 3, NGP], F32)
            P.dma("sp", ap_.t[:, :, :], apair_in[:, :, :], reads=[in_buf], writes=[gen_b])
            t64 = [sb("t64_%d" % i, [128, NGP], F32) for i in range(4)]
            t64i = sb("t64i", [128, NGP], I32)
            dtp, thp, fr64, tm64 = [t.t[:, :] for t in t64]
            G("act", lambda e: e.activation(dtp, ap_.t[:, 2, :], AF.Exp))
            G("dve", lambda e: e.tensor_tensor(thp, dtp, ap_.t[:, 0, :], ALU.mult))
            G("act", lambda e: e.activation(rho.t[:, :], thp, AF.Exp))
            G("dve", lambda e: e.tensor_tensor(thp, dtp, ap_.t[:, 1, :], ALU.mult))
            G("dve", lambda e: e.tensor_scalar(thp, thp, 1.0 / TWO_PI, None, ALU.mult))
            frac_wrap(fq.t[:, :], thp, NGP, t64i.t[:, :], tm64)
            G("dve", lambda e: e.tensor_scalar(thp, fq.t[:, :], float(NT), None, ALU.mult))
            frac_wrap(fr64, thp, NGP, t64i.t[:, :], tm64)
            G("act", lambda e: e.activation(s512.t[:, :], fr64, AF.Sin, scale=TWO_PI))
            G("dve", lambda e: e.tensor_scalar(fr64, fr64, 0.25, None, ALU.add))
            wrap_half(fr64, tm64)
            G("act", lambda e: e.activation(c512.t[:, :], fr64, AF.Sin, scale=TWO_PI))
            W = NKL * 64
            af = sb("a_feat", [128, 3, W], F32)
            bf_ = sb("b_feat", [128, 2, W], F32)
            P.dma("sp", af.t[:, :, :], afeat_in[:, :, :], reads=[in_buf], writes=[gen_b])
            P.dma("act", bf_.t[:, :, :], bfeat_in[:, :, :], reads=[in_buf], writes=[gen_b])
            tw = [sb("tw%d" % i, [128, W], F32) for i in range(8)]
            twi = sb("twi", [128, W], I32)
            dtf, mag, th, cs, sn, t5, t6, t7 = [t.t[:, :] for t in tw]
            ar = af.t[:, 0, :]
            ai = af.t[:, 1, :]
            G("act", lambda e: e.activation(dtf, af.t[:, 2, :], AF.Exp))
            G("dve", lambda e: e.tensor_tensor(th, dtf, ar, ALU.mult))
            G("act", lambda e: e.activation(mag, th, AF.Exp))
            G("dve", lambda e: e.tensor_tensor(th, dtf, ai, ALU.mult))
            G("dve", lambda e: e.tensor_scalar(th, th, 1.0 / TWO_PI, None, ALU.mult))
            frac_wrap(t5, th, W, twi.t[:, :], t6)
            G("act", lambda e: e.activation(sn, t5, AF.Sin, scale=TWO_PI))
            G("dve", lambda e: e.tensor_scalar(t5, t5, 0.25, None, ALU.add))
            wrap_half(t5, t6)
            G("act", lambda e: e.activation(cs, t5, AF.Sin, scale=TWO_PI))
            G("dve", lambda e: e.tensor_tensor(cs, cs, mag, ALU.mult))
            G("dve", lambda e: e.tensor_scalar(cs, cs, -1.0, None, ALU.add))
            G("dve", lambda e: e.tensor_tensor(sn, sn, mag, ALU.mult))
            G("dve", lambda e: e.tensor_tensor(t5, ar, ar, ALU.mult))
            G("dve", lambda e: e.tensor_tensor(t6, ai, ai, ALU.mult))
            G("dve", lambda e: e.tensor_tensor(t5, t5, t6, ALU.add))
            G("dve", lambda e: e.reciprocal(t5, t5))
            G("dve", lambda e: e.tensor_tensor(t6, cs, ar, ALU.mult))
            G("dve", lambda e: e.tensor_tensor(t7, sn, ai, ALU.mult))
            G("dve", lambda e: e.tensor_tensor(t6, t6, t7, ALU.add))
            G("dve", lambda e: e.tensor_tensor(t6, t6, t5, ALU.mult))
            G("dve", lambda e: e.tensor_tensor(t7, sn, ar, ALU.mult))
            G("dve", lambda e: e.tensor_tensor(mag, cs, ai, ALU.mult))
            G("dve", lambda e: e.tensor_tensor(t7, t7, mag, ALU.subtract))
            G("dve", lambda e: e.tensor_tensor(t7, t7, t5, ALU.mult))
            br = bf_.t[:, 0, :]
            bi = bf_.t[:, 1, :]
            G("dve", lambda e: e.tensor_tensor(cs, t6, br, ALU.mult))
            G("dve", lambda e: e.tensor_tensor(mag, t7, bi, ALU.mult))
            G("dve", lambda e: e.tensor_tensor(cs, cs, mag, ALU.subtract))
            G("dve", lambda e: e.tensor_tensor(sn, t6, bi, ALU.mult))
            G("dve", lambda e: e.tensor_tensor(mag, t7, br, ALU.mult))
            G("dve", lambda e: e.tensor_tensor(sn, sn, mag, ALU.add))
            bb = [tw[3].t, tw[4].t]
            for gp in range(NGP):
                kc, j = gp // 4, gp % 4
                r0 = 32 * j
                for ri in range(2):
                    for gi in range(2):
                        eng = "dve" if gi == 0 else "pool"
                        P.op(eng, lambda e, ri=ri, gi=gi, gp=gp, kc=kc, r0=r0: e.tensor_scalar(
                            LB[ri].t[r0:r0 + 32, gp, gi * 64:(gi + 1) * 64], bb[ri][r0:r0 + 32, kc * 64:(kc + 1) * 64],
                            pm.t[r0:r0 + 32, gi:gi + 1], None, ALU.mult),
                            reads=[gen_b] + pm.b, writes=LB[ri].b)
            cp = sb("c_pair", [128, 2, NGP, 16], F32)
            P.dma("sp", cp.t[:, :, :, :], cpair_in[:, :, :, :], reads=[in_buf], writes=[gen_b])
            for gp in range(NGP):
                j = gp % 4
                for ri in range(2):
                    for gi in range(2):
                        c0 = 32 * j + gi * 16
                        eng = "dve" if gi == 0 else "pool"
                        sgn = 1.0 if ri == 0 else -1.0
                        P.op(eng, lambda e, ri=ri, gi=gi, gp=gp, c0=c0, sgn=sgn: e.tensor_scalar(
                            LC[ri].t[gi * 64:(gi + 1) * 64, gp, c0:c0 + 16], cp.t[gi * 64:(gi + 1) * 64, ri, gp, :],
                            sgn, None, ALU.mult), reads=[gen_b], writes=LC[ri].b)
            scS.__exit__(None, None, None)

            scA = Scope()
            scA.__enter__()
            if first:
                xtok[:] = [sb("xtok%d" % i, [128, D], F32) for i in range(2)]
            for tt in range(NTT):
                t0 = tt * NT
                load_xT_tile(tt, first)
                if first:
                    store_xT_tile(tt, False)
                norm_mod(l, s)
                if pair:
                    P.dma("pool", cg_u[tt].i[:, :].rearrange("(kc p) t -> p kc t", p=128), hT.t[:, :, :],
                          reads=hT.b, writes=[cg_u[tt].ib])
                    cg_u[tt].go()
                else:
                    P.dma("pool", uT_d[:, t0:t0 + NT].rearrange("(kc p) t -> p kc t", p=128), hT.t[:, :, :],
                          reads=hT.b, writes=[u_buf])
            if pair:
                ucand = [[sb("ucand%d_%d" % (i, j), [128, NKL, NT], BF16) for j in range(2)] for i in range(2)]
                um_buf = Buf("um")
                it_ = 0
                for tt in range(NTT):
                    for r_ in range(2):
                        cs_ = ucand[it_ % 2]
                        it_ += 1
                        for j in range(2):
                            row0 = r_ * D + j * NKL * 128
                            P.dma(("sp", "act")[j], cs_[j].t[:, :, :],
                                  cg_u[tt].o[row0:row0 + NKL * 128, :].rearrange("(kc p) t -> p kc t", p=128),
                                  reads=[cg_u[tt].ob], writes=cs_[j].b)
                        select2(cs_[0].t[:, :, :], cs_[0].t[:, :, :], cs_[1].t[:, :, :], cs_[0].t[:, :, :], cs_[0].b + cs_[1].b, cs_[0].b)
                        g0 = r_ * TL + tt * NT
                        P.dma("pool", uT_d[:, g0:g0 + NT].rearrange("(kc p) t -> p kc t", p=128), cs_[0].t[:, :, :],
                              reads=cs_[0].b, writes=[um_buf])
                u_rd = um_buf
            else:
                u_rd = u_buf
            scA.__exit__(None, None, None)

            scB = Scope()
            scB.__enter__()
            cosTs = [sb("s_cos%d" % i, [128, NT], F32) for i in range(2)]
            sinTs = [sb("s_sin%d" % i, [128, NT], F32) for i in range(2)]
            rhobs = [sb("s_rhob%d" % i, [128, NT], F32) for i in range(2)]
            tg = [sb("s_tg%d" % i, [128, NT], F32) for i in range(2)]
            tgi = sb("s_tgi", [128, NT], I32)
            uch = [sb("s_u%d" % i, [128, NT], BF16) for i in range(3)]
            bsb = [[sb("s_b%d_%d" % (k, i), [128, NT], F32) for i in range(2)] for k in range(2)]
            fq_ = [[sb("s_f%d_%d" % (k, i), [128, NT], F32) for i in range(4)] for k in range(2)]
            bq_ = [sb("s_q%d" % i, [128, NT], F32) for i in range(4)]
            btr = [sb("s_btr%d" % i, [128, NT], F32) for i in range(2)]
            bti = [sb("s_bti%d" % i, [128, NT], F32) for i in range(2)]
            st_r = [sb("s_str%d" % i, [128, NT], F32) for i in range(2)]
            st_i = [sb("s_sti%d" % i, [128, NT], F32) for i in range(2)]
            sbf_r = [sb("s_sbr%d" % i, [128, NT], BF16) for i in range(2)]
            sbf_i = [sb("s_sbi%d" % i, [128, NT], BF16) for i in range(2)]
            y32 = [sb("s_y%d" % i, [128, NT], F32) for i in range(2)]
            init = [sb("s_init%d" % i, [128, 4], F32) for i in range(2)]
            b_banks = [[pbank[0], pbank[1]], [pbank[2], pbank[3]]]
            y_banks = [pbank[4], pbank[5]]

            def tables(gp):
                cosT, sinT, rhob = cosTs[gp % 2], sinTs[gp % 2], rhobs[gp % 2]
                fr = tg[0].t[:, :]
                tm = tg[1].t[:, :]
                tb = tg[0].b + tg[1].b + tgi.b
                P.op("dve", lambda e, gp=gp: e.tensor_scalar(tm, iota.t[:, :], fq.t[:, gp:gp + 1], None, ALU.mult),
                     reads=iota.b + fq.b + tb, writes=tb)

                def T_(eng, fn, extra_w=()):
                    P.op(eng, fn, reads=tb, writes=tb + list(extra_w))
                T_("dve", lambda e: e.tensor_copy(tgi.t[:, :], tm))
                T_("dve", lambda e: e.tensor_copy(fr, tgi.t[:, :]))
                T_("dve", lambda e: e.tensor_tensor(fr, tm, fr, ALU.subtract))

                def wrapT():
                    T_("dve", lambda e: e.tensor_scalar(tm, fr, 0.5, None, ALU.is_gt))
                    T_("dve", lambda e: e.tensor_tensor(fr, fr, tm, ALU.subtract))
                    T_("dve", lambda e: e.tensor_scalar(tm, fr, -0.5, None, ALU.is_lt))
                    T_("dve", lambda e: e.tensor_tensor(fr, fr, tm, ALU.add))
                wrapT()
                T_("act", lambda e: e.activation(sinT.t[:, :], fr, AF.Sin, scale=TWO_PI), extra_w=sinT.b)
                T_("dve", lambda e: e.tensor_scalar(fr, fr, 0.25, None, ALU.add))
                wrapT()
                T_("act", lambda e: e.activation(cosT.t[:, :], fr, AF.Sin, scale=TWO_PI), extra_w=cosT.b)
                P.op("pool", lambda e, gp=gp: e.tensor_scalar(rhob.t[:, :], iota.t[:, :], 0.0, rho.t[:, gp:gp + 1], ALU.mult, ALU.add),
                     reads=iota.b + rho.b, writes=rhob.b)
                ini = init[gp % 2]
                P.op("pool", lambda e, ini=ini: e.memset(ini.t[:, :], 0.0), writes=ini.b)

            def s_front(it, gp, tt):
                k = it % 2
                kc = gp // 4
                t0 = tt * NT
                cosT, sinT, rhob, ini = cosTs[gp % 2], sinTs[gp % 2], rhobs[gp % 2], init[gp % 2]
                u_ = uch[it % 3]
                P.dma("sp", u_.t[:, :], uT_d[kc * 128:(kc + 1) * 128, t0:t0 + NT], reads=[u_rd], writes=u_.b)
                pre, pim = b_banks[k]
                P.op("pe", lambda e: e.matmul(pre.t[:, :], LB[0].t[:, gp, :], u_.t[:, :], start=True, stop=True),
                     reads=LB[0].b + u_.b, writes=pre.b)
                P.op("pe", lambda e: e.matmul(pim.t[:, :], LB[1].t[:, gp, :], u_.t[:, :], start=True, stop=True),
                     reads=LB[1].b + u_.b, writes=pim.b)
                bre, bim = bsb[k]
                f1, f2, f3, f4 = fq_[k]
                P.op("act", lambda e: e.activation(bre.t[:, :], pre.t[:, :], AF.Copy), reads=pre.b, writes=bre.b)
                P.op("act", lambda e: e.activation(bim.t[:, :], pim.t[:, :], AF.Copy), reads=pim.b, writes=bim.b)
                P.op("dve", lambda e: e.tensor_tensor(f1.t[:, :], bre.t[:, :], cosT.t[:, :], ALU.mult), reads=bre.b + cosT.b, writes=f1.b)
                P.op("pool", lambda e: e.tensor_tensor(f2.t[:, :], bim.t[:, :], sinT.t[:, :], ALU.mult), reads=bim.b + sinT.b, writes=f2.b)
                P.op("pool", lambda e: e.tensor_tensor(f3.t[:, :], bim.t[:, :], cosT.t[:, :], ALU.mult), reads=bim.b + cosT.b, writes=f3.b)
                P.op("dve", lambda e: e.tensor_tensor(f4.t[:, :], bre.t[:, :], sinT.t[:, :], ALU.mult), reads=bre.b + sinT.b, writes=f4.b)
                br_, bi_ = btr[k], bti[k]
                P.op("dve", lambda e: e.tensor_tensor(br_.t[:, :], f1.t[:, :], f2.t[:, :], ALU.add), reads=f1.b + f2.b, writes=br_.b)
                P.op("pool", lambda e: e.tensor_tensor(bi_.t[:, :], f3.t[:, :], f4.t[:, :], ALU.subtract), reads=f3.b + f4.b, writes=bi_.b)
                sr, si = st_r[k], st_i[k]
                P.op("dve", lambda e: e.tensor_tensor_scan(sr.t[:, :], rhob.t[:, :], br_.t[:, :], ini.t[:, 0:1], ALU.mult, ALU.add),
                     reads=rhob.b + br_.b + ini.b, writes=sr.b)
                P.op("dve", lambda e: e.tensor_tensor_scan(si.t[:, :], rhob.t[:, :], bi_.t[:, :], ini.t[:, 1:2], ALU.mult, ALU.add),
                     reads=rhob.b + bi_.b + ini.b, writes=si.b)
                P.op("dve", lambda e: e.tensor_tensor(ini.t[:, 2:3], si.t[:, NT - 1:NT], s512.t[:, gp:gp + 1], ALU.mult),
                     reads=si.b + s512.b + ini.b, writes=ini.b)
                P.op("dve", lambda e: e.tensor_tensor(ini.t[:, 3:4], sr.t[:, NT - 1:NT], s512.t[:, gp:gp + 1], ALU.mult),
                     reads=sr.b + s512.b + ini.b, writes=ini.b)
                P.op("dve", lambda e: e.scalar_tensor_tensor(ini.t[:, 0:1], sr.t[:, NT - 1:NT], c512.t[:, gp:gp + 1], ini.t[:, 2:3],
                                                             ALU.mult, ALU.subtract), reads=sr.b + c512.b + ini.b, writes=ini.b)
                P.op("dve", lambda e: e.scalar_tensor_tensor(ini.t[:, 1:2], si.t[:, NT - 1:NT], c512.t[:, gp:gp + 1], ini.t[:, 3:4],
                                                             ALU.mult, ALU.add), reads=si.b + c512.b + ini.b, writes=ini.b)

            def s_back(it, gp, tt):
                k = it % 2
                kc, j = gp // 4, gp % 4
                r0 = 32 * j
                t0 = tt * NT
                cosT, sinT = cosTs[gp % 2], sinTs[gp % 2]
                sr, si = st_r[k], st_i[k]
                zr, zi = sbf_r[k], sbf_i[k]
                b1, b2, b3, b4 = bq_
                P.op("dve", lambda e: e.tensor_tensor(b1.t[:, :], sr.t[:, :], cosT.t[:, :], ALU.mult), reads=sr.b + cosT.b, writes=b1.b)
                P.op("pool", lambda e: e.tensor_tensor(b2.t[:, :], si.t[:, :], sinT.t[:, :], ALU.mult), reads=si.b + sinT.b, writes=b2.b)
                P.op("dve", lambda e: e.tensor_tensor(b3.t[:, :], si.t[:, :], cosT.t[:, :], ALU.mult), reads=si.b + cosT.b, writes=b3.b)
                P.op("pool", lambda e: e.tensor_tensor(b4.t[:, :], sr.t[:, :], sinT.t[:, :], ALU.mult), reads=sr.b + sinT.b, writes=b4.b)
                P.op("dve", lambda e: e.tensor_tensor(zr.t[:, :], b1.t[:, :], b2.t[:, :], ALU.subtract), reads=b1.b + b2.b, writes=zr.b)
                P.op("pool", lambda e: e.tensor_tensor(zi.t[:, :], b3.t[:, :], b4.t[:, :], ALU.add), reads=b3.b + b4.b, writes=zi.b)
                yb = y_banks[k]
                P.op("pe", lambda e: e.matmul(yb.t[:, :], LC[0].t[:, gp, :], zr.t[:, :], start=True, stop=False),
                     reads=LC[0].b + zr.b, writes=yb.b)
                P.op("pe", lambda e: e.matmul(yb.t[:, :], LC[1].t[:, gp, :], zi.t[:, :], start=False, stop=True),
                     reads=LC[1].b + zi.b, writes=yb.b)
                y_ = y32[k]
                P.op("act", lambda e: e.activation(y_.t[r0:r0 + 32, :], yb.t[r0:r0 + 32, :], AF.Copy), reads=yb.b, writes=y_.b)
                if pair:
                    P.dma("act", cg_y[tt].i[kc * 128 + r0:kc * 128 + r0 + 32, :], y_.t[r0:r0 + 32, :], reads=y_.b, writes=[cg_y[tt].ib])
                else:
                    P.dma("act", y_d[kc * 128 + r0:kc * 128 + r0 + 32, t0:t0 + NT], y_.t[r0:r0 + 32, :], reads=y_.b, writes=[y_buf])

            it = 0
            prev = None
            for gp in range(NGP):
                tables(gp)
                for tt in range(NTG):
                    s_front(it, gp, tt)
                    if prev is not None:
                        s_back(*prev)
                    prev = (it, gp, tt)
                    it += 1
            s_back(*prev)
            scB.__exit__(None, None, None)
            scL.__exit__(None, None, None)
            if dbg == "y2":
                dbg2 = nc.dram_tensor("dbg_y", [D, T], F32, kind="ExternalOutput").ap()
                P.dma("sp", dbg2[:, :], y_d[:, :], reads=[y_buf], writes=[Buf("dbg2")])
                P.barrier()

            if pair:
                for cgx in cg_y:
                    cgx.go()
            scC = Scope()
            scC.__enter__()
            if pair:
                ycand = sb("ycand", [128, KC, NT], F32)
            wst[:] = [sb("wst%d" % i, [128, KC, 128], F32) for i in range(2)]
            wbf[:] = [sb("wbf%d" % i, [128, KC, 128], BF16) for i in range(5)]
            h32 = sb("h32", [128, KC, NT], F32)
            yt = sb("yt", [128, KC, NT], F32)
            g1 = [sb("g1_%d" % i, [128, NT], F32) for i in range(2)]
            if last:
                xtok[:] = [sb("xtok%d" % i, [128, D], F32) for i in range(2)]
            GC = 2.0 * math.sqrt(2.0 / math.pi)
            for tt in range(NTT):
                t0 = tt * NT
                load_xT_tile(tt, False)
                norm_mod(l, s, h32=h32, want_bf=False)
                if pair:
                    src0 = cg_y[tt].o[:, :].rearrange("(kc p) t -> p kc t", p=128)
                    src1 = cg_y[NTT + tt].o[:, :].rearrange("(kc p) t -> p kc t", p=128)
                    P.dma("sp", yt.t[:, :, :], src0, reads=[cg_y[tt].ob], writes=yt.b)
                    P.dma("act", ycand.t[:, :, :], src1, reads=[cg_y[NTT + tt].ob], writes=ycand.b)
                    select2(yt.t[:, :, :], yt.t[:, :, :], ycand.t[:, :, :], yt.t[:, :, :], yt.b + ycand.b, yt.b)
                else:
                    src = y_d[:, t0:t0 + NT].rearrange("(kc p) t -> p kc t", p=128)
                    P.dma("sp", yt.t[:, 0:8, :], src[:, 0:8, :], reads=[y_buf], writes=yt.b)
                    P.dma("act", yt.t[:, 8:16, :], src[:, 8:16, :], reads=[y_buf], writes=yt.b)
                for kc in range(KC):
                    ga = g1[kc % 2]
                    P.op("dve", lambda e, kc=kc: e.scalar_tensor_tensor(
                        yt.t[:, kc, :], h32.t[:, kc, :], dT.t[:, kc:kc + 1], yt.t[:, kc, :], ALU.mult, ALU.add),
                        reads=h32.b + dT.b + yt.b, writes=yt.b)
                    P.op("act", lambda e, kc=kc, ga=ga: e.activation(ga.t[:, :], yt.t[:, kc, :], AF.Square), reads=yt.b, writes=ga.b)
                    P.op("dve", lambda e, ga=ga: e.tensor_scalar(ga.t[:, :], ga.t[:, :], 0.044715, 1.0, ALU.mult, ALU.add), reads=ga.b, writes=ga.b)
                    P.op("dve", lambda e, kc=kc, ga=ga: e.tensor_tensor(ga.t[:, :], ga.t[:, :], yt.t[:, kc, :], ALU.mult), reads=ga.b + yt.b, writes=ga.b)
                    P.op("act", lambda e, ga=ga: e.activation(ga.t[:, :], ga.t[:, :], AF.Sigmoid, scale=GC), reads=ga.b, writes=ga.b)
                    P.op("pool", lambda e, kc=kc, ga=ga: e.tensor_tensor(hT.t[:, kc, :], ga.t[:, :], yt.t[:, kc, :], ALU.mult),
                         reads=ga.b + yt.b, writes=hT.b)
                for dc in range(KC):
                    wv = load_w_chunk(wglu_in, dc * 128, "glu")
                    wg = load_w_chunk(wglu_in, D + dc * 128, "glu")
                    pv = rot(mm_banks, "pb")
                    pg = rot(mm_banks, "pb")
                    for kc in range(KC):
                        P.op("pe", lambda e, wv=wv, kc=kc, pv=pv: e.matmul(
                            pv.t[:, :], wv.t[:, kc, :], hT.t[:, kc, :], start=(kc == 0), stop=(kc == KC - 1)),
                            reads=wv.b + hT.b, writes=pv.b)
                    for kc in range(KC):
                        P.op("pe", lambda e, wg=wg, kc=kc, pg=pg: e.matmul(
                            pg.t[:, :], wg.t[:, kc, :], hT.t[:, kc, :], start=(kc == 0), stop=(kc == KC - 1)),
                            reads=wg.b + hT.b, writes=pg.b)
                    tf = rot(tmpf, "tmpf")
                    P.op("act", lambda e, tf=tf, pg=pg: e.activation(tf.t[:, :], pg.t[:, :], AF.Sigmoid), reads=pg.b, writes=tf.b)
                    P.op("dve", lambda e, tf=tf, pv=pv: e.tensor_tensor(tf.t[:, :], tf.t[:, :], pv.t[:, :], ALU.mult),
                         reads=tf.b + pv.b, writes=tf.b)
                    P.op("dve", lambda e, dc=dc, tf=tf: e.scalar_tensor_tensor(
                        xT.t[:, dc, :], tf.t[:, :], modG[l].t[:, s, dc:dc + 1], xT.t[:, dc, :], ALU.mult, ALU.add),
                        reads=tf.b + modG[l].b + xT.b, writes=xT.b)
                store_xT_tile(tt, last)
            scC.__exit__(None, None, None)
            sc.__exit__(None, None, None)

        for i, ph in enumerate(phases):
            first = (i == 0)
            last = (i == len(phases) - 1)
            if ph.startswith("ffn"):
                l = int(ph[3]); fi = int(ph[4])
                ffn(l, 0 if fi == 0 else 2, fi, first=first, last=last)
            elif ph.startswith("att"):
                attention(int(ph[3]), first=first, last=last)
            elif ph.startswith("ssm"):
                ssm(int(ph[3]), first=first, last=last)

        P.emit()
    return nc


def make_in_maps(inputs, T=SEQ, phases=None, pair=False, n_cores=N_CORES):
    if phases is None:
        phases = ALL_PHASES
    if pair:
        return make_in_maps_pair(inputs, T, phases, n_cores)
    x = np.asarray(inputs["x"], dtype=np.float32)
    c = np.asarray(inputs["c"], dtype=np.float32)
    pos = np.asarray(inputs["positions"], dtype=np.int32)
    shared = {
        "norm_g": np.ascontiguousarray(np.asarray(inputs["norm_g"], np.float32).reshape(96, 128)),
        "ada_b": np.ascontiguousarray(np.asarray(inputs["ada_b"], np.float32).reshape(2, 144, 128)),
        "ident": np.eye(128, dtype=np.float32),
    }
    for l in sorted(set(int(ph[3]) for ph in phases)):
        shared["ada_w%d" % l] = np.asarray(inputs["ada_w"][l], np.float32)
    for ph in phases:
        if ph.startswith("ffn"):
            l = int(ph[3]); fi = int(ph[4])
            shared["ffn_w_in" + ph[3:]] = np.ascontiguousarray(np.asarray(inputs["ffn_w_in"][l, fi], np.float32))
            shared["ffn_w_out" + ph[3:]] = np.ascontiguousarray(np.asarray(inputs["ffn_w_out"][l, fi], np.float32))
    if any(ph.startswith("ssm") for ph in phases):
        p = np.arange(128)
        shared["pm"] = np.stack([1 - (p // 16) % 2, (p // 16) % 2], axis=1).astype(np.float32)
        shared["ssm_dT"] = np.ascontiguousarray(np.asarray(inputs["ssm_d"][0], np.float32).reshape(16, 128).T)
        shared["iota"] = np.ascontiguousarray(np.broadcast_to(np.arange(NT, dtype=np.float32)[None], (128, NT)))
        a_re = np.asarray(inputs["ssm_a_re"][0], np.float32)
        a_im = np.asarray(inputs["ssm_a_im"][0], np.float32)
        ls = np.asarray(inputs["ssm_log_step"][0], np.float32)
        lsb = np.broadcast_to(ls[:, None], (128, 64))
        def pair(a):
            sh = a.shape
            a = a.reshape((64, 2, 64) + sh[2:])
            a = np.moveaxis(a, 0, 2)
            return np.ascontiguousarray(a.reshape((128, 64) + sh[2:]))
        shared["a_pair"] = np.ascontiguousarray(np.stack([pair(a_re), pair(a_im), pair(lsb)], axis=1))
        def feat(a):
            a = a.reshape(16, 8, 64)
            a = np.broadcast_to(a[:, :, None, :], (16, 8, 16, 64))
            return np.ascontiguousarray(a.transpose(1, 2, 0, 3).reshape(128, 1024))
        shared["a_feat"] = np.ascontiguousarray(np.stack([feat(a_re), feat(a_im), feat(lsb)], axis=1))
        def featb(b):
            b = b.reshape(16, 8, 64, 16)
            return np.ascontiguousarray(b.transpose(1, 3, 0, 2).reshape(128, 1024))
        shared["b_feat"] = np.ascontiguousarray(np.stack(
            [featb(np.asarray(inputs["ssm_b_re"][0], np.float32)), featb(np.asarray(inputs["ssm_b_im"][0], np.float32))], axis=1))
        def pairc(c):
            return pair(np.ascontiguousarray(c.transpose(0, 2, 1)))
        shared["c_pair"] = np.ascontiguousarray(np.stack(
            [pairc(np.asarray(inputs["ssm_c_re"][0], np.float32)), pairc(np.asarray(inputs["ssm_c_im"][0], np.float32))], axis=1))
        shared["w_glu"] = np.ascontiguousarray(np.asarray(inputs["ssm_w_glu"][0], np.float32))
    if any(ph.startswith("att") for ph in phases):
        shared["attn_w_in"] = np.ascontiguousarray(np.asarray(inputs["attn_w_in"][0], np.float32))
        shared["attn_w_out"] = np.ascontiguousarray(np.asarray(inputs["attn_w_out"][0], np.float32))
        invf = (500000.0 ** (-np.arange(0, 32, 2, dtype=np.float32) / np.float32(32))).astype(np.float32)
        ropec = np.zeros((32, 4), np.float32)
        ropec[:, 0] = np.concatenate([invf, invf])
        sign = np.concatenate([-np.ones(16), np.ones(16)]).astype(np.float32)
        ropec[:, 1] = sign
        ropec[:, 2] = -math.pi * sign
        ropec[:, 3] = -math.pi
        shared["ropec"] = ropec
        rotm = np.zeros((32, 32), np.float32)
        for m_ in range(32):
            rotm[(m_ + 16) % 32, m_] = 1.0
        shared["rotm"] = rotm
        qg = np.asarray(inputs["attn_q_norm"][0], np.float32)
        kg = np.asarray(inputs["attn_k_norm"][0], np.float32)
        shared["qkg"] = np.ascontiguousarray(np.stack([qg, kg], axis=1))
        shared["qkgrep"] = np.ascontiguousarray(np.broadcast_to(np.stack([qg, kg], axis=0)[None], (128, 2, 128)))
        shared["subg"] = np.ascontiguousarray(np.broadcast_to(np.asarray(inputs["attn_subln"][0], np.float32)[None], (128, 256)))
        shared["lamrep"] = np.ascontiguousarray(np.broadcast_to(np.asarray(inputs["attn_lambda"][0], np.float32)[None], (128, 4, 128)))
        kk = np.arange(128)[:, None]
        qq = np.arange(256)[None, :]
        mask = np.stack([(kk <= qq), (kk + 128 <= qq)], axis=1).astype(np.float32)
        shared["maskc"] = mask.astype(ml_dtypes.bfloat16)
    maps = []
    for core in range(N_CORES):
        b = core % 4
        m = dict(shared)
        m["x"] = np.ascontiguousarray(x[b, :T])
        m["c"] = np.ascontiguousarray(c[b].reshape(16, 128))
        m["pos"] = np.ascontiguousarray(np.broadcast_to(pos[b, :T].reshape(1, T), (32, T)))
        maps.append(m)
    return maps


def make_in_maps_pair(inputs, T, phases, n_cores):
    base = make_in_maps(inputs, T, phases, pair=False)
    TL = T // 2
    x = np.asarray(inputs["x"], np.float32)
    maps = []
    for core in range(n_cores):
        b, r = core // 2, core % 2
        m = dict(base[b])
        m["x"] = np.ascontiguousarray(x[b, r * TL:(r + 1) * TL])
        f = np.zeros((128, 2), np.float32)
        f[:, r] = 1.0
        m["rankf"] = f
        for l in range(2):
            if "ada_w%d" % l in m:
                m["ada_w%d" % l] = np.ascontiguousarray(m["ada_w%d" % l][:, 9216 * r:9216 * (r + 1)])
        m["ada_b"] = np.ascontiguousarray(m["ada_b"][:, 72 * r:72 * (r + 1), :])
        if "attn_w_in" in m:
            w = np.asarray(inputs["attn_w_in"][0], np.float32)
            cols = np.concatenate([np.arange(o + 1024 * r, o + 1024 * (r + 1)) for o in (0, 2048, 4096)])
            m["attn_w_in"] = np.ascontiguousarray(w[:, cols])
        if "a_pair" in m:
            m["a_pair"] = np.ascontiguousarray(m["a_pair"][:, :, 32 * r:32 * (r + 1)])
            m["c_pair"] = np.ascontiguousarray(m["c_pair"][:, :, 32 * r:32 * (r + 1), :])
            m["a_feat"] = np.ascontiguousarray(m["a_feat"][:, :, 512 * r:512 * (r + 1)])
            m["b_feat"] = np.ascontiguousarray(m["b_feat"][:, :, 512 * r:512 * (r + 1)])
        maps.append(m)
    return maps


def kernel(**inputs):
    nc = build(SEQ, pair=True)
    maps = make_in_maps(inputs, SEQ, pair=True)
    res = run_bass_kernel_spmd(nc, maps, core_ids=list(range(N_CORES)))
    out = np.stack([np.concatenate([res.results[2 * b]["out"], res.results[2 * b + 1]["out"]], axis=0) for b in range(4)], axis=0)
    return out.astype(np.float32)
```

```python
import math
from contextlib import ExitStack

import numpy as np
import ml_dtypes

import concourse.bass as bass
import concourse.mybir as mybir
from concourse.bass_utils import run_bass_kernel_spmd

F32 = mybir.dt.float32
BF16 = mybir.dt.bfloat16
I32 = mybir.dt.int32
ALU = mybir.AluOpType
AF = mybir.ActivationFunctionType
AX = mybir.AxisListType

D = 2048
KC = 16
SEQ = 4096
DFF = 5632
FC = 44
NT = 512
EPS = 1e-6
NH = 8
HD = 128
N_CORES = 8
TWO_PI = 2.0 * math.pi

SELF_SYNC = True
ALL_PHASES = ['ffn00', 'att0', 'ffn01', 'ffn10', 'ssm1', 'ffn11']


class Buf:
    __slots__ = ("w", "r", "name")

    def __init__(self, name=""):
        self.w = None
        self.r = []
        self.name = name


class Prog:
    NDS = {"sp": 12, "pool": 6, "act": 4}

    def __init__(self, nc, es):
        self.nc = nc
        self.engs = {"pe": nc.tensor, "act": nc.scalar, "dve": nc.vector, "pool": nc.gpsimd, "sp": nc.sync}
        self.sem = {e: es.enter_context(nc.semaphore("s_" + e)) for e in ("pe", "act", "dve", "pool")}
        self.cnt = {e: 0 for e in self.sem}
        self.ops = {e: [] for e in self.engs}
        self.seen = {e: {} for e in self.engs}
        self.dsem = {q: [es.enter_context(nc.semaphore("d_%s%d" % (q, i))) for i in range(n)]
                     for q, n in self.NDS.items()}
        self.duse = {q: [0] * n for q, n in self.NDS.items()}
        self.dnext = {q: 0 for q in self.NDS}
        self.pending = {e: [] for e in self.engs}
        self.ccsem = es.enter_context(nc.semaphore("s_cc"))
        self.cccnt = 0

    def _semof(self, key):
        if isinstance(key, tuple):
            return self.dsem[key[1]][key[2]]
        if key == "cc":
            return self.ccsem
        return self.sem[key]

    def _deps(self, e, reads, writes):
        toks = []
        for b in reads:
            if b.w is not None:
                toks.append(b.w)
        for b in writes:
            if b.w is not None:
                toks.append(b.w)
            toks.extend(b.r)
        waits = {}
        for k, v in toks:
            if k == e and (e == "pe" or not SELF_SYNC):
                continue
            if self.seen[e].get(k, 0) >= v:
                continue
            if waits.get(k, 0) < v:
                waits[k] = v
        for k, v in waits.items():
            self.seen[e][k] = v
        return list(waits.items())

    def op(self, e, fn, reads=(), writes=()):
        waits = self.pending[e] + self._deps(e, reads, writes)
        self.pending[e] = []
        self.cnt[e] += 1
        tok = (e, self.cnt[e])
        self.ops[e].append((waits, fn, self.sem[e], 1))
        for b in reads:
            b.r.append(tok)
        for b in writes:
            b.w = tok
            b.r = []

    def dma(self, q, out, in_, reads=(), writes=(), **kw):
        i = self.dnext[q]
        self.dnext[q] = (i + 1) % self.NDS[q]
        waits = self.pending[q] + self._deps(q, reads, writes)
        self.pending[q] = []
        key = ("d", q, i)
        prev = self.duse[q][i]
        if prev > 0 and self.seen[q].get(key, 0) < prev:
            waits.append((key, prev))
            self.seen[q][key] = prev
        self.duse[q][i] = prev + 16
        tok = (key, prev + 16)
        self.ops[q].append((waits, lambda eng: eng.dma_start(out=out, in_=in_, **kw), self.dsem[q][i], 16))
        for b in reads:
            b.r.append(tok)
        for b in writes:
            b.w = tok
            b.r = []

    def barrier(self):
        allw = []
        for q, n in self.NDS.items():
            for i in range(n):
                if self.duse[q][i] > 0:
                    allw.append((("d", q, i), self.duse[q][i]))
        for e in self.sem:
            if self.cnt[e] > 0:
                allw.append((e, self.cnt[e]))
        if self.cccnt > 0:
            allw.append(("cc", self.cccnt))
        for e in self.engs:
            lst = []
            for k, v in allw:
                if k == e:
                    continue
                if self.seen[e].get(k, 0) >= v:
                    continue
                self.seen[e][k] = v
                lst.append((k, v))
            self.pending[e].extend(lst)

    def coll(self, kind, groups, in_h, out_h, reads=(), writes=()):
        q = "pool"
        waits = self.pending[q] + self._deps(q, reads, writes)
        self.pending[q] = []
        self.cccnt += 1
        tok = ("cc", self.cccnt)

        def fn(eng):
            return eng.collective_compute(kind, ALU.bypass, replica_groups=groups,
                                          ins=[in_h.ap().opt()], outs=[out_h.ap().opt()])
        self.ops[q].append((waits, fn, self.ccsem, None))
        for b in reads:
            b.r.append(tok)
        for b in writes:
            b.w = tok
            b.r = []

    def emit(self):
        nc = self.nc
        fin = []
        for q, n in self.NDS.items():
            for i in range(n):
                if self.duse[q][i] > 0:
                    fin.append((("d", q, i), self.duse[q][i]))
        for e in self.sem:
            if self.cnt[e] > 0:
                fin.append((e, self.cnt[e]))
        if self.cccnt > 0:
            fin.append(("cc", self.cccnt))

        def replay(name, eng):
            for waits, fn, sem, inc in self.ops[name]:
                for k, v in waits:
                    eng.wait_ge(self._semof(k), v)
                if inc is None:
                    fn(eng).then_inc(sem)
                else:
                    fn(eng).then_inc(sem, inc)
            if name == "sp":
                for k, v in fin:
                    eng.wait_ge(self._semof(k), v)

        with nc.Block() as block:
            @block.sync
            def _(e):
                replay("sp", e)

            @block.scalar
            def _(e):
                replay("act", e)

            @block.vector
            def _(e):
                replay("dve", e)

            @block.gpsimd
            def _(e):
                replay("pool", e)

            @block.tensor
            def _(e):
                replay("pe", e)


class Tile:
    def __init__(self, t, nsub=1, name=""):
        self.t = t
        self.b = [Buf("%s[%d]" % (name, i)) for i in range(nsub)]


class K:
    pass


def build(T=SEQ, phases=None, dbg=None, pair=False, groups=None):
    nc = bass.Bass("TRN2", target_bir_lowering=False)
    TL = T // 2 if pair else T
    NTT = TL // NT
    NTG = T // NT
    NHL = NH // 2 if pair else NH
    if groups is None:
        groups = [[2 * i, 2 * i + 1] for i in range(N_CORES // 2)]
    es = ExitStack()
    with es:
        P = Prog(nc, es)

        def dram_in(name, shape, dt=F32):
            return nc.dram_tensor(name, list(shape), dt, kind="ExternalInput").ap()

        def dram_scr(name, shape, dt=F32):
            return nc.dram_tensor(name, list(shape), dt, kind="Internal").ap()

        cur = [es]
        uid = [0]

        def sb(name, shape, dt=F32, nsub=1):
            uid[0] += 1
            t = cur[0].enter_context(nc.sbuf_tensor("sb%d_%s" % (uid[0], name), list(shape), dt))
            return Tile(t, nsub, name)

        class Scope:
            def __enter__(self_):
                self_.st = ExitStack()
                self_.st.__enter__()
                self_.prev = cur[0]
                cur[0] = self_.st
                return self_

            def __exit__(self_, *a):
                P.barrier()
                cur[0] = self_.prev
                return self_.st.__exit__(*a)

        def ps(name, shape, dt=F32):
            t = es.enter_context(nc.psum_tensor("ps_" + name, list(shape), dt))
            return Tile(t, 1, name)

        class CG:
            def __init__(self, name, rows, cols, dt):
                self.i_h = nc.dram_tensor("cg_in_" + name, [rows, cols], dt)
                self.o_h = nc.dram_tensor("cg_out_" + name, [2 * rows, cols], dt)
                self.i = self.i_h.ap()
                self.o = self.o_h.ap()
                self.ib = Buf("cgi_" + name)
                self.ob = Buf("cgo_" + name)

            def go(self):
                P.coll("AllGather", groups, self.i_h, self.o_h, reads=[self.ib], writes=[self.ob])

        x_in = dram_in("x", [TL, D])
        c_in = dram_in("c", [16, 128])
        pos_in = dram_in("pos", [32, T], I32)
        normg_in = dram_in("norm_g", [96, 128])
        if phases is None:
            phases = ALL_PHASES
        layers = sorted(set(int(ph[3]) for ph in phases))
        NJ = 72 if pair else 144
        adaw_in = {l: dram_in("ada_w%d" % l, [D, NJ * 128]) for l in layers}
        adab_in = dram_in("ada_b", [2, NJ, 128])
        ffnwi_in = {}
        ffnwo_in = {}
        for ph in phases:
            if ph.startswith("ffn"):
                ffnwi_in[ph[3:]] = dram_in("ffn_w_in" + ph[3:], [D, 2 * DFF])
                ffnwo_in[ph[3:]] = dram_in("ffn_w_out" + ph[3:], [DFF, D])
        ident_in = dram_in("ident", [128, 128])
        if any(ph.startswith("ssm") for ph in phases):
            pm_in = dram_in("pm", [128, 2])
            ssmd_in = dram_in("ssm_dT", [128, 16])
            iota_in = dram_in("iota", [128, NT])
            NGP = 32 if pair else 64
            NKL = NGP // 4
            apair_in = dram_in("a_pair", [128, 3, NGP])
            afeat_in = dram_in("a_feat", [128, 3, NKL * 64])
            bfeat_in = dram_in("b_feat", [128, 2, NKL * 64])
            cpair_in = dram_in("c_pair", [128, 2, NGP, 16])
            wglu_in = dram_in("w_glu", [D, 2 * D])
            if pair:
                cg_u = [CG("u%d" % i, D, NT, BF16) for i in range(NTT)]
                cg_y = [CG("y%d" % i, NKL * 128, NT, F32) for i in range(NTG)]
                uT_d = dram_scr("um_scr", [NKL * 128, T], BF16)
            else:
                uT_d = dram_scr("uT_scr", [D, T], BF16)
            if pair:
                y_d = None
            elif dbg == "y":
                y_d = nc.dram_tensor("dbg_y", [D, T], F32, kind="ExternalOutput").ap()
            else:
                y_d = dram_scr("y_scr", [D, T])
            u_buf = Buf("u")
            y_buf = Buf("y")
        if any(ph.startswith("att") for ph in phases):
            attnwi_in = dram_in("attn_w_in", [D, 3 * NHL * 256])
            attnwo_in = dram_in("attn_w_out", [D, D])
            ropec_in = dram_in("ropec", [32, 4])
            rotm_in = dram_in("rotm", [32, 32])
            qkg_in = dram_in("qkg", [128, 2])
            subg_in = dram_in("subg", [128, 256])
            qkgrep_in = dram_in("qkgrep", [128, 2, 128])
            lamrep_in = dram_in("lamrep", [128, 4, 128])
            mask_in = dram_in("maskc", [128, 2, 256], BF16)
            qkT_d = dram_scr("qkT_scr", [4 * NHL, 128, T], BF16)
            V_d = dram_scr("V_scr", [T, NHL * 256], BF16)
            OTC = 1024
            if pair:
                cg_h = [CG("hT%d" % i, D, NT, BF16) for i in range(NTT)]
                cg_o = [CG("oT%d" % i, NHL * 256, OTC, BF16) for i in range(T // OTC)]
            else:
                oT_d = dram_scr("oT_scr", [D, T], BF16)
            qk_buf = Buf("qk")
            v_buf = Buf("v")
            oT_buf = Buf("oT")
        out_d = nc.dram_tensor("out", [TL, D], F32, kind="ExternalOutput").ap()
        xT_d = dram_scr("xT_scr", [D, TL])
        if pair:
            rankf_in = dram_in("rankf", [128, 2])

        xT_buf = Buf("xT_d")
        out_buf = Buf("out_d")
        in_buf = Buf("inputs")

        ident = sb("ident", [128, 128])
        identb = sb("identb", [128, 128], BF16)
        onesb = sb("onesb", [128, 128], BF16)
        vecrows = sb("vecrows", [128, 128])
        normgT = sb("normgT", [128, 96])
        condT = sb("condT", [128, 16, 2])
        adabT = [sb("adabT%d" % l, [128, 144]) for l in range(2)]
        modT = [sb("modT%d" % l, [128, 144]) for l in range(2)]
        modA = [sb("modA%d" % l, [128, 3, 16]) for l in range(2)]
        modG = [sb("modG%d" % l, [128, 3, 16]) for l in range(2)]

        rankf = sb("rankf", [128, 2], F32)
        if pair:
            P.dma("sp", rankf.t[:, :], rankf_in[:, :], reads=[in_buf], writes=rankf.b)

        def select2(dst_ap, c0_ap, c1_ap, tmp_ap, reads, writes, eng="dve"):
            P.op(eng, lambda e: e.tensor_scalar(tmp_ap, c0_ap, rankf.t[:, 0:1], None, ALU.mult),
                 reads=list(reads) + rankf.b, writes=list(writes))
            P.op(eng, lambda e: e.scalar_tensor_tensor(dst_ap, c1_ap, rankf.t[:, 1:2], tmp_ap, ALU.mult, ALU.add),
                 reads=list(reads) + rankf.b, writes=list(writes))

        pbank = [ps("pb%d" % i, [128, 512]) for i in range(7)]
        pbf = ps("pbf", [128, 1024], BF16)

        P.dma("sp", ident.t[:], ident_in[:, :], reads=[in_buf], writes=ident.b)
        P.op("dve", lambda e: e.tensor_copy(identb.t[:], ident.t[:]), reads=ident.b, writes=identb.b)
        P.op("dve", lambda e: e.memset(onesb.t[:], 1.0), writes=onesb.b)

        def transpose_rows(src_ap_dram, nrows, dst_ap, dst_bufs, bank):
            P.dma("sp", vecrows.t[0:nrows, :], src_ap_dram, reads=[in_buf], writes=vecrows.b)
            P.op("pe", lambda e: e.transpose(bank.t[:, 0:nrows], vecrows.t[0:nrows, :], ident.t[0:nrows, 0:nrows]),
                 reads=vecrows.b + ident.b, writes=bank.b)
            P.op("dve", lambda e: e.tensor_copy(dst_ap, bank.t[:, 0:nrows]), reads=bank.b, writes=dst_bufs)

        transpose_rows(normg_in[:, :], 96, normgT.t[:, :], normgT.b, pbank[0])
        transpose_rows(c_in[:, :], 16, condT.t[:, :, 0], condT.b, pbank[1])
        P.op("act", lambda e: e.activation(condT.t[:, :, 0], condT.t[:, :, 0], AF.Silu), reads=condT.b, writes=condT.b)
        P.op("act", lambda e: e.activation(condT.t[:, :, 1], condT.t[:, :, 0], AF.Copy), reads=condT.b, writes=condT.b)
        for l in range(2):
            if pair:
                transpose_rows(adab_in[l, 0:NJ, :], NJ, adabT[l].t[:, 0:NJ], adabT[l].b, pbank[2])
            else:
                transpose_rows(adab_in[l, 0:128, :], 128, adabT[l].t[:, 0:128], adabT[l].b, pbank[2])
                transpose_rows(adab_in[l, 128:144, :], 16, adabT[l].t[:, 128:144], adabT[l].b, pbank[3])

        CB = 512
        sc_ada = Scope()
        sc_ada.__enter__()
        adaw = [sb("adaw%d" % i, [128, 16, CB]) for i in range(2)]
        nblk = NJ * 128 // CB
        if pair:
            cg_mod = CG("mod", 128, 2 * NJ, F32)
            modh = sb("modh", [128, 2, NJ])
        for l in layers:
            bank = pbank[4 + l]
            for blk in range(nblk):
                wt = adaw[blk % 2]
                src = adaw_in[l][:, blk * CB:(blk + 1) * CB].rearrange("(kc p) c -> p kc c", p=128)
                P.dma("sp", wt.t[:, 0:8, :], src[:, 0:8, :], reads=[in_buf], writes=wt.b)
                P.dma("act", wt.t[:, 8:16, :], src[:, 8:16, :], reads=[in_buf], writes=wt.b)
                for jj in range(CB // 128):
                    j = blk * (CB // 128) + jj
                    for kc in range(KC):
                        P.op("pe", lambda e, wt=wt, jj=jj, kc=kc, j=j, bank=bank: e.matmul(
                            bank.t[:, 2 * j:2 * j + 2], wt.t[:, kc, jj * 128:(jj + 1) * 128], condT.t[:, kc, :],
                            start=(kc == 0), stop=(kc == KC - 1)),
                            reads=wt.b + condT.b, writes=bank.b)
            if pair:
                P.op("dve", lambda e, l=l, bank=bank: e.tensor_tensor(
                    modh.t[:, l, :], bank.t[:, 0:2 * NJ].rearrange("p (j two) -> p j two", two=2)[:, :, 0], adabT[l].t[:, 0:NJ], ALU.add),
                    reads=bank.b + adabT[l].b, writes=modh.b)
                continue
            P.op("dve", lambda e, l=l, bank=bank: e.tensor_tensor(
                modT[l].t[:, :], bank.t[:, 0:288].rearrange("p (j two) -> p j two", two=2)[:, :, 0], adabT[l].t[:, :], ALU.add),
                reads=bank.b + adabT[l].b, writes=modT[l].b)
        if pair:
            P.dma("sp", cg_mod.i[:, :], modh.t[:, :, :].rearrange("p l j -> p (l j)"), reads=modh.b, writes=[cg_mod.ib])
            cg_mod.go()
            for l in layers:
                for r_ in range(2):
                    P.dma(("sp", "act")[r_], modT[l].t[:, r_ * NJ:(r_ + 1) * NJ], cg_mod.o[r_ * 128:(r_ + 1) * 128, l * NJ:(l + 1) * NJ],
                          reads=[cg_mod.ob], writes=modT[l].b)
        for l in layers:
            for s in range(3):
                sh = (s * 3 + 0) * 16
                sc = (s * 3 + 1) * 16
                ga = (s * 3 + 2) * 16
                P.op("dve", lambda e, l=l, s=s, sc=sc: e.scalar_tensor_tensor(
                    modA[l].t[:, s, :], modT[l].t[:, sc:sc + 16], 1.0, normgT.t[:, (l * 3 + s) * 16:(l * 3 + s) * 16 + 16],
                    ALU.add, ALU.mult), reads=modT[l].b + normgT.b, writes=modA[l].b)
                wgt = 1.0 if s == 1 else 0.5
                P.op("dve", lambda e, l=l, s=s, ga=ga, wgt=wgt: e.tensor_scalar(
                    modG[l].t[:, s, :], modT[l].t[:, ga:ga + 16], wgt, None, ALU.mult),
                    reads=modT[l].b, writes=modG[l].b)

        sc_ada.__exit__(None, None, None)

        xT = sb("xT", [128, KC, NT], F32)
        hT = sb("hT", [128, KC, NT], BF16)
        tmpf = [sb("tmpf%d" % i, [128, NT], F32) for i in range(3)]
        tmpb = [sb("tmpb%d" % i, [128, NT], BF16) for i in range(3)]
        rstd = sb("rstd", [128, NT], F32)
        xtok = []
        wst = []
        wbf = []
        wost = []
        wobf = []
        gTh = [None]

        def alloc_ffn_tiles():
            gTh[0] = sb("gT", [128, FC, NT], BF16, nsub=FC)
            xtok[:] = [sb("xtok%d" % i, [128, D], F32) for i in range(2)]
            wst[:] = [sb("wst%d" % i, [128, KC, 128], F32) for i in range(2)]
            wbf[:] = [sb("wbf%d" % i, [128, KC, 128], BF16) for i in range(5)]
            wost[:] = [sb("wost%d" % i, [128, 22, 128], F32) for i in range(2)]
            wobf[:] = [sb("wobf%d" % i, [128, 22, 128], BF16) for i in range(3)]
        ctr = {"tmpf": 0, "tmpb": 0, "w": 0, "wb": 0, "wo": 0, "wob": 0, "pb": 0, "xtok": 0, "cast": 0, "ldq": 0}

        def rot(lst, key):
            i = ctr[key]
            ctr[key] = i + 1
            return lst[i % len(lst)]

        mm_banks = pbank[0:4]
        ssq_bank = pbank[4]
        tr_banks = pbank[5:7]

        def load_xT_tile(tt, from_input):
            t0 = tt * NT
            if not from_input:
                src = xT_d[:, t0:t0 + NT].rearrange("(kc p) t -> p kc t", p=128)
                P.dma("sp", xT.t[:, 0:8, :], src[:, 0:8, :], reads=[xT_buf], writes=xT.b)
                P.dma("act", xT.t[:, 8:16, :], src[:, 8:16, :], reads=[xT_buf], writes=xT.b)
                return
            for sub in range(NT // 128):
                xt = rot(xtok, "xtok")
                P.dma("sp", xt.t[:, :], x_in[t0 + sub * 128:t0 + (sub + 1) * 128, :], reads=[in_buf], writes=xt.b)
                for kc in range(KC):
                    bank = tr_banks[kc % 2]
                    P.op("pe", lambda e, xt=xt, kc=kc, bank=bank: e.transpose(
                        bank.t[:, 0:128], xt.t[:, kc * 128:(kc + 1) * 128], ident.t[:, :]),
                        reads=xt.b + ident.b, writes=bank.b)
                    eng = "dve" if kc % 2 == 0 else "act"
                    if eng == "dve":
                        P.op("dve", lambda e, kc=kc, sub=sub, bank=bank: e.tensor_copy(
                            xT.t[:, kc, sub * 128:(sub + 1) * 128], bank.t[:, 0:128]), reads=bank.b, writes=xT.b)
                    else:
                        P.op("act", lambda e, kc=kc, sub=sub, bank=bank: e.activation(
                            xT.t[:, kc, sub * 128:(sub + 1) * 128], bank.t[:, 0:128], AF.Copy), reads=bank.b, writes=xT.b)

        def store_xT_tile(tt, to_output):
            t0 = tt * NT
            if not to_output:
                dst = xT_d[:, t0:t0 + NT].rearrange("(kc p) t -> p kc t", p=128)
                P.dma("pool", dst[:, :, :], xT.t[:, :, :], reads=xT.b, writes=[xT_buf])
                return
            for sub in range(NT // 128):
                xt = rot(xtok, "xtok")
                for kc in range(KC):
                    bank = tr_banks[kc % 2]
                    P.op("pe", lambda e, kc=kc, sub=sub, bank=bank: e.transpose(
                        bank.t[:, 0:128], xT.t[:, kc, sub * 128:(sub + 1) * 128], ident.t[:, :]),
                        reads=xT.b + ident.b, writes=bank.b)
                    if kc % 2 == 0:
                        P.op("dve", lambda e, kc=kc, xt=xt, bank=bank: e.tensor_copy(
                            xt.t[:, kc * 128:(kc + 1) * 128], bank.t[:, 0:128]), reads=bank.b, writes=xt.b)
                    else:
                        P.op("act", lambda e, kc=kc, xt=xt, bank=bank: e.activation(
                            xt.t[:, kc * 128:(kc + 1) * 128], bank.t[:, 0:128], AF.Copy), reads=bank.b, writes=xt.b)
                P.dma("pool", out_d[t0 + sub * 128:t0 + (sub + 1) * 128, :], xt.t[:, :], reads=xt.b, writes=[out_buf])

        def norm_mod(l, s, h32=None, want_bf=True):
            for kc in range(KC):
                sq = rot(tmpb, "tmpb")
                P.op("act", lambda e, kc=kc, sq=sq: e.activation(sq.t[:, :], xT.t[:, kc, :], AF.Square),
                     reads=xT.b, writes=sq.b)
                P.op("pe", lambda e, kc=kc, sq=sq: e.matmul(ssq_bank.t[:, :], onesb.t[:, :], sq.t[:, :],
                                                           start=(kc == 0), stop=(kc == KC - 1)),
                     reads=sq.b + onesb.b, writes=ssq_bank.b)
            P.op("dve", lambda e: e.tensor_scalar(rstd.t[:, :], ssq_bank.t[:, :], 1.0 / D, EPS, ALU.mult, ALU.add),
                 reads=ssq_bank.b, writes=rstd.b)
            P.op("act", lambda e: e.activation(rstd.t[:, :], rstd.t[:, :], AF.Sqrt), reads=rstd.b, writes=rstd.b)
            P.op("dve", lambda e: e.reciprocal(rstd.t[:, :], rstd.t[:, :]), reads=rstd.b, writes=rstd.b)
            sh = (s * 3 + 0) * 16
            for kc in range(KC):
                tf = rot(tmpf, "tmpf")
                P.op("dve", lambda e, kc=kc, tf=tf: e.scalar_tensor_tensor(
                    tf.t[:, :], xT.t[:, kc, :], modA[l].t[:, s, kc:kc + 1], rstd.t[:, :], ALU.mult, ALU.mult),
                    reads=xT.b + modA[l].b + rstd.b, writes=tf.b)
                if h32 is not None:
                    P.op("act", lambda e, kc=kc, tf=tf: e.activation(
                        h32.t[:, kc, :], tf.t[:, :], AF.Identity, bias=modT[l].t[:, sh + kc:sh + kc + 1], scale=1.0),
                        reads=tf.b + modT[l].b, writes=h32.b)
                    if want_bf:
                        P.op("pool", lambda e, kc=kc: e.tensor_copy(hT.t[:, kc, :], h32.t[:, kc, :]), reads=h32.b, writes=hT.b)
                    continue
                P.op("act", lambda e, kc=kc, tf=tf: e.activation(
                    hT.t[:, kc, :], tf.t[:, :], AF.Identity, bias=modT[l].t[:, sh + kc:sh + kc + 1], scale=1.0),
                    reads=tf.b + modT[l].b, writes=hT.b)

        wcache = {}
        CAST_ENGS = ["act", "dve", "pool"]

        def cast_op(dst_ap, src_ap, reads, writes):
            e = CAST_ENGS[ctr["cast"] % 3]
            ctr["cast"] += 1
            if e == "act":
                P.op("act", lambda eng: eng.activation(dst_ap, src_ap, AF.Copy), reads=reads, writes=writes)
            else:
                P.op(e, lambda eng: eng.tensor_copy(dst_ap, src_ap), reads=reads, writes=writes)

        def ldq():
            q = ("sp", "pool")[ctr["ldq"] % 2]
            ctr["ldq"] += 1
            return q

        def load_w_chunk(w_ap, c0, wname=None):
            idx = c0 // 128
            wb = rot(wbf, "wb")
            bufs = None
            if wname is not None:
                if wname not in wcache:
                    wcache[wname] = (dram_scr("wc_" + wname, [w_ap.shape[1] // 128, 128, KC * 128], BF16), {})
                scr, bufs = wcache[wname]
            if bufs is not None and idx in bufs:
                P.dma(ldq(), wb.t[:, :, :], scr[idx].rearrange("p (k c) -> p k c", c=128), reads=[bufs[idx]], writes=wb.b)
                return wb
            st = rot(wst, "w")
            src = w_ap[:, c0:c0 + 128].rearrange("(kc p) c -> p kc c", p=128)
            P.dma("sp", st.t[:, :, :], src, reads=[in_buf], writes=st.b)
            cast_op(wb.t[:, :, :], st.t[:, :, :], st.b, wb.b)
            if bufs is not None:
                bufs[idx] = Buf("wc")
                P.dma("pool", scr[idx].rearrange("p (k c) -> p k c", c=128), wb.t[:, :, :], reads=wb.b, writes=[bufs[idx]])
            return wb

        def load_wo_half(w_out, dc, half, wname):
            wb = rot(wobf, "wob")
            if wname not in wcache:
                wcache[wname] = (dram_scr("wc_" + wname, [32, 128, 22 * 128], BF16), {})
            scr, bufs = wcache[wname]
            idx = dc * 2 + half
            if idx in bufs:
                P.dma(ldq(), wb.t[:, :, :], scr[idx].rearrange("p (k c) -> p k c", c=128), reads=[bufs[idx]], writes=wb.b)
                return wb
            st = rot(wost, "wo")
            src = w_out[half * 22 * 128:(half + 1) * 22 * 128, dc * 128:(dc + 1) * 128].rearrange("(j p) c -> p j c", p=128)
            P.dma("sp", st.t[:, :, :], src, reads=[in_buf], writes=st.b)
            cast_op(wb.t[:, :, :], st.t[:, :, :], st.b, wb.b)
            bufs[idx] = Buf("wc")
            P.dma("pool", scr[idx].rearrange("p (k c) -> p k c", c=128), wb.t[:, :, :], reads=wb.b, writes=[bufs[idx]])
            return wb

        def ffn(l, s, fi, first=False, last=False):
            w_in = ffnwi_in['%d%d' % (l, fi)]
            w_out = ffnwo_in['%d%d' % (l, fi)]
            sc = Scope()
            sc.__enter__()
            alloc_ffn_tiles()
            gT = gTh[0]
            for tt in range(NTT):
                load_xT_tile(tt, first)
                norm_mod(l, s)
                for j in range(FC):
                    wa = load_w_chunk(w_in, j * 128, "fi%d%d" % (l, fi))
                    wb_ = load_w_chunk(w_in, DFF + j * 128, "fi%d%d" % (l, fi))
                    pa = rot(mm_banks, "pb")
                    pb_ = rot(mm_banks, "pb")
                    for kc in range(KC):
                        P.op("pe", lambda e, wa=wa, kc=kc, pa=pa: e.matmul(
                            pa.t[:, :], wa.t[:, kc, :], hT.t[:, kc, :], start=(kc == 0), stop=(kc == KC - 1)),
                            reads=wa.b + hT.b, writes=pa.b)
                    for kc in range(KC):
                        P.op("pe", lambda e, wb_=wb_, kc=kc, pb_=pb_: e.matmul(
                            pb_.t[:, :], wb_.t[:, kc, :], hT.t[:, kc, :], start=(kc == 0), stop=(kc == KC - 1)),
                            reads=wb_.b + hT.b, writes=pb_.b)
                    tf = rot(tmpf, "tmpf")
                    P.op("act", lambda e, tf=tf, pa=pa: e.activation(tf.t[:, :], pa.t[:, :], AF.Silu),
                         reads=pa.b, writes=tf.b)
                    P.op("dve", lambda e, tf=tf, pb_=pb_, j=j: e.tensor_tensor(
                        gT.t[:, j, :], tf.t[:, :], pb_.t[:, :], ALU.mult),
                        reads=tf.b + pb_.b, writes=[gT.b[j]])
                for dc in range(KC):
                    po = rot(mm_banks, "pb")
                    for half in range(2):
                        wb = load_wo_half(w_out, dc, half, "fo%d%d" % (l, fi))
                        for jj in range(22):
                            j = half * 22 + jj
                            P.op("pe", lambda e, wb=wb, jj=jj, j=j, po=po: e.matmul(
                                po.t[:, :], wb.t[:, jj, :], gT.t[:, j, :], start=(j == 0), stop=(j == FC - 1)),
                                reads=wb.b + [gT.b[j]], writes=po.b)
                    P.op("dve", lambda e, dc=dc, po=po: e.scalar_tensor_tensor(
                        xT.t[:, dc, :], po.t[:, :], modG[l].t[:, s, dc:dc + 1], xT.t[:, dc, :], ALU.mult, ALU.add),
                        reads=po.b + modG[l].b + xT.b, writes=xT.b)
                store_xT_tile(tt, last)
            sc.__exit__(None, None, None)


        def attention(l, first=False, last=False):
            s = 1
            lambda_init = 0.8 - 0.6 * math.exp(-0.3 * l)
            scale = HD ** -0.5
            w_in = attnwi_in
            w_out = attnwo_in
            NQB = T // 256
            NKT = T // 128
            NQC = NHL * 2
            sc = Scope()
            sc.__enter__()
            if pair:
                scH = Scope()
                scH.__enter__()
                if first:
                    xtok[:] = [sb("xtok%d" % i, [128, D], F32) for i in range(2)]
                for tt in range(NTT):
                    load_xT_tile(tt, first)
                    if first:
                        store_xT_tile(tt, False)
                    norm_mod(l, s)
                    P.dma("pool", cg_h[tt].i[:, :].rearrange("(kc p) t -> p kc t", p=128), hT.t[:, :, :],
                          reads=hT.b, writes=[cg_h[tt].ib])
                    cg_h[tt].go()
                scH.__exit__(None, None, None)
            ropec = sb("ropec", [32, 4], F32)
            rotm = sb("rotm", [32, 32], F32)
            qkg = sb("qkg", [128, 2], F32)
            negmb = sb("negmb", [128, 1], F32)
            nlam = sb("nlam", [128, 1], F32)
            subg = sb("subg", [128, 256], F32)
            P.dma("sp", ropec.t[:, :], ropec_in[:, :], reads=[in_buf], writes=ropec.b)
            P.dma("sp", rotm.t[:, :], rotm_in[:, :], reads=[in_buf], writes=rotm.b)
            P.dma("sp", qkg.t[:, :], qkg_in[:, :], reads=[in_buf], writes=qkg.b)
            P.dma("sp", subg.t[:, :], subg_in[:, :], reads=[in_buf], writes=subg.b)
            P.op("dve", lambda e: e.tensor_scalar(subg.t[:, :], subg.t[:, :], 1.0 - lambda_init, None, ALU.mult),
                 reads=subg.b, writes=subg.b)
            scA = Scope()
            scA.__enter__()
            wst[:] = [sb("wst%d" % i, [128, KC, 128], F32) for i in range(2)]
            wbf[:] = [sb("wbf%d" % i, [128, KC, 128], BF16) for i in range(5)]
            cosT = sb("cosT", [32, T], F32)
            sinT = sb("sinT", [32, T], F32)
            if first:
                xtok[:] = [sb("xtok%d" % i, [128, D], F32) for i in range(2)]
            sc2 = Scope()
            sc2.__enter__()
            grep_ = sb("grep", [128, 2, 128], F32)
            lrep = sb("lrep", [128, 4, 128], F32)
            sm = sb("sm", [128, 8], F32)
            posi = sb("posi", [32, T], I32)
            ang = sb("ang", [32, T], F32)
            ang2 = sb("ang2", [32, T], F32)
            P.dma("sp", grep_.t[:, :, :], qkgrep_in[:, :, :], reads=[in_buf], writes=grep_.b)
            P.dma("sp", lrep.t[:, :, :], lamrep_in[:, :, :], reads=[in_buf], writes=lrep.b)
            P.dma("sp", posi.t[:, :], pos_in[:, :], reads=[in_buf], writes=posi.b)
            for i in range(2):
                P.op("dve", lambda e, i=i: e.reduce_max(sm.t[:, i:i + 1], grep_.t[:, i, :], AX.X, apply_absolute_value=True),
                     reads=grep_.b, writes=sm.b)
            P.op("dve", lambda e: e.tensor_tensor(sm.t[:, 2:3], sm.t[:, 0:1], sm.t[:, 1:2], ALU.mult), reads=sm.b, writes=sm.b)
            P.op("dve", lambda e: e.tensor_scalar(negmb.t[:, :], sm.t[:, 2:3], -(HD * scale), None, ALU.mult),
                 reads=sm.b, writes=negmb.b)
            for i in range(2):
                P.op("dve", lambda e, i=i: e.tensor_tensor(grep_.t[:, i, :], lrep.t[:, 2 * i, :], lrep.t[:, 2 * i + 1, :], ALU.mult),
                     reads=lrep.b + grep_.b, writes=grep_.b)
                P.op("dve", lambda e, i=i: e.reduce_sum(sm.t[:, 3 + i:4 + i], grep_.t[:, i, :], AX.X), reads=grep_.b, writes=sm.b)
                P.op("act", lambda e, i=i: e.activation(sm.t[:, 5 + i:6 + i], sm.t[:, 3 + i:4 + i], AF.Exp), reads=sm.b, writes=sm.b)
            P.op("dve", lambda e: e.scalar_tensor_tensor(nlam.t[:, :], sm.t[:, 6:7], -lambda_init, sm.t[:, 5:6], ALU.add, ALU.subtract),
                 reads=sm.b, writes=nlam.b)
            P.op("dve", lambda e: e.tensor_copy(ang.t[:, :], posi.t[:, :]), reads=posi.b, writes=ang.b)
            P.op("dve", lambda e: e.tensor_scalar(ang.t[:, :], ang.t[:, :], ropec.t[:, 0:1], None, ALU.mult),
                 reads=ang.b + ropec.b, writes=ang.b)
            P.op("dve", lambda e: e.tensor_scalar(ang2.t[:, :], ang.t[:, :], 1.0 / TWO_PI, None, ALU.mult), reads=ang.b, writes=ang2.b)
            P.op("dve", lambda e: e.tensor_copy(posi.t[:, :], ang2.t[:, :]), reads=ang2.b, writes=posi.b)
            P.op("dve", lambda e: e.tensor_copy(ang2.t[:, :], posi.t[:, :]), reads=posi.b, writes=ang2.b)
            P.op("dve", lambda e: e.scalar_tensor_tensor(ang.t[:, :], ang2.t[:, :], -TWO_PI, ang.t[:, :], ALU.mult, ALU.add),
                 reads=ang2.b + ang.b, writes=ang.b)

            def wrap_pi():
                P.op("dve", lambda e: e.tensor_scalar(ang2.t[:, :], ang.t[:, :], math.pi, TWO_PI, ALU.is_gt, ALU.mult),
                     reads=ang.b, writes=ang2.b)
                P.op("dve", lambda e: e.tensor_tensor(ang.t[:, :], ang.t[:, :], ang2.t[:, :], ALU.subtract),
                     reads=ang.b + ang2.b, writes=ang.b)
            wrap_pi()
            P.op("act", lambda e: e.activation(sinT.t[:, :], ang.t[:, :], AF.Sin, scale=ropec.t[:, 1:2]),
                 reads=ang.b + ropec.b, writes=sinT.b)
            P.op("dve", lambda e: e.tensor_scalar(ang.t[:, :], ang.t[:, :], 0.5 * math.pi, None, ALU.add), reads=ang.b + sinT.b, writes=ang.b)
            wrap_pi()
            P.op("act", lambda e: e.activation(cosT.t[:, :], ang.t[:, :], AF.Sin), reads=ang.b, writes=cosT.b)
            sc2.__exit__(None, None, None)

            qn = [sb("qn%d" % i, [128, NT], F32) for i in range(2)]
            rr = [sb("rr%d" % i, [128, NT], F32) for i in range(2)]
            rt1 = sb("rt1", [32, NT], F32)
            rt2 = sb("rt2", [32, NT], F32)
            qkb = [sb("qkb%d" % i, [128, NT], BF16) for i in range(3)]
            vtile = sb("vtile", [128, 4, NHL * 256], BF16)
            actr = {"qn": 0, "qkb": 0}
            qk_bufs = [Buf("qk%d" % i) for i in range(4 * NHL)]
            for tt in range(NTG):
                t0 = tt * NT
                if pair:
                    cgx = cg_h[tt % NTT]
                    src = cgx.o[(tt // NTT) * D:(tt // NTT + 1) * D, :].rearrange("(kc p) t -> p kc t", p=128)
                    P.dma("sp", hT.t[:, :, :], src, reads=[cgx.ob], writes=hT.b)
                else:
                    load_xT_tile(tt, first)
                    if first:
                        store_xT_tile(tt, False)
                    norm_mod(l, s)
                def stA(ch):
                    w_ = load_w_chunk(w_in, ch * 128, "awi")
                    p_ = rot(mm_banks, "pb")
                    for kc in range(KC):
                        P.op("pe", lambda e, w_=w_, kc=kc, p_=p_: e.matmul(
                            p_.t[:, :], w_.t[:, kc, :], hT.t[:, kc, :], start=(kc == 0), stop=(kc == KC - 1)),
                            reads=w_.b + hT.b, writes=p_.b)
                    return {"ch": ch, "p": p_}

                def stB(st):
                    ch, pq = st["ch"], st["p"]
                    if ch >= 2 * NQC:
                        vb = qkb[actr["qkb"] % 3]
                        actr["qkb"] += 1
                        P.op("act", lambda e, vb=vb, pq=pq: e.activation(vb.t[:, :], pq.t[:, :], AF.Copy), reads=pq.b, writes=vb.b)
                        for sub in range(4):
                            P.op("pe", lambda e, vb=vb, sub=sub: e.transpose(
                                pbf.t[:, sub * 128:(sub + 1) * 128], vb.t[:, sub * 128:(sub + 1) * 128], identb.t[:, :]),
                                reads=vb.b + identb.b, writes=pbf.b)
                        vc = ch - 2 * NQC
                        P.op("dve", lambda e, vc=vc: e.tensor_copy(
                            vtile.t[:, :, vc * 128:(vc + 1) * 128], pbf.t[:, 0:512].rearrange("p (s c) -> p s c", c=128)),
                            reads=pbf.b, writes=vtile.b)
                        return
                    sq = rot(tmpb, "tmpb")
                    P.op("act", lambda e, sq=sq, pq=pq: e.activation(sq.t[:, :], pq.t[:, :], AF.Square), reads=pq.b, writes=sq.b)
                    P.op("pe", lambda e, sq=sq: e.matmul(ssq_bank.t[:, :], onesb.t[:, :], sq.t[:, :], start=True, stop=True),
                         reads=sq.b + onesb.b, writes=ssq_bank.b)
                    r = rr[ch % 2]
                    q = qn[ch % 2]
                    st["q"] = q
                    P.op("dve", lambda e, r=r: e.tensor_scalar(r.t[:, :], ssq_bank.t[:, :], 1.0 / HD, EPS, ALU.mult, ALU.add),
                         reads=ssq_bank.b, writes=r.b)
                    P.op("act", lambda e, r=r: e.activation(r.t[:, :], r.t[:, :], AF.Sqrt), reads=r.b, writes=r.b)
                    P.op("dve", lambda e, r=r: e.reciprocal(r.t[:, :], r.t[:, :]), reads=r.b, writes=r.b)
                    gi = 0 if ch < NQC else 1
                    P.op("dve", lambda e, q=q, r=r, pq=pq, gi=gi: e.scalar_tensor_tensor(
                        q.t[:, :], pq.t[:, :], qkg.t[:, gi:gi + 1], r.t[:, :], ALU.mult, ALU.mult),
                        reads=pq.b + qkg.b + r.b, writes=q.b)

                def stC(st):
                    ch = st["ch"]
                    if ch >= 2 * NQC:
                        return
                    q = st["q"]
                    rb = tr_banks[ch % 2]
                    P.op("pe", lambda e, q=q, rb=rb: e.matmul(rb.t[0:32, :], rotm.t[:, :], q.t[0:32, :], start=True, stop=True),
                         reads=q.b + rotm.b, writes=rb.b)
                    P.op("pool", lambda e, q=q, t0=t0: e.tensor_tensor(rt1.t[:, :], q.t[0:32, :], cosT.t[:, t0:t0 + NT], ALU.mult),
                         reads=q.b + cosT.b, writes=rt1.b)
                    P.op("dve", lambda e, rb=rb, t0=t0: e.tensor_tensor(rt2.t[:, :], rb.t[0:32, :], sinT.t[:, t0:t0 + NT], ALU.mult),
                         reads=rb.b + sinT.b, writes=rt2.b)
                    P.op("pool", lambda e, q=q: e.tensor_tensor(q.t[0:32, :], rt1.t[:, :], rt2.t[:, :], ALU.add),
                         reads=rt1.b + rt2.b, writes=q.b)
                    qb_ = qkb[actr["qkb"] % 3]
                    actr["qkb"] += 1
                    P.op("act", lambda e, q=q, qb_=qb_: e.activation(qb_.t[:, :], q.t[:, :], AF.Copy), reads=q.b, writes=qb_.b)
                    P.dma("act", qkT_d[ch, :, t0:t0 + NT], qb_.t[:, :], reads=qb_.b, writes=[qk_bufs[ch]])

                sts = []
                nch = 3 * NQC
                for i in range(nch + 2):
                    if i < nch:
                        sts.append(stA(i))
                    if 1 <= i <= nch:
                        stB(sts[i - 1])
                    if i >= 2:
                        stC(sts[i - 2])
                P.dma("act", V_d[t0:t0 + NT, :].rearrange("(s p) c -> p s c", p=128), vtile.t[:, :, :],
                      reads=vtile.b, writes=[v_buf])
            scA.__exit__(None, None, None)

            scB = Scope()
            scB.__enter__()
            qk_t = [[sb("qk_%d_%d" % (i, j), [128, T], BF16) for j in range(4)] for i in range(2)]
            v_t = [sb("v_%d" % i, [128, NKT, 257], BF16) for i in range(2)]
            maskt = sb("maskt", [128, 2, 256], BF16)
            pT = [sb("pT%d" % i, [128, 256], BF16) for i in range(5)]
            oacc = [[sb("oacc%d_%d" % (c, sub), [128, 257], F32) for sub in range(2)] for c in range(2)]
            sm2 = [sb("sm2_%d" % i, [128, 8], F32) for i in range(2)]
            ot = [sb("ot%d" % i, [128, 256], F32) for i in range(2)]
            osq = sb("osq", [128, 256], F32)
            onb = [sb("onb%d" % i, [128, 256], BF16) for i in range(2)]
            oTt = [sb("oTt%d" % i, [128, 2, 128], BF16) for i in range(2)]
            P.dma("sp", maskt.t[:, :, :], mask_in[:, :, :], reads=[in_buf], writes=maskt.b)
            for i in range(2):
                P.op("dve", lambda e, i=i: e.memset(v_t[i].t[:, :, 256:257], 1.0), writes=v_t[i].b)
            acc_banks = [[pbank[0], pbank[1]], [pbank[2], pbank[3]]]
            sc_banks = [pbank[4], pbank[5], pbank[6]]
            bctr = {"sc": 0, "pT": 0, "o": 0}
            from collections import deque
            LA = 2
            pendq = deque()

            def front(h, qb, c, kt, qkh):
                q0 = qb * 256
                sb_ = sc_banks[bctr["sc"] % 3]
                bctr["sc"] += 1
                P.op("pe", lambda e, sb_=sb_, kt=kt, c=c, q0=q0, qkh=qkh: e.matmul(
                    sb_.t[:, 0:256], qkh[2 + c].t[:, kt * 128:(kt + 1) * 128], qkh[c].t[:, q0:q0 + 256],
                    start=True, stop=True), reads=qkh[2 + c].b + qkh[c].b, writes=sb_.b)
                p_ = pT[bctr["pT"] % len(pT)]
                bctr["pT"] += 1
                P.op("act", lambda e, p_=p_, sb_=sb_: e.activation(
                    p_.t[:, :], sb_.t[:, 0:256], AF.Exp, bias=negmb.t[:, 0:1], scale=scale),
                    reads=sb_.b + negmb.b, writes=p_.b)
                if kt >= 2 * qb:
                    mi = kt - 2 * qb
                    P.op("dve", lambda e, p_=p_, mi=mi: e.tensor_tensor(p_.t[:, :], p_.t[:, :], maskt.t[:, mi, :], ALU.mult),
                         reads=p_.b + maskt.b, writes=p_.b)
                return p_

            def back(h, qb, c, kt, vh, p_):
                q0 = qb * 256
                nkt = 2 * qb + 2
                accs = acc_banks[c]
                for sub in range(2):
                    last_kt = 2 * qb + sub
                    if kt > last_kt:
                        continue
                    P.op("pe", lambda e, p_=p_, sub=sub, kt=kt, vh=vh, accs=accs, last_kt=last_kt: e.matmul(
                        accs[sub].t[:, 0:257], p_.t[:, sub * 128:(sub + 1) * 128], vh.t[:, kt, :],
                        start=(kt == 0), stop=(kt == last_kt)), reads=p_.b + vh.b, writes=accs[sub].b)
                if kt != nkt - 1:
                    return
                for sub in range(2):
                    if sub == 0:
                        P.op("act", lambda e, c=c, sub=sub, accs=accs: e.activation(
                            oacc[c][sub].t[:, :], accs[sub].t[:, 0:257], AF.Copy), reads=accs[sub].b, writes=oacc[c][sub].b)
                    else:
                        P.op("dve", lambda e, c=c, sub=sub, accs=accs: e.tensor_copy(
                            oacc[c][sub].t[:, :], accs[sub].t[:, 0:257]), reads=accs[sub].b, writes=oacc[c][sub].b)
                if c != 1:
                    return
                for sub in range(2):
                    k_ = bctr["o"] % 2
                    bctr["o"] += 1
                    sm_ = sm2[k_]
                    o_ = ot[k_]
                    on_ = onb[k_]
                    oT_ = oTt[k_]
                    o0 = oacc[0][sub]
                    o1 = oacc[1][sub]
                    P.op("dve", lambda e, sm_=sm_, o0=o0: e.reciprocal(sm_.t[:, 0:1], o0.t[:, 256:257]), reads=o0.b, writes=sm_.b)
                    P.op("dve", lambda e, sm_=sm_, o1=o1: e.reciprocal(sm_.t[:, 1:2], o1.t[:, 256:257]), reads=o1.b + sm_.b, writes=sm_.b)
                    P.op("dve", lambda e, sm_=sm_: e.tensor_tensor(sm_.t[:, 2:3], sm_.t[:, 1:2], nlam.t[:, 0:1], ALU.mult),
                         reads=sm_.b + nlam.b, writes=sm_.b)
                    P.op("dve", lambda e, sm_=sm_, o_=o_, o0=o0: e.tensor_scalar(
                        o_.t[:, :], o0.t[:, 0:256], sm_.t[:, 0:1], None, ALU.mult), reads=o0.b + sm_.b, writes=o_.b)
                    P.op("dve", lambda e, sm_=sm_, o_=o_, o1=o1: e.scalar_tensor_tensor(
                        o_.t[:, :], o1.t[:, 0:256], sm_.t[:, 2:3], o_.t[:, :], ALU.mult, ALU.add),
                        reads=o1.b + sm_.b + o_.b, writes=o_.b)
                    P.op("dve", lambda e, o_=o_: e.tensor_tensor(osq.t[:, :], o_.t[:, :], o_.t[:, :], ALU.mult), reads=o_.b, writes=osq.b)
                    P.op("dve", lambda e, sm_=sm_: e.reduce_sum(sm_.t[:, 3:4], osq.t[:, :], AX.X), reads=osq.b + sm_.b, writes=sm_.b)
                    P.op("dve", lambda e, sm_=sm_: e.tensor_scalar(sm_.t[:, 4:5], sm_.t[:, 3:4], 1.0 / 256, EPS, ALU.mult, ALU.add),
                         reads=sm_.b, writes=sm_.b)
                    P.op("act", lambda e, sm_=sm_: e.activation(sm_.t[:, 5:6], sm_.t[:, 4:5], AF.Sqrt), reads=sm_.b, writes=sm_.b)
                    P.op("dve", lambda e, sm_=sm_: e.reciprocal(sm_.t[:, 6:7], sm_.t[:, 5:6]), reads=sm_.b, writes=sm_.b)
                    P.op("dve", lambda e, sm_=sm_, o_=o_, on_=on_: e.scalar_tensor_tensor(
                        on_.t[:, :], o_.t[:, :], sm_.t[:, 6:7], subg.t[:, :], ALU.mult, ALU.mult),
                        reads=o_.b + sm_.b + subg.b, writes=on_.b)
                    for fc in range(2):
                        P.op("pe", lambda e, on_=on_, fc=fc: e.transpose(
                            pbf.t[:, fc * 128:(fc + 1) * 128], on_.t[:, fc * 128:(fc + 1) * 128], identb.t[:, :]),
                            reads=on_.b + identb.b, writes=pbf.b)
                    P.op("act", lambda e, oT_=oT_: e.activation(
                        oT_.t[:, :, :], pbf.t[:, 0:256].rearrange("p (f c) -> p f c", c=128), AF.Copy), reads=pbf.b, writes=oT_.b)
                    qs = q0 + sub * 128
                    if pair:
                        cgx = cg_o[qs // OTC]
                        P.dma("pool", cgx.i[h * 256:(h + 1) * 256, qs % OTC:qs % OTC + 128].rearrange("(f p) q -> p f q", p=128),
                              oT_.t[:, :, :], reads=oT_.b, writes=[cgx.ib])
                    else:
                        P.dma("pool", oT_d[h * 256:(h + 1) * 256, qs:qs + 128].rearrange("(f p) q -> p f q", p=128),
                              oT_.t[:, :, :], reads=oT_.b, writes=[oT_buf])

            for h in range(NHL):
                qkh = qk_t[h % 2]
                vh = v_t[h % 2]
                for c in range(2):
                    P.dma("sp", qkh[c].t[:, :], qkT_d[h * 2 + c, :, :], reads=[qk_bufs[h * 2 + c]], writes=qkh[c].b)
                    P.dma("sp", qkh[2 + c].t[:, :], qkT_d[NQC + h * 2 + c, :, :], reads=[qk_bufs[NQC + h * 2 + c]], writes=qkh[2 + c].b)
                P.dma("act", vh.t[:, :, 0:256], V_d[:, h * 256:(h + 1) * 256].rearrange("(kt p) c -> p kt c", p=128),
                      reads=[v_buf], writes=vh.b)
                for qb in range(NQB):
                    for c in range(2):
                        for kt in range(2 * qb + 2):
                            p_ = front(h, qb, c, kt, qkh)
                            pendq.append((h, qb, c, kt, vh, p_))
                            if len(pendq) > LA:
                                back(*pendq.popleft())
            while pendq:
                back(*pendq.popleft())
            scB.__exit__(None, None, None)

            if pair:
                for cgx in cg_o:
                    cgx.go()
            scC = Scope()
            scC.__enter__()
            if pair:
                ocand = [sb("ocand%d" % i, [128, KC, NT], BF16) for i in range(2)]
            wst[:] = [sb("wst%d" % i, [128, KC, 128], F32) for i in range(2)]
            wbf[:] = [sb("wbf%d" % i, [128, KC, 128], BF16) for i in range(5)]
            if last:
                xtok[:] = [sb("xtok%d" % i, [128, D], F32) for i in range(2)]
            for tt in range(NTT):
                t0 = tt * NT
                load_xT_tile(tt, False)
                if pair:
                    for r_ in range(2):
                        g0 = r_ * TL + t0
                        cgx = cg_o[g0 // OTC]
                        src = cgx.o[:, g0 % OTC:g0 % OTC + NT].rearrange("(kc p) t -> p kc t", p=128)
                        P.dma(("sp", "act")[r_], ocand[r_].t[:, :, :], src, reads=[cgx.ob], writes=ocand[r_].b)
                    select2(hT.t[:, :, :], ocand[0].t[:, :, :], ocand[1].t[:, :, :], ocand[0].t[:, :, :],
                            ocand[0].b + ocand[1].b, hT.b + ocand[0].b)
                else:
                    src = oT_d[:, t0:t0 + NT].rearrange("(kc p) t -> p kc t", p=128)
                    P.dma("sp", hT.t[:, :, :], src, reads=[oT_buf], writes=hT.b)
                for dc in range(KC):
                    wo = load_w_chunk(w_out, dc * 128, "awo")
                    po = rot(mm_banks, "pb")
                    for kc in range(KC):
                        P.op("pe", lambda e, wo=wo, kc=kc, po=po: e.matmul(
                            po.t[:, :], wo.t[:, kc, :], hT.t[:, kc, :], start=(kc == 0), stop=(kc == KC - 1)),
                            reads=wo.b + hT.b, writes=po.b)
                    P.op("dve", lambda e, dc=dc, po=po: e.scalar_tensor_tensor(
                        xT.t[:, dc, :], po.t[:, :], modG[l].t[:, s, dc:dc + 1], xT.t[:, dc, :], ALU.mult, ALU.add),
                        reads=po.b + modG[l].b + xT.b, writes=xT.b)
                store_xT_tile(tt, last)
            scC.__exit__(None, None, None)
            sc.__exit__(None, None, None)


        def ssm(l, first=False, last=False):
            s = 1
            sc = Scope()
            sc.__enter__()
            rho = sb("rho", [128, NGP], F32)
            fq = sb("fq", [128, NGP], F32)
            c512 = sb("c512", [128, NGP], F32)
            s512 = sb("s512", [128, NGP], F32)
            pm = sb("pm", [128, 2], F32)
            dT = sb("dT", [128, 16], F32)
            iota = sb("iota", [128, NT], F32)
            P.dma("sp", pm.t[:, :], pm_in[:, :], reads=[in_buf], writes=pm.b)
            P.dma("sp", dT.t[:, :], ssmd_in[:, :], reads=[in_buf], writes=dT.b)
            P.dma("sp", iota.t[:, :], iota_in[:, :], reads=[in_buf], writes=iota.b)
            scL = Scope()
            scL.__enter__()
            LB = [sb("LB%d" % i, [128, NGP, 128], BF16) for i in range(2)]
            LC = [sb("LC%d" % i, [128, NGP, 128], BF16) for i in range(2)]
            for i in range(2):
                P.op("pool", lambda e, i=i: e.memset(LB[i].t[:, :, :], 0.0), writes=LB[i].b)
                P.op("pool", lambda e, i=i: e.memset(LC[i].t[:, :, :], 0.0), writes=LC[i].b)

            def frac_wrap(dst, src, W, tmpi, tmpf_):
                P.op("dve", lambda e: e.tensor_copy(tmpi, src), reads=[gen_b], writes=[gen_b])
                P.op("dve", lambda e: e.tensor_copy(tmpf_, tmpi), reads=[gen_b], writes=[gen_b])
                P.op("dve", lambda e: e.tensor_tensor(dst, src, tmpf_, ALU.subtract), reads=[gen_b], writes=[gen_b])
                wrap_half(dst, tmpf_)

            def wrap_half(dst, tmpf_):
                P.op("dve", lambda e: e.tensor_scalar(tmpf_, dst, 0.5, None, ALU.is_gt), reads=[gen_b], writes=[gen_b])
                P.op("dve", lambda e: e.tensor_tensor(dst, dst, tmpf_, ALU.subtract), reads=[gen_b], writes=[gen_b])
                P.op("dve", lambda e: e.tensor_scalar(tmpf_, dst, -0.5, None, ALU.is_lt), reads=[gen_b], writes=[gen_b])
                P.op("dve", lambda e: e.tensor_tensor(dst, dst, tmpf_, ALU.add), reads=[gen_b], writes=[gen_b])

            gen_b = Buf("ssm_setup")

            def G(eng, fn):
                P.op(eng, fn, reads=[gen_b], writes=[gen_b])

            scS = Scope()
            scS.__enter__()
            ap_ = sb("a_pair", [128, 3, NGP], F32)
            P.dma("sp", ap_.t[:, :, :], apair_in[:, :, :], reads=[in_buf], writes=[gen_b])
            t64 = [sb("t64_%d" % i, [128, NGP], F32) for i in range(4)]
            t64i = sb("t64i", [128, NGP], I32)
            dtp, thp, fr64, tm64 = [t.t[:, :] for t in t64]
            G("act", lambda e: e.activation(dtp, ap_.t[:, 2, :], AF.Exp))
            G("dve", lambda e: e.tensor_tensor(thp, dtp, ap_.t[:, 0, :], ALU.mult))
            G("act", lambda e: e.activation(rho.t[:, :], thp, AF.Exp))
            G("dve", lambda e: e.tensor_tensor(thp, dtp, ap_.t[:, 1, :], ALU.mult))
            G("dve", lambda e: e.tensor_scalar(thp, thp, 1.0 / TWO_PI, None, ALU.mult))
            frac_wrap(fq.t[:, :], thp, NGP, t64i.t[:, :], tm64)
            G("dve", lambda e: e.tensor_scalar(thp, fq.t[:, :], float(NT), None, ALU.mult))
            frac_wrap(fr64, thp, NGP, t64i.t[:, :], tm64)
            G("act", lambda e: e.activation(s512.t[:, :], fr64, AF.Sin, scale=TWO_PI))
            G("dve", lambda e: e.tensor_scalar(fr64, fr64, 0.25, None, ALU.add))
            wrap_half(fr64, tm64)
            G("act", lambda e: e.activation(c512.t[:, :], fr64, AF.Sin, scale=TWO_PI))
            W = NKL * 64
            af = sb("a_feat", [128, 3, W], F32)
            bf_ = sb("b_feat", [128, 2, W], F32)
            P.dma("sp", af.t[:, :, :], afeat_in[:, :, :], reads=[in_buf], writes=[gen_b])
            P.dma("act", bf_.t[:, :, :], bfeat_in[:, :, :], reads=[in_buf], writes=[gen_b])
            tw = [sb("tw%d" % i, [128, W], F32) for i in range(8)]
            twi = sb("twi", [128, W], I32)
            dtf, mag, th, cs, sn, t5, t6, t7 = [t.t[:, :] for t in tw]
            ar = af.t[:, 0, :]
            ai = af.t[:, 1, :]
            G("act", lambda e: e.activation(dtf, af.t[:, 2, :], AF.Exp))
            G("dve", lambda e: e.tensor_tensor(th, dtf, ar, ALU.mult))
            G("act", lambda e: e.activation(mag, th, AF.Exp))
            G("dve", lambda e: e.tensor_tensor(th, dtf, ai, ALU.mult))
            G("dve", lambda e: e.tensor_scalar(th, th, 1.0 / TWO_PI, None, ALU.mult))
            frac_wrap(t5, th, W, twi.t[:, :], t6)
            G("act", lambda e: e.activation(sn, t5, AF.Sin, scale=TWO_PI))
            G("dve", lambda e: e.tensor_scalar(t5, t5, 0.25, None, ALU.add))
            wrap_half(t5, t6)
            G("act", lambda e: e.activation(cs, t5, AF.Sin, scale=TWO_PI))
            G("dve", lambda e: e.tensor_tensor(cs, cs, mag, ALU.mult))
            G("dve", lambda e: e.tensor_scalar(cs, cs, -1.0, None, ALU.add))
            G("dve", lambda e: e.tensor_tensor(sn, sn, mag, ALU.mult))
            G("dve", lambda e: e.tensor_tensor(t5, ar, ar, ALU.mult))
            G("dve", lambda e: e.tensor_tensor(t6, ai, ai, ALU.mult))
            G("dve", lambda e: e.tensor_tensor(t5, t5, t6, ALU.add))
            G("dve", lambda e: e.reciprocal(t5, t5))
            G("dve", lambda e: e.tensor_tensor(t6, cs, ar, ALU.mult))
            G("dve", lambda e: e.tensor_tensor(t7, sn, ai, ALU.mult))
            G("dve", lambda e: e.tensor_tensor(t6, t6, t7, ALU.add))
            G("dve", lambda e: e.tensor_tensor(t6, t6, t5, ALU.mult))
            G("dve", lambda e: e.tensor_tensor(t7, sn, ar, ALU.mult))
            G("dve", lambda e: e.tensor_tensor(mag, cs, ai, ALU.mult))
            G("dve", lambda e: e.tensor_tensor(t7, t7, mag, ALU.subtract))
            G("dve", lambda e: e.tensor_tensor(t7, t7, t5, ALU.mult))
            br = bf_.t[:, 0, :]
            bi = bf_.t[:, 1, :]
            G("dve", lambda e: e.tensor_tensor(cs, t6, br, ALU.mult))
            G("dve", lambda e: e.tensor_tensor(mag, t7, bi, ALU.mult))
            G("dve", lambda e: e.tensor_tensor(cs, cs, mag, ALU.subtract))
            G("dve", lambda e: e.tensor_tensor(sn, t6, bi, ALU.mult))
            G("dve", lambda e: e.tensor_tensor(mag, t7, br, ALU.mult))
            G("dve", lambda e: e.tensor_tensor(sn, sn, mag, ALU.add))
            bb = [tw[3].t, tw[4].t]
            for gp in range(NGP):
                kc, j = gp // 4, gp % 4
                r0 = 32 * j
                for ri in range(2):
                    for gi in range(2):
                        eng = "dve" if gi == 0 else "pool"
                        P.op(eng, lambda e, ri=ri, gi=gi, gp=gp, kc=kc, r0=r0: e.tensor_scalar(
                            LB[ri].t[r0:r0 + 32, gp, gi * 64:(gi + 1) * 64], bb[ri][r0:r0 + 32, kc * 64:(kc + 1) * 64],
                            pm.t[r0:r0 + 32, gi:gi + 1], None, ALU.mult),
                            reads=[gen_b] + pm.b, writes=LB[ri].b)
            cp = sb("c_pair", [128, 2, NGP, 16], F32)
            P.dma("sp", cp.t[:, :, :, :], cpair_in[:, :, :, :], reads=[in_buf], writes=[gen_b])
            for gp in range(NGP):
                j = gp % 4
                for ri in range(2):
                    for gi in range(2):
                        c0 = 32 * j + gi * 16
                        eng = "dve" if gi == 0 else "pool"
                        sgn = 1.0 if ri == 0 else -1.0
                        P.op(eng, lambda e, ri=ri, gi=gi, gp=gp, c0=c0, sgn=sgn: e.tensor_scalar(
                            LC[ri].t[gi * 64:(gi + 1) * 64, gp, c0:c0 + 16], cp.t[gi * 64:(gi + 1) * 64, ri, gp, :],
                            sgn, None, ALU.mult), reads=[gen_b], writes=LC[ri].b)
            scS.__exit__(None, None, None)

            scA = Scope()
            scA.__enter__()
            if first:
                xtok[:] = [sb("xtok%d" % i, [128, D], F32) for i in range(2)]
            for tt in range(NTT):
                t0 = tt * NT
                load_xT_tile(tt, first)
                if first:
                    store_xT_tile(tt, False)
                norm_mod(l, s)
                if pair:
                    P.dma("pool", cg_u[tt].i[:, :].rearrange("(kc p) t -> p kc t", p=128), hT.t[:, :, :],
                          reads=hT.b, writes=[cg_u[tt].ib])
                    cg_u[tt].go()
                else:
                    P.dma("pool", uT_d[:, t0:t0 + NT].rearrange("(kc p) t -> p kc t", p=128), hT.t[:, :, :],
                          reads=hT.b, writes=[u_buf])
            if pair:
                ucand = [[sb("ucand%d_%d" % (i, j), [128, NKL, NT], BF16) for j in range(2)] for i in range(2)]
                um_buf = Buf("um")
                it_ = 0
                for tt in range(NTT):
                    for r_ in range(2):
                        cs_ = ucand[it_ % 2]
                        it_ += 1
                        for j in range(2):
                            row0 = r_ * D + j * NKL * 128
                            P.dma(("sp", "act")[j], cs_[j].t[:, :, :],
                                  cg_u[tt].o[row0:row0 + NKL * 128, :].rearrange("(kc p) t -> p kc t", p=128),
                                  reads=[cg_u[tt].ob], writes=cs_[j].b)
                        select2(cs_[0].t[:, :, :], cs_[0].t[:, :, :], cs_[1].t[:, :, :], cs_[0].t[:, :, :], cs_[0].b + cs_[1].b, cs_[0].b)
                        g0 = r_ * TL + tt * NT
                        P.dma("pool", uT_d[:, g0:g0 + NT].rearrange("(kc p) t -> p kc t", p=128), cs_[0].t[:, :, :],
                              reads=cs_[0].b, writes=[um_buf])
                u_rd = um_buf
            else:
                u_rd = u_buf
            scA.__exit__(None, None, None)

            scB = Scope()
            scB.__enter__()
            cosTs = [sb("s_cos%d" % i, [128, NT], F32) for i in range(2)]
            sinTs = [sb("s_sin%d" % i, [128, NT], F32) for i in range(2)]
            rhobs = [sb("s_rhob%d" % i, [128, NT], F32) for i in range(2)]
            tg = [sb("s_tg%d" % i, [128, NT], F32) for i in range(2)]
            tgi = sb("s_tgi", [128, NT], I32)
            uch = [sb("s_u%d" % i, [128, NT], BF16) for i in range(3)]
            bsb = [[sb("s_b%d_%d" % (k, i), [128, NT], F32) for i in range(2)] for k in range(2)]
            fq_ = [[sb("s_f%d_%d" % (k, i), [128, NT], F32) for i in range(4)] for k in range(2)]
            bq_ = [sb("s_q%d" % i, [128, NT], F32) for i in range(4)]
            btr = [sb("s_btr%d" % i, [128, NT], F32) for i in range(2)]
            bti = [sb("s_bti%d" % i, [128, NT], F32) for i in range(2)]
            st_r = [sb("s_str%d" % i, [128, NT], F32) for i in range(2)]
            st_i = [sb("s_sti%d" % i, [128, NT], F32) for i in range(2)]
            sbf_r = [sb("s_sbr%d" % i, [128, NT], BF16) for i in range(2)]
            sbf_i = [sb("s_sbi%d" % i, [128, NT], BF16) for i in range(2)]
            y32 = [sb("s_y%d" % i, [128, NT], F32) for i in range(2)]
            init = [sb("s_init%d" % i, [128, 4], F32) for i in range(2)]
            b_banks = [[pbank[0], pbank[1]], [pbank[2], pbank[3]]]
            y_banks = [pbank[4], pbank[5]]

            def tables(gp):
                cosT, sinT, rhob = cosTs[gp % 2], sinTs[gp % 2], rhobs[gp % 2]
                fr = tg[0].t[:, :]
                tm = tg[1].t[:, :]
                tb = tg[0].b + tg[1].b + tgi.b
                P.op("dve", lambda e, gp=gp: e.tensor_scalar(tm, iota.t[:, :], fq.t[:, gp:gp + 1], None, ALU.mult),
                     reads=iota.b + fq.b + tb, writes=tb)

                def T_(eng, fn, extra_w=()):
                    P.op(eng, fn, reads=tb, writes=tb + list(extra_w))
                T_("dve", lambda e: e.tensor_copy(tgi.t[:, :], tm))
                T_("dve", lambda e: e.tensor_copy(fr, tgi.t[:, :]))
                T_("dve", lambda e: e.tensor_tensor(fr, tm, fr, ALU.subtract))

                def wrapT():
                    T_("dve", lambda e: e.tensor_scalar(tm, fr, 0.5, None, ALU.is_gt))
                    T_("dve", lambda e: e.tensor_tensor(fr, fr, tm, ALU.subtract))
                    T_("dve", lambda e: e.tensor_scalar(tm, fr, -0.5, None, ALU.is_lt))
                    T_("dve", lambda e: e.tensor_tensor(fr, fr, tm, ALU.add))
                wrapT()
                T_("act", lambda e: e.activation(sinT.t[:, :], fr, AF.Sin, scale=TWO_PI), extra_w=sinT.b)
                T_("dve", lambda e: e.tensor_scalar(fr, fr, 0.25, None, ALU.add))
                wrapT()
                T_("act", lambda e: e.activation(cosT.t[:, :], fr, AF.Sin, scale=TWO_PI), extra_w=cosT.b)
                P.op("pool", lambda e, gp=gp: e.tensor_scalar(rhob.t[:, :], iota.t[:, :], 0.0, rho.t[:, gp:gp + 1], ALU.mult, ALU.add),
                     reads=iota.b + rho.b, writes=rhob.b)
                ini = init[gp % 2]
                P.op("pool", lambda e, ini=ini: e.memset(ini.t[:, :], 0.0), writes=ini.b)

            def s_front(it, gp, tt):
                k = it % 2
                kc = gp // 4
                t0 = tt * NT
                cosT, sinT, rhob, ini = cosTs[gp % 2], sinTs[gp % 2], rhobs[gp % 2], init[gp % 2]
                u_ = uch[it % 3]
                P.dma("sp", u_.t[:, :], uT_d[kc * 128:(kc + 1) * 128, t0:t0 + NT], reads=[u_rd], writes=u_.b)
                pre, pim = b_banks[k]
                P.op("pe", lambda e: e.matmul(pre.t[:, :], LB[0].t[:, gp, :], u_.t[:, :], start=True, stop=True),
                     reads=LB[0].b + u_.b, writes=pre.b)
                P.op("pe", lambda e: e.matmul(pim.t[:, :], LB[1].t[:, gp, :], u_.t[:, :], start=True, stop=True),
                     reads=LB[1].b + u_.b, writes=pim.b)
                bre, bim = bsb[k]
                f1, f2, f3, f4 = fq_[k]
                P.op("act", lambda e: e.activation(bre.t[:, :], pre.t[:, :], AF.Copy), reads=pre.b, writes=bre.b)
                P.op("act", lambda e: e.activation(bim.t[:, :], pim.t[:, :], AF.Copy), reads=pim.b, writes=bim.b)
                P.op("dve", lambda e: e.tensor_tensor(f1.t[:, :], bre.t[:, :], cosT.t[:, :], ALU.mult), reads=bre.b + cosT.b, writes=f1.b)
                P.op("dve", lambda e: e.tensor_tensor(f2.t[:, :], bim.t[:, :], sinT.t[:, :], ALU.mult), reads=bim.b + sinT.b, writes=f2.b)
                P.op("pool", lambda e: e.tensor_tensor(f3.t[:, :], bim.t[:, :], cosT.t[:, :], ALU.mult), reads=bim.b + cosT.b, writes=f3.b)
                P.op("pool", lambda e: e.tensor_tensor(f4.t[:, :], bre.t[:, :], sinT.t[:, :], ALU.mult), reads=bre.b + sinT.b, writes=f4.b)
                br_, bi_ = btr[k], bti[k]
                P.op("dve", lambda e: e.tensor_tensor(br_.t[:, :], f1.t[:, :], f2.t[:, :], ALU.add), reads=f1.b + f2.b, writes=br_.b)
                P.op("pool", lambda e: e.tensor_tensor(bi_.t[:, :], f3.t[:, :], f4.t[:, :], ALU.subtract), reads=f3.b + f4.b, writes=bi_.b)
                sr, si = st_r[k], st_i[k]
                P.op("dve", lambda e: e.tensor_tensor_scan(sr.t[:, :], rhob.t[:, :], br_.t[:, :], ini.t[:, 0:1], ALU.mult, ALU.add),
                     reads=rhob.b + br_.b + ini.b, writes=sr.b)
                P.op("dve", lambda e: e.tensor_tensor_scan(si.t[:, :], rhob.t[:, :], bi_.t[:, :], ini.t[:, 1:2], ALU.mult, ALU.add),
                     reads=rhob.b + bi_.b + ini.b, writes=si.b)
                P.op("dve", lambda e: e.tensor_tensor(ini.t[:, 2:3], si.t[:, NT - 1:NT], s512.t[:, gp:gp + 1], ALU.mult),
                     reads=si.b + s512.b + ini.b, writes=ini.b)
                P.op("dve", lambda e: e.tensor_tensor(ini.t[:, 3:4], sr.t[:, NT - 1:NT], s512.t[:, gp:gp + 1], ALU.mult),
                     reads=sr.b + s512.b + ini.b, writes=ini.b)
                P.op("dve", lambda e: e.scalar_tensor_tensor(ini.t[:, 0:1], sr.t[:, NT - 1:NT], c512.t[:, gp:gp + 1], ini.t[:, 2:3],
                                                             ALU.mult, ALU.subtract), reads=sr.b + c512.b + ini.b, writes=ini.b)
                P.op("dve", lambda e: e.scalar_tensor_tensor(ini.t[:, 1:2], si.t[:, NT - 1:NT], c512.t[:, gp:gp + 1], ini.t[:, 3:4],
                                                             ALU.mult, ALU.add), reads=si.b + c512.b + ini.b, writes=ini.b)

            def s_back(it, gp, tt):
                k = it % 2
                kc, j = gp // 4, gp % 4
                r0 = 32 * j
                t0 = tt * NT
                cosT, sinT = cosTs[gp % 2], sinTs[gp % 2]
                sr, si = st_r[k], st_i[k]
                zr, zi = sbf_r[k], sbf_i[k]
                b1, b2, b3, b4 = bq_
                P.op("dve", lambda e: e.tensor_tensor(b1.t[:, :], sr.t[:, :], cosT.t[:, :], ALU.mult), reads=sr.b + cosT.b, writes=b1.b)
                P.op("dve", lambda e: e.tensor_tensor(b2.t[:, :], si.t[:, :], sinT.t[:, :], ALU.mult), reads=si.b + sinT.b, writes=b2.b)
                P.op("pool", lambda e: e.tensor_tensor(b3.t[:, :], si.t[:, :], cosT.t[:, :], ALU.mult), reads=si.b + cosT.b, writes=b3.b)
                P.op("pool", lambda e: e.tensor_tensor(b4.t[:, :], sr.t[:, :], sinT.t[:, :], ALU.mult), reads=sr.b + sinT.b, writes=b4.b)
                P.op("dve", lambda e: e.tensor_tensor(zr.t[:, :], b1.t[:, :], b2.t[:, :], ALU.subtract), reads=b1.b + b2.b, writes=zr.b)
                P.op("pool", lambda e: e.tensor_tensor(zi.t[:, :], b3.t[:, :], b4.t[:, :], ALU.add), reads=b3.b + b4.b, writes=zi.b)
                yb = y_banks[k]
                P.op("pe", lambda e: e.matmul(yb.t[:, :], LC[0].t[:, gp, :], zr.t[:, :], start=True, stop=False),
                     reads=LC[0].b + zr.b, writes=yb.b)
                P.op("pe", lambda e: e.matmul(yb.t[:, :], LC[1].t[:, gp, :], zi.t[:, :], start=False, stop=True),
                     reads=LC[1].b + zi.b, writes=yb.b)
                y_ = y32[k]
                P.op("act", lambda e: e.activation(y_.t[r0:r0 + 32, :], yb.t[r0:r0 + 32, :], AF.Copy), reads=yb.b, writes=y_.b)
                if pair:
                    P.dma("act", cg_y[tt].i[kc * 128 + r0:kc * 128 + r0 + 32, :], y_.t[r0:r0 + 32, :], reads=y_.b, writes=[cg_y[tt].ib])
                else:
                    P.dma("act", y_d[kc * 128 + r0:kc * 128 + r0 + 32, t0:t0 + NT], y_.t[r0:r0 + 32, :], reads=y_.b, writes=[y_buf])

            it = 0
            prev = None
            for gp in range(NGP):
                tables(gp)
                for tt in range(NTG):
                    s_front(it, gp, tt)
                    if prev is not None:
                        s_back(*prev)
                    prev = (it, gp, tt)
                    it += 1
            s_back(*prev)
            scB.__exit__(None, None, None)
            scL.__exit__(None, None, None)
            if dbg == "y2":
                dbg2 = nc.dram_tensor("dbg_y", [D, T], F32, kind="ExternalOutput").ap()
                P.dma("sp", dbg2[:, :], y_d[:, :], reads=[y_buf], writes=[Buf("dbg2")])
                P.barrier()

            if pair:
                for cgx in cg_y:
                    cgx.go()
            scC = Scope()
            scC.__enter__()
            if pair:
                ycand = sb("ycand", [128, KC, NT], F32)
            wst[:] = [sb("wst%d" % i, [128, KC, 128], F32) for i in range(2)]
            wbf[:] = [sb("wbf%d" % i, [128, KC, 128], BF16) for i in range(5)]
            h32 = sb("h32", [128, KC, NT], F32)
            yt = sb("yt", [128, KC, NT], F32)
            g1 = [sb("g1_%d" % i, [128, NT], F32) for i in range(2)]
            if last:
                xtok[:] = [sb("xtok%d" % i, [128, D], F32) for i in range(2)]
            GC = 2.0 * math.sqrt(2.0 / math.pi)
            for tt in range(NTT):
                t0 = tt * NT
                load_xT_tile(tt, False)
                norm_mod(l, s, h32=h32, want_bf=False)
                if pair:
                    src0 = cg_y[tt].o[:, :].rearrange("(kc p) t -> p kc t", p=128)
                    src1 = cg_y[NTT + tt].o[:, :].rearrange("(kc p) t -> p kc t", p=128)
                    P.dma("sp", yt.t[:, :, :], src0, reads=[cg_y[tt].ob], writes=yt.b)
                    P.dma("act", ycand.t[:, :, :], src1, reads=[cg_y[NTT + tt].ob], writes=ycand.b)
                    select2(yt.t[:, :, :], yt.t[:, :, :], ycand.t[:, :, :], yt.t[:, :, :], yt.b + ycand.b, yt.b)
                else:
                    src = y_d[:, t0:t0 + NT].rearrange("(kc p) t -> p kc t", p=128)
                    P.dma("sp", yt.t[:, 0:8, :], src[:, 0:8, :], reads=[y_buf], writes=yt.b)
                    P.dma("act", yt.t[:, 8:16, :], src[:, 8:16, :], reads=[y_buf], writes=yt.b)
                for kc in range(KC):
                    ga = g1[kc % 2]
                    P.op("dve", lambda e, kc=kc: e.scalar_tensor_tensor(
                        yt.t[:, kc, :], h32.t[:, kc, :], dT.t[:, kc:kc + 1], yt.t[:, kc, :], ALU.mult, ALU.add),
                        reads=h32.b + dT.b + yt.b, writes=yt.b)
                    P.op("act", lambda e, kc=kc, ga=ga: e.activation(ga.t[:, :], yt.t[:, kc, :], AF.Square), reads=yt.b, writes=ga.b)
                    P.op("dve", lambda e, ga=ga: e.tensor_scalar(ga.t[:, :], ga.t[:, :], 0.044715, 1.0, ALU.mult, ALU.add), reads=ga.b, writes=ga.b)
                    P.op("dve", lambda e, kc=kc, ga=ga: e.tensor_tensor(ga.t[:, :], ga.t[:, :], yt.t[:, kc, :], ALU.mult), reads=ga.b + yt.b, writes=ga.b)
                    P.op("act", lambda e, ga=ga: e.activation(ga.t[:, :], ga.t[:, :], AF.Sigmoid, scale=GC), reads=ga.b, writes=ga.b)
                    P.op("pool", lambda e, kc=kc, ga=ga: e.tensor_tensor(hT.t[:, kc, :], ga.t[:, :], yt.t[:, kc, :], ALU.mult),
                         reads=ga.b + yt.b, writes=hT.b)
                for dc in range(KC):
                    wv = load_w_chunk(wglu_in, dc * 128, "glu")
                    wg = load_w_chunk(wglu_in, D + dc * 128, "glu")
                    pv = rot(mm_banks, "pb")
                    pg = rot(mm_banks, "pb")
                    for kc in range(KC):
                        P.op("pe", lambda e, wv=wv, kc=kc, pv=pv: e.matmul(
                            pv.t[:, :], wv.t[:, kc, :], hT.t[:, kc, :], start=(kc == 0), stop=(kc == KC - 1)),
                            reads=wv.b + hT.b, writes=pv.b)
                    for kc in range(KC):
                        P.op("pe", lambda e, wg=wg, kc=kc, pg=pg: e.matmul(
                            pg.t[:, :], wg.t[:, kc, :], hT.t[:, kc, :], start=(kc == 0), stop=(kc == KC - 1)),
                            reads=wg.b + hT.b, writes=pg.b)
                    tf = rot(tmpf, "tmpf")
                    P.op("act", lambda e, tf=tf, pg=pg: e.activation(tf.t[:, :], pg.t[:, :], AF.Sigmoid), reads=pg.b, writes=tf.b)
                    P.op("dve", lambda e, tf=tf, pv=pv: e.tensor_tensor(tf.t[:, :], tf.t[:, :], pv.t[:, :], ALU.mult),
                         reads=tf.b + pv.b, writes=tf.b)
                    P.op("dve", lambda e, dc=dc, tf=tf: e.scalar_tensor_tensor(
                        xT.t[:, dc, :], tf.t[:, :], modG[l].t[:, s, dc:dc + 1], xT.t[:, dc, :], ALU.mult, ALU.add),
                        reads=tf.b + modG[l].b + xT.b, writes=xT.b)
                store_xT_tile(tt, last)
            scC.__exit__(None, None, None)
            sc.__exit__(None, None, None)

        for i, ph in enumerate(phases):
            first = (i == 0)
            last = (i == len(phases) - 1)
            if ph.startswith("ffn"):
                l = int(ph[3]); fi = int(ph[4])
                ffn(l, 0 if fi == 0 else 2, fi, first=first, last=last)
            elif ph.startswith("att"):
                attention(int(ph[3]), first=first, last=last)
            elif ph.startswith("ssm"):
                ssm(int(ph[3]), first=first, last=last)

        P.emit()
    return nc


def make_in_maps(inputs, T=SEQ, phases=None, pair=False, n_cores=N_CORES):
    if phases is None:
        phases = ALL_PHASES
    if pair:
        return make_in_maps_pair(inputs, T, phases, n_cores)
    x = np.asarray(inputs["x"], dtype=np.float32)
    c = np.asarray(inputs["c"], dtype=np.float32)
    pos = np.asarray(inputs["positions"], dtype=np.int32)
    shared = {
        "norm_g": np.ascontiguousarray(np.asarray(inputs["norm_g"], np.float32).reshape(96, 128)),
        "ada_b": np.ascontiguousarray(np.asarray(inputs["ada_b"], np.float32).reshape(2, 144, 128)),
        "ident": np.eye(128, dtype=np.float32),
    }
    for l in sorted(set(int(ph[3]) for ph in phases)):
        shared["ada_w%d" % l] = np.asarray(inputs["ada_w"][l], np.float32)
    for ph in phases:
        if ph.startswith("ffn"):
            l = int(ph[3]); fi = int(ph[4])
            shared["ffn_w_in" + ph[3:]] = np.ascontiguousarray(np.asarray(inputs["ffn_w_in"][l, fi], np.float32))
            shared["ffn_w_out" + ph[3:]] = np.ascontiguousarray(np.asarray(inputs["ffn_w_out"][l, fi], np.float32))
    if any(ph.startswith("ssm") for ph in phases):
        p = np.arange(128)
        shared["pm"] = np.stack([1 - (p // 16) % 2, (p // 16) % 2], axis=1).astype(np.float32)
        shared["ssm_dT"] = np.ascontiguousarray(np.asarray(inputs["ssm_d"][0], np.float32).reshape(16, 128).T)
        shared["iota"] = np.ascontiguousarray(np.broadcast_to(np.arange(NT, dtype=np.float32)[None], (128, NT)))
        a_re = np.asarray(inputs["ssm_a_re"][0], np.float32)
        a_im = np.asarray(inputs["ssm_a_im"][0], np.float32)
        ls = np.asarray(inputs["ssm_log_step"][0], np.float32)
        lsb = np.broadcast_to(ls[:, None], (128, 64))
        def pair(a):
            sh = a.shape
            a = a.reshape((64, 2, 64) + sh[2:])
            a = np.moveaxis(a, 0, 2)
            return np.ascontiguousarray(a.reshape((128, 64) + sh[2:]))
        shared["a_pair"] = np.ascontiguousarray(np.stack([pair(a_re), pair(a_im), pair(lsb)], axis=1))
        def feat(a):
            a = a.reshape(16, 8, 64)
            a = np.broadcast_to(a[:, :, None, :], (16, 8, 16, 64))
            return np.ascontiguousarray(a.transpose(1, 2, 0, 3).reshape(128, 1024))
        shared["a_feat"] = np.ascontiguousarray(np.stack([feat(a_re), feat(a_im), feat(lsb)], axis=1))
        def featb(b):
            b = b.reshape(16, 8, 64, 16)
            return np.ascontiguousarray(b.transpose(1, 3, 0, 2).reshape(128, 1024))
        shared["b_feat"] = np.ascontiguousarray(np.stack(
            [featb(np.asarray(inputs["ssm_b_re"][0], np.float32)), featb(np.asarray(inputs["ssm_b_im"][0], np.float32))], axis=1))
        def pairc(c):
            return pair(np.ascontiguousarray(c.transpose(0, 2, 1)))
        shared["c_pair"] = np.ascontiguousarray(np.stack(
            [pairc(np.asarray(inputs["ssm_c_re"][0], np.float32)), pairc(np.asarray(inputs["ssm_c_im"][0], np.float32))], axis=1))
        shared["w_glu"] = np.ascontiguousarray(np.asarray(inputs["ssm_w_glu"][0], np.float32))
    if any(ph.startswith("att") for ph in phases):
        shared["attn_w_in"] = np.ascontiguousarray(np.asarray(inputs["attn_w_in"][0], np.float32))
        shared["attn_w_out"] = np.ascontiguousarray(np.asarray(inputs["attn_w_out"][0], np.float32))
        invf = (500000.0 ** (-np.arange(0, 32, 2, dtype=np.float32) / np.float32(32))).astype(np.float32)
        ropec = np.zeros((32, 4), np.float32)
        ropec[:, 0] = np.concatenate([invf, invf])
        sign = np.concatenate([-np.ones(16), np.ones(16)]).astype(np.float32)
        ropec[:, 1] = sign
        ropec[:, 2] = -math.pi * sign
        ropec[:, 3] = -math.pi
        shared["ropec"] = ropec
        rotm = np.zeros((32, 32), np.float32)
        for m_ in range(32):
            rotm[(m_ + 16) % 32, m_] = 1.0
        shared["rotm"] = rotm
        qg = np.asarray(inputs["attn_q_norm"][0], np.float32)
        kg = np.asarray(inputs["attn_k_norm"][0], np.float32)
        shared["qkg"] = np.ascontiguousarray(np.stack([qg, kg], axis=1))
        shared["qkgrep"] = np.ascontiguousarray(np.broadcast_to(np.stack([qg, kg], axis=0)[None], (128, 2, 128)))
        shared["subg"] = np.ascontiguousarray(np.broadcast_to(np.asarray(inputs["attn_subln"][0], np.float32)[None], (128, 256)))
        shared["lamrep"] = np.ascontiguousarray(np.broadcast_to(np.asarray(inputs["attn_lambda"][0], np.float32)[None], (128, 4, 128)))
        kk = np.arange(128)[:, None]
        qq = np.arange(256)[None, :]
        mask = np.stack([(kk <= qq), (kk + 128 <= qq)], axis=1).astype(np.float32)
        shared["maskc"] = mask.astype(ml_dtypes.bfloat16)
    maps = []
    for core in range(N_CORES):
        b = core % 4
        m = dict(shared)
        m["x"] = np.ascontiguousarray(x[b, :T])
        m["c"] = np.ascontiguousarray(c[b].reshape(16, 128))
        m["pos"] = np.ascontiguousarray(np.broadcast_to(pos[b, :T].reshape(1, T), (32, T)))
        maps.append(m)
    return maps


def make_in_maps_pair(inputs, T, phases, n_cores):
    base = make_in_maps(inputs, T, phases, pair=False)
    TL = T // 2
    x = np.asarray(inputs["x"], np.float32)
    maps = []
    for core in range(n_cores):
        b, r = core // 2, core % 2
        m = dict(base[b])
        m["x"] = np.ascontiguousarray(x[b, r * TL:(r + 1) * TL])
        f = np.zeros((128, 2), np.float32)
        f[:, r] = 1.0
        m["rankf"] = f
        for l in range(2):
            if "ada_w%d" % l in m:
                m["ada_w%d" % l] = np.ascontiguousarray(m["ada_w%d" % l][:, 9216 * r:9216 * (r + 1)])
        m["ada_b"] = np.ascontiguousarray(m["ada_b"][:, 72 * r:72 * (r + 1), :])
        if "attn_w_in" in m:
            w = np.asarray(inputs["attn_w_in"][0], np.float32)
            cols = np.concatenate([np.arange(o + 1024 * r, o + 1024 * (r + 1)) for o in (0, 2048, 4096)])
            m["attn_w_in"] = np.ascontiguousarray(w[:, cols])
        if "a_pair" in m:
            m["a_pair"] = np.ascontiguousarray(m["a_pair"][:, :, 32 * r:32 * (r + 1)])
            m["c_pair"] = np.ascontiguousarray(m["c_pair"][:, :, 32 * r:32 * (r + 1), :])
            m["a_feat"] = np.ascontiguousarray(m["a_feat"][:, :, 512 * r:512 * (r + 1)])
            m["b_feat"] = np.ascontiguousarray(m["b_feat"][:, :, 512 * r:512 * (r + 1)])
        maps.append(m)
    return maps


def kernel(**inputs):
    nc = build(SEQ, pair=True)
    maps = make_in_maps(inputs, SEQ, pair=True)
    res = run_bass_kernel_spmd(nc, maps, core_ids=list(range(N_CORES)))
    out = np.stack([np.concatenate([res.results[2 * b]["out"], res.results[2 * b + 1]["out"]], axis=0) for b in range(4)], axis=0)
    return out.astype(np.float32)
```

```python
import math
from contextlib import ExitStack

import numpy as np
import ml_dtypes

import concourse.bass as bass
import concourse.mybir as mybir
from concourse.bass_utils import run_bass_kernel_spmd

F32 = mybir.dt.float32
BF16 = mybir.dt.bfloat16
I32 = mybir.dt.int32
ALU = mybir.AluOpType
AF = mybir.ActivationFunctionType
AX = mybir.AxisListType

D = 2048
KC = 16
SEQ = 4096
DFF = 5632
FC = 44
NT = 512
EPS = 1e-6
NH = 8
HD = 128
N_CORES = 8
TWO_PI = 2.0 * math.pi

SELF_SYNC = True
NO_SELF_SYNC = ()
ALL_PHASES = ['ffn00', 'att0', 'ffn01', 'ffn10', 'ssm1', 'ffn11']


class Buf:
    __slots__ = ("w", "r", "name")

    def __init__(self, name=""):
        self.w = None
        self.r = []
        self.name = name


class Prog:
    NDS = {"sp": 12, "pool": 6, "act": 4}

    def __init__(self, nc, es):
        self.nc = nc
        self.engs = {"pe": nc.tensor, "act": nc.scalar, "dve": nc.vector, "pool": nc.gpsimd, "sp": nc.sync}
        self.sem = {e: es.enter_context(nc.semaphore("s_" + e)) for e in ("pe", "act", "dve", "pool")}
        self.cnt = {e: 0 for e in self.sem}
        self.ops = {e: [] for e in self.engs}
        self.seen = {e: {} for e in self.engs}
        self.dsem = {q: [es.enter_context(nc.semaphore("d_%s%d" % (q, i))) for i in range(n)]
                     for q, n in self.NDS.items()}
        self.duse = {q: [0] * n for q, n in self.NDS.items()}
        self.dnext = {q: 0 for q in self.NDS}
        self.pending = {e: [] for e in self.engs}
        self.ccsem = es.enter_context(nc.semaphore("s_cc"))
        self.cccnt = 0

    def _semof(self, key):
        if isinstance(key, tuple):
            return self.dsem[key[1]][key[2]]
        if key == "cc":
            return self.ccsem
        return self.sem[key]

    def _deps(self, e, reads, writes):
        toks = []
        for b in reads:
            if b.w is not None:
                toks.append(b.w)
        for b in writes:
            if b.w is not None:
                toks.append(b.w)
            toks.extend(b.r)
        waits = {}
        for k, v in toks:
            if k == e and (e == "pe" or e in NO_SELF_SYNC):
                continue
            if self.seen[e].get(k, 0) >= v:
                continue
            if waits.get(k, 0) < v:
                waits[k] = v
        for k, v in waits.items():
            self.seen[e][k] = v
        return list(waits.items())

    def op(self, e, fn, reads=(), writes=()):
        waits = self.pending[e] + self._deps(e, reads, writes)
        self.pending[e] = []
        self.cnt[e] += 1
        tok = (e, self.cnt[e])
        self.ops[e].append((waits, fn, self.sem[e], 1))
        for b in reads:
            b.r.append(tok)
        for b in writes:
            b.w = tok
            b.r = []

    def dma(self, q, out, in_, reads=(), writes=(), **kw):
        i = self.dnext[q]
        self.dnext[q] = (i + 1) % self.NDS[q]
        waits = self.pending[q] + self._deps(q, reads, writes)
        self.pending[q] = []
        key = ("d", q, i)
        prev = self.duse[q][i]
        if prev > 0 and self.seen[q].get(key, 0) < prev:
            waits.append((key, prev))
            self.seen[q][key] = prev
        self.duse[q][i] = prev + 16
        tok = (key, prev + 16)
        self.ops[q].append((waits, lambda eng: eng.dma_start(out=out, in_=in_, **kw), self.dsem[q][i], 16))
        for b in reads:
            b.r.append(tok)
        for b in writes:
            b.w = tok
            b.r = []

    def barrier(self):
        allw = []
        for q, n in self.NDS.items():
            for i in range(n):
                if self.duse[q][i] > 0:
                    allw.append((("d", q, i), self.duse[q][i]))
        for e in self.sem:
            if self.cnt[e] > 0:
                allw.append((e, self.cnt[e]))
        if self.cccnt > 0:
            allw.append(("cc", self.cccnt))
        for e in self.engs:
            lst = []
            for k, v in allw:
                if k == e:
                    continue
                if self.seen[e].get(k, 0) >= v:
                    continue
                self.seen[e][k] = v
                lst.append((k, v))
            self.pending[e].extend(lst)

    def coll(self, kind, groups, in_h, out_h, reads=(), writes=()):
        q = "pool"
        waits = self.pending[q] + self._deps(q, reads, writes)
        self.pending[q] = []
        self.cccnt += 1
        tok = ("cc", self.cccnt)

        def fn(eng):
            return eng.collective_compute(kind, ALU.bypass, replica_groups=groups,
                                          ins=[in_h.ap().opt()], outs=[out_h.ap().opt()])
        self.ops[q].append((waits, fn, self.ccsem, None))
        for b in reads:
            b.r.append(tok)
        for b in writes:
            b.w = tok
            b.r = []

    def emit(self):
        nc = self.nc
        fin = []
        for q, n in self.NDS.items():
            for i in range(n):
                if self.duse[q][i] > 0:
                    fin.append((("d", q, i), self.duse[q][i]))
        for e in self.sem:
            if self.cnt[e] > 0:
                fin.append((e, self.cnt[e]))
        if self.cccnt > 0:
            fin.append(("cc", self.cccnt))

        def replay(name, eng):
            for waits, fn, sem, inc in self.ops[name]:
                for k, v in waits:
                    eng.wait_ge(self._semof(k), v)
                if inc is None:
                    fn(eng).then_inc(sem)
                else:
                    fn(eng).then_inc(sem, inc)
            if name == "sp":
                for k, v in fin:
                    eng.wait_ge(self._semof(k), v)

        with nc.Block() as block:
            @block.sync
            def _(e):
                replay("sp", e)

            @block.scalar
            def _(e):
                replay("act", e)

            @block.vector
            def _(e):
                replay("dve", e)

            @block.gpsimd
            def _(e):
                replay("pool", e)

            @block.tensor
            def _(e):
                replay("pe", e)


class Tile:
    def __init__(self, t, nsub=1, name=""):
        self.t = t
        self.b = [Buf("%s[%d]" % (name, i)) for i in range(nsub)]


class K:
    pass


def build(T=SEQ, phases=None, dbg=None, pair=False, groups=None):
    nc = bass.Bass("TRN2", target_bir_lowering=False)
    TL = T // 2 if pair else T
    NTT = TL // NT
    NTG = T // NT
    NHL = NH // 2 if pair else NH
    if groups is None:
        groups = [[2 * i, 2 * i + 1] for i in range(N_CORES // 2)]
    es = ExitStack()
    with es:
        P = Prog(nc, es)

        def dram_in(name, shape, dt=F32):
            return nc.dram_tensor(name, list(shape), dt, kind="ExternalInput").ap()

        def dram_scr(name, shape, dt=F32):
            return nc.dram_tensor(name, list(shape), dt, kind="Internal").ap()

        cur = [es]
        uid = [0]

        def sb(name, shape, dt=F32, nsub=1):
            uid[0] += 1
            t = cur[0].enter_context(nc.sbuf_tensor("sb%d_%s" % (uid[0], name), list(shape), dt))
            return Tile(t, nsub, name)

        class Scope:
            def __enter__(self_):
                self_.st = ExitStack()
                self_.st.__enter__()
                self_.prev = cur[0]
                cur[0] = self_.st
                return self_

            def __exit__(self_, *a):
                P.barrier()
                cur[0] = self_.prev
                return self_.st.__exit__(*a)

        def ps(name, shape, dt=F32):
            t = es.enter_context(nc.psum_tensor("ps_" + name, list(shape), dt))
            return Tile(t, 1, name)

        class CG:
            def __init__(self, name, rows, cols, dt):
                self.i_h = nc.dram_tensor("cg_in_" + name, [rows, cols], dt)
                self.o_h = nc.dram_tensor("cg_out_" + name, [2 * rows, cols], dt)
                self.i = self.i_h.ap()
                self.o = self.o_h.ap()
                self.ib = Buf("cgi_" + name)
                self.ob = Buf("cgo_" + name)

            def go(self):
                P.coll("AllGather", groups, self.i_h, self.o_h, reads=[self.ib], writes=[self.ob])

        x_in = dram_in("x", [TL, D])
        c_in = dram_in("c", [16, 128])
        pos_in = dram_in("pos", [32, T], I32)
        normg_in = dram_in("norm_g", [96, 128])
        if phases is None:
            phases = ALL_PHASES
        layers = sorted(set(int(ph[3]) for ph in phases))
        NJ = 72 if pair else 144
        adaw_in = {l: dram_in("ada_w%d" % l, [D, NJ * 128]) for l in layers}
        adab_in = dram_in("ada_b", [2, NJ, 128])
        ffnwi_in = {}
        ffnwo_in = {}
        for ph in phases:
            if ph.startswith("ffn"):
                ffnwi_in[ph[3:]] = dram_in("ffn_w_in" + ph[3:], [D, 2 * DFF])
                ffnwo_in[ph[3:]] = dram_in("ffn_w_out" + ph[3:], [DFF, D])
        ident_in = dram_in("ident", [128, 128])
        if any(ph.startswith("ssm") for ph in phases):
            pm_in = dram_in("pm", [128, 2])
            ssmd_in = dram_in("ssm_dT", [128, 16])
            iota_in = dram_in("iota", [128, NT])
            NGP = 32 if pair else 64
            NKL = NGP // 4
            apair_in = dram_in("a_pair", [128, 3, NGP])
            afeat_in = dram_in("a_feat", [128, 3, NKL * 64])
            bfeat_in = dram_in("b_feat", [128, 2, NKL * 64])
            cpair_in = dram_in("c_pair", [128, 2, NGP, 16])
            wglu_in = dram_in("w_glu", [D, 2 * D])
            if pair:
                cg_u = [CG("u%d" % i, D, NT, BF16) for i in range(NTT)]
                cg_y = [CG("y%d" % i, NKL * 128, NT, F32) for i in range(NTG)]
                uT_d = dram_scr("um_scr", [NKL * 128, T], BF16)
            else:
                uT_d = dram_scr("uT_scr", [D, T], BF16)
            if pair:
                y_d = None
            elif dbg == "y":
                y_d = nc.dram_tensor("dbg_y", [D, T], F32, kind="ExternalOutput").ap()
            else:
                y_d = dram_scr("y_scr", [D, T])
            u_buf = Buf("u")
            y_buf = Buf("y")
        if any(ph.startswith("att") for ph in phases):
            attnwi_in = dram_in("attn_w_in", [D, 3 * NHL * 256])
            attnwo_in = dram_in("attn_w_out", [D, D])
            ropec_in = dram_in("ropec", [32, 4])
            rotm_in = dram_in("rotm", [32, 32])
            qkg_in = dram_in("qkg", [128, 2])
            subg_in = dram_in("subg", [128, 256])
            qkgrep_in = dram_in("qkgrep", [128, 2, 128])
            lamrep_in = dram_in("lamrep", [128, 4, 128])
            mask_in = dram_in("maskc", [128, 2, 256], BF16)
            qkT_d = dram_scr("qkT_scr", [4 * NHL, 128, T], BF16)
            V_d = dram_scr("V_scr", [T, NHL * 256], BF16)
            OTC = 1024
            if pair:
                cg_h = [CG("hT%d" % i, D, NT, BF16) for i in range(NTT)]
                cg_o = [CG("oT%d" % i, NHL * 256, OTC, BF16) for i in range(T // OTC)]
            else:
                oT_d = dram_scr("oT_scr", [D, T], BF16)
            qk_buf = Buf("qk")
            v_buf = Buf("v")
            oT_buf = Buf("oT")
        out_d = nc.dram_tensor("out", [TL, D], F32, kind="ExternalOutput").ap()
        xT_d = dram_scr("xT_scr", [D, TL])
        if pair:
            rankf_in = dram_in("rankf", [128, 2])

        xT_buf = Buf("xT_d")
        out_buf = Buf("out_d")
        in_buf = Buf("inputs")

        ident = sb("ident", [128, 128])
        identb = sb("identb", [128, 128], BF16)
        onesb = sb("onesb", [128, 128], BF16)
        vecrows = sb("vecrows", [128, 128])
        normgT = sb("normgT", [128, 96])
        condT = sb("condT", [128, 16, 2])
        adabT = [sb("adabT%d" % l, [128, 144]) for l in range(2)]
        modT = [sb("modT%d" % l, [128, 144]) for l in range(2)]
        modA = [sb("modA%d" % l, [128, 3, 16]) for l in range(2)]
        modG = [sb("modG%d" % l, [128, 3, 16]) for l in range(2)]

        rankf = sb("rankf", [128, 2], F32)
        if pair:
            P.dma("sp", rankf.t[:, :], rankf_in[:, :], reads=[in_buf], writes=rankf.b)

        def select2(dst_ap, c0_ap, c1_ap, tmp_ap, reads, writes, eng="dve"):
            P.op(eng, lambda e: e.tensor_scalar(tmp_ap, c0_ap, rankf.t[:, 0:1], None, ALU.mult),
                 reads=list(reads) + rankf.b, writes=list(writes))
            P.op(eng, lambda e: e.scalar_tensor_tensor(dst_ap, c1_ap, rankf.t[:, 1:2], tmp_ap, ALU.mult, ALU.add),
                 reads=list(reads) + rankf.b, writes=list(writes))

        pbank = [ps("pb%d" % i, [128, 512]) for i in range(7)]
        pbf = ps("pbf", [128, 1024], BF16)

        P.dma("sp", ident.t[:], ident_in[:, :], reads=[in_buf], writes=ident.b)
        P.op("dve", lambda e: e.tensor_copy(identb.t[:], ident.t[:]), reads=ident.b, writes=identb.b)
        P.op("dve", lambda e: e.memset(onesb.t[:], 1.0), writes=onesb.b)

        def transpose_rows(src_ap_dram, nrows, dst_ap, dst_bufs, bank):
            P.dma("sp", vecrows.t[0:nrows, :], src_ap_dram, reads=[in_buf], writes=vecrows.b)
            P.op("pe", lambda e: e.transpose(bank.t[:, 0:nrows], vecrows.t[0:nrows, :], ident.t[0:nrows, 0:nrows]),
                 reads=vecrows.b + ident.b, writes=bank.b)
            P.op("dve", lambda e: e.tensor_copy(dst_ap, bank.t[:, 0:nrows]), reads=bank.b, writes=dst_bufs)

        transpose_rows(normg_in[:, :], 96, normgT.t[:, :], normgT.b, pbank[0])
        transpose_rows(c_in[:, :], 16, condT.t[:, :, 0], condT.b, pbank[1])
        P.op("act", lambda e: e.activation(condT.t[:, :, 0], condT.t[:, :, 0], AF.Silu), reads=condT.b, writes=condT.b)
        P.op("act", lambda e: e.activation(condT.t[:, :, 1], condT.t[:, :, 0], AF.Copy), reads=condT.b, writes=condT.b)
        for l in range(2):
            if pair:
                transpose_rows(adab_in[l, 0:NJ, :], NJ, adabT[l].t[:, 0:NJ], adabT[l].b, pbank[2])
            else:
                transpose_rows(adab_in[l, 0:128, :], 128, adabT[l].t[:, 0:128], adabT[l].b, pbank[2])
                transpose_rows(adab_in[l, 128:144, :], 16, adabT[l].t[:, 128:144], adabT[l].b, pbank[3])

        CB = 512
        sc_ada = Scope()
        sc_ada.__enter__()
        adaw = [sb("adaw%d" % i, [128, 16, CB]) for i in range(2)]
        nblk = NJ * 128 // CB
        if pair:
            cg_mod = CG("mod", 128, 2 * NJ, F32)
            modh = sb("modh", [128, 2, NJ])
        for l in layers:
            bank = pbank[4 + l]
            for blk in range(nblk):
                wt = adaw[blk % 2]
                src = adaw_in[l][:, blk * CB:(blk + 1) * CB].rearrange("(kc p) c -> p kc c", p=128)
                P.dma("sp", wt.t[:, 0:8, :], src[:, 0:8, :], reads=[in_buf], writes=wt.b)
                P.dma("act", wt.t[:, 8:16, :], src[:, 8:16, :], reads=[in_buf], writes=wt.b)
                for jj in range(CB // 128):
                    j = blk * (CB // 128) + jj
                    for kc in range(KC):
                        P.op("pe", lambda e, wt=wt, jj=jj, kc=kc, j=j, bank=bank: e.matmul(
                            bank.t[:, 2 * j:2 * j + 2], wt.t[:, kc, jj * 128:(jj + 1) * 128], condT.t[:, kc, :],
                            start=(kc == 0), stop=(kc == KC - 1)),
                            reads=wt.b + condT.b, writes=bank.b)
            if pair:
                P.op("dve", lambda e, l=l, bank=bank: e.tensor_tensor(
                    modh.t[:, l, :], bank.t[:, 0:2 * NJ].rearrange("p (j two) -> p j two", two=2)[:, :, 0], adabT[l].t[:, 0:NJ], ALU.add),
                    reads=bank.b + adabT[l].b, writes=modh.b)
                continue
            P.op("dve", lambda e, l=l, bank=bank: e.tensor_tensor(
                modT[l].t[:, :], bank.t[:, 0:288].rearrange("p (j two) -> p j two", two=2)[:, :, 0], adabT[l].t[:, :], ALU.add),
                reads=bank.b + adabT[l].b, writes=modT[l].b)
        if pair:
            P.dma("sp", cg_mod.i[:, :], modh.t[:, :, :].rearrange("p l j -> p (l j)"), reads=modh.b, writes=[cg_mod.ib])
            cg_mod.go()
            for l in layers:
                for r_ in range(2):
                    P.dma(("sp", "act")[r_], modT[l].t[:, r_ * NJ:(r_ + 1) * NJ], cg_mod.o[r_ * 128:(r_ + 1) * 128, l * NJ:(l + 1) * NJ],
                          reads=[cg_mod.ob], writes=modT[l].b)
        for l in layers:
            for s in range(3):
                sh = (s * 3 + 0) * 16
                sc = (s * 3 + 1) * 16
                ga = (s * 3 + 2) * 16
                P.op("dve", lambda e, l=l, s=s, sc=sc: e.scalar_tensor_tensor(
                    modA[l].t[:, s, :], modT[l].t[:, sc:sc + 16], 1.0, normgT.t[:, (l * 3 + s) * 16:(l * 3 + s) * 16 + 16],
                    ALU.add, ALU.mult), reads=modT[l].b + normgT.b, writes=modA[l].b)
                wgt = 1.0 if s == 1 else 0.5
                P.op("dve", lambda e, l=l, s=s, ga=ga, wgt=wgt: e.tensor_scalar(
                    modG[l].t[:, s, :], modT[l].t[:, ga:ga + 16], wgt, None, ALU.mult),
                    reads=modT[l].b, writes=modG[l].b)

        sc_ada.__exit__(None, None, None)

        xT = sb("xT", [128, KC, NT], F32)
        hT = sb("hT", [128, KC, NT], BF16)
        tmpf = [sb("tmpf%d" % i, [128, NT], F32) for i in range(3)]
        tmpb = [sb("tmpb%d" % i, [128, NT], BF16) for i in range(3)]
        rstd = sb("rstd", [128, NT], F32)
        xtok = []
        wst = []
        wbf = []
        wost = []
        wobf = []
        gTh = [None]

        def alloc_ffn_tiles():
            gTh[0] = sb("gT", [128, FC, NT], BF16, nsub=FC)
            xtok[:] = [sb("xtok%d" % i, [128, D], F32) for i in range(2)]
            wst[:] = [sb("wst%d" % i, [128, KC, 128], F32) for i in range(2)]
            wbf[:] = [sb("wbf%d" % i, [128, KC, 128], BF16) for i in range(5)]
            wost[:] = [sb("wost%d" % i, [128, 22, 128], F32) for i in range(2)]
            wobf[:] = [sb("wobf%d" % i, [128, 22, 128], BF16) for i in range(3)]
        ctr = {"tmpf": 0, "tmpb": 0, "w": 0, "wb": 0, "wo": 0, "wob": 0, "pb": 0, "xtok": 0, "cast": 0, "ldq": 0}

        def rot(lst, key):
            i = ctr[key]
            ctr[key] = i + 1
            return lst[i % len(lst)]

        mm_banks = pbank[0:4]
        ssq_bank = pbank[4]
        tr_banks = pbank[5:7]

        def load_xT_tile(tt, from_input):
            t0 = tt * NT
            if not from_input:
                src = xT_d[:, t0:t0 + NT].rearrange("(kc p) t -> p kc t", p=128)
                P.dma("sp", xT.t[:, 0:8, :], src[:, 0:8, :], reads=[xT_buf], writes=xT.b)
                P.dma("act", xT.t[:, 8:16, :], src[:, 8:16, :], reads=[xT_buf], writes=xT.b)
                return
            for sub in range(NT // 128):
                xt = rot(xtok, "xtok")
                P.dma("sp", xt.t[:, :], x_in[t0 + sub * 128:t0 + (sub + 1) * 128, :], reads=[in_buf], writes=xt.b)
                for kc in range(KC):
                    bank = tr_banks[kc % 2]
                    P.op("pe", lambda e, xt=xt, kc=kc, bank=bank: e.transpose(
                        bank.t[:, 0:128], xt.t[:, kc * 128:(kc + 1) * 128], ident.t[:, :]),
                        reads=xt.b + ident.b, writes=bank.b)
                    eng = "dve" if kc % 2 == 0 else "act"
                    if eng == "dve":
                        P.op("dve", lambda e, kc=kc, sub=sub, bank=bank: e.tensor_copy(
                            xT.t[:, kc, sub * 128:(sub + 1) * 128], bank.t[:, 0:128]), reads=bank.b, writes=xT.b)
                    else:
                        P.op("act", lambda e, kc=kc, sub=sub, bank=bank: e.activation(
                            xT.t[:, kc, sub * 128:(sub + 1) * 128], bank.t[:, 0:128], AF.Copy), reads=bank.b, writes=xT.b)

        def store_xT_tile(tt, to_output):
            t0 = tt * NT
            if not to_output:
                dst = xT_d[:, t0:t0 + NT].rearrange("(kc p) t -> p kc t", p=128)
                P.dma("pool", dst[:, :, :], xT.t[:, :, :], reads=xT.b, writes=[xT_buf])
                return
            for sub in range(NT // 128):
                xt = rot(xtok, "xtok")
                for kc in range(KC):
                    bank = tr_banks[kc % 2]
                    P.op("pe", lambda e, kc=kc, sub=sub, bank=bank: e.transpose(
                        bank.t[:, 0:128], xT.t[:, kc, sub * 128:(sub + 1) * 128], ident.t[:, :]),
                        reads=xT.b + ident.b, writes=bank.b)
                    if kc % 2 == 0:
                        P.op("dve", lambda e, kc=kc, xt=xt, bank=bank: e.tensor_copy(
                            xt.t[:, kc * 128:(kc + 1) * 128], bank.t[:, 0:128]), reads=bank.b, writes=xt.b)
                    else:
                        P.op("act", lambda e, kc=kc, xt=xt, bank=bank: e.activation(
                            xt.t[:, kc * 128:(kc + 1) * 128], bank.t[:, 0:128], AF.Copy), reads=bank.b, writes=xt.b)
                P.dma("pool", out_d[t0 + sub * 128:t0 + (sub + 1) * 128, :], xt.t[:, :], reads=xt.b, writes=[out_buf])

        def norm_mod(l, s, h32=None, want_bf=True):
            for kc in range(KC):
                sq = rot(tmpb, "tmpb")
                P.op("act", lambda e, kc=kc, sq=sq: e.activation(sq.t[:, :], xT.t[:, kc, :], AF.Square),
                     reads=xT.b, writes=sq.b)
                P.op("pe", lambda e, kc=kc, sq=sq: e.matmul(ssq_bank.t[:, :], onesb.t[:, :], sq.t[:, :],
                                                           start=(kc == 0), stop=(kc == KC - 1)),
                     reads=sq.b + onesb.b, writes=ssq_bank.b)
            P.op("dve", lambda e: e.tensor_scalar(rstd.t[:, :], ssq_bank.t[:, :], 1.0 / D, EPS, ALU.mult, ALU.add),
                 reads=ssq_bank.b, writes=rstd.b)
            P.op("act", lambda e: e.activation(rstd.t[:, :], rstd.t[:, :], AF.Sqrt), reads=rstd.b, writes=rstd.b)
            P.op("dve", lambda e: e.reciprocal(rstd.t[:, :], rstd.t[:, :]), reads=rstd.b, writes=rstd.b)
            sh = (s * 3 + 0) * 16
            for kc in range(KC):
                tf = rot(tmpf, "tmpf")
                P.op("dve", lambda e, kc=kc, tf=tf: e.scalar_tensor_tensor(
                    tf.t[:, :], xT.t[:, kc, :], modA[l].t[:, s, kc:kc + 1], rstd.t[:, :], ALU.mult, ALU.mult),
                    reads=xT.b + modA[l].b + rstd.b, writes=tf.b)
                if h32 is not None:
                    P.op("act", lambda e, kc=kc, tf=tf: e.activation(
                        h32.t[:, kc, :], tf.t[:, :], AF.Identity, bias=modT[l].t[:, sh + kc:sh + kc + 1], scale=1.0),
                        reads=tf.b + modT[l].b, writes=h32.b)
                    if want_bf:
                        P.op("pool", lambda e, kc=kc: e.tensor_copy(hT.t[:, kc, :], h32.t[:, kc, :]), reads=h32.b, writes=hT.b)
                    continue
                P.op("act", lambda e, kc=kc, tf=tf: e.activation(
                    hT.t[:, kc, :], tf.t[:, :], AF.Identity, bias=modT[l].t[:, sh + kc:sh + kc + 1], scale=1.0),
                    reads=tf.b + modT[l].b, writes=hT.b)

        wcache = {}
        CAST_ENGS = ["act", "dve", "pool"]

        def cast_op(dst_ap, src_ap, reads, writes):
            e = CAST_ENGS[ctr["cast"] % 3]
            ctr["cast"] += 1
            if e == "act":
                P.op("act", lambda eng: eng.activation(dst_ap, src_ap, AF.Copy), reads=reads, writes=writes)
            else:
                P.op(e, lambda eng: eng.tensor_copy(dst_ap, src_ap), reads=reads, writes=writes)

        def ldq():
            q = ("sp", "pool")[ctr["ldq"] % 2]
            ctr["ldq"] += 1
            return q

        def load_w_chunk(w_ap, c0, wname=None):
            idx = c0 // 128
            wb = rot(wbf, "wb")
            bufs = None
            if wname is not None:
                if wname not in wcache:
                    wcache[wname] = (dram_scr("wc_" + wname, [w_ap.shape[1] // 128, 128, KC * 128], BF16), {})
                scr, bufs = wcache[wname]
            if bufs is not None and idx in bufs:
                P.dma(ldq(), wb.t[:, :, :], scr[idx].rearrange("p (k c) -> p k c", c=128), reads=[bufs[idx]], writes=wb.b)
                return wb
            st = rot(wst, "w")
            src = w_ap[:, c0:c0 + 128].rearrange("(kc p) c -> p kc c", p=128)
            P.dma("sp", st.t[:, :, :], src, reads=[in_buf], writes=st.b)
            cast_op(wb.t[:, :, :], st.t[:, :, :], st.b, wb.b)
            if bufs is not None:
                bufs[idx] = Buf("wc")
                P.dma("pool", scr[idx].rearrange("p (k c) -> p k c", c=128), wb.t[:, :, :], reads=wb.b, writes=[bufs[idx]])
            return wb

        def load_wo_half(w_out, dc, half, wname):
            wb = rot(wobf, "wob")
            if wname not in wcache:
                wcache[wname] = (dram_scr("wc_" + wname, [32, 128, 22 * 128], BF16), {})
            scr, bufs = wcache[wname]
            idx = dc * 2 + half
            if idx in bufs:
                P.dma(ldq(), wb.t[:, :, :], scr[idx].rearrange("p (k c) -> p k c", c=128), reads=[bufs[idx]], writes=wb.b)
                return wb
            st = rot(wost, "wo")
            src = w_out[half * 22 * 128:(half + 1) * 22 * 128, dc * 128:(dc + 1) * 128].rearrange("(j p) c -> p j c", p=128)
            P.dma("sp", st.t[:, :, :], src, reads=[in_buf], writes=st.b)
            cast_op(wb.t[:, :, :], st.t[:, :, :], st.b, wb.b)
            bufs[idx] = Buf("wc")
            P.dma("pool", scr[idx].rearrange("p (k c) -> p k c", c=128), wb.t[:, :, :], reads=wb.b, writes=[bufs[idx]])
            return wb

        def ffn(l, s, fi, first=False, last=False):
            w_in = ffnwi_in['%d%d' % (l, fi)]
            w_out = ffnwo_in['%d%d' % (l, fi)]
            sc = Scope()
            sc.__enter__()
            alloc_ffn_tiles()
            gT = gTh[0]
            for tt in range(NTT):
                load_xT_tile(tt, first)
                norm_mod(l, s)
                for j in range(FC):
                    wa = load_w_chunk(w_in, j * 128, "fi%d%d" % (l, fi))
                    wb_ = load_w_chunk(w_in, DFF + j * 128, "fi%d%d" % (l, fi))
                    pa = rot(mm_banks, "pb")
                    pb_ = rot(mm_banks, "pb")
                    for kc in range(KC):
                        P.op("pe", lambda e, wa=wa, kc=kc, pa=pa: e.matmul(
                            pa.t[:, :], wa.t[:, kc, :], hT.t[:, kc, :], start=(kc == 0), stop=(kc == KC - 1)),
                            reads=wa.b + hT.b, writes=pa.b)
                    for kc in range(KC):
                        P.op("pe", lambda e, wb_=wb_, kc=kc, pb_=pb_: e.matmul(
                            pb_.t[:, :], wb_.t[:, kc, :], hT.t[:, kc, :], start=(kc == 0), stop=(kc == KC - 1)),
                            reads=wb_.b + hT.b, writes=pb_.b)
                    tf = rot(tmpf, "tmpf")
                    P.op("act", lambda e, tf=tf, pa=pa: e.activation(tf.t[:, :], pa.t[:, :], AF.Silu),
                         reads=pa.b, writes=tf.b)
                    P.op("dve", lambda e, tf=tf, pb_=pb_, j=j: e.tensor_tensor(
                        gT.t[:, j, :], tf.t[:, :], pb_.t[:, :], ALU.mult),
                        reads=tf.b + pb_.b, writes=[gT.b[j]])
                for dc in range(KC):
                    po = rot(mm_banks, "pb")
                    for half in range(2):
                        wb = load_wo_half(w_out, dc, half, "fo%d%d" % (l, fi))
                        for jj in range(22):
                            j = half * 22 + jj
                            P.op("pe", lambda e, wb=wb, jj=jj, j=j, po=po: e.matmul(
                                po.t[:, :], wb.t[:, jj, :], gT.t[:, j, :], start=(j == 0), stop=(j == FC - 1)),
                                reads=wb.b + [gT.b[j]], writes=po.b)
                    P.op("dve", lambda e, dc=dc, po=po: e.scalar_tensor_tensor(
                        xT.t[:, dc, :], po.t[:, :], modG[l].t[:, s, dc:dc + 1], xT.t[:, dc, :], ALU.mult, ALU.add),
                        reads=po.b + modG[l].b + xT.b, writes=xT.b)
                store_xT_tile(tt, last)
            sc.__exit__(None, None, None)


        def attention(l, first=False, last=False):
            s = 1
            lambda_init = 0.8 - 0.6 * math.exp(-0.3 * l)
            scale = HD ** -0.5
            w_in = attnwi_in
            w_out = attnwo_in
            NQB = T // 256
            NKT = T // 128
            NQC = NHL * 2
            sc = Scope()
            sc.__enter__()
            if pair:
                scH = Scope()
                scH.__enter__()
                if first:
                    xtok[:] = [sb("xtok%d" % i, [128, D], F32) for i in range(2)]
                for tt in range(NTT):
                    load_xT_tile(tt, first)
                    if first:
                        store_xT_tile(tt, False)
                    norm_mod(l, s)
                    P.dma("pool", cg_h[tt].i[:, :].rearrange("(kc p) t -> p kc t", p=128), hT.t[:, :, :],
                          reads=hT.b, writes=[cg_h[tt].ib])
                    cg_h[tt].go()
                scH.__exit__(None, None, None)
            ropec = sb("ropec", [32, 4], F32)
            rotm = sb("rotm", [32, 32], F32)
            qkg = sb("qkg", [128, 2], F32)
            negmb = sb("negmb", [128, 1], F32)
            nlam = sb("nlam", [128, 1], F32)
            subg = sb("subg", [128, 256], F32)
            P.dma("sp", ropec.t[:, :], ropec_in[:, :], reads=[in_buf], writes=ropec.b)
            P.dma("sp", rotm.t[:, :], rotm_in[:, :], reads=[in_buf], writes=rotm.b)
            P.dma("sp", qkg.t[:, :], qkg_in[:, :], reads=[in_buf], writes=qkg.b)
            P.dma("sp", subg.t[:, :], subg_in[:, :], reads=[in_buf], writes=subg.b)
            P.op("dve", lambda e: e.tensor_scalar(subg.t[:, :], subg.t[:, :], 1.0 - lambda_init, None, ALU.mult),
                 reads=subg.b, writes=subg.b)
            scA = Scope()
            scA.__enter__()
            wst[:] = [sb("wst%d" % i, [128, KC, 128], F32) for i in range(2)]
            wbf[:] = [sb("wbf%d" % i, [128, KC, 128], BF16) for i in range(5)]
            cosT = sb("cosT", [32, T], F32)
            sinT = sb("sinT", [32, T], F32)
            if first:
                xtok[:] = [sb("xtok%d" % i, [128, D], F32) for i in range(2)]
            sc2 = Scope()
            sc2.__enter__()
            grep_ = sb("grep", [128, 2, 128], F32)
            lrep = sb("lrep", [128, 4, 128], F32)
            sm = sb("sm", [128, 8], F32)
            posi = sb("posi", [32, T], I32)
            ang = sb("ang", [32, T], F32)
            ang2 = sb("ang2", [32, T], F32)
            P.dma("sp", grep_.t[:, :, :], qkgrep_in[:, :, :], reads=[in_buf], writes=grep_.b)
            P.dma("sp", lrep.t[:, :, :], lamrep_in[:, :, :], reads=[in_buf], writes=lrep.b)
            P.dma("sp", posi.t[:, :], pos_in[:, :], reads=[in_buf], writes=posi.b)
            for i in range(2):
                P.op("dve", lambda e, i=i: e.reduce_max(sm.t[:, i:i + 1], grep_.t[:, i, :], AX.X, apply_absolute_value=True),
                     reads=grep_.b, writes=sm.b)
            P.op("dve", lambda e: e.tensor_tensor(sm.t[:, 2:3], sm.t[:, 0:1], sm.t[:, 1:2], ALU.mult), reads=sm.b, writes=sm.b)
            P.op("dve", lambda e: e.tensor_scalar(negmb.t[:, :], sm.t[:, 2:3], -(HD * scale), None, ALU.mult),
                 reads=sm.b, writes=negmb.b)
            for i in range(2):
                P.op("dve", lambda e, i=i: e.tensor_tensor(grep_.t[:, i, :], lrep.t[:, 2 * i, :], lrep.t[:, 2 * i + 1, :], ALU.mult),
                     reads=lrep.b + grep_.b, writes=grep_.b)
                P.op("dve", lambda e, i=i: e.reduce_sum(sm.t[:, 3 + i:4 + i], grep_.t[:, i, :], AX.X), reads=grep_.b, writes=sm.b)
                P.op("act", lambda e, i=i: e.activation(sm.t[:, 5 + i:6 + i], sm.t[:, 3 + i:4 + i], AF.Exp), reads=sm.b, writes=sm.b)
            P.op("dve", lambda e: e.scalar_tensor_tensor(nlam.t[:, :], sm.t[:, 6:7], -lambda_init, sm.t[:, 5:6], ALU.add, ALU.subtract),
                 reads=sm.b, writes=nlam.b)
            P.op("dve", lambda e: e.tensor_copy(ang.t[:, :], posi.t[:, :]), reads=posi.b, writes=ang.b)
            P.op("dve", lambda e: e.tensor_scalar(ang.t[:, :], ang.t[:, :], ropec.t[:, 0:1], None, ALU.mult),
                 reads=ang.b + ropec.b, writes=ang.b)
            P.op("dve", lambda e: e.tensor_scalar(ang2.t[:, :], ang.t[:, :], 1.0 / TWO_PI, None, ALU.mult), reads=ang.b, writes=ang2.b)
            P.op("dve", lambda e: e.tensor_copy(posi.t[:, :], ang2.t[:, :]), reads=ang2.b, writes=posi.b)
            P.op("dve", lambda e: e.tensor_copy(ang2.t[:, :], posi.t[:, :]), reads=posi.b, writes=ang2.b)
            P.op("dve", lambda e: e.scalar_tensor_tensor(ang.t[:, :], ang2.t[:, :], -TWO_PI, ang.t[:, :], ALU.mult, ALU.add),
                 reads=ang2.b + ang.b, writes=ang.b)

            def wrap_pi():
                P.op("dve", lambda e: e.tensor_scalar(ang2.t[:, :], ang.t[:, :], math.pi, TWO_PI, ALU.is_gt, ALU.mult),
                     reads=ang.b, writes=ang2.b)
                P.op("dve", lambda e: e.tensor_tensor(ang.t[:, :], ang.t[:, :], ang2.t[:, :], ALU.subtract),
                     reads=ang.b + ang2.b, writes=ang.b)
            wrap_pi()
            P.op("act", lambda e: e.activation(sinT.t[:, :], ang.t[:, :], AF.Sin, scale=ropec.t[:, 1:2]),
                 reads=ang.b + ropec.b, writes=sinT.b)
            P.op("dve", lambda e: e.tensor_scalar(ang.t[:, :], ang.t[:, :], 0.5 * math.pi, None, ALU.add), reads=ang.b + sinT.b, writes=ang.b)
            wrap_pi()
            P.op("act", lambda e: e.activation(cosT.t[:, :], ang.t[:, :], AF.Sin), reads=ang.b, writes=cosT.b)
            sc2.__exit__(None, None, None)

            qn = [sb("qn%d" % i, [128, NT], F32) for i in range(6)]
            rr = [sb("rr%d" % i, [128, NT], F32) for i in range(6)]
            rt1s = [sb("rt1_%d" % i, [32, NT], F32) for i in range(4)]
            rt2s = [sb("rt2_%d" % i, [32, NT], F32) for i in range(2)]
            qkb = [sb("qkb%d" % i, [128, NT], BF16) for i in range(4)]
            vtile = sb("vtile", [128, 4, NHL * 256], BF16)
            actr = {"qn": 0, "qkb": 0}
            qk_bufs = [Buf("qk%d" % i) for i in range(4 * NHL)]
            for tt in range(NTG):
                t0 = tt * NT
                if pair:
                    cgx = cg_h[tt % NTT]
                    src = cgx.o[(tt // NTT) * D:(tt // NTT + 1) * D, :].rearrange("(kc p) t -> p kc t", p=128)
                    P.dma("sp", hT.t[:, :, :], src, reads=[cgx.ob], writes=hT.b)
                else:
                    load_xT_tile(tt, first)
                    if first:
                        store_xT_tile(tt, False)
                    norm_mod(l, s)
                def stA(ch):
                    w_ = load_w_chunk(w_in, ch * 128, "awi")
                    p_ = rot(mm_banks, "pb")
                    for kc in range(KC):
                        P.op("pe", lambda e, w_=w_, kc=kc, p_=p_: e.matmul(
                            p_.t[:, :], w_.t[:, kc, :], hT.t[:, kc, :], start=(kc == 0), stop=(kc == KC - 1)),
                            reads=w_.b + hT.b, writes=p_.b)
                    return {"ch": ch, "p": p_}

                NB_ = 6

                def stB1(st):
                    ch, pq = st["ch"], st["p"]
                    if ch >= 2 * NQC:
                        vb = qkb[actr["qkb"] % len(qkb)]
                        actr["qkb"] += 1
                        st["vb"] = vb
                        P.op("act", lambda e, vb=vb, pq=pq: e.activation(vb.t[:, :], pq.t[:, :], AF.Copy), reads=pq.b, writes=vb.b)
                        return
                    sq = rot(tmpb, "tmpb")
                    P.op("act", lambda e, sq=sq, pq=pq: e.activation(sq.t[:, :], pq.t[:, :], AF.Square), reads=pq.b, writes=sq.b)
                    P.op("pe", lambda e, sq=sq: e.matmul(ssq_bank.t[:, :], onesb.t[:, :], sq.t[:, :], start=True, stop=True),
                         reads=sq.b + onesb.b, writes=ssq_bank.b)
                    r = rr[ch % NB_]
                    st["r"] = r
                    P.op("dve", lambda e, r=r: e.tensor_scalar(r.t[:, :], ssq_bank.t[:, :], 1.0 / HD, EPS, ALU.mult, ALU.add),
                         reads=ssq_bank.b, writes=r.b)

                def stB2(st):
                    ch = st["ch"]
                    if ch >= 2 * NQC:
                        vb = st["vb"]
                        for sub in range(4):
                            P.op("pe", lambda e, vb=vb, sub=sub: e.transpose(
                                pbf.t[:, sub * 128:(sub + 1) * 128], vb.t[:, sub * 128:(sub + 1) * 128], identb.t[:, :]),
                                reads=vb.b + identb.b, writes=pbf.b)
                        vc = ch - 2 * NQC
                        P.op("dve", lambda e, vc=vc: e.tensor_copy(
                            vtile.t[:, :, vc * 128:(vc + 1) * 128], pbf.t[:, 0:512].rearrange("p (s c) -> p s c", c=128)),
                            reads=pbf.b, writes=vtile.b)
                        return
                    r = st["r"]
                    P.op("act", lambda e, r=r: e.activation(r.t[:, :], r.t[:, :], AF.Sqrt), reads=r.b, writes=r.b)

                def stB3(st):
                    ch, pq = st["ch"], st["p"]
                    if ch >= 2 * NQC:
                        return
                    r = st["r"]
                    q = qn[ch % NB_]
                    st["q"] = q
                    P.op("dve", lambda e, r=r: e.reciprocal(r.t[:, :], r.t[:, :]), reads=r.b, writes=r.b)
                    gi = 0 if ch < NQC else 1
                    P.op("dve", lambda e, q=q, r=r, pq=pq, gi=gi: e.scalar_tensor_tensor(
                        q.t[:, :], pq.t[:, :], qkg.t[:, gi:gi + 1], r.t[:, :], ALU.mult, ALU.mult),
                        reads=pq.b + qkg.b + r.b, writes=q.b)

                def stC1(st):
                    ch = st["ch"]
                    if ch >= 2 * NQC:
                        return
                    q = st["q"]
                    rb = tr_banks[ch % 2]
                    st["rb"] = rb
                    P.op("pe", lambda e, q=q, rb=rb: e.matmul(rb.t[0:32, :], rotm.t[:, :], q.t[0:32, :], start=True, stop=True),
                         reads=q.b + rotm.b, writes=rb.b)
                    r1 = rt1s[ch % 4]
                    st["r1"] = r1
                    P.op("pool", lambda e, q=q, t0=t0, r1=r1: e.tensor_tensor(r1.t[:, :], q.t[0:32, :], cosT.t[:, t0:t0 + NT], ALU.mult),
                         reads=q.b + cosT.b, writes=r1.b)

                def stC2(st):
                    ch = st["ch"]
                    if ch >= 2 * NQC:
                        return
                    q, rb, r1 = st["q"], st["rb"], st["r1"]
                    r2 = rt2s[ch % 2]
                    st["r2"] = r2
                    P.op("dve", lambda e, rb=rb, t0=t0, r2=r2: e.tensor_tensor(r2.t[:, :], rb.t[0:32, :], sinT.t[:, t0:t0 + NT], ALU.mult),
                         reads=rb.b + sinT.b, writes=r2.b)

                def stC3(st):
                    ch = st["ch"]
                    if ch >= 2 * NQC:
                        return
                    q, r1, r2 = st["q"], st["r1"], st["r2"]
                    P.op("pool", lambda e, q=q, r1=r1, r2=r2: e.tensor_tensor(q.t[0:32, :], r1.t[:, :], r2.t[:, :], ALU.add),
                         reads=r1.b + r2.b, writes=q.b)

                def stC4(st):
                    ch = st["ch"]
                    if ch >= 2 * NQC:
                        return
                    q = st["q"]
                    qb_ = qkb[actr["qkb"] % len(qkb)]
                    actr["qkb"] += 1
                    P.op("act", lambda e, q=q, qb_=qb_: e.activation(qb_.t[:, :], q.t[:, :], AF.Copy), reads=q.b, writes=qb_.b)
                    P.dma("act", qkT_d[ch, :, t0:t0 + NT], qb_.t[:, :], reads=qb_.b, writes=[qk_bufs[ch]])

                stages = [stA, stB1, stB2, stB3, stC1, stC2, stC3, stC4]
                sts = []
                nch = 3 * NQC
                for i in range(nch + len(stages) - 1):
                    for lag, fn in enumerate(stages):
                        j = i - lag
                        if 0 <= j < nch:
                            if lag == 0:
                                sts.append(fn(j))
                            else:
                                fn(sts[j])
                P.dma("act", V_d[t0:t0 + NT, :].rearrange("(s p) c -> p s c", p=128), vtile.t[:, :, :],
                      reads=vtile.b, writes=[v_buf])
            scA.__exit__(None, None, None)

            scB = Scope()
            scB.__enter__()
            qk_t = [[sb("qk_%d_%d" % (i, j), [128, T], BF16) for j in range(4)] for i in range(2)]
            v_t = [sb("v_%d" % i, [128, NKT, 257], BF16) for i in range(2)]
            maskt = sb("maskt", [128, 2, 256], BF16)
            pT = [sb("pT%d" % i, [128, 256], BF16) for i in range(5)]
            oacc = [[sb("oacc%d_%d" % (c, sub), [128, 257], F32) for sub in range(2)] for c in range(2)]
            sm2 = [sb("sm2_%d" % i, [128, 8], F32) for i in range(2)]
            ot = [sb("ot%d" % i, [128, 256], F32) for i in range(2)]
            osq = sb("osq", [128, 256], F32)
            onb = [sb("onb%d" % i, [128, 256], BF16) for i in range(2)]
            oTt = [sb("oTt%d" % i, [128, 2, 128], BF16) for i in range(2)]
            P.dma("sp", maskt.t[:, :, :], mask_in[:, :, :], reads=[in_buf], writes=maskt.b)
            for i in range(2):
                P.op("dve", lambda e, i=i: e.memset(v_t[i].t[:, :, 256:257], 1.0), writes=v_t[i].b)
            acc_banks = [[pbank[0], pbank[1]], [pbank[2], pbank[3]]]
            sc_banks = [pbank[4], pbank[5], pbank[6]]
            bctr = {"sc": 0, "pT": 0, "o": 0}
            from collections import deque
            LA = 2
            pendq = deque()

            def front(h, qb, c, kt, qkh):
                q0 = qb * 256
                sb_ = sc_banks[bctr["sc"] % 3]
                bctr["sc"] += 1
                P.op("pe", lambda e, sb_=sb_, kt=kt, c=c, q0=q0, qkh=qkh: e.matmul(
                    sb_.t[:, 0:256], qkh[2 + c].t[:, kt * 128:(kt + 1) * 128], qkh[c].t[:, q0:q0 + 256],
                    start=True, stop=True), reads=qkh[2 + c].b + qkh[c].b, writes=sb_.b)
                p_ = pT[bctr["pT"] % len(pT)]
                bctr["pT"] += 1
                P.op("act", lambda e, p_=p_, sb_=sb_: e.activation(
                    p_.t[:, :], sb_.t[:, 0:256], AF.Exp, bias=negmb.t[:, 0:1], scale=scale),
                    reads=sb_.b + negmb.b, writes=p_.b)
                if kt >= 2 * qb:
                    mi = kt - 2 * qb
                    P.op("dve", lambda e, p_=p_, mi=mi: e.tensor_tensor(p_.t[:, :], p_.t[:, :], maskt.t[:, mi, :], ALU.mult),
                         reads=p_.b + maskt.b, writes=p_.b)
                return p_

            def back(h, qb, c, kt, vh, p_):
                q0 = qb * 256
                nkt = 2 * qb + 2
                accs = acc_banks[c]
                for sub in range(2):
                    last_kt = 2 * qb + sub
                    if kt > last_kt:
                        continue
                    P.op("pe", lambda e, p_=p_, sub=sub, kt=kt, vh=vh, accs=accs, last_kt=last_kt: e.matmul(
                        accs[sub].t[:, 0:257], p_.t[:, sub * 128:(sub + 1) * 128], vh.t[:, kt, :],
                        start=(kt == 0), stop=(kt == last_kt)), reads=p_.b + vh.b, writes=accs[sub].b)
                if kt != nkt - 1:
                    return
                for sub in range(2):
                    if sub == 0:
                        P.op("act", lambda e, c=c, sub=sub, accs=accs: e.activation(
                            oacc[c][sub].t[:, :], accs[sub].t[:, 0:257], AF.Copy), reads=accs[sub].b, writes=oacc[c][sub].b)
                    else:
                        P.op("dve", lambda e, c=c, sub=sub, accs=accs: e.tensor_copy(
                            oacc[c][sub].t[:, :], accs[sub].t[:, 0:257]), reads=accs[sub].b, writes=oacc[c][sub].b)
                if c != 1:
                    return
                for sub in range(2):
                    k_ = bctr["o"] % 2
                    bctr["o"] += 1
                    sm_ = sm2[k_]
                    o_ = ot[k_]
                    on_ = onb[k_]
                    oT_ = oTt[k_]
                    o0 = oacc[0][sub]
                    o1 = oacc[1][sub]
                    P.op("dve", lambda e, sm_=sm_, o0=o0: e.reciprocal(sm_.t[:, 0:1], o0.t[:, 256:257]), reads=o0.b, writes=sm_.b)
                    P.op("dve", lambda e, sm_=sm_, o1=o1: e.reciprocal(sm_.t[:, 1:2], o1.t[:, 256:257]), reads=o1.b + sm_.b, writes=sm_.b)
                    P.op("dve", lambda e, sm_=sm_: e.tensor_tensor(sm_.t[:, 2:3], sm_.t[:, 1:2], nlam.t[:, 0:1], ALU.mult),
                         reads=sm_.b + nlam.b, writes=sm_.b)
                    P.op("dve", lambda e, sm_=sm_, o_=o_, o0=o0: e.tensor_scalar(
                        o_.t[:, :], o0.t[:, 0:256], sm_.t[:, 0:1], None, ALU.mult), reads=o0.b + sm_.b, writes=o_.b)
                    P.op("dve", lambda e, sm_=sm_, o_=o_, o1=o1: e.scalar_tensor_tensor(
                        o_.t[:, :], o1.t[:, 0:256], sm_.t[:, 2:3], o_.t[:, :], ALU.mult, ALU.add),
                        reads=o1.b + sm_.b + o_.b, writes=o_.b)
                    P.op("dve", lambda e, o_=o_: e.tensor_tensor(osq.t[:, :], o_.t[:, :], o_.t[:, :], ALU.mult), reads=o_.b, writes=osq.b)
                    P.op("dve", lambda e, sm_=sm_: e.reduce_sum(sm_.t[:, 3:4], osq.t[:, :], AX.X), reads=osq.b + sm_.b, writes=sm_.b)
                    P.op("dve", lambda e, sm_=sm_: e.tensor_scalar(sm_.t[:, 4:5], sm_.t[:, 3:4], 1.0 / 256, EPS, ALU.mult, ALU.add),
                         reads=sm_.b, writes=sm_.b)
                    P.op("act", lambda e, sm_=sm_: e.activation(sm_.t[:, 5:6], sm_.t[:, 4:5], AF.Sqrt), reads=sm_.b, writes=sm_.b)
                    P.op("dve", lambda e, sm_=sm_: e.reciprocal(sm_.t[:, 6:7], sm_.t[:, 5:6]), reads=sm_.b, writes=sm_.b)
                    P.op("dve", lambda e, sm_=sm_, o_=o_, on_=on_: e.scalar_tensor_tensor(
                        on_.t[:, :], o_.t[:, :], sm_.t[:, 6:7], subg.t[:, :], ALU.mult, ALU.mult),
                        reads=o_.b + sm_.b + subg.b, writes=on_.b)
                    for fc in range(2):
                        P.op("pe", lambda e, on_=on_, fc=fc: e.transpose(
                            pbf.t[:, fc * 128:(fc + 1) * 128], on_.t[:, fc * 128:(fc + 1) * 128], identb.t[:, :]),
                            reads=on_.b + identb.b, writes=pbf.b)
                    P.op("act", lambda e, oT_=oT_: e.activation(
                        oT_.t[:, :, :], pbf.t[:, 0:256].rearrange("p (f c) -> p f c", c=128), AF.Copy), reads=pbf.b, writes=oT_.b)
                    qs = q0 + sub * 128
                    if pair:
                        cgx = cg_o[qs // OTC]
                        P.dma("pool", cgx.i[h * 256:(h + 1) * 256, qs % OTC:qs % OTC + 128].rearrange("(f p) q -> p f q", p=128),
                              oT_.t[:, :, :], reads=oT_.b, writes=[cgx.ib])
                    else:
                        P.dma("pool", oT_d[h * 256:(h + 1) * 256, qs:qs + 128].rearrange("(f p) q -> p f q", p=128),
                              oT_.t[:, :, :], reads=oT_.b, writes=[oT_buf])

            for h in range(NHL):
                qkh = qk_t[h % 2]
                vh = v_t[h % 2]
                for c in range(2):
                    P.dma("sp", qkh[c].t[:, :], qkT_d[h * 2 + c, :, :], reads=[qk_bufs[h * 2 + c]], writes=qkh[c].b)
                    P.dma("sp", qkh[2 + c].t[:, :], qkT_d[NQC + h * 2 + c, :, :], reads=[qk_bufs[NQC + h * 2 + c]], writes=qkh[2 + c].b)
                P.dma("act", vh.t[:, :, 0:256], V_d[:, h * 256:(h + 1) * 256].rearrange("(kt p) c -> p kt c", p=128),
                      reads=[v_buf], writes=vh.b)
                for qb in range(NQB):
                    for c in range(2):
                        for kt in range(2 * qb + 2):
                            p_ = front(h, qb, c, kt, qkh)
                            pendq.append((h, qb, c, kt, vh, p_))
                            if len(pendq) > LA:
                                back(*pendq.popleft())
            while pendq:
                back(*pendq.popleft())
            scB.__exit__(None, None, None)

            if pair:
                for cgx in cg_o:
                    cgx.go()
            scC = Scope()
            scC.__enter__()
            if pair:
                ocand = [sb("ocand%d" % i, [128, KC, NT], BF16) for i in range(2)]
            wst[:] = [sb("wst%d" % i, [128, KC, 128], F32) for i in range(2)]
            wbf[:] = [sb("wbf%d" % i, [128, KC, 128], BF16) for i in range(5)]
            if last:
                xtok[:] = [sb("xtok%d" % i, [128, D], F32) for i in range(2)]
            for tt in range(NTT):
                t0 = tt * NT
                load_xT_tile(tt, False)
                if pair:
                    for r_ in range(2):
                        g0 = r_ * TL + t0
                        cgx = cg_o[g0 // OTC]
                        src = cgx.o[:, g0 % OTC:g0 % OTC + NT].rearrange("(kc p) t -> p kc t", p=128)
                        P.dma(("sp", "act")[r_], ocand[r_].t[:, :, :], src, reads=[cgx.ob], writes=ocand[r_].b)
                    select2(hT.t[:, :, :], ocand[0].t[:, :, :], ocand[1].t[:, :, :], ocand[0].t[:, :, :],
                            ocand[0].b + ocand[1].b, hT.b + ocand[0].b)
                else:
                    src = oT_d[:, t0:t0 + NT].rearrange("(kc p) t -> p kc t", p=128)
                    P.dma("sp", hT.t[:, :, :], src, reads=[oT_buf], writes=hT.b)
                for dc in range(KC):
                    wo = load_w_chunk(w_out, dc * 128, "awo")
                    po = rot(mm_banks, "pb")
                    for kc in range(KC):
                        P.op("pe", lambda e, wo=wo, kc=kc, po=po: e.matmul(
                            po.t[:, :], wo.t[:, kc, :], hT.t[:, kc, :], start=(kc == 0), stop=(kc == KC - 1)),
                            reads=wo.b + hT.b, writes=po.b)
                    P.op("dve", lambda e, dc=dc, po=po: e.scalar_tensor_tensor(
                        xT.t[:, dc, :], po.t[:, :], modG[l].t[:, s, dc:dc + 1], xT.t[:, dc, :], ALU.mult, ALU.add),
                        reads=po.b + modG[l].b + xT.b, writes=xT.b)
                store_xT_tile(tt, last)
            scC.__exit__(None, None, None)
            sc.__exit__(None, None, None)


        def ssm(l, first=False, last=False):
            s = 1
            sc = Scope()
            sc.__enter__()
            rho = sb("rho", [128, NGP], F32)
            fq = sb("fq", [128, NGP], F32)
            c512 = sb("c512", [128, NGP], F32)
            s512 = sb("s512", [128, NGP], F32)
            pm = sb("pm", [128, 2], F32)
            dT = sb("dT", [128, 16], F32)
            iota = sb("iota", [128, NT], F32)
            P.dma("sp", pm.t[:, :], pm_in[:, :], reads=[in_buf], writes=pm.b)
            P.dma("sp", dT.t[:, :], ssmd_in[:, :], reads=[in_buf], writes=dT.b)
            P.dma("sp", iota.t[:, :], iota_in[:, :], reads=[in_buf], writes=iota.b)
            scL = Scope()
            scL.__enter__()
            LB = [sb("LB%d" % i, [128, NGP, 128], BF16) for i in range(2)]
            LC = [sb("LC%d" % i, [128, NGP, 128], BF16) for i in range(2)]
            for i in range(2):
                P.op("pool", lambda e, i=i: e.memset(LB[i].t[:, :, :], 0.0), writes=LB[i].b)
                P.op("pool", lambda e, i=i: e.memset(LC[i].t[:, :, :], 0.0), writes=LC[i].b)

            def frac_wrap(dst, src, W, tmpi, tmpf_):
                P.op("dve", lambda e: e.tensor_copy(tmpi, src), reads=[gen_b], writes=[gen_b])
                P.op("dve", lambda e: e.tensor_copy(tmpf_, tmpi), reads=[gen_b], writes=[gen_b])
                P.op("dve", lambda e: e.tensor_tensor(dst, src, tmpf_, ALU.subtract), reads=[gen_b], writes=[gen_b])
                wrap_half(dst, tmpf_)

            def wrap_half(dst, tmpf_):
                P.op("dve", lambda e: e.tensor_scalar(tmpf_, dst, 0.5, None, ALU.is_gt), reads=[gen_b], writes=[gen_b])
                P.op("dve", lambda e: e.tensor_tensor(dst, dst, tmpf_, ALU.subtract), reads=[gen_b], writes=[gen_b])
                P.op("dve", lambda e: e.tensor_scalar(tmpf_, dst, -0.5, None, ALU.is_lt), reads=[gen_b], writes=[gen_b])
                P.op("dve", lambda e: e.tensor_tensor(dst, dst, tmpf_, ALU.add), reads=[gen_b], writes=[gen_b])

            gen_b = Buf("ssm_setup")

            def G(eng, fn):
                P.op(eng, fn, reads=[gen_b], writes=[gen_b])

            scS = Scope()
            scS.__enter__()
            ap_ = sb("a_pair", [128, 3, NGP], F32)
            P.dma("sp", ap_.t[:, :, :], apair_in[:, :, :], reads=[in_buf], writes=[gen_b])
            t64 = [sb("t64_%d" % i, [128, NGP], F32) for i in range(4)]
            t64i = sb("t64i", [128, NGP], I32)
            dtp, thp, fr64, tm64 = [t.t[:, :] for t in t64]
            G("act", lambda e: e.activation(dtp, ap_.t[:, 2, :], AF.Exp))
            G("dve", lambda e: e.tensor_tensor(thp, dtp, ap_.t[:, 0, :], ALU.mult))
            G("act", lambda e: e.activation(rho.t[:, :], thp, AF.Exp))
            G("dve", lambda e: e.tensor_tensor(thp, dtp, ap_.t[:, 1, :], ALU.mult))
            G("dve", lambda e: e.tensor_scalar(thp, thp, 1.0 / TWO_PI, None, ALU.mult))
            frac_wrap(fq.t[:, :], thp, NGP, t64i.t[:, :], tm64)
            G("dve", lambda e: e.tensor_scalar(thp, fq.t[:, :], float(NT), None, ALU.mult))
            frac_wrap(fr64, thp, NGP, t64i.t[:, :], tm64)
            G("act", lambda e: e.activation(s512.t[:, :], fr64, AF.Sin, scale=TWO_PI))
            G("dve", lambda e: e.tensor_scalar(fr64, fr64, 0.25, None, ALU.add))
            wrap_half(fr64, tm64)
            G("act", lambda e: e.activation(c512.t[:, :], fr64, AF.Sin, scale=TWO_PI))
            W = NKL * 64
            af = sb("a_feat", [128, 3, W], F32)
            bf_ = sb("b_feat", [128, 2, W], F32)
            P.dma("sp", af.t[:, :, :], afeat_in[:, :, :], reads=[in_buf], writes=[gen_b])
            P.dma("act", bf_.t[:, :, :], bfeat_in[:, :, :], reads=[in_buf], writes=[gen_b])
            tw = [sb("tw%d" % i, [128, W], F32) for i in range(8)]
            twi = sb("twi", [128, W], I32)
            dtf, mag, th, cs, sn, t5, t6, t7 = [t.t[:, :] for t in tw]
            ar = af.t[:, 0, :]
            ai = af.t[:, 1, :]
            G("act", lambda e: e.activation(dtf, af.t[:, 2, :], AF.Exp))
            G("dve", lambda e: e.tensor_tensor(th, dtf, ar, ALU.mult))
            G("act", lambda e: e.activation(mag, th, AF.Exp))
            G("dve", lambda e: e.tensor_tensor(th, dtf, ai, ALU.mult))
            G("dve", lambda e: e.tensor_scalar(th, th, 1.0 / TWO_PI, None, ALU.mult))
            frac_wrap(t5, th, W, twi.t[:, :], t6)
            G("act", lambda e: e.activation(sn, t5, AF.Sin, scale=TWO_PI))
            G("dve", lambda e: e.tensor_scalar(t5, t5, 0.25, None, ALU.add))
            wrap_half(t5, t6)
            G("act", lambda e: e.activation(cs, t5, AF.Sin, scale=TWO_PI))
            G("dve", lambda e: e.tensor_tensor(cs, cs, mag, ALU.mult))
            G("dve", lambda e: e.tensor_scalar(cs, cs, -1.0, None, ALU.add))
            G("dve", lambda e: e.tensor_tensor(sn, sn, mag, ALU.mult))
            G("dve", lambda e: e.tensor_tensor(t5, ar, ar, ALU.mult))
            G("dve", lambda e: e.tensor_tensor(t6, ai, ai, ALU.mult))
            G("dve", lambda e: e.tensor_tensor(t5, t5, t6, ALU.add))
            G("dve", lambda e: e.reciprocal(t5, t5))
            G("dve", lambda e: e.tensor_tensor(t6, cs, ar, ALU.mult))
            G("dve", lambda e: e.tensor_tensor(t7, sn, ai, ALU.mult))
            G("dve", lambda e: e.tensor_tensor(t6, t6, t7, ALU.add))
            G("dve", lambda e: e.tensor_tensor(t6, t6, t5, ALU.mult))
            G("dve", lambda e: e.tensor_tensor(t7, sn, ar, ALU.mult))
            G("dve", lambda e: e.tensor_tensor(mag, cs, ai, ALU.mult))
            G("dve", lambda e: e.tensor_tensor(t7, t7, mag, ALU.subtract))
            G("dve", lambda e: e.tensor_tensor(t7, t7, t5, ALU.mult))
            br = bf_.t[:, 0, :]
            bi = bf_.t[:, 1, :]
            G("dve", lambda e: e.tensor_tensor(cs, t6, br, ALU.mult))
            G("dve", lambda e: e.tensor_tensor(mag, t7, bi, ALU.mult))
            G("dve", lambda e: e.tensor_tensor(cs, cs, mag, ALU.subtract))
            G("dve", lambda e: e.tensor_tensor(sn, t6, bi, ALU.mult))
            G("dve", lambda e: e.tensor_tensor(mag, t7, br, ALU.mult))
            G("dve", lambda e: e.tensor_tensor(sn, sn, mag, ALU.add))
            bb = [tw[3].t, tw[4].t]
            for gp in range(NGP):
                kc, j = gp // 4, gp % 4
                r0 = 32 * j
                for ri in range(2):
                    for gi in range(2):
                        eng = "dve" if gi == 0 else "pool"
                        P.op(eng, lambda e, ri=ri, gi=gi, gp=gp, kc=kc, r0=r0: e.tensor_scalar(
                            LB[ri].t[r0:r0 + 32, gp, gi * 64:(gi + 1) * 64], bb[ri][r0:r0 + 32, kc * 64:(kc + 1) * 64],
                            pm.t[r0:r0 + 32, gi:gi + 1], None, ALU.mult),
                            reads=[gen_b] + pm.b, writes=LB[ri].b)
            cp = sb("c_pair", [128, 2, NGP, 16], F32)
            P.dma("sp", cp.t[:, :, :, :], cpair_in[:, :, :, :], reads=[in_buf], writes=[gen_b])
            for gp in range(NGP):
                j = gp % 4
                for ri in range(2):
                    for gi in range(2):
                        c0 = 32 * j + gi * 16
                        eng = "dve" if gi == 0 else "pool"
                        sgn = 1.0 if ri == 0 else -1.0
                        P.op(eng, lambda e, ri=ri, gi=gi, gp=gp, c0=c0, sgn=sgn: e.tensor_scalar(
                            LC[ri].t[gi * 64:(gi + 1) * 64, gp, c0:c0 + 16], cp.t[gi * 64:(gi + 1) * 64, ri, gp, :],
                            sgn, None, ALU.mult), reads=[gen_b], writes=LC[ri].b)
            scS.__exit__(None, None, None)

            scA = Scope()
            scA.__enter__()
            if first:
                xtok[:] = [sb("xtok%d" % i, [128, D], F32) for i in range(2)]
            for tt in range(NTT):
                t0 = tt * NT
                load_xT_tile(tt, first)
                if first:
                    store_xT_tile(tt, False)
                norm_mod(l, s)
                if pair:
                    P.dma("pool", cg_u[tt].i[:, :].rearrange("(kc p) t -> p kc t", p=128), hT.t[:, :, :],
                          reads=hT.b, writes=[cg_u[tt].ib])
                    cg_u[tt].go()
                else:
                    P.dma("pool", uT_d[:, t0:t0 + NT].rearrange("(kc p) t -> p kc t", p=128), hT.t[:, :, :],
                          reads=hT.b, writes=[u_buf])
            if pair:
                ucand = [[sb("ucand%d_%d" % (i, j), [128, NKL, NT], BF16) for j in range(2)] for i in range(2)]
                um_buf = Buf("um")
                it_ = 0
                for tt in range(NTT):
                    for r_ in range(2):
                        cs_ = ucand[it_ % 2]
                        it_ += 1
                        for j in range(2):
                            row0 = r_ * D + j * NKL * 128
                            P.dma(("sp", "act")[j], cs_[j].t[:, :, :],
                                  cg_u[tt].o[row0:row0 + NKL * 128, :].rearrange("(kc p) t -> p kc t", p=128),
                                  reads=[cg_u[tt].ob], writes=cs_[j].b)
                        select2(cs_[0].t[:, :, :], cs_[0].t[:, :, :], cs_[1].t[:, :, :], cs_[0].t[:, :, :], cs_[0].b + cs_[1].b, cs_[0].b)
                        g0 = r_ * TL + tt * NT
                        P.dma("pool", uT_d[:, g0:g0 + NT].rearrange("(kc p) t -> p kc t", p=128), cs_[0].t[:, :, :],
                              reads=cs_[0].b, writes=[um_buf])
                u_rd = um_buf
            else:
                u_rd = u_buf
            scA.__exit__(None, None, None)

            scB = Scope()
            scB.__enter__()
            cosTs = [sb("s_cos%d" % i, [128, NT], F32) for i in range(2)]
            sinTs = [sb("s_sin%d" % i, [128, NT], F32) for i in range(2)]
            rhobs = [sb("s_rhob%d" % i, [128, NT], F32) for i in range(2)]
            tg = [sb("s_tg%d" % i, [128, NT], F32) for i in range(2)]
            tgi = sb("s_tgi", [128, NT], I32)
            uch = [sb("s_u%d" % i, [128, NT], BF16) for i in range(3)]
            bsb = [[sb("s_b%d_%d" % (k, i), [128, NT], F32) for i in range(2)] for k in range(2)]
            fq_ = [[sb("s_f%d_%d" % (k, i), [128, NT], F32) for i in range(4)] for k in range(2)]
            bq_ = [sb("s_q%d" % i, [128, NT], F32) for i in range(4)]
            btr = [sb("s_btr%d" % i, [128, NT], F32) for i in range(2)]
            bti = [sb("s_bti%d" % i, [128, NT], F32) for i in range(2)]
            st_r = [sb("s_str%d" % i, [128, NT], F32) for i in range(2)]
            st_i = [sb("s_sti%d" % i, [128, NT], F32) for i in range(2)]
            sbf_r = [sb("s_sbr%d" % i, [128, NT], BF16) for i in range(2)]
            sbf_i = [sb("s_sbi%d" % i, [128, NT], BF16) for i in range(2)]
            y32 = [sb("s_y%d" % i, [128, NT], F32) for i in range(2)]
            init = [sb("s_init%d" % i, [128, 4], F32) for i in range(2)]
            b_banks = [[pbank[0], pbank[1]], [pbank[2], pbank[3]]]
            y_banks = [pbank[4], pbank[5]]

            def tables(gp):
                cosT, sinT, rhob = cosTs[gp % 2], sinTs[gp % 2], rhobs[gp % 2]
                fr = tg[0].t[:, :]
                tm = tg[1].t[:, :]
                tb = tg[0].b + tg[1].b + tgi.b
                P.op("dve", lambda e, gp=gp: e.tensor_scalar(tm, iota.t[:, :], fq.t[:, gp:gp + 1], None, ALU.mult),
                     reads=iota.b + fq.b + tb, writes=tb)

                def T_(eng, fn, extra_w=()):
                    P.op(eng, fn, reads=tb, writes=tb + list(extra_w))
                T_("dve", lambda e: e.tensor_copy(tgi.t[:, :], tm))
                T_("dve", lambda e: e.tensor_copy(fr, tgi.t[:, :]))
                T_("dve", lambda e: e.tensor_tensor(fr, tm, fr, ALU.subtract))

                def wrapT():
                    T_("dve", lambda e: e.tensor_scalar(tm, fr, 0.5, None, ALU.is_gt))
                    T_("dve", lambda e: e.tensor_tensor(fr, fr, tm, ALU.subtract))
                    T_("dve", lambda e: e.tensor_scalar(tm, fr, -0.5, None, ALU.is_lt))
                    T_("dve", lambda e: e.tensor_tensor(fr, fr, tm, ALU.add))
                wrapT()
                T_("act", lambda e: e.activation(sinT.t[:, :], fr, AF.Sin, scale=TWO_PI), extra_w=sinT.b)
                T_("dve", lambda e: e.tensor_scalar(fr, fr, 0.25, None, ALU.add))
                wrapT()
                T_("act", lambda e: e.activation(cosT.t[:, :], fr, AF.Sin, scale=TWO_PI), extra_w=cosT.b)
                P.op("pool", lambda e, gp=gp: e.tensor_scalar(rhob.t[:, :], iota.t[:, :], 0.0, rho.t[:, gp:gp + 1], ALU.mult, ALU.add),
                     reads=iota.b + rho.b, writes=rhob.b)
                ini = init[gp % 2]
                P.op("pool", lambda e, ini=ini: e.memset(ini.t[:, :], 0.0), writes=ini.b)

            def F1(it, gp, tt):
                k = it % 2
                kc = gp // 4
                t0 = tt * NT
                cosT, sinT = cosTs[gp % 2], sinTs[gp % 2]
                u_ = uch[it % 3]
                P.dma("sp", u_.t[:, :], uT_d[kc * 128:(kc + 1) * 128, t0:t0 + NT], reads=[u_rd], writes=u_.b)
                pre, pim = b_banks[k]
                P.op("pe", lambda e: e.matmul(pre.t[:, :], LB[0].t[:, gp, :], u_.t[:, :], start=True, stop=True),
                     reads=LB[0].b + u_.b, writes=pre.b)
                P.op("pe", lambda e: e.matmul(pim.t[:, :], LB[1].t[:, gp, :], u_.t[:, :], start=True, stop=True),
                     reads=LB[1].b + u_.b, writes=pim.b)
                bre, bim = bsb[k]
                f1, f2, f3, f4 = fq_[k]
                P.op("act", lambda e: e.activation(bre.t[:, :], pre.t[:, :], AF.Copy), reads=pre.b, writes=bre.b)
                P.op("act", lambda e: e.activation(bim.t[:, :], pim.t[:, :], AF.Copy), reads=pim.b, writes=bim.b)
                P.op("dve", lambda e: e.tensor_tensor(f1.t[:, :], bre.t[:, :], cosT.t[:, :], ALU.mult), reads=bre.b + cosT.b, writes=f1.b)
                P.op("dve", lambda e: e.tensor_tensor(f2.t[:, :], bim.t[:, :], sinT.t[:, :], ALU.mult), reads=bim.b + sinT.b, writes=f2.b)
                P.op("pool", lambda e: e.tensor_tensor(f3.t[:, :], bim.t[:, :], cosT.t[:, :], ALU.mult), reads=bim.b + cosT.b, writes=f3.b)
                P.op("pool", lambda e: e.tensor_tensor(f4.t[:, :], bre.t[:, :], sinT.t[:, :], ALU.mult), reads=bre.b + sinT.b, writes=f4.b)

            def F2(it, gp, tt):
                k = it % 2
                f1, f2, f3, f4 = fq_[k]
                br_, bi_ = btr[k], bti[k]
                P.op("dve", lambda e: e.tensor_tensor(br_.t[:, :], f1.t[:, :], f2.t[:, :], ALU.add), reads=f1.b + f2.b, writes=br_.b)
                P.op("pool", lambda e: e.tensor_tensor(bi_.t[:, :], f3.t[:, :], f4.t[:, :], ALU.subtract), reads=f3.b + f4.b, writes=bi_.b)

            def F3(it, gp, tt):
                k = it % 2
                rhob, ini = rhobs[gp % 2], init[gp % 2]
                br_, bi_ = btr[k], bti[k]
                sr, si = st_r[k], st_i[k]
                P.op("dve", lambda e: e.tensor_tensor_scan(sr.t[:, :], rhob.t[:, :], br_.t[:, :], ini.t[:, 0:1], ALU.mult, ALU.add),
                     reads=rhob.b + br_.b + ini.b, writes=sr.b)
                P.op("dve", lambda e: e.tensor_tensor_scan(si.t[:, :], rhob.t[:, :], bi_.t[:, :], ini.t[:, 1:2], ALU.mult, ALU.add),
                     reads=rhob.b + bi_.b + ini.b, writes=si.b)
                P.op("dve", lambda e: e.tensor_tensor(ini.t[:, 3:4], sr.t[:, NT - 1:NT], s512.t[:, gp:gp + 1], ALU.mult),
                     reads=sr.b + s512.b + ini.b, writes=ini.b)
                P.op("dve", lambda e: e.tensor_tensor(ini.t[:, 2:3], si.t[:, NT - 1:NT], s512.t[:, gp:gp + 1], ALU.mult),
                     reads=si.b + s512.b + ini.b, writes=ini.b)
                P.op("dve", lambda e: e.scalar_tensor_tensor(ini.t[:, 1:2], si.t[:, NT - 1:NT], c512.t[:, gp:gp + 1], ini.t[:, 3:4],
                                                             ALU.mult, ALU.add), reads=si.b + c512.b + ini.b, writes=ini.b)
                P.op("dve", lambda e: e.scalar_tensor_tensor(ini.t[:, 0:1], sr.t[:, NT - 1:NT], c512.t[:, gp:gp + 1], ini.t[:, 2:3],
                                                             ALU.mult, ALU.subtract), reads=sr.b + c512.b + ini.b, writes=ini.b)

            def K1(it, gp, tt):
                k = it % 2
                cosT, sinT = cosTs[gp % 2], sinTs[gp % 2]
                sr, si = st_r[k], st_i[k]
                b1, b2, b3, b4 = bq_
                P.op("dve", lambda e: e.tensor_tensor(b1.t[:, :], sr.t[:, :], cosT.t[:, :], ALU.mult), reads=sr.b + cosT.b, writes=b1.b)
                P.op("dve", lambda e: e.tensor_tensor(b2.t[:, :], si.t[:, :], sinT.t[:, :], ALU.mult), reads=si.b + sinT.b, writes=b2.b)
                P.op("pool", lambda e: e.tensor_tensor(b3.t[:, :], si.t[:, :], cosT.t[:, :], ALU.mult), reads=si.b + cosT.b, writes=b3.b)
                P.op("pool", lambda e: e.tensor_tensor(b4.t[:, :], sr.t[:, :], sinT.t[:, :], ALU.mult), reads=sr.b + sinT.b, writes=b4.b)

            def K2(it, gp, tt):
                k = it % 2
                kc, j = gp // 4, gp % 4
                r0 = 32 * j
                t0 = tt * NT
                zr, zi = sbf_r[k], sbf_i[k]
                b1, b2, b3, b4 = bq_
                P.op("dve", lambda e: e.tensor_tensor(zr.t[:, :], b1.t[:, :], b2.t[:, :], ALU.subtract), reads=b1.b + b2.b, writes=zr.b)
                P.op("pool", lambda e: e.tensor_tensor(zi.t[:, :], b3.t[:, :], b4.t[:, :], ALU.add), reads=b3.b + b4.b, writes=zi.b)
                yb = y_banks[k]
                P.op("pe", lambda e: e.matmul(yb.t[:, :], LC[0].t[:, gp, :], zr.t[:, :], start=True, stop=False),
                     reads=LC[0].b + zr.b, writes=yb.b)
                P.op("pe", lambda e: e.matmul(yb.t[:, :], LC[1].t[:, gp, :], zi.t[:, :], start=False, stop=True),
                     reads=LC[1].b + zi.b, writes=yb.b)
                y_ = y32[k]
                P.op("act", lambda e: e.activation(y_.t[r0:r0 + 32, :], yb.t[r0:r0 + 32, :], AF.Copy), reads=yb.b, writes=y_.b)
                if pair:
                    P.dma("act", cg_y[tt].i[kc * 128 + r0:kc * 128 + r0 + 32, :], y_.t[r0:r0 + 32, :], reads=y_.b, writes=[cg_y[tt].ib])
                else:
                    P.dma("act", y_d[kc * 128 + r0:kc * 128 + r0 + 32, t0:t0 + NT], y_.t[r0:r0 + 32, :], reads=y_.b, writes=[y_buf])

            units = []
            for gp in range(NGP):
                for tt in range(NTG):
                    units.append((len(units), gp, tt))
            for i, un in enumerate(units):
                if un[2] == 0:
                    tables(un[1])
                F1(*un)
                if i >= 1:
                    K1(*units[i - 1])
                F2(*un)
                if i >= 1:
                    K2(*units[i - 1])
                F3(*un)
            K1(*units[-1])
            K2(*units[-1])
            scB.__exit__(None, None, None)
            scL.__exit__(None, None, None)
            if dbg == "y2":
                dbg2 = nc.dram_tensor("dbg_y", [D, T], F32, kind="ExternalOutput").ap()
                P.dma("sp", dbg2[:, :], y_d[:, :], reads=[y_buf], writes=[Buf("dbg2")])
                P.barrier()

            if pair:
                for cgx in cg_y:
                    cgx.go()
            scC = Scope()
            scC.__enter__()
            if pair:
                ycand = sb("ycand", [128, KC, NT], F32)
            wst[:] = [sb("wst%d" % i, [128, KC, 128], F32) for i in range(2)]
            wbf[:] = [sb("wbf%d" % i, [128, KC, 128], BF16) for i in range(5)]
            h32 = sb("h32", [128, KC, NT], F32)
            yt = sb("yt", [128, KC, NT], F32)
            g1 = [sb("g1_%d" % i, [128, NT], F32) for i in range(2)]
            if last:
                xtok[:] = [sb("xtok%d" % i, [128, D], F32) for i in range(2)]
            GC = 2.0 * math.sqrt(2.0 / math.pi)
            for tt in range(NTT):
                t0 = tt * NT
                load_xT_tile(tt, False)
                norm_mod(l, s, h32=h32, want_bf=False)
                if pair:
                    src0 = cg_y[tt].o[:, :].rearrange("(kc p) t -> p kc t", p=128)
                    src1 = cg_y[NTT + tt].o[:, :].rearrange("(kc p) t -> p kc t", p=128)
                    P.dma("sp", yt.t[:, :, :], src0, reads=[cg_y[tt].ob], writes=yt.b)
                    P.dma("act", ycand.t[:, :, :], src1, reads=[cg_y[NTT + tt].ob], writes=ycand.b)
                    select2(yt.t[:, :, :], yt.t[:, :, :], ycand.t[:, :, :], yt.t[:, :, :], yt.b + ycand.b, yt.b)
                else:
                    src = y_d[:, t0:t0 + NT].rearrange("(kc p) t -> p kc t", p=128)
                    P.dma("sp", yt.t[:, 0:8, :], src[:, 0:8, :], reads=[y_buf], writes=yt.b)
                    P.dma("act", yt.t[:, 8:16, :], src[:, 8:16, :], reads=[y_buf], writes=yt.b)
                for kc in range(KC):
                    ga = g1[kc % 2]
                    P.op("dve", lambda e, kc=kc: e.scalar_tensor_tensor(
                        yt.t[:, kc, :], h32.t[:, kc, :], dT.t[:, kc:kc + 1], yt.t[:, kc, :], ALU.mult, ALU.add),
                        reads=h32.b + dT.b + yt.b, writes=yt.b)
                    P.op("act", lambda e, kc=kc, ga=ga: e.activation(ga.t[:, :], yt.t[:, kc, :], AF.Square), reads=yt.b, writes=ga.b)
                    P.op("dve", lambda e, ga=ga: e.tensor_scalar(ga.t[:, :], ga.t[:, :], 0.044715, 1.0, ALU.mult, ALU.add), reads=ga.b, writes=ga.b)
                    P.op("dve", lambda e, kc=kc, ga=ga: e.tensor_tensor(ga.t[:, :], ga.t[:, :], yt.t[:, kc, :], ALU.mult), reads=ga.b + yt.b, writes=ga.b)
                    P.op("act", lambda e, ga=ga: e.activation(ga.t[:, :], ga.t[:, :], AF.Sigmoid, scale=GC), reads=ga.b, writes=ga.b)
                    P.op("pool", lambda e, kc=kc, ga=ga: e.tensor_tensor(hT.t[:, kc, :], ga.t[:, :], yt.t[:, kc, :], ALU.mult),
                         reads=ga.b + yt.b, writes=hT.b)
                for dc in range(KC):
                    wv = load_w_chunk(wglu_in, dc * 128, "glu")
                    wg = load_w_chunk(wglu_in, D + dc * 128, "glu")
                    pv = rot(mm_banks, "pb")
                    pg = rot(mm_banks, "pb")
                    for kc in range(KC):
                        P.op("pe", lambda e, wv=wv, kc=kc, pv=pv: e.matmul(
                            pv.t[:, :], wv.t[:, kc, :], hT.t[:, kc, :], start=(kc == 0), stop=(kc == KC - 1)),
                            reads=wv.b + hT.b, writes=pv.b)
                    for kc in range(KC):
                        P.op("pe", lambda e, wg=wg, kc=kc, pg=pg: e.matmul(
                            pg.t[:, :], wg.t[:, kc, :], hT.t[:, kc, :], start=(kc == 0), stop=(kc == KC - 1)),
                            reads=wg.b + hT.b, writes=pg.b)
                    tf = rot(tmpf, "tmpf")
                    P.op("act", lambda e, tf=tf, pg=pg: e.activation(tf.t[:, :], pg.t[:, :], AF.Sigmoid), reads=pg.b, writes=tf.b)
                    P.op("dve", lambda e, tf=tf, pv=pv: e.tensor_tensor(tf.t[:, :], tf.t[:, :], pv.t[:, :], ALU.mult),
                         reads=tf.b + pv.b, writes=tf.b)
                    P.op("dve", lambda e, dc=dc, tf=tf: e.scalar_tensor_tensor(
                        xT.t[:, dc, :], tf.t[:, :], modG[l].t[:, s, dc:dc + 1], xT.t[:, dc, :], ALU.mult, ALU.add),
                        reads=tf.b + modG[l].b + xT.b, writes=xT.b)
                store_xT_tile(tt, last)
            scC.__exit__(None, None, None)
            sc.__exit__(None, None, None)

        for i, ph in enumerate(phases):
            first = (i == 0)
            last = (i == len(phases) - 1)
            if ph.startswith("ffn"):
                l = int(ph[3]); fi = int(ph[4])
                ffn(l, 0 if fi == 0 else 2, fi, first=first, last=last)
            elif ph.startswith("att"):
                attention(int(ph[3]), first=first, last=last)
            elif ph.startswith("ssm"):
                ssm(int(ph[3]), first=first, last=last)

        P.emit()
    return nc


def make_in_maps(inputs, T=SEQ, phases=None, pair=False, n_cores=N_CORES):
    if phases is None:
        phases = ALL_PHASES
    if pair:
        return make_in_maps_pair(inputs, T, phases, n_cores)
    x = np.asarray(inputs["x"], dtype=np.float32)
    c = np.asarray(inputs["c"], dtype=np.float32)
    pos = np.asarray(inputs["positions"], dtype=np.int32)
    shared = {
        "norm_g": np.ascontiguousarray(np.asarray(inputs["norm_g"], np.float32).reshape(96, 128)),
        "ada_b": np.ascontiguousarray(np.asarray(inputs["ada_b"], np.float32).reshape(2, 144, 128)),
        "ident": np.eye(128, dtype=np.float32),
    }
    for l in sorted(set(int(ph[3]) for ph in phases)):
        shared["ada_w%d" % l] = np.asarray(inputs["ada_w"][l], np.float32)
    for ph in phases:
        if ph.startswith("ffn"):
            l = int(ph[3]); fi = int(ph[4])
            shared["ffn_w_in" + ph[3:]] = np.ascontiguousarray(np.asarray(inputs["ffn_w_in"][l, fi], np.float32))
            shared["ffn_w_out" + ph[3:]] = np.ascontiguousarray(np.asarray(inputs["ffn_w_out"][l, fi], np.float32))
    if any(ph.startswith("ssm") for ph in phases):
        p = np.arange(128)
        shared["pm"] = np.stack([1 - (p // 16) % 2, (p // 16) % 2], axis=1).astype(np.float32)
        shared["ssm_dT"] = np.ascontiguousarray(np.asarray(inputs["ssm_d"][0], np.float32).reshape(16, 128).T)
        shared["iota"] = np.ascontiguousarray(np.broadcast_to(np.arange(NT, dtype=np.float32)[None], (128, NT)))
        a_re = np.asarray(inputs["ssm_a_re"][0], np.float32)
        a_im = np.asarray(inputs["ssm_a_im"][0], np.float32)
        ls = np.asarray(inputs["ssm_log_step"][0], np.float32)
        lsb = np.broadcast_to(ls[:, None], (128, 64))
        def pair(a):
            sh = a.shape
            a = a.reshape((64, 2, 64) + sh[2:])
            a = np.moveaxis(a, 0, 2)
            return np.ascontiguousarray(a.reshape((128, 64) + sh[2:]))
        shared["a_pair"] = np.ascontiguousarray(np.stack([pair(a_re), pair(a_im), pair(lsb)], axis=1))
        def feat(a):
            a = a.reshape(16, 8, 64)
            a = np.broadcast_to(a[:, :, None, :], (16, 8, 16, 64))
            return np.ascontiguousarray(a.transpose(1, 2, 0, 3).reshape(128, 1024))
        shared["a_feat"] = np.ascontiguousarray(np.stack([feat(a_re), feat(a_im), feat(lsb)], axis=1))
        def featb(b):
            b = b.reshape(16, 8, 64, 16)
            return np.ascontiguousarray(b.transpose(1, 3, 0, 2).reshape(128, 1024))
        shared["b_feat"] = np.ascontiguousarray(np.stack(
            [featb(np.asarray(inputs["ssm_b_re"][0], np.float32)), featb(np.asarray(inputs["ssm_b_im"][0], np.float32))], axis=1))
        def pairc(c):
            return pair(np.ascontiguousarray(c.transpose(0, 2, 1)))
        shared["c_pair"] = np.ascontiguousarray(np.stack(
            [pairc(np.asarray(inputs["ssm_c_re"][0], np.float32)), pairc(np.asarray(inputs["ssm_c_im"][0], np.float32))], axis=1))
        shared["w_glu"] = np.ascontiguousarray(np.asarray(inputs["ssm_w_glu"][0], np.float32))
    if any(ph.startswith("att") for ph in phases):
        shared["attn_w_in"] = np.ascontiguousarray(np.asarray(inputs["attn_w_in"][0], np.float32))
        shared["attn_w_out"] = np.ascontiguousarray(np.asarray(inputs["attn_w_out"][0], np.float32))
        invf = (500000.0 ** (-np.arange(0, 32, 2, dtype=np.float32) / np.float32(32))).astype(np.float32)
        ropec = np.zeros((32, 4), np.float32)
        ropec[:, 0] = np.concatenate([invf, invf])
        sign = np.concatenate([-np.ones(16), np.ones(16)]).astype(np.float32)
        ropec[:, 1] = sign
        ropec[:, 2] = -math.pi * sign
        ropec[:, 3] = -math.pi
        shared["ropec"] = ropec
        rotm = np.zeros((32, 32), np.float32)
        for m_ in range(32):
            rotm[(m_ + 16) % 32, m_] = 1.0
        shared["rotm"] = rotm
        qg = np.asarray(inputs["attn_q_norm"][0], np.float32)
        kg = np.asarray(inputs["attn_k_norm"][0], np.float32)
        shared["qkg"] = np.ascontiguousarray(np.stack([qg, kg], axis=1))
        shared["qkgrep"] = np.ascontiguousarray(np.broadcast_to(np.stack([qg, kg], axis=0)[None], (128, 2, 128)))
        shared["subg"] = np.ascontiguousarray(np.broadcast_to(np.asarray(inputs["attn_subln"][0], np.float32)[None], (128, 256)))
        shared["lamrep"] = np.ascontiguousarray(np.broadcast_to(np.asarray(inputs["attn_lambda"][0], np.float32)[None], (128, 4, 128)))
        kk = np.arange(128)[:, None]
        qq = np.arange(256)[None, :]
        mask = np.stack([(kk <= qq), (kk + 128 <= qq)], axis=1).astype(np.float32)
        shared["maskc"] = mask.astype(ml_dtypes.bfloat16)
    maps = []
    for core in range(N_CORES):
        b = core % 4
        m = dict(shared)
        m["x"] = np.ascontiguousarray(x[b, :T])
        m["c"] = np.ascontiguousarray(c[b].reshape(16, 128))
        m["pos"] = np.ascontiguousarray(np.broadcast_to(pos[b, :T].reshape(1, T), (32, T)))
        maps.append(m)
    return maps


def make_in_maps_pair(inputs, T, phases, n_cores):
    base = make_in_maps(inputs, T, phases, pair=False)
    TL = T // 2
    x = np.asarray(inputs["x"], np.float32)
    maps = []
    for core in range(n_cores):
        b, r = core // 2, core % 2
        m = dict(base[b])
        m["x"] = np.ascontiguousarray(x[b, r * TL:(r + 1) * TL])
        f = np.zeros((128, 2), np.float32)
        f[:, r] = 1.0
        m["rankf"] = f
        for l in range(2):
            if "ada_w%d" % l in m:
                m["ada_w%d" % l] = np.ascontiguousarray(m["ada_w%d" % l][:, 9216 * r:9216 * (r + 1)])
        m["ada_b"] = np.ascontiguousarray(m["ada_b"][:, 72 * r:72 * (r + 1), :])
        if "attn_w_in" in m:
            w = np.asarray(inputs["attn_w_in"][0], np.float32)
            cols = np.concatenate([np.arange(o + 1024 * r, o + 1024 * (r + 1)) for o in (0, 2048, 4096)])
            m["attn_w_in"] = np.ascontiguousarray(w[:, cols])
        if "a_pair" in m:
            m["a_pair"] = np.ascontiguousarray(m["a_pair"][:, :, 32 * r:32 * (r + 1)])
            m["c_pair"] = np.ascontiguousarray(m["c_pair"][:, :, 32 * r:32 * (r + 1), :])
            m["a_feat"] = np.ascontiguousarray(m["a_feat"][:, :, 512 * r:512 * (r + 1)])
            m["b_feat"] = np.ascontiguousarray(m["b_feat"][:, :, 512 * r:512 * (r + 1)])
        maps.append(m)
    return maps


def kernel(**inputs):
    nc = build(SEQ, pair=True)
    maps = make_in_maps(inputs, SEQ, pair=True)
    res = run_bass_kernel_spmd(nc, maps, core_ids=list(range(N_CORES)))
    out = np.stack([np.concatenate([res.results[2 * b]["out"], res.results[2 * b + 1]["out"]], axis=0) for b in range(4)], axis=0)
    return out.astype(np.float32)
```

```python
import math
from contextlib import ExitStack

import numpy as np
import ml_dtypes

import concourse.bass as bass
import concourse.mybir as mybir
from concourse.bass_utils import run_bass_kernel_spmd

F32 = mybir.dt.float32
BF16 = mybir.dt.bfloat16
I32 = mybir.dt.int32
ALU = mybir.AluOpType
AF = mybir.ActivationFunctionType
AX = mybir.AxisListType

D = 2048
KC = 16
SEQ = 4096
DFF = 5632
FC = 44
NT = 512
EPS = 1e-6
NH = 8
HD = 128
N_CORES = 8
TWO_PI = 2.0 * math.pi

SELF_SYNC = True
BG_ENABLE = True
NO_SELF_SYNC = ()
ALL_PHASES = ['ffn00', 'att0', 'ffn01', 'ffn10', 'ssm1', 'ffn11']


class Buf:
    __slots__ = ("w", "r", "name")

    def __init__(self, name=""):
        self.w = None
        self.r = []
        self.name = name


class Prog:
    NDS = {"sp": 12, "pool": 6, "act": 4}

    def __init__(self, nc, es):
        self.nc = nc
        self.engs = {"pe": nc.tensor, "act": nc.scalar, "dve": nc.vector, "pool": nc.gpsimd, "sp": nc.sync}
        self.sem = {e: es.enter_context(nc.semaphore("s_" + e)) for e in ("pe", "act", "dve", "pool")}
        self.cnt = {e: 0 for e in self.sem}
        self.ops = {e: [] for e in self.engs}
        self.seen = {e: {} for e in self.engs}
        self.dsem = {q: [es.enter_context(nc.semaphore("d_%s%d" % (q, i))) for i in range(n)]
                     for q, n in self.NDS.items()}
        self.duse = {q: [0] * n for q, n in self.NDS.items()}
        self.dnext = {q: 0 for q in self.NDS}
        self.pending = {e: [] for e in self.engs}
        self.ccsem = es.enter_context(nc.semaphore("s_cc"))
        self.cccnt = 0

    def _semof(self, key):
        if isinstance(key, tuple):
            return self.dsem[key[1]][key[2]]
        if key == "cc":
            return self.ccsem
        return self.sem[key]

    def _deps(self, e, reads, writes):
        toks = []
        for b in reads:
            if b.w is not None:
                toks.append(b.w)
        for b in writes:
            if b.w is not None:
                toks.append(b.w)
            toks.extend(b.r)
        waits = {}
        for k, v in toks:
            if k == e and (e == "pe" or e in NO_SELF_SYNC):
                continue
            if self.seen[e].get(k, 0) >= v:
                continue
            if waits.get(k, 0) < v:
                waits[k] = v
        for k, v in waits.items():
            self.seen[e][k] = v
        return list(waits.items())

    def op(self, e, fn, reads=(), writes=()):
        waits = self.pending[e] + self._deps(e, reads, writes)
        self.pending[e] = []
        self.cnt[e] += 1
        tok = (e, self.cnt[e])
        self.ops[e].append((waits, fn, self.sem[e], 1))
        for b in reads:
            b.r.append(tok)
        for b in writes:
            b.w = tok
            b.r = []

    def dma(self, q, out, in_, reads=(), writes=(), **kw):
        i = self.dnext[q]
        self.dnext[q] = (i + 1) % self.NDS[q]
        waits = self.pending[q] + self._deps(q, reads, writes)
        self.pending[q] = []
        key = ("d", q, i)
        prev = self.duse[q][i]
        if prev > 0 and self.seen[q].get(key, 0) < prev:
            waits.append((key, prev))
            self.seen[q][key] = prev
        self.duse[q][i] = prev + 16
        tok = (key, prev + 16)
        self.ops[q].append((waits, lambda eng: eng.dma_start(out=out, in_=in_, **kw), self.dsem[q][i], 16))
        for b in reads:
            b.r.append(tok)
        for b in writes:
            b.w = tok
            b.r = []

    def barrier(self):
        allw = []
        for q, n in self.NDS.items():
            for i in range(n):
                if self.duse[q][i] > 0:
                    allw.append((("d", q, i), self.duse[q][i]))
        for e in self.sem:
            if self.cnt[e] > 0:
                allw.append((e, self.cnt[e]))
        if self.cccnt > 0:
            allw.append(("cc", self.cccnt))
        for e in self.engs:
            lst = []
            for k, v in allw:
                if k == e:
                    continue
                if self.seen[e].get(k, 0) >= v:
                    continue
                self.seen[e][k] = v
                lst.append((k, v))
            self.pending[e].extend(lst)

    def coll(self, kind, groups, in_h, out_h, reads=(), writes=()):
        q = "pool"
        waits = self.pending[q] + self._deps(q, reads, writes)
        self.pending[q] = []
        self.cccnt += 1
        tok = ("cc", self.cccnt)

        def fn(eng):
            return eng.collective_compute(kind, ALU.bypass, replica_groups=groups,
                                          ins=[in_h.ap().opt()], outs=[out_h.ap().opt()])
        self.ops[q].append((waits, fn, self.ccsem, None))
        for b in reads:
            b.r.append(tok)
        for b in writes:
            b.w = tok
            b.r = []

    def emit(self):
        nc = self.nc
        fin = []
        for q, n in self.NDS.items():
            for i in range(n):
                if self.duse[q][i] > 0:
                    fin.append((("d", q, i), self.duse[q][i]))
        for e in self.sem:
            if self.cnt[e] > 0:
                fin.append((e, self.cnt[e]))
        if self.cccnt > 0:
            fin.append(("cc", self.cccnt))

        def replay(name, eng):
            for waits, fn, sem, inc in self.ops[name]:
                for k, v in waits:
                    eng.wait_ge(self._semof(k), v)
                if inc is None:
                    fn(eng).then_inc(sem)
                else:
                    fn(eng).then_inc(sem, inc)
            if name == "sp":
                for k, v in fin:
                    eng.wait_ge(self._semof(k), v)

        with nc.Block() as block:
            @block.sync
            def _(e):
                replay("sp", e)

            @block.scalar
            def _(e):
                replay("act", e)

            @block.vector
            def _(e):
                replay("dve", e)

            @block.gpsimd
            def _(e):
                replay("pool", e)

            @block.tensor
            def _(e):
                replay("pe", e)


class Tile:
    def __init__(self, t, nsub=1, name=""):
        self.t = t
        self.b = [Buf("%s[%d]" % (name, i)) for i in range(nsub)]


class K:
    pass


def build(T=SEQ, phases=None, dbg=None, pair=False, groups=None):
    nc = bass.Bass("TRN2", target_bir_lowering=False)
    TL = T // 2 if pair else T
    NTT = TL // NT
    NTG = T // NT
    NHL = NH // 2 if pair else NH
    if groups is None:
        groups = [[2 * i, 2 * i + 1] for i in range(N_CORES // 2)]
    es = ExitStack()
    with es:
        P = Prog(nc, es)

        def dram_in(name, shape, dt=F32):
            return nc.dram_tensor(name, list(shape), dt, kind="ExternalInput").ap()

        def dram_scr(name, shape, dt=F32):
            return nc.dram_tensor(name, list(shape), dt, kind="Internal").ap()

        cur = [es]
        uid = [0]

        def sb(name, shape, dt=F32, nsub=1):
            uid[0] += 1
            t = cur[0].enter_context(nc.sbuf_tensor("sb%d_%s" % (uid[0], name), list(shape), dt))
            return Tile(t, nsub, name)

        class Scope:
            def __enter__(self_):
                self_.st = ExitStack()
                self_.st.__enter__()
                self_.prev = cur[0]
                cur[0] = self_.st
                return self_

            def __exit__(self_, *a):
                P.barrier()
                cur[0] = self_.prev
                return self_.st.__exit__(*a)

        def ps(name, shape, dt=F32):
            t = es.enter_context(nc.psum_tensor("ps_" + name, list(shape), dt))
            return Tile(t, 1, name)

        class CG:
            def __init__(self, name, rows, cols, dt):
                self.i_h = nc.dram_tensor("cg_in_" + name, [rows, cols], dt)
                self.o_h = nc.dram_tensor("cg_out_" + name, [2 * rows, cols], dt)
                self.i = self.i_h.ap()
                self.o = self.o_h.ap()
                self.ib = Buf("cgi_" + name)
                self.ob = Buf("cgo_" + name)

            def go(self):
                P.coll("AllGather", groups, self.i_h, self.o_h, reads=[self.ib], writes=[self.ob])

        x_in = dram_in("x", [TL, D])
        c_in = dram_in("c", [16, 128])
        pos_in = dram_in("pos", [32, T], I32)
        normg_in = dram_in("norm_g", [96, 128])
        if phases is None:
            phases = ALL_PHASES
        layers = sorted(set(int(ph[3]) for ph in phases))
        NJ = 72 if pair else 144
        adaw_in = {l: dram_in("ada_w%d" % l, [D, NJ * 128]) for l in layers}
        adab_in = dram_in("ada_b", [2, NJ, 128])
        ffnwi_in = {}
        ffnwo_in = {}
        for ph in phases:
            if ph.startswith("ffn"):
                ffnwi_in[ph[3:]] = dram_in("ffn_w_in" + ph[3:], [D, 2 * DFF])
                ffnwo_in[ph[3:]] = dram_in("ffn_w_out" + ph[3:], [DFF, D])
        ident_in = dram_in("ident", [128, 128])
        if any(ph.startswith("ssm") for ph in phases):
            pm_in = dram_in("pm", [128, 2])
            ssmd_in = dram_in("ssm_dT", [128, 16])
            iota_in = dram_in("iota", [128, NT])
            NGP = 32 if pair else 64
            NKL = NGP // 4
            apair_in = dram_in("a_pair", [128, 3, NGP])
            afeat_in = dram_in("a_feat", [128, 3, NKL * 64])
            bfeat_in = dram_in("b_feat", [128, 2, NKL * 64])
            cpair_in = dram_in("c_pair", [128, 2, NGP, 16])
            wglu_in = dram_in("w_glu", [D, 2 * D])
            if pair:
                cg_u = [CG("u%d" % i, D, NT, BF16) for i in range(NTT)]
                cg_y = [CG("y%d" % i, NKL * 128, NT, F32) for i in range(NTG)]
                uT_d = dram_scr("um_scr", [NKL * 128, T], BF16)
            else:
                uT_d = dram_scr("uT_scr", [D, T], BF16)
            if pair:
                y_d = None
            elif dbg == "y":
                y_d = nc.dram_tensor("dbg_y", [D, T], F32, kind="ExternalOutput").ap()
            else:
                y_d = dram_scr("y_scr", [D, T])
            u_buf = Buf("u")
            y_buf = Buf("y")
        if any(ph.startswith("att") for ph in phases):
            attnwi_in = dram_in("attn_w_in", [D, 3 * NHL * 256])
            attnwo_in = dram_in("attn_w_out", [D, D])
            ropec_in = dram_in("ropec", [32, 4])
            rotm_in = dram_in("rotm", [32, 32])
            qkg_in = dram_in("qkg", [128, 2])
            subg_in = dram_in("subg", [128, 256])
            qkgrep_in = dram_in("qkgrep", [128, 2, 128])
            lamrep_in = dram_in("lamrep", [128, 4, 128])
            mask_in = dram_in("maskc", [128, 2, 256], BF16)
            qkT_d = dram_scr("qkT_scr", [4 * NHL, 128, T], BF16)
            V_d = dram_scr("V_scr", [T, NHL * 256], BF16)
            OTC = 1024
            if pair:
                cg_h = [CG("hT%d" % i, D, NT, BF16) for i in range(NTT)]
                cg_o = [CG("oT%d" % i, NHL * 256, OTC, BF16) for i in range(T // OTC)]
            else:
                oT_d = dram_scr("oT_scr", [D, T], BF16)
            qk_buf = Buf("qk")
            v_buf = Buf("v")
            oT_buf = Buf("oT")
        out_d = nc.dram_tensor("out", [TL, D], F32, kind="ExternalOutput").ap()
        xT_d = dram_scr("xT_scr", [D, TL])
        if pair:
            rankf_in = dram_in("rankf", [128, 2])

        xT_buf = Buf("xT_d")
        out_buf = Buf("out_d")
        in_buf = Buf("inputs")

        ident = sb("ident", [128, 128])
        identb = sb("identb", [128, 128], BF16)
        onesb = sb("onesb", [128, 128], BF16)
        vecrows = sb("vecrows", [128, 128])
        normgT = sb("normgT", [128, 96])
        condT = sb("condT", [128, 16, 2])
        adabT = [sb("adabT%d" % l, [128, 144]) for l in range(2)]
        modT = [sb("modT%d" % l, [128, 144]) for l in range(2)]
        modA = [sb("modA%d" % l, [128, 3, 16]) for l in range(2)]
        modG = [sb("modG%d" % l, [128, 3, 16]) for l in range(2)]

        rankf = sb("rankf", [128, 2], F32)
        if pair:
            P.dma("sp", rankf.t[:, :], rankf_in[:, :], reads=[in_buf], writes=rankf.b)

        def select2(dst_ap, c0_ap, c1_ap, tmp_ap, reads, writes, eng="dve"):
            P.op(eng, lambda e: e.tensor_scalar(tmp_ap, c0_ap, rankf.t[:, 0:1], None, ALU.mult),
                 reads=list(reads) + rankf.b, writes=list(writes))
            P.op(eng, lambda e: e.scalar_tensor_tensor(dst_ap, c1_ap, rankf.t[:, 1:2], tmp_ap, ALU.mult, ALU.add),
                 reads=list(reads) + rankf.b, writes=list(writes))

        pbank = [ps("pb%d" % i, [128, 512]) for i in range(7)]
        pbf = ps("pbf", [128, 1024], BF16)

        P.dma("sp", ident.t[:], ident_in[:, :], reads=[in_buf], writes=ident.b)
        P.op("dve", lambda e: e.tensor_copy(identb.t[:], ident.t[:]), reads=ident.b, writes=identb.b)
        P.op("dve", lambda e: e.memset(onesb.t[:], 1.0), writes=onesb.b)

        def transpose_rows(src_ap_dram, nrows, dst_ap, dst_bufs, bank):
            P.dma("sp", vecrows.t[0:nrows, :], src_ap_dram, reads=[in_buf], writes=vecrows.b)
            P.op("pe", lambda e: e.transpose(bank.t[:, 0:nrows], vecrows.t[0:nrows, :], ident.t[0:nrows, 0:nrows]),
                 reads=vecrows.b + ident.b, writes=bank.b)
            P.op("dve", lambda e: e.tensor_copy(dst_ap, bank.t[:, 0:nrows]), reads=bank.b, writes=dst_bufs)

        transpose_rows(normg_in[:, :], 96, normgT.t[:, :], normgT.b, pbank[0])
        transpose_rows(c_in[:, :], 16, condT.t[:, :, 0], condT.b, pbank[1])
        P.op("act", lambda e: e.activation(condT.t[:, :, 0], condT.t[:, :, 0], AF.Silu), reads=condT.b, writes=condT.b)
        P.op("act", lambda e: e.activation(condT.t[:, :, 1], condT.t[:, :, 0], AF.Copy), reads=condT.b, writes=condT.b)
        for l in range(2):
            if pair:
                transpose_rows(adab_in[l, 0:NJ, :], NJ, adabT[l].t[:, 0:NJ], adabT[l].b, pbank[2])
            else:
                transpose_rows(adab_in[l, 0:128, :], 128, adabT[l].t[:, 0:128], adabT[l].b, pbank[2])
                transpose_rows(adab_in[l, 128:144, :], 16, adabT[l].t[:, 128:144], adabT[l].b, pbank[3])

        CB = 512
        sc_ada = Scope()
        sc_ada.__enter__()
        adaw = [sb("adaw%d" % i, [128, 16, CB]) for i in range(2)]
        nblk = NJ * 128 // CB
        if pair:
            cg_mod = CG("mod", 128, 2 * NJ, F32)
            modh = sb("modh", [128, 2, NJ])
        for l in layers:
            bank = pbank[4 + l]
            for blk in range(nblk):
                wt = adaw[blk % 2]
                src = adaw_in[l][:, blk * CB:(blk + 1) * CB].rearrange("(kc p) c -> p kc c", p=128)
                P.dma("sp", wt.t[:, 0:8, :], src[:, 0:8, :], reads=[in_buf], writes=wt.b)
                P.dma("act", wt.t[:, 8:16, :], src[:, 8:16, :], reads=[in_buf], writes=wt.b)
                for jj in range(CB // 128):
                    j = blk * (CB // 128) + jj
                    for kc in range(KC):
                        P.op("pe", lambda e, wt=wt, jj=jj, kc=kc, j=j, bank=bank: e.matmul(
                            bank.t[:, 2 * j:2 * j + 2], wt.t[:, kc, jj * 128:(jj + 1) * 128], condT.t[:, kc, :],
                            start=(kc == 0), stop=(kc == KC - 1)),
                            reads=wt.b + condT.b, writes=bank.b)
            if pair:
                P.op("dve", lambda e, l=l, bank=bank: e.tensor_tensor(
                    modh.t[:, l, :], bank.t[:, 0:2 * NJ].rearrange("p (j two) -> p j two", two=2)[:, :, 0], adabT[l].t[:, 0:NJ], ALU.add),
                    reads=bank.b + adabT[l].b, writes=modh.b)
                continue
            P.op("dve", lambda e, l=l, bank=bank: e.tensor_tensor(
                modT[l].t[:, :], bank.t[:, 0:288].rearrange("p (j two) -> p j two", two=2)[:, :, 0], adabT[l].t[:, :], ALU.add),
                reads=bank.b + adabT[l].b, writes=modT[l].b)
        if pair:
            P.dma("sp", cg_mod.i[:, :], modh.t[:, :, :].rearrange("p l j -> p (l j)"), reads=modh.b, writes=[cg_mod.ib])
            cg_mod.go()
            for l in layers:
                for r_ in range(2):
                    P.dma(("sp", "act")[r_], modT[l].t[:, r_ * NJ:(r_ + 1) * NJ], cg_mod.o[r_ * 128:(r_ + 1) * 128, l * NJ:(l + 1) * NJ],
                          reads=[cg_mod.ob], writes=modT[l].b)
        for l in layers:
            for s in range(3):
                sh = (s * 3 + 0) * 16
                sc = (s * 3 + 1) * 16
                ga = (s * 3 + 2) * 16
                P.op("dve", lambda e, l=l, s=s, sc=sc: e.scalar_tensor_tensor(
                    modA[l].t[:, s, :], modT[l].t[:, sc:sc + 16], 1.0, normgT.t[:, (l * 3 + s) * 16:(l * 3 + s) * 16 + 16],
                    ALU.add, ALU.mult), reads=modT[l].b + normgT.b, writes=modA[l].b)
                wgt = 1.0 if s == 1 else 0.5
                P.op("dve", lambda e, l=l, s=s, ga=ga, wgt=wgt: e.tensor_scalar(
                    modG[l].t[:, s, :], modT[l].t[:, ga:ga + 16], wgt, None, ALU.mult),
                    reads=modT[l].b, writes=modG[l].b)

        sc_ada.__exit__(None, None, None)

        xT = sb("xT", [128, KC, NT], F32)
        hT = sb("hT", [128, KC, NT], BF16)
        tmpf = [sb("tmpf%d" % i, [128, NT], F32) for i in range(3)]
        tmpb = [sb("tmpb%d" % i, [128, NT], BF16) for i in range(3)]
        rstd = sb("rstd", [128, NT], F32)
        xtok = []
        wst = []
        wbf = []
        wost = []
        wobf = []
        gTh = [None]

        def alloc_ffn_tiles(need_xtok=True):
            gTh[0] = sb("gT", [128, FC, NT], BF16, nsub=FC)
            if need_xtok:
                xtok[:] = [sb("xtok%d" % i, [128, D], F32) for i in range(2)]
            wst[:] = [sb("wst%d" % i, [128, KC, 128], F32) for i in range(2)]
            wbf[:] = [sb("wbf%d" % i, [128, KC, 128], BF16) for i in range(5)]
            wost[:] = [sb("wost%d" % i, [128, 22, 128], F32) for i in range(2)]
            wobf[:] = [sb("wobf%d" % i, [128, 22, 128], BF16) for i in range(3)]
        ctr = {"tmpf": 0, "tmpb": 0, "w": 0, "wb": 0, "wo": 0, "wob": 0, "pb": 0, "xtok": 0, "cast": 0, "ldq": 0}

        def rot(lst, key):
            i = ctr[key]
            ctr[key] = i + 1
            return lst[i % len(lst)]

        mm_banks = pbank[0:4]
        ssq_bank = pbank[4]
        tr_banks = pbank[5:7]

        def load_xT_tile(tt, from_input):
            t0 = tt * NT
            if not from_input:
                src = xT_d[:, t0:t0 + NT].rearrange("(kc p) t -> p kc t", p=128)
                P.dma("sp", xT.t[:, 0:8, :], src[:, 0:8, :], reads=[xT_buf], writes=xT.b)
                P.dma("act", xT.t[:, 8:16, :], src[:, 8:16, :], reads=[xT_buf], writes=xT.b)
                return
            for sub in range(NT // 128):
                xt = rot(xtok, "xtok")
                P.dma("sp", xt.t[:, :], x_in[t0 + sub * 128:t0 + (sub + 1) * 128, :], reads=[in_buf], writes=xt.b)
                for kc in range(KC):
                    bank = tr_banks[kc % 2]
                    P.op("pe", lambda e, xt=xt, kc=kc, bank=bank: e.transpose(
                        bank.t[:, 0:128], xt.t[:, kc * 128:(kc + 1) * 128], ident.t[:, :]),
                        reads=xt.b + ident.b, writes=bank.b)
                    eng = "dve" if kc % 2 == 0 else "act"
                    if eng == "dve":
                        P.op("dve", lambda e, kc=kc, sub=sub, bank=bank: e.tensor_copy(
                            xT.t[:, kc, sub * 128:(sub + 1) * 128], bank.t[:, 0:128]), reads=bank.b, writes=xT.b)
                    else:
                        P.op("act", lambda e, kc=kc, sub=sub, bank=bank: e.activation(
                            xT.t[:, kc, sub * 128:(sub + 1) * 128], bank.t[:, 0:128], AF.Copy), reads=bank.b, writes=xT.b)

        def store_xT_tile(tt, to_output):
            t0 = tt * NT
            if not to_output:
                dst = xT_d[:, t0:t0 + NT].rearrange("(kc p) t -> p kc t", p=128)
                P.dma("pool", dst[:, :, :], xT.t[:, :, :], reads=xT.b, writes=[xT_buf])
                return
            for sub in range(NT // 128):
                xt = rot(xtok, "xtok")
                for kc in range(KC):
                    bank = tr_banks[kc % 2]
                    P.op("pe", lambda e, kc=kc, sub=sub, bank=bank: e.transpose(
                        bank.t[:, 0:128], xT.t[:, kc, sub * 128:(sub + 1) * 128], ident.t[:, :]),
                        reads=xT.b + ident.b, writes=bank.b)
                    if kc % 2 == 0:
                        P.op("dve", lambda e, kc=kc, xt=xt, bank=bank: e.tensor_copy(
                            xt.t[:, kc * 128:(kc + 1) * 128], bank.t[:, 0:128]), reads=bank.b, writes=xt.b)
                    else:
                        P.op("act", lambda e, kc=kc, xt=xt, bank=bank: e.activation(
                            xt.t[:, kc * 128:(kc + 1) * 128], bank.t[:, 0:128], AF.Copy), reads=bank.b, writes=xt.b)
                P.dma("pool", out_d[t0 + sub * 128:t0 + (sub + 1) * 128, :], xt.t[:, :], reads=xt.b, writes=[out_buf])

        def norm_mod(l, s, h32=None, want_bf=True):
            for kc in range(KC):
                sq = rot(tmpb, "tmpb")
                P.op("act", lambda e, kc=kc, sq=sq: e.activation(sq.t[:, :], xT.t[:, kc, :], AF.Square),
                     reads=xT.b, writes=sq.b)
                P.op("pe", lambda e, kc=kc, sq=sq: e.matmul(ssq_bank.t[:, :], onesb.t[:, :], sq.t[:, :],
                                                           start=(kc == 0), stop=(kc == KC - 1)),
                     reads=sq.b + onesb.b, writes=ssq_bank.b)
            P.op("dve", lambda e: e.tensor_scalar(rstd.t[:, :], ssq_bank.t[:, :], 1.0 / D, EPS, ALU.mult, ALU.add),
                 reads=ssq_bank.b, writes=rstd.b)
            P.op("act", lambda e: e.activation(rstd.t[:, :], rstd.t[:, :], AF.Sqrt), reads=rstd.b, writes=rstd.b)
            P.op("dve", lambda e: e.reciprocal(rstd.t[:, :], rstd.t[:, :]), reads=rstd.b, writes=rstd.b)
            sh = (s * 3 + 0) * 16
            for kc in range(KC):
                tf = rot(tmpf, "tmpf")
                P.op("dve", lambda e, kc=kc, tf=tf: e.scalar_tensor_tensor(
                    tf.t[:, :], xT.t[:, kc, :], modA[l].t[:, s, kc:kc + 1], rstd.t[:, :], ALU.mult, ALU.mult),
                    reads=xT.b + modA[l].b + rstd.b, writes=tf.b)
                if h32 is not None:
                    P.op("act", lambda e, kc=kc, tf=tf: e.activation(
                        h32.t[:, kc, :], tf.t[:, :], AF.Identity, bias=modT[l].t[:, sh + kc:sh + kc + 1], scale=1.0),
                        reads=tf.b + modT[l].b, writes=h32.b)
                    if want_bf:
                        P.op("pool", lambda e, kc=kc: e.tensor_copy(hT.t[:, kc, :], h32.t[:, kc, :]), reads=h32.b, writes=hT.b)
                    continue
                P.op("act", lambda e, kc=kc, tf=tf: e.activation(
                    hT.t[:, kc, :], tf.t[:, :], AF.Identity, bias=modT[l].t[:, sh + kc:sh + kc + 1], scale=1.0),
                    reads=tf.b + modT[l].b, writes=hT.b)

        wcache = {}
        CAST_ENGS = ["act", "dve", "pool"]

        def cast_op(dst_ap, src_ap, reads, writes):
            e = CAST_ENGS[ctr["cast"] % 3]
            ctr["cast"] += 1
            if e == "act":
                P.op("act", lambda eng: eng.activation(dst_ap, src_ap, AF.Copy), reads=reads, writes=writes)
            else:
                P.op(e, lambda eng: eng.tensor_copy(dst_ap, src_ap), reads=reads, writes=writes)

        def ldq():
            q = ("sp", "pool")[ctr["ldq"] % 2]
            ctr["ldq"] += 1
            return q

        def load_w_chunk(w_ap, c0, wname=None):
            idx = c0 // 128
            wb = rot(wbf, "wb")
            bufs = None
            if wname is not None:
                if wname not in wcache:
                    wcache[wname] = (dram_scr("wc_" + wname, [w_ap.shape[1] // 128, 128, KC * 128], BF16), {})
                scr, bufs = wcache[wname]
            if bufs is not None and idx in bufs:
                P.dma(ldq(), wb.t[:, :, :], scr[idx].rearrange("p (k c) -> p k c", c=128), reads=[bufs[idx]], writes=wb.b)
                return wb
            st = rot(wst, "w")
            src = w_ap[:, c0:c0 + 128].rearrange("(kc p) c -> p kc c", p=128)
            P.dma("sp", st.t[:, :, :], src, reads=[in_buf], writes=st.b)
            cast_op(wb.t[:, :, :], st.t[:, :, :], st.b, wb.b)
            if bufs is not None:
                bufs[idx] = Buf("wc")
                P.dma("pool", scr[idx].rearrange("p (k c) -> p k c", c=128), wb.t[:, :, :], reads=wb.b, writes=[bufs[idx]])
            return wb

        def load_wo_half(w_out, dc, half, wname):
            wb = rot(wobf, "wob")
            if wname not in wcache:
                wcache[wname] = (dram_scr("wc_" + wname, [32, 128, 22 * 128], BF16), {})
            scr, bufs = wcache[wname]
            idx = dc * 2 + half
            if idx in bufs:
                P.dma(ldq(), wb.t[:, :, :], scr[idx].rearrange("p (k c) -> p k c", c=128), reads=[bufs[idx]], writes=wb.b)
                return wb
            st = rot(wost, "wo")
            src = w_out[half * 22 * 128:(half + 1) * 22 * 128, dc * 128:(dc + 1) * 128].rearrange("(j p) c -> p j c", p=128)
            P.dma("sp", st.t[:, :, :], src, reads=[in_buf], writes=st.b)
            cast_op(wb.t[:, :, :], st.t[:, :, :], st.b, wb.b)
            bufs[idx] = Buf("wc")
            P.dma("pool", scr[idx].rearrange("p (k c) -> p k c", c=128), wb.t[:, :, :], reads=wb.b, writes=[bufs[idx]])
            return wb

        bg = {"st": [], "bf": [], "i": 0}

        def bg_alloc():
            bg["st"] = [sb("bgst%d" % i, [128, KC * 128], F32) for i in range(2)]
            bg["bf"] = [sb("bgbf%d" % i, [128, KC * 128], BF16) for i in range(2)]

        def bg_jobs_in(w_ap, wname):
            for idx in range(w_ap.shape[1] // 128):
                yield ("in", w_ap, wname, idx, 0)

        def bg_jobs_out(w_out, wname):
            for idx in range(32):
                for piece in range(2):
                    yield ("out", w_out, wname, idx, piece)

        def bg_step(job, cast_eng, ld_q, st_q):
            kind, w_ap, wname, idx, piece = job
            i = bg["i"]
            bg["i"] += 1
            st, wb = bg["st"][i % 2], bg["bf"][i % 2]
            if wname not in wcache:
                shape = [w_ap.shape[1] // 128, 128, KC * 128] if kind == "in" else [32, 128, 22 * 128]
                wcache[wname] = (dram_scr("wc_" + wname, shape, BF16), {})
            scr, bufs = wcache[wname]
            if kind == "in":
                n = KC * 128
                src = w_ap[:, idx * 128:(idx + 1) * 128].rearrange("(kc p) c -> p kc c", p=128)
                dst = scr[idx].rearrange("p (k c) -> p k c", c=128)
            else:
                n = 11 * 128
                dc, half = idx // 2, idx % 2
                r0_ = (half * 22 + piece * 11) * 128
                src = w_ap[r0_:r0_ + 11 * 128, dc * 128:(dc + 1) * 128].rearrange("(j p) c -> p j c", p=128)
                dst = scr[idx][:, piece * n:(piece + 1) * n].rearrange("p (k c) -> p k c", c=128)
            sv = st.t[:, 0:n].rearrange("p (k c) -> p k c", c=128)
            bv = wb.t[:, 0:n].rearrange("p (k c) -> p k c", c=128)
            P.dma(ld_q, sv, src, reads=[in_buf], writes=st.b)
            if cast_eng == "act":
                P.op("act", lambda e: e.activation(wb.t[:, 0:n], st.t[:, 0:n], AF.Copy), reads=st.b, writes=wb.b)
            else:
                P.op(cast_eng, lambda e: e.tensor_copy(wb.t[:, 0:n], st.t[:, 0:n]), reads=st.b, writes=wb.b)
            if idx not in bufs:
                bufs[idx] = Buf("wc")
            P.dma(st_q, dst, bv, reads=wb.b, writes=[bufs[idx]])

        def bg_chain(*gens):
            for g in gens:
                for job in g:
                    yield job

        def ffn(l, s, fi, first=False, last=False, bg_next=None):
            w_in = ffnwi_in['%d%d' % (l, fi)]
            w_out = ffnwo_in['%d%d' % (l, fi)]
            sc = Scope()
            sc.__enter__()
            alloc_ffn_tiles(first or last)
            gT = gTh[0]
            if bg_next is not None:
                bg_alloc()
            for tt in range(NTT):
                load_xT_tile(tt, first)
                norm_mod(l, s)
                for j in range(FC):
                    if bg_next is not None and tt >= 1:
                        for _ in range(2):
                            job = next(bg_next, None)
                            if job is not None:
                                bg_step(job, ("dve", "pool", "act")[bg["i"] % 3], "act", "pool")
                    wa = load_w_chunk(w_in, j * 128, "fi%d%d" % (l, fi))
                    wb_ = load_w_chunk(w_in, DFF + j * 128, "fi%d%d" % (l, fi))
                    pa = rot(mm_banks, "pb")
                    pb_ = rot(mm_banks, "pb")
                    for kc in range(KC):
                        P.op("pe", lambda e, wa=wa, kc=kc, pa=pa: e.matmul(
                            pa.t[:, :], wa.t[:, kc, :], hT.t[:, kc, :], start=(kc == 0), stop=(kc == KC - 1)),
                            reads=wa.b + hT.b, writes=pa.b)
                    for kc in range(KC):
                        P.op("pe", lambda e, wb_=wb_, kc=kc, pb_=pb_: e.matmul(
                            pb_.t[:, :], wb_.t[:, kc, :], hT.t[:, kc, :], start=(kc == 0), stop=(kc == KC - 1)),
                            reads=wb_.b + hT.b, writes=pb_.b)
                    tf = rot(tmpf, "tmpf")
                    P.op("act", lambda e, tf=tf, pa=pa: e.activation(tf.t[:, :], pa.t[:, :], AF.Silu),
                         reads=pa.b, writes=tf.b)
                    P.op("dve", lambda e, tf=tf, pb_=pb_, j=j: e.tensor_tensor(
                        gT.t[:, j, :], tf.t[:, :], pb_.t[:, :], ALU.mult),
                        reads=tf.b + pb_.b, writes=[gT.b[j]])
                for dc in range(KC):
                    po = rot(mm_banks, "pb")
                    for half in range(2):
                        wb = load_wo_half(w_out, dc, half, "fo%d%d" % (l, fi))
                        for jj in range(22):
                            j = half * 22 + jj
                            P.op("pe", lambda e, wb=wb, jj=jj, j=j, po=po: e.matmul(
                                po.t[:, :], wb.t[:, jj, :], gT.t[:, j, :], start=(j == 0), stop=(j == FC - 1)),
                                reads=wb.b + [gT.b[j]], writes=po.b)
                    P.op("dve", lambda e, dc=dc, po=po: e.scalar_tensor_tensor(
                        xT.t[:, dc, :], po.t[:, :], modG[l].t[:, s, dc:dc + 1], xT.t[:, dc, :], ALU.mult, ALU.add),
                        reads=po.b + modG[l].b + xT.b, writes=xT.b)
                store_xT_tile(tt, last)
            if bg_next is not None:
                for job in bg_next:
                    bg_step(job, ("dve", "pool", "act")[bg["i"] % 3], "act", "pool")
            sc.__exit__(None, None, None)


        def attention(l, first=False, last=False):
            s = 1
            lambda_init = 0.8 - 0.6 * math.exp(-0.3 * l)
            scale = HD ** -0.5
            w_in = attnwi_in
            w_out = attnwo_in
            NQB = T // 256
            NKT = T // 128
            NQC = NHL * 2
            sc = Scope()
            sc.__enter__()
            if pair:
                scH = Scope()
                scH.__enter__()
                if first:
                    xtok[:] = [sb("xtok%d" % i, [128, D], F32) for i in range(2)]
                for tt in range(NTT):
                    load_xT_tile(tt, first)
                    if first:
                        store_xT_tile(tt, False)
                    norm_mod(l, s)
                    P.dma("pool", cg_h[tt].i[:, :].rearrange("(kc p) t -> p kc t", p=128), hT.t[:, :, :],
                          reads=hT.b, writes=[cg_h[tt].ib])
                    cg_h[tt].go()
                scH.__exit__(None, None, None)
            ropec = sb("ropec", [32, 4], F32)
            rotm = sb("rotm", [32, 32], F32)
            qkg = sb("qkg", [128, 2], F32)
            negmb = sb("negmb", [128, 1], F32)
            nlam = sb("nlam", [128, 1], F32)
            subg = sb("subg", [128, 256], F32)
            P.dma("sp", ropec.t[:, :], ropec_in[:, :], reads=[in_buf], writes=ropec.b)
            P.dma("sp", rotm.t[:, :], rotm_in[:, :], reads=[in_buf], writes=rotm.b)
            P.dma("sp", qkg.t[:, :], qkg_in[:, :], reads=[in_buf], writes=qkg.b)
            P.dma("sp", subg.t[:, :], subg_in[:, :], reads=[in_buf], writes=subg.b)
            P.op("dve", lambda e: e.tensor_scalar(subg.t[:, :], subg.t[:, :], 1.0 - lambda_init, None, ALU.mult),
                 reads=subg.b, writes=subg.b)
            scA = Scope()
            scA.__enter__()
            wst[:] = [sb("wst%d" % i, [128, KC, 128], F32) for i in range(2)]
            wbf[:] = [sb("wbf%d" % i, [128, KC, 128], BF16) for i in range(5)]
            cosT = sb("cosT", [32, T], F32)
            sinT = sb("sinT", [32, T], F32)
            if first:
                xtok[:] = [sb("xtok%d" % i, [128, D], F32) for i in range(2)]
            sc2 = Scope()
            sc2.__enter__()
            grep_ = sb("grep", [128, 2, 128], F32)
            lrep = sb("lrep", [128, 4, 128], F32)
            sm = sb("sm", [128, 8], F32)
            posi = sb("posi", [32, T], I32)
            ang = sb("ang", [32, T], F32)
            ang2 = sb("ang2", [32, T], F32)
            P.dma("sp", grep_.t[:, :, :], qkgrep_in[:, :, :], reads=[in_buf], writes=grep_.b)
            P.dma("sp", lrep.t[:, :, :], lamrep_in[:, :, :], reads=[in_buf], writes=lrep.b)
            P.dma("sp", posi.t[:, :], pos_in[:, :], reads=[in_buf], writes=posi.b)
            for i in range(2):
                P.op("dve", lambda e, i=i: e.reduce_max(sm.t[:, i:i + 1], grep_.t[:, i, :], AX.X, apply_absolute_value=True),
                     reads=grep_.b, writes=sm.b)
            P.op("dve", lambda e: e.tensor_tensor(sm.t[:, 2:3], sm.t[:, 0:1], sm.t[:, 1:2], ALU.mult), reads=sm.b, writes=sm.b)
            P.op("dve", lambda e: e.tensor_scalar(negmb.t[:, :], sm.t[:, 2:3], -(HD * scale), None, ALU.mult),
                 reads=sm.b, writes=negmb.b)
            for i in range(2):
                P.op("dve", lambda e, i=i: e.tensor_tensor(grep_.t[:, i, :], lrep.t[:, 2 * i, :], lrep.t[:, 2 * i + 1, :], ALU.mult),
                     reads=lrep.b + grep_.b, writes=grep_.b)
                P.op("dve", lambda e, i=i: e.reduce_sum(sm.t[:, 3 + i:4 + i], grep_.t[:, i, :], AX.X), reads=grep_.b, writes=sm.b)
                P.op("act", lambda e, i=i: e.activation(sm.t[:, 5 + i:6 + i], sm.t[:, 3 + i:4 + i], AF.Exp), reads=sm.b, writes=sm.b)
            P.op("dve", lambda e: e.scalar_tensor_tensor(nlam.t[:, :], sm.t[:, 6:7], -lambda_init, sm.t[:, 5:6], ALU.add, ALU.subtract),
                 reads=sm.b, writes=nlam.b)
            P.op("dve", lambda e: e.tensor_copy(ang.t[:, :], posi.t[:, :]), reads=posi.b, writes=ang.b)
            P.op("dve", lambda e: e.tensor_scalar(ang.t[:, :], ang.t[:, :], ropec.t[:, 0:1], None, ALU.mult),
                 reads=ang.b + ropec.b, writes=ang.b)
            P.op("dve", lambda e: e.tensor_scalar(ang2.t[:, :], ang.t[:, :], 1.0 / TWO_PI, None, ALU.mult), reads=ang.b, writes=ang2.b)
            P.op("dve", lambda e: e.tensor_copy(posi.t[:, :], ang2.t[:, :]), reads=ang2.b, writes=posi.b)
            P.op("dve", lambda e: e.tensor_copy(ang2.t[:, :], posi.t[:, :]), reads=posi.b, writes=ang2.b)
            P.op("dve", lambda e: e.scalar_tensor_tensor(ang.t[:, :], ang2.t[:, :], -TWO_PI, ang.t[:, :], ALU.mult, ALU.add),
                 reads=ang2.b + ang.b, writes=ang.b)

            def wrap_pi():
                P.op("dve", lambda e: e.tensor_scalar(ang2.t[:, :], ang.t[:, :], math.pi, TWO_PI, ALU.is_gt, ALU.mult),
                     reads=ang.b, writes=ang2.b)
                P.op("dve", lambda e: e.tensor_tensor(ang.t[:, :], ang.t[:, :], ang2.t[:, :], ALU.subtract),
                     reads=ang.b + ang2.b, writes=ang.b)
            wrap_pi()
            P.op("act", lambda e: e.activation(sinT.t[:, :], ang.t[:, :], AF.Sin, scale=ropec.t[:, 1:2]),
                 reads=ang.b + ropec.b, writes=sinT.b)
            P.op("dve", lambda e: e.tensor_scalar(ang.t[:, :], ang.t[:, :], 0.5 * math.pi, None, ALU.add), reads=ang.b + sinT.b, writes=ang.b)
            wrap_pi()
            P.op("act", lambda e: e.activation(cosT.t[:, :], ang.t[:, :], AF.Sin), reads=ang.b, writes=cosT.b)
            sc2.__exit__(None, None, None)

            qn = [sb("qn%d" % i, [128, NT], F32) for i in range(6)]
            rr = [sb("rr%d" % i, [128, NT], F32) for i in range(6)]
            rt1s = [sb("rt1_%d" % i, [32, NT], F32) for i in range(4)]
            rt2s = [sb("rt2_%d" % i, [32, NT], F32) for i in range(2)]
            qkb = [sb("qkb%d" % i, [128, NT], BF16) for i in range(4)]
            vtile = sb("vtile", [128, 4, NHL * 256], BF16)
            actr = {"qn": 0, "qkb": 0}
            qk_bufs = [Buf("qk%d" % i) for i in range(4 * NHL)]
            for tt in range(NTG):
                t0 = tt * NT
                if pair:
                    cgx = cg_h[tt % NTT]
                    src = cgx.o[(tt // NTT) * D:(tt // NTT + 1) * D, :].rearrange("(kc p) t -> p kc t", p=128)
                    P.dma("sp", hT.t[:, :, :], src, reads=[cgx.ob], writes=hT.b)
                else:
                    load_xT_tile(tt, first)
                    if first:
                        store_xT_tile(tt, False)
                    norm_mod(l, s)
                def stA(ch):
                    w_ = load_w_chunk(w_in, ch * 128, "awi")
                    p_ = rot(mm_banks, "pb")
                    for kc in range(KC):
                        P.op("pe", lambda e, w_=w_, kc=kc, p_=p_: e.matmul(
                            p_.t[:, :], w_.t[:, kc, :], hT.t[:, kc, :], start=(kc == 0), stop=(kc == KC - 1)),
                            reads=w_.b + hT.b, writes=p_.b)
                    return {"ch": ch, "p": p_}

                NB_ = 6

                def stB1(st):
                    ch, pq = st["ch"], st["p"]
                    if ch >= 2 * NQC:
                        vb = qkb[actr["qkb"] % len(qkb)]
                        actr["qkb"] += 1
                        st["vb"] = vb
                        P.op("act", lambda e, vb=vb, pq=pq: e.activation(vb.t[:, :], pq.t[:, :], AF.Copy), reads=pq.b, writes=vb.b)
                        return
                    sq = rot(tmpb, "tmpb")
                    P.op("act", lambda e, sq=sq, pq=pq: e.activation(sq.t[:, :], pq.t[:, :], AF.Square), reads=pq.b, writes=sq.b)
                    P.op("pe", lambda e, sq=sq: e.matmul(ssq_bank.t[:, :], onesb.t[:, :], sq.t[:, :], start=True, stop=True),
                         reads=sq.b + onesb.b, writes=ssq_bank.b)
                    r = rr[ch % NB_]
                    st["r"] = r
                    P.op("dve", lambda e, r=r: e.tensor_scalar(r.t[:, :], ssq_bank.t[:, :], 1.0 / HD, EPS, ALU.mult, ALU.add),
                         reads=ssq_bank.b, writes=r.b)

                def stB2(st):
                    ch = st["ch"]
                    if ch >= 2 * NQC:
                        vb = st["vb"]
                        for sub in range(4):
                            P.op("pe", lambda e, vb=vb, sub=sub: e.transpose(
                                pbf.t[:, sub * 128:(sub + 1) * 128], vb.t[:, sub * 128:(sub + 1) * 128], identb.t[:, :]),
                                reads=vb.b + identb.b, writes=pbf.b)
                        vc = ch - 2 * NQC
                        P.op("dve", lambda e, vc=vc: e.tensor_copy(
                            vtile.t[:, :, vc * 128:(vc + 1) * 128], pbf.t[:, 0:512].rearrange("p (s c) -> p s c", c=128)),
                            reads=pbf.b, writes=vtile.b)
                        return
                    r = st["r"]
                    P.op("act", lambda e, r=r: e.activation(r.t[:, :], r.t[:, :], AF.Sqrt), reads=r.b, writes=r.b)

                def stB3(st):
                    ch, pq = st["ch"], st["p"]
                    if ch >= 2 * NQC:
                        return
                    r = st["r"]
                    q = qn[ch % NB_]
                    st["q"] = q
                    P.op("dve", lambda e, r=r: e.reciprocal(r.t[:, :], r.t[:, :]), reads=r.b, writes=r.b)
                    gi = 0 if ch < NQC else 1
                    P.op("dve", lambda e, q=q, r=r, pq=pq, gi=gi: e.scalar_tensor_tensor(
                        q.t[:, :], pq.t[:, :], qkg.t[:, gi:gi + 1], r.t[:, :], ALU.mult, ALU.mult),
                        reads=pq.b + qkg.b + r.b, writes=q.b)

                def stC1(st):
                    ch = st["ch"]
                    if ch >= 2 * NQC:
                        return
                    q = st["q"]
                    rb = tr_banks[ch % 2]
                    st["rb"] = rb
                    P.op("pe", lambda e, q=q, rb=rb: e.matmul(rb.t[0:32, :], rotm.t[:, :], q.t[0:32, :], start=True, stop=True),
                         reads=q.b + rotm.b, writes=rb.b)
                    r1 = rt1s[ch % 4]
                    st["r1"] = r1
                    P.op("pool", lambda e, q=q, t0=t0, r1=r1: e.tensor_tensor(r1.t[:, :], q.t[0:32, :], cosT.t[:, t0:t0 + NT], ALU.mult),
                         reads=q.b + cosT.b, writes=r1.b)

                def stC2(st):
                    ch = st["ch"]
                    if ch >= 2 * NQC:
                        return
                    q, rb, r1 = st["q"], st["rb"], st["r1"]
                    r2 = rt2s[ch % 2]
                    st["r2"] = r2
                    P.op("dve", lambda e, rb=rb, t0=t0, r2=r2: e.tensor_tensor(r2.t[:, :], rb.t[0:32, :], sinT.t[:, t0:t0 + NT], ALU.mult),
                         reads=rb.b + sinT.b, writes=r2.b)

                def stC3(st):
                    ch = st["ch"]
                    if ch >= 2 * NQC:
                        return
                    q, r1, r2 = st["q"], st["r1"], st["r2"]
                    P.op("pool", lambda e, q=q, r1=r1, r2=r2: e.tensor_tensor(q.t[0:32, :], r1.t[:, :], r2.t[:, :], ALU.add),
                         reads=r1.b + r2.b, writes=q.b)

                def stC4(st):
                    ch = st["ch"]
                    if ch >= 2 * NQC:
                        return
                    q = st["q"]
                    qb_ = qkb[actr["qkb"] % len(qkb)]
                    actr["qkb"] += 1
                    P.op("act", lambda e, q=q, qb_=qb_: e.activation(qb_.t[:, :], q.t[:, :], AF.Copy), reads=q.b, writes=qb_.b)
                    P.dma("act", qkT_d[ch, :, t0:t0 + NT], qb_.t[:, :], reads=qb_.b, writes=[qk_bufs[ch]])

                stages = [stA, stB1, stB2, stB3, stC1, stC2, stC3, stC4]
                sts = []
                nch = 3 * NQC
                for i in range(nch + len(stages) - 1):
                    for lag, fn in enumerate(stages):
                        j = i - lag
                        if 0 <= j < nch:
                            if lag == 0:
                                sts.append(fn(j))
                            else:
                                fn(sts[j])
                P.dma("act", V_d[t0:t0 + NT, :].rearrange("(s p) c -> p s c", p=128), vtile.t[:, :, :],
                      reads=vtile.b, writes=[v_buf])
            scA.__exit__(None, None, None)

            scB = Scope()
            scB.__enter__()
            qk_t = [[sb("qk_%d_%d" % (i, j), [128, T], BF16) for j in range(4)] for i in range(2)]
            v_t = [sb("v_%d" % i, [128, NKT, 257], BF16) for i in range(2)]
            maskt = sb("maskt", [128, 2, 256], BF16)
            pT = [sb("pT%d" % i, [128, 256], BF16) for i in range(5)]
            oacc = [[sb("oacc%d_%d" % (c, sub), [128, 257], F32) for sub in range(2)] for c in range(2)]
            sm2 = [sb("sm2_%d" % i, [128, 8], F32) for i in range(2)]
            ot = [sb("ot%d" % i, [128, 256], F32) for i in range(2)]
            osq = sb("osq", [128, 256], F32)
            onb = [sb("onb%d" % i, [128, 256], BF16) for i in range(2)]
            oTt = [sb("oTt%d" % i, [128, 2, 128], BF16) for i in range(2)]
            P.dma("sp", maskt.t[:, :, :], mask_in[:, :, :], reads=[in_buf], writes=maskt.b)
            for i in range(2):
                P.op("dve", lambda e, i=i: e.memset(v_t[i].t[:, :, 256:257], 1.0), writes=v_t[i].b)
            acc_banks = [[pbank[0], pbank[1]], [pbank[2], pbank[3]]]
            sc_banks = [pbank[4], pbank[5], pbank[6]]
            bg_att = None
            if pair and BG_ENABLE and "ffn01" in phases:
                bg_alloc()
                bg_att = bg_chain(bg_jobs_in(attnwo_in, "awo"), bg_jobs_in(ffnwi_in["01"], "fi01"), bg_jobs_out(ffnwo_in["01"], "fo01"))
                nfront = NHL * sum(2 * (2 * qb_ + 2) for qb_ in range(NQB))
                BG_EVERY = max(1, nfront // (16 + 88 + 64 + 8))
            bctr = {"sc": 0, "pT": 0, "o": 0}
            from collections import deque
            LA = 2
            pendq = deque()

            def front(h, qb, c, kt, qkh):
                q0 = qb * 256
                sb_ = sc_banks[bctr["sc"] % 3]
                bctr["sc"] += 1
                P.op("pe", lambda e, sb_=sb_, kt=kt, c=c, q0=q0, qkh=qkh: e.matmul(
                    sb_.t[:, 0:256], qkh[2 + c].t[:, kt * 128:(kt + 1) * 128], qkh[c].t[:, q0:q0 + 256],
                    start=True, stop=True), reads=qkh[2 + c].b + qkh[c].b, writes=sb_.b)
                p_ = pT[bctr["pT"] % len(pT)]
                bctr["pT"] += 1
                P.op("act", lambda e, p_=p_, sb_=sb_: e.activation(
                    p_.t[:, :], sb_.t[:, 0:256], AF.Exp, bias=negmb.t[:, 0:1], scale=scale),
                    reads=sb_.b + negmb.b, writes=p_.b)
                if kt >= 2 * qb:
                    mi = kt - 2 * qb
                    P.op("dve", lambda e, p_=p_, mi=mi: e.tensor_tensor(p_.t[:, :], p_.t[:, :], maskt.t[:, mi, :], ALU.mult),
                         reads=p_.b + maskt.b, writes=p_.b)
                return p_

            def back(h, qb, c, kt, vh, p_):
                q0 = qb * 256
                nkt = 2 * qb + 2
                accs = acc_banks[c]
                for sub in range(2):
                    last_kt = 2 * qb + sub
                    if kt > last_kt:
                        continue
                    P.op("pe", lambda e, p_=p_, sub=sub, kt=kt, vh=vh, accs=accs, last_kt=last_kt: e.matmul(
                        accs[sub].t[:, 0:257], p_.t[:, sub * 128:(sub + 1) * 128], vh.t[:, kt, :],
                        start=(kt == 0), stop=(kt == last_kt)), reads=p_.b + vh.b, writes=accs[sub].b)
                if kt != nkt - 1:
                    return
                for sub in range(2):
                    if sub == 0:
                        P.op("act", lambda e, c=c, sub=sub, accs=accs: e.activation(
                            oacc[c][sub].t[:, :], accs[sub].t[:, 0:257], AF.Copy), reads=accs[sub].b, writes=oacc[c][sub].b)
                    else:
                        P.op("dve", lambda e, c=c, sub=sub, accs=accs: e.tensor_copy(
                            oacc[c][sub].t[:, :], accs[sub].t[:, 0:257]), reads=accs[sub].b, writes=oacc[c][sub].b)
                if c != 1:
                    return
                for sub in range(2):
                    k_ = bctr["o"] % 2
                    bctr["o"] += 1
                    sm_ = sm2[k_]
                    o_ = ot[k_]
                    on_ = onb[k_]
                    oT_ = oTt[k_]
                    o0 = oacc[0][sub]
                    o1 = oacc[1][sub]
                    P.op("dve", lambda e, sm_=sm_, o0=o0: e.reciprocal(sm_.t[:, 0:1], o0.t[:, 256:257]), reads=o0.b, writes=sm_.b)
                    P.op("dve", lambda e, sm_=sm_, o1=o1: e.reciprocal(sm_.t[:, 1:2], o1.t[:, 256:257]), reads=o1.b + sm_.b, writes=sm_.b)
                    P.op("dve", lambda e, sm_=sm_: e.tensor_tensor(sm_.t[:, 2:3], sm_.t[:, 1:2], nlam.t[:, 0:1], ALU.mult),
                         reads=sm_.b + nlam.b, writes=sm_.b)
                    P.op("dve", lambda e, sm_=sm_, o_=o_, o0=o0: e.tensor_scalar(
                        o_.t[:, :], o0.t[:, 0:256], sm_.t[:, 0:1], None, ALU.mult), reads=o0.b + sm_.b, writes=o_.b)
                    P.op("dve", lambda e, sm_=sm_, o_=o_, o1=o1: e.scalar_tensor_tensor(
                        o_.t[:, :], o1.t[:, 0:256], sm_.t[:, 2:3], o_.t[:, :], ALU.mult, ALU.add),
                        reads=o1.b + sm_.b + o_.b, writes=o_.b)
                    P.op("dve", lambda e, o_=o_: e.tensor_tensor(osq.t[:, :], o_.t[:, :], o_.t[:, :], ALU.mult), reads=o_.b, writes=osq.b)
                    P.op("dve", lambda e, sm_=sm_: e.reduce_sum(sm_.t[:, 3:4], osq.t[:, :], AX.X), reads=osq.b + sm_.b, writes=sm_.b)
                    P.op("dve", lambda e, sm_=sm_: e.tensor_scalar(sm_.t[:, 4:5], sm_.t[:, 3:4], 1.0 / 256, EPS, ALU.mult, ALU.add),
                         reads=sm_.b, writes=sm_.b)
                    P.op("act", lambda e, sm_=sm_: e.activation(sm_.t[:, 5:6], sm_.t[:, 4:5], AF.Sqrt), reads=sm_.b, writes=sm_.b)
                    P.op("dve", lambda e, sm_=sm_: e.reciprocal(sm_.t[:, 6:7], sm_.t[:, 5:6]), reads=sm_.b, writes=sm_.b)
                    P.op("dve", lambda e, sm_=sm_, o_=o_, on_=on_: e.scalar_tensor_tensor(
                        on_.t[:, :], o_.t[:, :], sm_.t[:, 6:7], subg.t[:, :], ALU.mult, ALU.mult),
                        reads=o_.b + sm_.b + subg.b, writes=on_.b)
                    for fc in range(2):
                        P.op("pe", lambda e, on_=on_, fc=fc: e.transpose(
                            pbf.t[:, fc * 128:(fc + 1) * 128], on_.t[:, fc * 128:(fc + 1) * 128], identb.t[:, :]),
                            reads=on_.b + identb.b, writes=pbf.b)
                    P.op("act", lambda e, oT_=oT_: e.activation(
                        oT_.t[:, :, :], pbf.t[:, 0:256].rearrange("p (f c) -> p f c", c=128), AF.Copy), reads=pbf.b, writes=oT_.b)
                    qs = q0 + sub * 128
                    if pair:
                        cgx = cg_o[qs // OTC]
                        P.dma("pool", cgx.i[h * 256:(h + 1) * 256, qs % OTC:qs % OTC + 128].rearrange("(f p) q -> p f q", p=128),
                              oT_.t[:, :, :], reads=oT_.b, writes=[cgx.ib])
                    else:
                        P.dma("pool", oT_d[h * 256:(h + 1) * 256, qs:qs + 128].rearrange("(f p) q -> p f q", p=128),
                              oT_.t[:, :, :], reads=oT_.b, writes=[oT_buf])

            for h in range(NHL):
                qkh = qk_t[h % 2]
                vh = v_t[h % 2]
                for c in range(2):
                    P.dma("sp", qkh[c].t[:, :], qkT_d[h * 2 + c, :, :], reads=[qk_bufs[h * 2 + c]], writes=qkh[c].b)
                    P.dma("sp", qkh[2 + c].t[:, :], qkT_d[NQC + h * 2 + c, :, :], reads=[qk_bufs[NQC + h * 2 + c]], writes=qkh[2 + c].b)
                P.dma("act", vh.t[:, :, 0:256], V_d[:, h * 256:(h + 1) * 256].rearrange("(kt p) c -> p kt c", p=128),
                      reads=[v_buf], writes=vh.b)
                for qb in range(NQB):
                    for c in range(2):
                        for kt in range(2 * qb + 2):
                            bctr["bg"] = bctr.get("bg", 0) + 1
                            if bg_att is not None and bctr["bg"] % BG_EVERY == 0:
                                job = next(bg_att, None)
                                if job is not None:
                                    bg_step(job, ("dve", "pool", "dve")[bg["i"] % 3], "sp", "pool")
                            p_ = front(h, qb, c, kt, qkh)
                            pendq.append((h, qb, c, kt, vh, p_))
                            if len(pendq) > LA:
                                back(*pendq.popleft())
            while pendq:
                back(*pendq.popleft())
            if bg_att is not None:
                for job in bg_att:
                    bg_step(job, ("dve", "pool", "act")[bg["i"] % 3], "sp", "pool")
            scB.__exit__(None, None, None)

            if pair:
                for cgx in cg_o:
                    cgx.go()
            scC = Scope()
            scC.__enter__()
            if pair:
                ocand = [sb("ocand%d" % i, [128, KC, NT], BF16) for i in range(2)]
            wst[:] = [sb("wst%d" % i, [128, KC, 128], F32) for i in range(2)]
            wbf[:] = [sb("wbf%d" % i, [128, KC, 128], BF16) for i in range(5)]
            if last:
                xtok[:] = [sb("xtok%d" % i, [128, D], F32) for i in range(2)]
            for tt in range(NTT):
                t0 = tt * NT
                load_xT_tile(tt, False)
                if pair:
                    for r_ in range(2):
                        g0 = r_ * TL + t0
                        cgx = cg_o[g0 // OTC]
                        src = cgx.o[:, g0 % OTC:g0 % OTC + NT].rearrange("(kc p) t -> p kc t", p=128)
                        P.dma(("sp", "act")[r_], ocand[r_].t[:, :, :], src, reads=[cgx.ob], writes=ocand[r_].b)
                    select2(hT.t[:, :, :], ocand[0].t[:, :, :], ocand[1].t[:, :, :], ocand[0].t[:, :, :],
                            ocand[0].b + ocand[1].b, hT.b + ocand[0].b)
                else:
                    src = oT_d[:, t0:t0 + NT].rearrange("(kc p) t -> p kc t", p=128)
                    P.dma("sp", hT.t[:, :, :], src, reads=[oT_buf], writes=hT.b)
                for dc in range(KC):
                    wo = load_w_chunk(w_out, dc * 128, "awo")
                    po = rot(mm_banks, "pb")
                    for kc in range(KC):
                        P.op("pe", lambda e, wo=wo, kc=kc, po=po: e.matmul(
                            po.t[:, :], wo.t[:, kc, :], hT.t[:, kc, :], start=(kc == 0), stop=(kc == KC - 1)),
                            reads=wo.b + hT.b, writes=po.b)
                    P.op("dve", lambda e, dc=dc, po=po: e.scalar_tensor_tensor(
                        xT.t[:, dc, :], po.t[:, :], modG[l].t[:, s, dc:dc + 1], xT.t[:, dc, :], ALU.mult, ALU.add),
                        reads=po.b + modG[l].b + xT.b, writes=xT.b)
                store_xT_tile(tt, last)
            scC.__exit__(None, None, None)
            sc.__exit__(None, None, None)


        def ssm(l, first=False, last=False):
            s = 1
            sc = Scope()
            sc.__enter__()
            rho = sb("rho", [128, NGP], F32)
            fq = sb("fq", [128, NGP], F32)
            c512 = sb("c512", [128, NGP], F32)
            s512 = sb("s512", [128, NGP], F32)
            pm = sb("pm", [128, 2], F32)
            dT = sb("dT", [128, 16], F32)
            iota = sb("iota", [128, NT], F32)
            P.dma("sp", pm.t[:, :], pm_in[:, :], reads=[in_buf], writes=pm.b)
            P.dma("sp", dT.t[:, :], ssmd_in[:, :], reads=[in_buf], writes=dT.b)
            P.dma("sp", iota.t[:, :], iota_in[:, :], reads=[in_buf], writes=iota.b)
            scL = Scope()
            scL.__enter__()
            LB = [sb("LB%d" % i, [128, NGP, 128], BF16) for i in range(2)]
            LC = [sb("LC%d" % i, [128, NGP, 128], BF16) for i in range(2)]
            for i in range(2):
                P.op("pool", lambda e, i=i: e.memset(LB[i].t[:, :, :], 0.0), writes=LB[i].b)
                P.op("pool", lambda e, i=i: e.memset(LC[i].t[:, :, :], 0.0), writes=LC[i].b)

            def frac_wrap(dst, src, W, tmpi, tmpf_):
                P.op("dve", lambda e: e.tensor_copy(tmpi, src), reads=[gen_b], writes=[gen_b])
                P.op("dve", lambda e: e.tensor_copy(tmpf_, tmpi), reads=[gen_b], writes=[gen_b])
                P.op("dve", lambda e: e.tensor_tensor(dst, src, tmpf_, ALU.subtract), reads=[gen_b], writes=[gen_b])
                wrap_half(dst, tmpf_)

            def wrap_half(dst, tmpf_):
                P.op("dve", lambda e: e.tensor_scalar(tmpf_, dst, 0.5, None, ALU.is_gt), reads=[gen_b], writes=[gen_b])
                P.op("dve", lambda e: e.tensor_tensor(dst, dst, tmpf_, ALU.subtract), reads=[gen_b], writes=[gen_b])
                P.op("dve", lambda e: e.tensor_scalar(tmpf_, dst, -0.5, None, ALU.is_lt), reads=[gen_b], writes=[gen_b])
                P.op("dve", lambda e: e.tensor_tensor(dst, dst, tmpf_, ALU.add), reads=[gen_b], writes=[gen_b])

            gen_b = Buf("ssm_setup")

            def G(eng, fn):
                P.op(eng, fn, reads=[gen_b], writes=[gen_b])

            scS = Scope()
            scS.__enter__()
            ap_ = sb("a_pair", [128, 3, NGP], F32)
            P.dma("sp", ap_.t[:, :, :], apair_in[:, :, :], reads=[in_buf], writes=[gen_b])
            t64 = [sb("t64_%d" % i, [128, NGP], F32) for i in range(4)]
            t64i = sb("t64i", [128, NGP], I32)
            dtp, thp, fr64, tm64 = [t.t[:, :] for t in t64]
            G("act", lambda e: e.activation(dtp, ap_.t[:, 2, :], AF.Exp))
            G("dve", lambda e: e.tensor_tensor(thp, dtp, ap_.t[:, 0, :], ALU.mult))
            G("act", lambda e: e.activation(rho.t[:, :], thp, AF.Exp))
            G("dve", lambda e: e.tensor_tensor(thp, dtp, ap_.t[:, 1, :], ALU.mult))
            G("dve", lambda e: e.tensor_scalar(thp, thp, 1.0 / TWO_PI, None, ALU.mult))
            frac_wrap(fq.t[:, :], thp, NGP, t64i.t[:, :], tm64)
            G("dve", lambda e: e.tensor_scalar(thp, fq.t[:, :], float(NT), None, ALU.mult))
            frac_wrap(fr64, thp, NGP, t64i.t[:, :], tm64)
            G("act", lambda e: e.activation(s512.t[:, :], fr64, AF.Sin, scale=TWO_PI))
            G("dve", lambda e: e.tensor_scalar(fr64, fr64, 0.25, None, ALU.add))
            wrap_half(fr64, tm64)
            G("act", lambda e: e.activation(c512.t[:, :], fr64, AF.Sin, scale=TWO_PI))
            W = NKL * 64
            af = sb("a_feat", [128, 3, W], F32)
            bf_ = sb("b_feat", [128, 2, W], F32)
            P.dma("sp", af.t[:, :, :], afeat_in[:, :, :], reads=[in_buf], writes=[gen_b])
            P.dma("act", bf_.t[:, :, :], bfeat_in[:, :, :], reads=[in_buf], writes=[gen_b])
            tw = [sb("tw%d" % i, [128, W], F32) for i in range(8)]
            twi = sb("twi", [128, W], I32)
            dtf, mag, th, cs, sn, t5, t6, t7 = [t.t[:, :] for t in tw]
            ar = af.t[:, 0, :]
            ai = af.t[:, 1, :]
            G("act", lambda e: e.activation(dtf, af.t[:, 2, :], AF.Exp))
            G("dve", lambda e: e.tensor_tensor(th, dtf, ar, ALU.mult))
            G("act", lambda e: e.activation(mag, th, AF.Exp))
            G("dve", lambda e: e.tensor_tensor(th, dtf, ai, ALU.mult))
            G("dve", lambda e: e.tensor_scalar(th, th, 1.0 / TWO_PI, None, ALU.mult))
            frac_wrap(t5, th, W, twi.t[:, :], t6)
            G("act", lambda e: e.activation(sn, t5, AF.Sin, scale=TWO_PI))
            G("dve", lambda e: e.tensor_scalar(t5, t5, 0.25, None, ALU.add))
            wrap_half(t5, t6)
            G("act", lambda e: e.activation(cs, t5, AF.Sin, scale=TWO_PI))
            G("dve", lambda e: e.tensor_tensor(cs, cs, mag, ALU.mult))
            G("dve", lambda e: e.tensor_scalar(cs, cs, -1.0, None, ALU.add))
            G("dve", lambda e: e.tensor_tensor(sn, sn, mag, ALU.mult))
            G("dve", lambda e: e.tensor_tensor(t5, ar, ar, ALU.mult))
            G("dve", lambda e: e.tensor_tensor(t6, ai, ai, ALU.mult))
            G("dve", lambda e: e.tensor_tensor(t5, t5, t6, ALU.add))
            G("dve", lambda e: e.reciprocal(t5, t5))
            G("dve", lambda e: e.tensor_tensor(t6, cs, ar, ALU.mult))
            G("dve", lambda e: e.tensor_tensor(t7, sn, ai, ALU.mult))
            G("dve", lambda e: e.tensor_tensor(t6, t6, t7, ALU.add))
            G("dve", lambda e: e.tensor_tensor(t6, t6, t5, ALU.mult))
            G("dve", lambda e: e.tensor_tensor(t7, sn, ar, ALU.mult))
            G("dve", lambda e: e.tensor_tensor(mag, cs, ai, ALU.mult))
            G("dve", lambda e: e.tensor_tensor(t7, t7, mag, ALU.subtract))
            G("dve", lambda e: e.tensor_tensor(t7, t7, t5, ALU.mult))
            br = bf_.t[:, 0, :]
            bi = bf_.t[:, 1, :]
            G("dve", lambda e: e.tensor_tensor(cs, t6, br, ALU.mult))
            G("dve", lambda e: e.tensor_tensor(mag, t7, bi, ALU.mult))
            G("dve", lambda e: e.tensor_tensor(cs, cs, mag, ALU.subtract))
            G("dve", lambda e: e.tensor_tensor(sn, t6, bi, ALU.mult))
            G("dve", lambda e: e.tensor_tensor(mag, t7, br, ALU.mult))
            G("dve", lambda e: e.tensor_tensor(sn, sn, mag, ALU.add))
            bb = [tw[3].t, tw[4].t]
            for gp in range(NGP):
                kc, j = gp // 4, gp % 4
                r0 = 32 * j
                for ri in range(2):
                    for gi in range(2):
                        eng = "dve" if gi == 0 else "pool"
                        P.op(eng, lambda e, ri=ri, gi=gi, gp=gp, kc=kc, r0=r0: e.tensor_scalar(
                            LB[ri].t[r0:r0 + 32, gp, gi * 64:(gi + 1) * 64], bb[ri][r0:r0 + 32, kc * 64:(kc + 1) * 64],
                            pm.t[r0:r0 + 32, gi:gi + 1], None, ALU.mult),
                            reads=[gen_b] + pm.b, writes=LB[ri].b)
            cp = sb("c_pair", [128, 2, NGP, 16], F32)
            P.dma("sp", cp.t[:, :, :, :], cpair_in[:, :, :, :], reads=[in_buf], writes=[gen_b])
            for gp in range(NGP):
                j = gp % 4
                for ri in range(2):
                    for gi in range(2):
                        c0 = 32 * j + gi * 16
                        eng = "dve" if gi == 0 else "pool"
                        sgn = 1.0 if ri == 0 else -1.0
                        P.op(eng, lambda e, ri=ri, gi=gi, gp=gp, c0=c0, sgn=sgn: e.tensor_scalar(
                            LC[ri].t[gi * 64:(gi + 1) * 64, gp, c0:c0 + 16], cp.t[gi * 64:(gi + 1) * 64, ri, gp, :],
                            sgn, None, ALU.mult), reads=[gen_b], writes=LC[ri].b)
            scS.__exit__(None, None, None)

            scA = Scope()
            scA.__enter__()
            if first:
                xtok[:] = [sb("xtok%d" % i, [128, D], F32) for i in range(2)]
            for tt in range(NTT):
                t0 = tt * NT
                load_xT_tile(tt, first)
                if first:
                    store_xT_tile(tt, False)
                norm_mod(l, s)
                if pair:
                    P.dma("pool", cg_u[tt].i[:, :].rearrange("(kc p) t -> p kc t", p=128), hT.t[:, :, :],
                          reads=hT.b, writes=[cg_u[tt].ib])
                    cg_u[tt].go()
                else:
                    P.dma("pool", uT_d[:, t0:t0 + NT].rearrange("(kc p) t -> p kc t", p=128), hT.t[:, :, :],
                          reads=hT.b, writes=[u_buf])
            if pair:
                ucand = [[sb("ucand%d_%d" % (i, j), [128, NKL, NT], BF16) for j in range(2)] for i in range(2)]
                um_buf = Buf("um")
                it_ = 0
                for tt in range(NTT):
                    for r_ in range(2):
                        cs_ = ucand[it_ % 2]
                        it_ += 1
                        for j in range(2):
                            row0 = r_ * D + j * NKL * 128
                            P.dma(("sp", "act")[j], cs_[j].t[:, :, :],
                                  cg_u[tt].o[row0:row0 + NKL * 128, :].rearrange("(kc p) t -> p kc t", p=128),
                                  reads=[cg_u[tt].ob], writes=cs_[j].b)
                        select2(cs_[0].t[:, :, :], cs_[0].t[:, :, :], cs_[1].t[:, :, :], cs_[0].t[:, :, :], cs_[0].b + cs_[1].b, cs_[0].b)
                        g0 = r_ * TL + tt * NT
                        P.dma("pool", uT_d[:, g0:g0 + NT].rearrange("(kc p) t -> p kc t", p=128), cs_[0].t[:, :, :],
                              reads=cs_[0].b, writes=[um_buf])
                u_rd = um_buf
            else:
                u_rd = u_buf
            scA.__exit__(None, None, None)

            scB = Scope()
            scB.__enter__()
            cosTs = [sb("s_cos%d" % i, [128, NT], F32) for i in range(2)]
            sinTs = [sb("s_sin%d" % i, [128, NT], F32) for i in range(2)]
            rhobs = [sb("s_rhob%d" % i, [128, NT], F32) for i in range(2)]
            tg = [sb("s_tg%d" % i, [128, NT], F32) for i in range(2)]
            tgi = sb("s_tgi", [128, NT], I32)
            uch = [sb("s_u%d" % i, [128, NT], BF16) for i in range(3)]
            bsb = [[sb("s_b%d_%d" % (k, i), [128, NT], F32) for i in range(2)] for k in range(2)]
            fq_ = [[sb("s_f%d_%d" % (k, i), [128, NT], F32) for i in range(4)] for k in range(2)]
            bq_ = [sb("s_q%d" % i, [128, NT], F32) for i in range(4)]
            btr = [sb("s_btr%d" % i, [128, NT], F32) for i in range(2)]
            bti = [sb("s_bti%d" % i, [128, NT], F32) for i in range(2)]
            st_r = [sb("s_str%d" % i, [128, NT], F32) for i in range(2)]
            st_i = [sb("s_sti%d" % i, [128, NT], F32) for i in range(2)]
            sbf_r = [sb("s_sbr%d" % i, [128, NT], BF16) for i in range(2)]
            sbf_i = [sb("s_sbi%d" % i, [128, NT], BF16) for i in range(2)]
            y32 = [sb("s_y%d" % i, [128, NT], F32) for i in range(2)]
            init = [sb("s_init%d" % i, [128, 4], F32) for i in range(2)]
            b_banks = [[pbank[0], pbank[1]], [pbank[2], pbank[3]]]
            y_banks = [pbank[4], pbank[5]]

            def tables(gp):
                cosT, sinT, rhob = cosTs[gp % 2], sinTs[gp % 2], rhobs[gp % 2]
                fr = tg[0].t[:, :]
                tm = tg[1].t[:, :]
                tb = tg[0].b + tg[1].b + tgi.b
                P.op("dve", lambda e, gp=gp: e.tensor_scalar(tm, iota.t[:, :], fq.t[:, gp:gp + 1], None, ALU.mult),
                     reads=iota.b + fq.b + tb, writes=tb)

                def T_(eng, fn, extra_w=()):
                    P.op(eng, fn, reads=tb, writes=tb + list(extra_w))
                T_("dve", lambda e: e.tensor_copy(tgi.t[:, :], tm))
                T_("dve", lambda e: e.tensor_copy(fr, tgi.t[:, :]))
                T_("dve", lambda e: e.tensor_tensor(fr, tm, fr, ALU.subtract))

                def wrapT():
                    T_("dve", lambda e: e.tensor_scalar(tm, fr, 0.5, None, ALU.is_gt))
                    T_("dve", lambda e: e.tensor_tensor(fr, fr, tm, ALU.subtract))
                    T_("dve", lambda e: e.tensor_scalar(tm, fr, -0.5, None, ALU.is_lt))
                    T_("dve", lambda e: e.tensor_tensor(fr, fr, tm, ALU.add))
                wrapT()
                T_("act", lambda e: e.activation(sinT.t[:, :], fr, AF.Sin, scale=TWO_PI), extra_w=sinT.b)
                T_("dve", lambda e: e.tensor_scalar(fr, fr, 0.25, None, ALU.add))
                wrapT()
                T_("act", lambda e: e.activation(cosT.t[:, :], fr, AF.Sin, scale=TWO_PI), extra_w=cosT.b)
                P.op("pool", lambda e, gp=gp: e.tensor_scalar(rhob.t[:, :], iota.t[:, :], 0.0, rho.t[:, gp:gp + 1], ALU.mult, ALU.add),
                     reads=iota.b + rho.b, writes=rhob.b)
                ini = init[gp % 2]
                P.op("pool", lambda e, ini=ini: e.memset(ini.t[:, :], 0.0), writes=ini.b)

            def F1(it, gp, tt):
                k = it % 2
                kc = gp // 4
                t0 = tt * NT
                cosT, sinT = cosTs[gp % 2], sinTs[gp % 2]
                u_ = uch[it % 3]
                P.dma("sp", u_.t[:, :], uT_d[kc * 128:(kc + 1) * 128, t0:t0 + NT], reads=[u_rd], writes=u_.b)
                pre, pim = b_banks[k]
                P.op("pe", lambda e: e.matmul(pre.t[:, :], LB[0].t[:, gp, :], u_.t[:, :], start=True, stop=True),
                     reads=LB[0].b + u_.b, writes=pre.b)
                P.op("pe", lambda e: e.matmul(pim.t[:, :], LB[1].t[:, gp, :], u_.t[:, :], start=True, stop=True),
                     reads=LB[1].b + u_.b, writes=pim.b)
                bre, bim = bsb[k]
                f1, f2, f3, f4 = fq_[k]
                P.op("act", lambda e: e.activation(bre.t[:, :], pre.t[:, :], AF.Copy), reads=pre.b, writes=bre.b)
                P.op("act", lambda e: e.activation(bim.t[:, :], pim.t[:, :], AF.Copy), reads=pim.b, writes=bim.b)
                P.op("dve", lambda e: e.tensor_tensor(f1.t[:, :], bre.t[:, :], cosT.t[:, :], ALU.mult), reads=bre.b + cosT.b, writes=f1.b)
                P.op("dve", lambda e: e.tensor_tensor(f2.t[:, :], bim.t[:, :], sinT.t[:, :], ALU.mult), reads=bim.b + sinT.b, writes=f2.b)
                P.op("pool", lambda e: e.tensor_tensor(f3.t[:, :], bim.t[:, :], cosT.t[:, :], ALU.mult), reads=bim.b + cosT.b, writes=f3.b)
                P.op("pool", lambda e: e.tensor_tensor(f4.t[:, :], bre.t[:, :], sinT.t[:, :], ALU.mult), reads=bre.b + sinT.b, writes=f4.b)

            def F2(it, gp, tt):
                k = it % 2
                f1, f2, f3, f4 = fq_[k]
                br_, bi_ = btr[k], bti[k]
                P.op("dve", lambda e: e.tensor_tensor(br_.t[:, :], f1.t[:, :], f2.t[:, :], ALU.add), reads=f1.b + f2.b, writes=br_.b)
                P.op("pool", lambda e: e.tensor_tensor(bi_.t[:, :], f3.t[:, :], f4.t[:, :], ALU.subtract), reads=f3.b + f4.b, writes=bi_.b)

            def F3(it, gp, tt):
                k = it % 2
                rhob, ini = rhobs[gp % 2], init[gp % 2]
                br_, bi_ = btr[k], bti[k]
                sr, si = st_r[k], st_i[k]
                P.op("dve", lambda e: e.tensor_tensor_scan(sr.t[:, :], rhob.t[:, :], br_.t[:, :], ini.t[:, 0:1], ALU.mult, ALU.add),
                     reads=rhob.b + br_.b + ini.b, writes=sr.b)
                P.op("dve", lambda e: e.tensor_tensor_scan(si.t[:, :], rhob.t[:, :], bi_.t[:, :], ini.t[:, 1:2], ALU.mult, ALU.add),
                     reads=rhob.b + bi_.b + ini.b, writes=si.b)
                P.op("dve", lambda e: e.tensor_tensor(ini.t[:, 3:4], sr.t[:, NT - 1:NT], s512.t[:, gp:gp + 1], ALU.mult),
                     reads=sr.b + s512.b + ini.b, writes=ini.b)
                P.op("dve", lambda e: e.tensor_tensor(ini.t[:, 2:3], si.t[:, NT - 1:NT], s512.t[:, gp:gp + 1], ALU.mult),
                     reads=si.b + s512.b + ini.b, writes=ini.b)
                P.op("dve", lambda e: e.scalar_tensor_tensor(ini.t[:, 1:2], si.t[:, NT - 1:NT], c512.t[:, gp:gp + 1], ini.t[:, 3:4],
                                                             ALU.mult, ALU.add), reads=si.b + c512.b + ini.b, writes=ini.b)
                P.op("dve", lambda e: e.scalar_tensor_tensor(ini.t[:, 0:1], sr.t[:, NT - 1:NT], c512.t[:, gp:gp + 1], ini.t[:, 2:3],
                                                             ALU.mult, ALU.subtract), reads=sr.b + c512.b + ini.b, writes=ini.b)

            def K1(it, gp, tt):
                k = it % 2
                cosT, sinT = cosTs[gp % 2], sinTs[gp % 2]
                sr, si = st_r[k], st_i[k]
                b1, b2, b3, b4 = bq_
                P.op("dve", lambda e: e.tensor_tensor(b1.t[:, :], sr.t[:, :], cosT.t[:, :], ALU.mult), reads=sr.b + cosT.b, writes=b1.b)
                P.op("dve", lambda e: e.tensor_tensor(b2.t[:, :], si.t[:, :], sinT.t[:, :], ALU.mult), reads=si.b + sinT.b, writes=b2.b)
                P.op("pool", lambda e: e.tensor_tensor(b3.t[:, :], si.t[:, :], cosT.t[:, :], ALU.mult), reads=si.b + cosT.b, writes=b3.b)
                P.op("pool", lambda e: e.tensor_tensor(b4.t[:, :], sr.t[:, :], sinT.t[:, :], ALU.mult), reads=sr.b + sinT.b, writes=b4.b)

            def K2(it, gp, tt):
                k = it % 2
                kc, j = gp // 4, gp % 4
                r0 = 32 * j
                t0 = tt * NT
                zr, zi = sbf_r[k], sbf_i[k]
                b1, b2, b3, b4 = bq_
                P.op("dve", lambda e: e.tensor_tensor(zr.t[:, :], b1.t[:, :], b2.t[:, :], ALU.subtract), reads=b1.b + b2.b, writes=zr.b)
                P.op("pool", lambda e: e.tensor_tensor(zi.t[:, :], b3.t[:, :], b4.t[:, :], ALU.add), reads=b3.b + b4.b, writes=zi.b)
                yb = y_banks[k]
                P.op("pe", lambda e: e.matmul(yb.t[:, :], LC[0].t[:, gp, :], zr.t[:, :], start=True, stop=False),
                     reads=LC[0].b + zr.b, writes=yb.b)
                P.op("pe", lambda e: e.matmul(yb.t[:, :], LC[1].t[:, gp, :], zi.t[:, :], start=False, stop=True),
                     reads=LC[1].b + zi.b, writes=yb.b)
                y_ = y32[k]
                P.op("act", lambda e: e.activation(y_.t[r0:r0 + 32, :], yb.t[r0:r0 + 32, :], AF.Copy), reads=yb.b, writes=y_.b)
                if pair:
                    P.dma("act", cg_y[tt].i[kc * 128 + r0:kc * 128 + r0 + 32, :], y_.t[r0:r0 + 32, :], reads=y_.b, writes=[cg_y[tt].ib])
                else:
                    P.dma("act", y_d[kc * 128 + r0:kc * 128 + r0 + 32, t0:t0 + NT], y_.t[r0:r0 + 32, :], reads=y_.b, writes=[y_buf])

            units = []
            for gp in range(NGP):
                for tt in range(NTG):
                    units.append((len(units), gp, tt))
            bg_ssm = None
            if pair and BG_ENABLE and "ffn11" in phases:
                bg_alloc()
                bg_ssm = bg_chain(bg_jobs_in(wglu_in, "glu"), bg_jobs_in(ffnwi_in["11"], "fi11"), bg_jobs_out(ffnwo_in["11"], "fo11"))
            for i, un in enumerate(units):
                if bg_ssm is not None:
                    job = next(bg_ssm, None)
                    if job is not None:
                        bg_step(job, "act", "sp", "pool")
                if un[2] == 0:
                    tables(un[1])
                F1(*un)
                if i >= 1:
                    K1(*units[i - 1])
                F2(*un)
                if i >= 1:
                    K2(*units[i - 1])
                F3(*un)
            K1(*units[-1])
            K2(*units[-1])
            if bg_ssm is not None:
                for job in bg_ssm:
                    bg_step(job, "act", "sp", "pool")
            scB.__exit__(None, None, None)
            scL.__exit__(None, None, None)
            if dbg == "y2":
                dbg2 = nc.dram_tensor("dbg_y", [D, T], F32, kind="ExternalOutput").ap()
                P.dma("sp", dbg2[:, :], y_d[:, :], reads=[y_buf], writes=[Buf("dbg2")])
                P.barrier()

            if pair:
                for cgx in cg_y:
                    cgx.go()
            scC = Scope()
            scC.__enter__()
            if pair:
                ycand = sb("ycand", [128, KC, NT], F32)
            wst[:] = [sb("wst%d" % i, [128, KC, 128], F32) for i in range(2)]
            wbf[:] = [sb("wbf%d" % i, [128, KC, 128], BF16) for i in range(5)]
            h32 = sb("h32", [128, KC, NT], F32)
            yt = sb("yt", [128, KC, NT], F32)
            g1 = [sb("g1_%d" % i, [128, NT], F32) for i in range(2)]
            if last:
                xtok[:] = [sb("xtok%d" % i, [128, D], F32) for i in range(2)]
            GC = 2.0 * math.sqrt(2.0 / math.pi)
            for tt in range(NTT):
                t0 = tt * NT
                load_xT_tile(tt, False)
                norm_mod(l, s, h32=h32, want_bf=False)
                if pair:
                    src0 = cg_y[tt].o[:, :].rearrange("(kc p) t -> p kc t", p=128)
                    src1 = cg_y[NTT + tt].o[:, :].rearrange("(kc p) t -> p kc t", p=128)
                    P.dma("sp", yt.t[:, :, :], src0, reads=[cg_y[tt].ob], writes=yt.b)
                    P.dma("act", ycand.t[:, :, :], src1, reads=[cg_y[NTT + tt].ob], writes=ycand.b)
                    select2(yt.t[:, :, :], yt.t[:, :, :], ycand.t[:, :, :], yt.t[:, :, :], yt.b + ycand.b, yt.b)
                else:
                    src = y_d[:, t0:t0 + NT].rearrange("(kc p) t -> p kc t", p=128)
                    P.dma("sp", yt.t[:, 0:8, :], src[:, 0:8, :], reads=[y_buf], writes=yt.b)
                    P.dma("act", yt.t[:, 8:16, :], src[:, 8:16, :], reads=[y_buf], writes=yt.b)
                for kc in range(KC):
                    ga = g1[kc % 2]
                    P.op("dve", lambda e, kc=kc: e.scalar_tensor_tensor(
                        yt.t[:, kc, :], h32.t[:, kc, :], dT.t[:, kc:kc + 1], yt.t[:, kc, :], ALU.mult, ALU.add),
                        reads=h32.b + dT.b + yt.b, writes=yt.b)
                    P.op("act", lambda e, kc=kc, ga=ga: e.activation(ga.t[:, :], yt.t[:, kc, :], AF.Square), reads=yt.b, writes=ga.b)
                    P.op("dve", lambda e, ga=ga: e.tensor_scalar(ga.t[:, :], ga.t[:, :], 0.044715, 1.0, ALU.mult, ALU.add), reads=ga.b, writes=ga.b)
                    P.op("dve", lambda e, kc=kc, ga=ga: e.tensor_tensor(ga.t[:, :], ga.t[:, :], yt.t[:, kc, :], ALU.mult), reads=ga.b + yt.b, writes=ga.b)
                    P.op("act", lambda e, ga=ga: e.activation(ga.t[:, :], ga.t[:, :], AF.Sigmoid, scale=GC), reads=ga.b, writes=ga.b)
                    P.op("pool", lambda e, kc=kc, ga=ga: e.tensor_tensor(hT.t[:, kc, :], ga.t[:, :], yt.t[:, kc, :], ALU.mult),
                         reads=ga.b + yt.b, writes=hT.b)
                for dc in range(KC):
                    wv = load_w_chunk(wglu_in, dc * 128, "glu")
                    wg = load_w_chunk(wglu_in, D + dc * 128, "glu")
                    pv = rot(mm_banks, "pb")
                    pg = rot(mm_banks, "pb")
                    for kc in range(KC):
                        P.op("pe", lambda e, wv=wv, kc=kc, pv=pv: e.matmul(
                            pv.t[:, :], wv.t[:, kc, :], hT.t[:, kc, :], start=(kc == 0), stop=(kc == KC - 1)),
                            reads=wv.b + hT.b, writes=pv.b)
                    for kc in range(KC):
                        P.op("pe", lambda e, wg=wg, kc=kc, pg=pg: e.matmul(
                            pg.t[:, :], wg.t[:, kc, :], hT.t[:, kc, :], start=(kc == 0), stop=(kc == KC - 1)),
                            reads=wg.b + hT.b, writes=pg.b)
                    tf = rot(tmpf, "tmpf")
                    P.op("act", lambda e, tf=tf, pg=pg: e.activation(tf.t[:, :], pg.t[:, :], AF.Sigmoid), reads=pg.b, writes=tf.b)
                    P.op("dve", lambda e, tf=tf, pv=pv: e.tensor_tensor(tf.t[:, :], tf.t[:, :], pv.t[:, :], ALU.mult),
                         reads=tf.b + pv.b, writes=tf.b)
                    P.op("dve", lambda e, dc=dc, tf=tf: e.scalar_tensor_tensor(
                        xT.t[:, dc, :], tf.t[:, :], modG[l].t[:, s, dc:dc + 1], xT.t[:, dc, :], ALU.mult, ALU.add),
                        reads=tf.b + modG[l].b + xT.b, writes=xT.b)
                store_xT_tile(tt, last)
            scC.__exit__(None, None, None)
            sc.__exit__(None, None, None)

        for i, ph in enumerate(phases):
            first = (i == 0)
            last = (i == len(phases) - 1)
            if ph.startswith("ffn"):
                l = int(ph[3]); fi = int(ph[4])
                bgn = None
                if pair and BG_ENABLE and ph == "ffn01" and "ffn10" in phases:
                    bgn = bg_chain(bg_jobs_in(ffnwi_in["10"], "fi10"), bg_jobs_out(ffnwo_in["10"], "fo10"))
                ffn(l, 0 if fi == 0 else 2, fi, first=first, last=last, bg_next=bgn)
            elif ph.startswith("att"):
                attention(int(ph[3]), first=first, last=last)
            elif ph.startswith("ssm"):
                ssm(int(ph[3]), first=first, last=last)

        P.emit()
    return nc


def make_in_maps(inputs, T=SEQ, phases=None, pair=False, n_cores=N_CORES):
    if phases is None:
        phases = ALL_PHASES
    if pair:
        return make_in_maps_pair(inputs, T, phases, n_cores)
    x = np.asarray(inputs["x"], dtype=np.float32)
    c = np.asarray(inputs["c"], dtype=np.float32)
    pos = np.asarray(inputs["positions"], dtype=np.int32)
    shared = {
        "norm_g": np.ascontiguousarray(np.asarray(inputs["norm_g"], np.float32).reshape(96, 128)),
        "ada_b": np.ascontiguousarray(np.asarray(inputs["ada_b"], np.float32).reshape(2, 144, 128)),
        "ident": np.eye(128, dtype=np.float32),
    }
    for l in sorted(set(int(ph[3]) for ph in phases)):
        shared["ada_w%d" % l] = np.asarray(inputs["ada_w"][l], np.float32)
    for ph in phases:
        if ph.startswith("ffn"):
            l = int(ph[3]); fi = int(ph[4])
            shared["ffn_w_in" + ph[3:]] = np.ascontiguousarray(np.asarray(inputs["ffn_w_in"][l, fi], np.float32))
            shared["ffn_w_out" + ph[3:]] = np.ascontiguousarray(np.asarray(inputs["ffn_w_out"][l, fi], np.float32))
    if any(ph.startswith("ssm") for ph in phases):
        p = np.arange(128)
        shared["pm"] = np.stack([1 - (p // 16) % 2, (p // 16) % 2], axis=1).astype(np.float32)
        shared["ssm_dT"] = np.ascontiguousarray(np.asarray(inputs["ssm_d"][0], np.float32).reshape(16, 128).T)
        shared["iota"] = np.ascontiguousarray(np.broadcast_to(np.arange(NT, dtype=np.float32)[None], (128, NT)))
        a_re = np.asarray(inputs["ssm_a_re"][0], np.float32)
        a_im = np.asarray(inputs["ssm_a_im"][0], np.float32)
        ls = np.asarray(inputs["ssm_log_step"][0], np.float32)
        lsb = np.broadcast_to(ls[:, None], (128, 64))
        def pair(a):
            sh = a.shape
            a = a.reshape((64, 2, 64) + sh[2:])
            a = np.moveaxis(a, 0, 2)
            return np.ascontiguousarray(a.reshape((128, 64) + sh[2:]))
        shared["a_pair"] = np.ascontiguousarray(np.stack([pair(a_re), pair(a_im), pair(lsb)], axis=1))
        def feat(a):
            a = a.reshape(16, 8, 64)
            a = np.broadcast_to(a[:, :, None, :], (16, 8, 16, 64))
            return np.ascontiguousarray(a.transpose(1, 2, 0, 3).reshape(128, 1024))
        shared["a_feat"] = np.ascontiguousarray(np.stack([feat(a_re), feat(a_im), feat(lsb)], axis=1))
        def featb(b):
            b = b.reshape(16, 8, 64, 16)
            return np.ascontiguousarray(b.transpose(1, 3, 0, 2).reshape(128, 1024))
        shared["b_feat"] = np.ascontiguousarray(np.stack(
            [featb(np.asarray(inputs["ssm_b_re"][0], np.float32)), featb(np.asarray(inputs["ssm_b_im"][0], np.float32))], axis=1))
        def pairc(c):
            return pair(np.ascontiguousarray(c.transpose(0, 2, 1)))
        shared["c_pair"] = np.ascontiguousarray(np.stack(
            [pairc(np.asarray(inputs["ssm_c_re"][0], np.float32)), pairc(np.asarray(inputs["ssm_c_im"][0], np.float32))], axis=1))
        shared["w_glu"] = np.ascontiguousarray(np.asarray(inputs["ssm_w_glu"][0], np.float32))
    if any(ph.startswith("att") for ph in phases):
        shared["attn_w_in"] = np.ascontiguousarray(np.asarray(inputs["attn_w_in"][0], np.float32))
        shared["attn_w_out"] = np.ascontiguousarray(np.asarray(inputs["attn_w_out"][0], np.float32))
        invf = (500000.0 ** (-np.arange(0, 32, 2, dtype=np.float32) / np.float32(32))).astype(np.float32)
        ropec = np.zeros((32, 4), np.float32)
        ropec[:, 0] = np.concatenate([invf, invf])
        sign = np.concatenate([-np.ones(16), np.ones(16)]).astype(np.float32)
        ropec[:, 1] = sign
        ropec[:, 2] = -math.pi * sign
        ropec[:, 3] = -math.pi
        shared["ropec"] = ropec
        rotm = np.zeros((32, 32), np.float32)
        for m_ in range(32):
            rotm[(m_ + 16) % 32, m_] = 1.0
        shared["rotm"] = rotm
        qg = np.asarray(inputs["attn_q_norm"][0], np.float32)
        kg = np.asarray(inputs["attn_k_norm"][0], np.float32)
        shared["qkg"] = np.ascontiguousarray(np.stack([qg, kg], axis=1))
        shared["qkgrep"] = np.ascontiguousarray(np.broadcast_to(np.stack([qg, kg], axis=0)[None], (128, 2, 128)))
        shared["subg"] = np.ascontiguousarray(np.broadcast_to(np.asarray(inputs["attn_subln"][0], np.float32)[None], (128, 256)))
        shared["lamrep"] = np.ascontiguousarray(np.broadcast_to(np.asarray(inputs["attn_lambda"][0], np.float32)[None], (128, 4, 128)))
        kk = np.arange(128)[:, None]
        qq = np.arange(256)[None, :]
        mask = np.stack([(kk <= qq), (kk + 128 <= qq)], axis=1).astype(np.float32)
        shared["maskc"] = mask.astype(ml_dtypes.bfloat16)
    maps = []
    for core in range(N_CORES):
        b = core % 4
        m = dict(shared)
        m["x"] = np.ascontiguousarray(x[b, :T])
        m["c"] = np.ascontiguousarray(c[b].reshape(16, 128))
        m["pos"] = np.ascontiguousarray(np.broadcast_to(pos[b, :T].reshape(1, T), (32, T)))
        maps.append(m)
    return maps


def make_in_maps_pair(inputs, T, phases, n_cores):
    base = make_in_maps(inputs, T, phases, pair=False)
    TL = T // 2
    x = np.asarray(inputs["x"], np.float32)
    maps = []
    for core in range(n_cores):
        b, r = core // 2, core % 2
        m = dict(base[b])
        m["x"] = np.ascontiguousarray(x[b, r * TL:(r + 1) * TL])
        f = np.zeros((128, 2), np.float32)
        f[:, r] = 1.0
        m["rankf"] = f
        for l in range(2):
            if "ada_w%d" % l in m:
                m["ada_w%d" % l] = np.ascontiguousarray(m["ada_w%d" % l][:, 9216 * r:9216 * (r + 1)])
        m["ada_b"] = np.ascontiguousarray(m["ada_b"][:, 72 * r:72 * (r + 1), :])
        if "attn_w_in" in m:
            w = np.asarray(inputs["attn_w_in"][0], np.float32)
            cols = np.concatenate([np.arange(o + 1024 * r, o + 1024 * (r + 1)) for o in (0, 2048, 4096)])
            m["attn_w_in"] = np.ascontiguousarray(w[:, cols])
        if "a_pair" in m:
            m["a_pair"] = np.ascontiguousarray(m["a_pair"][:, :, 32 * r:32 * (r + 1)])
            m["c_pair"] = np.ascontiguousarray(m["c_pair"][:, :, 32 * r:32 * (r + 1), :])
            m["a_feat"] = np.ascontiguousarray(m["a_feat"][:, :, 512 * r:512 * (r + 1)])
            m["b_feat"] = np.ascontiguousarray(m["b_feat"][:, :, 512 * r:512 * (r + 1)])
        maps.append(m)
    return maps


def kernel(**inputs):
    nc = build(SEQ, pair=True)
    maps = make_in_maps(inputs, SEQ, pair=True)
    res = run_bass_kernel_spmd(nc, maps, core_ids=list(range(N_CORES)))
    out = np.stack([np.concatenate([res.results[2 * b]["out"], res.results[2 * b + 1]["out"]], axis=0) for b in range(4)], axis=0)
    return out.astype(np.float32)
```

```python
import math
from contextlib import ExitStack

import numpy as np
import ml_dtypes

import concourse.bass as bass
import concourse.mybir as mybir
from concourse.bass_utils import run_bass_kernel_spmd

F32 = mybir.dt.float32
BF16 = mybir.dt.bfloat16
I32 = mybir.dt.int32
ALU = mybir.AluOpType
AF = mybir.ActivationFunctionType
AX = mybir.AxisListType

D = 2048
KC = 16
SEQ = 4096
DFF = 5632
FC = 44
NT = 512
EPS = 1e-6
NH = 8
HD = 128
N_CORES = 8
TWO_PI = 2.0 * math.pi

SELF_SYNC = True
BG_ENABLE = True
NO_SELF_SYNC = ()
ALL_PHASES = ['ffn00', 'att0', 'ffn01', 'ffn10', 'ssm1', 'ffn11']


class Buf:
    __slots__ = ("w", "r", "name")

    def __init__(self, name=""):
        self.w = None
        self.r = []
        self.name = name


class Prog:
    NDS = {"sp": 12, "pool": 6, "act": 4}

    def __init__(self, nc, es):
        self.nc = nc
        self.engs = {"pe": nc.tensor, "act": nc.scalar, "dve": nc.vector, "pool": nc.gpsimd, "sp": nc.sync}
        self.sem = {e: es.enter_context(nc.semaphore("s_" + e)) for e in ("pe", "act", "dve", "pool")}
        self.cnt = {e: 0 for e in self.sem}
        self.ops = {e: [] for e in self.engs}
        self.seen = {e: {} for e in self.engs}
        self.dsem = {q: [es.enter_context(nc.semaphore("d_%s%d" % (q, i))) for i in range(n)]
                     for q, n in self.NDS.items()}
        self.duse = {q: [0] * n for q, n in self.NDS.items()}
        self.dnext = {q: 0 for q in self.NDS}
        self.pending = {e: [] for e in self.engs}
        self.ccsem = es.enter_context(nc.semaphore("s_cc"))
        self.cccnt = 0

    def _semof(self, key):
        if isinstance(key, tuple):
            return self.dsem[key[1]][key[2]]
        if key == "cc":
            return self.ccsem
        return self.sem[key]

    def _deps(self, e, reads, writes):
        toks = []
        for b in reads:
            if b.w is not None:
                toks.append(b.w)
        for b in writes:
            if b.w is not None:
                toks.append(b.w)
            toks.extend(b.r)
        waits = {}
        for k, v in toks:
            if k == e and (e == "pe" or e in NO_SELF_SYNC):
                continue
            if self.seen[e].get(k, 0) >= v:
                continue
            if waits.get(k, 0) < v:
                waits[k] = v
        for k, v in waits.items():
            self.seen[e][k] = v
        return list(waits.items())

    def op(self, e, fn, reads=(), writes=()):
        waits = self.pending[e] + self._deps(e, reads, writes)
        self.pending[e] = []
        self.cnt[e] += 1
        tok = (e, self.cnt[e])
        self.ops[e].append((waits, fn, self.sem[e], 1))
        for b in reads:
            b.r.append(tok)
        for b in writes:
            b.w = tok
            b.r = []

    def dma(self, q, out, in_, reads=(), writes=(), **kw):
        i = self.dnext[q]
        self.dnext[q] = (i + 1) % self.NDS[q]
        waits = self.pending[q] + self._deps(q, reads, writes)
        self.pending[q] = []
        key = ("d", q, i)
        prev = self.duse[q][i]
        if prev > 0 and self.seen[q].get(key, 0) < prev:
            waits.append((key, prev))
            self.seen[q][key] = prev
        self.duse[q][i] = prev + 16
        tok = (key, prev + 16)
        self.ops[q].append((waits, lambda eng: eng.dma_start(out=out, in_=in_, **kw), self.dsem[q][i], 16))
        for b in reads:
            b.r.append(tok)
        for b in writes:
            b.w = tok
            b.r = []

    def barrier(self):
        allw = []
        for q, n in self.NDS.items():
            for i in range(n):
                if self.duse[q][i] > 0:
                    allw.append((("d", q, i), self.duse[q][i]))
        for e in self.sem:
            if self.cnt[e] > 0:
                allw.append((e, self.cnt[e]))
        if self.cccnt > 0:
            allw.append(("cc", self.cccnt))
        for e in self.engs:
            lst = []
            for k, v in allw:
                if k == e:
                    continue
                if self.seen[e].get(k, 0) >= v:
                    continue
                self.seen[e][k] = v
                lst.append((k, v))
            self.pending[e].extend(lst)

    def coll(self, kind, groups, in_h, out_h, reads=(), writes=()):
        q = "pool"
        waits = self.pending[q] + self._deps(q, reads, writes)
        self.pending[q] = []
        self.cccnt += 1
        tok = ("cc", self.cccnt)

        def fn(eng):
            return eng.collective_compute(kind, ALU.bypass, replica_groups=groups,
                                          ins=[in_h.ap().opt()], outs=[out_h.ap().opt()])
        self.ops[q].append((waits, fn, self.ccsem, None))
        for b in reads:
            b.r.append(tok)
        for b in writes:
            b.w = tok
            b.r = []

    def emit(self):
        nc = self.nc
        fin = []
        for q, n in self.NDS.items():
            for i in range(n):
                if self.duse[q][i] > 0:
                    fin.append((("d", q, i), self.duse[q][i]))
        for e in self.sem:
            if self.cnt[e] > 0:
                fin.append((e, self.cnt[e]))
        if self.cccnt > 0:
            fin.append(("cc", self.cccnt))

        def replay(name, eng):
            for waits, fn, sem, inc in self.ops[name]:
                for k, v in waits:
                    eng.wait_ge(self._semof(k), v)
                if inc is None:
                    fn(eng).then_inc(sem)
                else:
                    fn(eng).then_inc(sem, inc)
            if name == "sp":
                for k, v in fin:
                    eng.wait_ge(self._semof(k), v)

        with nc.Block() as block:
            @block.sync
            def _(e):
                replay("sp", e)

            @block.scalar
            def _(e):
                replay("act", e)

            @block.vector
            def _(e):
                replay("dve", e)

            @block.gpsimd
            def _(e):
                replay("pool", e)

            @block.tensor
            def _(e):
                replay("pe", e)


class Tile:
    def __init__(self, t, nsub=1, name=""):
        self.t = t
        self.b = [Buf("%s[%d]" % (name, i)) for i in range(nsub)]


class K:
    pass


def build(T=SEQ, phases=None, dbg=None, pair=False, groups=None):
    nc = bass.Bass("TRN2", target_bir_lowering=False)
    TL = T // 2 if pair else T
    NTT = TL // NT
    NTG = T // NT
    NHL = NH // 2 if pair else NH
    if groups is None:
        groups = [[2 * i, 2 * i + 1] for i in range(N_CORES // 2)]
    es = ExitStack()
    with es:
        P = Prog(nc, es)

        def dram_in(name, shape, dt=F32):
            return nc.dram_tensor(name, list(shape), dt, kind="ExternalInput").ap()

        def dram_scr(name, shape, dt=F32):
            return nc.dram_tensor(name, list(shape), dt, kind="Internal").ap()

        cur = [es]
        uid = [0]

        def sb(name, shape, dt=F32, nsub=1):
            uid[0] += 1
            t = cur[0].enter_context(nc.sbuf_tensor("sb%d_%s" % (uid[0], name), list(shape), dt))
            return Tile(t, nsub, name)

        class Scope:
            def __enter__(self_):
                self_.st = ExitStack()
                self_.st.__enter__()
                self_.prev = cur[0]
                cur[0] = self_.st
                return self_

            def __exit__(self_, *a):
                P.barrier()
                cur[0] = self_.prev
                return self_.st.__exit__(*a)

        def ps(name, shape, dt=F32):
            t = es.enter_context(nc.psum_tensor("ps_" + name, list(shape), dt))
            return Tile(t, 1, name)

        class CG:
            def __init__(self, name, rows, cols, dt):
                self.i_h = nc.dram_tensor("cg_in_" + name, [rows, cols], dt)
                self.o_h = nc.dram_tensor("cg_out_" + name, [2 * rows, cols], dt)
                self.i = self.i_h.ap()
                self.o = self.o_h.ap()
                self.ib = Buf("cgi_" + name)
                self.ob = Buf("cgo_" + name)

            def go(self):
                P.coll("AllGather", groups, self.i_h, self.o_h, reads=[self.ib], writes=[self.ob])

        x_in = dram_in("x", [TL, D])
        c_in = dram_in("c", [16, 128])
        pos_in = dram_in("pos", [32, T], I32)
        normg_in = dram_in("norm_g", [96, 128])
        if phases is None:
            phases = ALL_PHASES
        layers = sorted(set(int(ph[3]) for ph in phases))
        NJ = 72 if pair else 144
        adaw_in = {l: dram_in("ada_w%d" % l, [D, NJ * 128]) for l in layers}
        adab_in = dram_in("ada_b", [2, NJ, 128])
        ffnwi_in = {}
        ffnwo_in = {}
        for ph in phases:
            if ph.startswith("ffn"):
                ffnwi_in[ph[3:]] = dram_in("ffn_w_in" + ph[3:], [D, 2 * DFF])
                ffnwo_in[ph[3:]] = dram_in("ffn_w_out" + ph[3:], [DFF, D])
        ident_in = dram_in("ident", [128, 128])
        if any(ph.startswith("ssm") for ph in phases):
            pm_in = dram_in("pm", [128, 2])
            ssmd_in = dram_in("ssm_dT", [128, 16])
            iota_in = dram_in("iota", [128, NT])
            NGP = 32 if pair else 64
            NKL = NGP // 4
            apair_in = dram_in("a_pair", [128, 3, NGP])
            afeat_in = dram_in("a_feat", [128, 3, NKL * 64])
            bfeat_in = dram_in("b_feat", [128, 2, NKL * 64])
            cpair_in = dram_in("c_pair", [128, 2, NGP, 16])
            wglu_in = dram_in("w_glu", [D, 2 * D])
            if pair:
                cg_u = [CG("u%d" % i, D, NT, BF16) for i in range(NTT)]
                cg_y = [CG("y%d" % i, NKL * 128, NT, F32) for i in range(NTG)]
                uT_d = dram_scr("um_scr", [NKL * 128, T], BF16)
            else:
                uT_d = dram_scr("uT_scr", [D, T], BF16)
            if pair:
                y_d = None
            elif dbg == "y":
                y_d = nc.dram_tensor("dbg_y", [D, T], F32, kind="ExternalOutput").ap()
            else:
                y_d = dram_scr("y_scr", [D, T])
            u_buf = Buf("u")
            y_buf = Buf("y")
        if any(ph.startswith("att") for ph in phases):
            attnwi_in = dram_in("attn_w_in", [D, 3 * NHL * 256])
            attnwo_in = dram_in("attn_w_out", [D, D])
            ropec_in = dram_in("ropec", [32, 4])
            rotm_in = dram_in("rotm", [32, 32])
            qkg_in = dram_in("qkg", [128, 2])
            subg_in = dram_in("subg", [128, 256])
            qkgrep_in = dram_in("qkgrep", [128, 2, 128])
            lamrep_in = dram_in("lamrep", [128, 4, 128])
            mask_in = dram_in("maskc", [128, 2, 256], BF16)
            qkT_d = dram_scr("qkT_scr", [4 * NHL, 128, T], BF16)
            V_d = dram_scr("V_scr", [T, NHL * 256], BF16)
            OTC = 1024
            if pair:
                cg_h = [CG("hT%d" % i, D, NT, BF16) for i in range(NTT)]
                cg_o = [CG("oT%d" % i, NHL * 256, OTC, BF16) for i in range(T // OTC)]
            else:
                oT_d = dram_scr("oT_scr", [D, T], BF16)
            qk_buf = Buf("qk")
            v_buf = Buf("v")
            oT_buf = Buf("oT")
        out_d = nc.dram_tensor("out", [TL, D], F32, kind="ExternalOutput").ap()
        xT_d = dram_scr("xT_scr", [D, TL])
        if pair:
            rankf_in = dram_in("rankf", [128, 2])

        xT_buf = Buf("xT_d")
        out_buf = Buf("out_d")
        in_buf = Buf("inputs")

        ident = sb("ident", [128, 128])
        identb = sb("identb", [128, 128], BF16)
        onesb = sb("onesb", [128, 128], BF16)
        vecrows = sb("vecrows", [128, 128])
        normgT = sb("normgT", [128, 96])
        condT = sb("condT", [128, 16, 2])
        adabT = [sb("adabT%d" % l, [128, 144]) for l in range(2)]
        modT = [sb("modT%d" % l, [128, 144]) for l in range(2)]
        modA = [sb("modA%d" % l, [128, 3, 16]) for l in range(2)]
        modG = [sb("modG%d" % l, [128, 3, 16]) for l in range(2)]

        rankf = sb("rankf", [128, 2], F32)
        if pair:
            P.dma("sp", rankf.t[:, :], rankf_in[:, :], reads=[in_buf], writes=rankf.b)

        def select2(dst_ap, c0_ap, c1_ap, tmp_ap, reads, writes, eng="dve"):
            P.op(eng, lambda e: e.tensor_scalar(tmp_ap, c0_ap, rankf.t[:, 0:1], None, ALU.mult),
                 reads=list(reads) + rankf.b, writes=list(writes))
            P.op(eng, lambda e: e.scalar_tensor_tensor(dst_ap, c1_ap, rankf.t[:, 1:2], tmp_ap, ALU.mult, ALU.add),
                 reads=list(reads) + rankf.b, writes=list(writes))

        pbank = [ps("pb%d" % i, [128, 512]) for i in range(7)]
        pbf = ps("pbf", [128, 1024], BF16)

        P.dma("sp", ident.t[:], ident_in[:, :], reads=[in_buf], writes=ident.b)
        P.op("dve", lambda e: e.tensor_copy(identb.t[:], ident.t[:]), reads=ident.b, writes=identb.b)
        P.op("dve", lambda e: e.memset(onesb.t[:], 1.0), writes=onesb.b)

        def transpose_rows(src_ap_dram, nrows, dst_ap, dst_bufs, bank):
            P.dma("sp", vecrows.t[0:nrows, :], src_ap_dram, reads=[in_buf], writes=vecrows.b)
            P.op("pe", lambda e: e.transpose(bank.t[:, 0:nrows], vecrows.t[0:nrows, :], ident.t[0:nrows, 0:nrows]),
                 reads=vecrows.b + ident.b, writes=bank.b)
            P.op("dve", lambda e: e.tensor_copy(dst_ap, bank.t[:, 0:nrows]), reads=bank.b, writes=dst_bufs)

        transpose_rows(normg_in[:, :], 96, normgT.t[:, :], normgT.b, pbank[0])
        transpose_rows(c_in[:, :], 16, condT.t[:, :, 0], condT.b, pbank[1])
        P.op("act", lambda e: e.activation(condT.t[:, :, 0], condT.t[:, :, 0], AF.Silu), reads=condT.b, writes=condT.b)
        P.op("act", lambda e: e.activation(condT.t[:, :, 1], condT.t[:, :, 0], AF.Copy), reads=condT.b, writes=condT.b)
        for l in range(2):
            if pair:
                transpose_rows(adab_in[l, 0:NJ, :], NJ, adabT[l].t[:, 0:NJ], adabT[l].b, pbank[2])
            else:
                transpose_rows(adab_in[l, 0:128, :], 128, adabT[l].t[:, 0:128], adabT[l].b, pbank[2])
                transpose_rows(adab_in[l, 128:144, :], 16, adabT[l].t[:, 128:144], adabT[l].b, pbank[3])

        CB = 512
        sc_ada = Scope()
        sc_ada.__enter__()
        adaw = [sb("adaw%d" % i, [128, 16, CB]) for i in range(2)]
        nblk = NJ * 128 // CB
        if pair:
            cg_mod = CG("mod", 128, 2 * NJ, F32)
            modh = sb("modh", [128, 2, NJ])
        for l in layers:
            bank = pbank[4 + l]
            for blk in range(nblk):
                wt = adaw[blk % 2]
                src = adaw_in[l][:, blk * CB:(blk + 1) * CB].rearrange("(kc p) c -> p kc c", p=128)
                P.dma("sp", wt.t[:, 0:8, :], src[:, 0:8, :], reads=[in_buf], writes=wt.b)
                P.dma("act", wt.t[:, 8:16, :], src[:, 8:16, :], reads=[in_buf], writes=wt.b)
                for jj in range(CB // 128):
                    j = blk * (CB // 128) + jj
                    for kc in range(KC):
                        P.op("pe", lambda e, wt=wt, jj=jj, kc=kc, j=j, bank=bank: e.matmul(
                            bank.t[:, 2 * j:2 * j + 2], wt.t[:, kc, jj * 128:(jj + 1) * 128], condT.t[:, kc, :],
                            start=(kc == 0), stop=(kc == KC - 1)),
                            reads=wt.b + condT.b, writes=bank.b)
            if pair:
                P.op("dve", lambda e, l=l, bank=bank: e.tensor_tensor(
                    modh.t[:, l, :], bank.t[:, 0:2 * NJ].rearrange("p (j two) -> p j two", two=2)[:, :, 0], adabT[l].t[:, 0:NJ], ALU.add),
                    reads=bank.b + adabT[l].b, writes=modh.b)
                continue
            P.op("dve", lambda e, l=l, bank=bank: e.tensor_tensor(
                modT[l].t[:, :], bank.t[:, 0:288].rearrange("p (j two) -> p j two", two=2)[:, :, 0], adabT[l].t[:, :], ALU.add),
                reads=bank.b + adabT[l].b, writes=modT[l].b)
        if pair:
            P.dma("sp", cg_mod.i[:, :], modh.t[:, :, :].rearrange("p l j -> p (l j)"), reads=modh.b, writes=[cg_mod.ib])
            cg_mod.go()
            for l in layers:
                for r_ in range(2):
                    P.dma(("sp", "act")[r_], modT[l].t[:, r_ * NJ:(r_ + 1) * NJ], cg_mod.o[r_ * 128:(r_ + 1) * 128, l * NJ:(l + 1) * NJ],
                          reads=[cg_mod.ob], writes=modT[l].b)
        for l in layers:
            for s in range(3):
                sh = (s * 3 + 0) * 16
                sc = (s * 3 + 1) * 16
                ga = (s * 3 + 2) * 16
                P.op("dve", lambda e, l=l, s=s, sc=sc: e.scalar_tensor_tensor(
                    modA[l].t[:, s, :], modT[l].t[:, sc:sc + 16], 1.0, normgT.t[:, (l * 3 + s) * 16:(l * 3 + s) * 16 + 16],
                    ALU.add, ALU.mult), reads=modT[l].b + normgT.b, writes=modA[l].b)
                wgt = 1.0 if s == 1 else 0.5
                P.op("dve", lambda e, l=l, s=s, ga=ga, wgt=wgt: e.tensor_scalar(
                    modG[l].t[:, s, :], modT[l].t[:, ga:ga + 16], wgt, None, ALU.mult),
                    reads=modT[l].b, writes=modG[l].b)

        sc_ada.__exit__(None, None, None)

        xT = sb("xT", [128, KC, NT], F32)
        hT = sb("hT", [128, KC, NT], BF16)
        tmpf = [sb("tmpf%d" % i, [128, NT], F32) for i in range(3)]
        tmpb = [sb("tmpb%d" % i, [128, NT], BF16) for i in range(3)]
        rstd = sb("rstd", [128, NT], F32)
        xtok = []
        wst = []
        wbf = []
        wost = []
        wobf = []
        gTh = [None]

        def alloc_ffn_tiles(need_xtok=True):
            gTh[0] = sb("gT", [128, FC, NT], BF16, nsub=FC)
            if need_xtok:
                xtok[:] = [sb("xtok%d" % i, [128, D], F32) for i in range(2)]
            wst[:] = [sb("wst%d" % i, [128, KC, 128], F32) for i in range(2)]
            wbf[:] = [sb("wbf%d" % i, [128, KC, 128], BF16) for i in range(5)]
            wost[:] = [sb("wost%d" % i, [128, 22, 128], F32) for i in range(2)]
            wobf[:] = [sb("wobf%d" % i, [128, 22, 128], BF16) for i in range(3)]
        ctr = {"tmpf": 0, "tmpb": 0, "w": 0, "wb": 0, "wo": 0, "wob": 0, "pb": 0, "xtok": 0, "cast": 0, "ldq": 0}

        def rot(lst, key):
            i = ctr[key]
            ctr[key] = i + 1
            return lst[i % len(lst)]

        mm_banks = pbank[0:4]
        ssq_bank = pbank[4]
        tr_banks = pbank[5:7]

        def load_xT_tile(tt, from_input):
            t0 = tt * NT
            if not from_input:
                src = xT_d[:, t0:t0 + NT].rearrange("(kc p) t -> p kc t", p=128)
                P.dma("sp", xT.t[:, 0:8, :], src[:, 0:8, :], reads=[xT_buf], writes=xT.b)
                P.dma("act", xT.t[:, 8:16, :], src[:, 8:16, :], reads=[xT_buf], writes=xT.b)
                return
            for sub in range(NT // 128):
                xt = rot(xtok, "xtok")
                P.dma("sp", xt.t[:, :], x_in[t0 + sub * 128:t0 + (sub + 1) * 128, :], reads=[in_buf], writes=xt.b)
                for kc in range(KC):
                    bank = tr_banks[kc % 2]
                    P.op("pe", lambda e, xt=xt, kc=kc, bank=bank: e.transpose(
                        bank.t[:, 0:128], xt.t[:, kc * 128:(kc + 1) * 128], ident.t[:, :]),
                        reads=xt.b + ident.b, writes=bank.b)
                    eng = "dve" if kc % 2 == 0 else "act"
                    if eng == "dve":
                        P.op("dve", lambda e, kc=kc, sub=sub, bank=bank: e.tensor_copy(
                            xT.t[:, kc, sub * 128:(sub + 1) * 128], bank.t[:, 0:128]), reads=bank.b, writes=xT.b)
                    else:
                        P.op("act", lambda e, kc=kc, sub=sub, bank=bank: e.activation(
                            xT.t[:, kc, sub * 128:(sub + 1) * 128], bank.t[:, 0:128], AF.Copy), reads=bank.b, writes=xT.b)

        def store_xT_tile(tt, to_output):
            t0 = tt * NT
            if not to_output:
                dst = xT_d[:, t0:t0 + NT].rearrange("(kc p) t -> p kc t", p=128)
                P.dma("pool", dst[:, :, :], xT.t[:, :, :], reads=xT.b, writes=[xT_buf])
                return
            for sub in range(NT // 128):
                xt = rot(xtok, "xtok")
                for kc in range(KC):
                    bank = tr_banks[kc % 2]
                    P.op("pe", lambda e, kc=kc, sub=sub, bank=bank: e.transpose(
                        bank.t[:, 0:128], xT.t[:, kc, sub * 128:(sub + 1) * 128], ident.t[:, :]),
                        reads=xT.b + ident.b, writes=bank.b)
                    if kc % 2 == 0:
                        P.op("dve", lambda e, kc=kc, xt=xt, bank=bank: e.tensor_copy(
                            xt.t[:, kc * 128:(kc + 1) * 128], bank.t[:, 0:128]), reads=bank.b, writes=xt.b)
                    else:
                        P.op("act", lambda e, kc=kc, xt=xt, bank=bank: e.activation(
                            xt.t[:, kc * 128:(kc + 1) * 128], bank.t[:, 0:128], AF.Copy), reads=bank.b, writes=xt.b)
                P.dma("pool", out_d[t0 + sub * 128:t0 + (sub + 1) * 128, :], xt.t[:, :], reads=xt.b, writes=[out_buf])

        def norm_mod(l, s, h32=None, want_bf=True):
            for kc in range(KC):
                sq = rot(tmpb, "tmpb")
                P.op("act", lambda e, kc=kc, sq=sq: e.activation(sq.t[:, :], xT.t[:, kc, :], AF.Square),
                     reads=xT.b, writes=sq.b)
                P.op("pe", lambda e, kc=kc, sq=sq: e.matmul(ssq_bank.t[:, :], onesb.t[:, :], sq.t[:, :],
                                                           start=(kc == 0), stop=(kc == KC - 1)),
                     reads=sq.b + onesb.b, writes=ssq_bank.b)
            P.op("dve", lambda e: e.tensor_scalar(rstd.t[:, :], ssq_bank.t[:, :], 1.0 / D, EPS, ALU.mult, ALU.add),
                 reads=ssq_bank.b, writes=rstd.b)
            P.op("act", lambda e: e.activation(rstd.t[:, :], rstd.t[:, :], AF.Sqrt), reads=rstd.b, writes=rstd.b)
            P.op("dve", lambda e: e.reciprocal(rstd.t[:, :], rstd.t[:, :]), reads=rstd.b, writes=rstd.b)
            sh = (s * 3 + 0) * 16
            for kc in range(KC):
                tf = rot(tmpf, "tmpf")
                P.op("dve", lambda e, kc=kc, tf=tf: e.scalar_tensor_tensor(
                    tf.t[:, :], xT.t[:, kc, :], modA[l].t[:, s, kc:kc + 1], rstd.t[:, :], ALU.mult, ALU.mult),
                    reads=xT.b + modA[l].b + rstd.b, writes=tf.b)
                if h32 is not None:
                    P.op("act", lambda e, kc=kc, tf=tf: e.activation(
                        h32.t[:, kc, :], tf.t[:, :], AF.Identity, bias=modT[l].t[:, sh + kc:sh + kc + 1], scale=1.0),
                        reads=tf.b + modT[l].b, writes=h32.b)
                    if want_bf:
                        P.op("pool", lambda e, kc=kc: e.tensor_copy(hT.t[:, kc, :], h32.t[:, kc, :]), reads=h32.b, writes=hT.b)
                    continue
                P.op("act", lambda e, kc=kc, tf=tf: e.activation(
                    hT.t[:, kc, :], tf.t[:, :], AF.Identity, bias=modT[l].t[:, sh + kc:sh + kc + 1], scale=1.0),
                    reads=tf.b + modT[l].b, writes=hT.b)

        wcache = {}
        CAST_ENGS = ["act", "dve", "pool"]

        def cast_op(dst_ap, src_ap, reads, writes):
            e = CAST_ENGS[ctr["cast"] % 3]
            ctr["cast"] += 1
            if e == "act":
                P.op("act", lambda eng: eng.activation(dst_ap, src_ap, AF.Copy), reads=reads, writes=writes)
            else:
                P.op(e, lambda eng: eng.tensor_copy(dst_ap, src_ap), reads=reads, writes=writes)

        def ldq():
            q = ("sp", "pool")[ctr["ldq"] % 2]
            ctr["ldq"] += 1
            return q

        def load_w_chunk(w_ap, c0, wname=None):
            idx = c0 // 128
            wb = rot(wbf, "wb")
            bufs = None
            if wname is not None:
                if wname not in wcache:
                    wcache[wname] = (dram_scr("wc_" + wname, [w_ap.shape[1] // 128, 128, KC * 128], BF16), {})
                scr, bufs = wcache[wname]
            if bufs is not None and idx in bufs:
                P.dma(ldq(), wb.t[:, :, :], scr[idx].rearrange("p (k c) -> p k c", c=128), reads=[bufs[idx]], writes=wb.b)
                return wb
            st = rot(wst, "w")
            src = w_ap[:, c0:c0 + 128].rearrange("(kc p) c -> p kc c", p=128)
            P.dma("sp", st.t[:, :, :], src, reads=[in_buf], writes=st.b)
            cast_op(wb.t[:, :, :], st.t[:, :, :], st.b, wb.b)
            if bufs is not None:
                bufs[idx] = Buf("wc")
                P.dma("pool", scr[idx].rearrange("p (k c) -> p k c", c=128), wb.t[:, :, :], reads=wb.b, writes=[bufs[idx]])
            return wb

        def load_wo_half(w_out, dc, half, wname):
            wb = rot(wobf, "wob")
            if wname not in wcache:
                wcache[wname] = (dram_scr("wc_" + wname, [32, 128, 22 * 128], BF16), {})
            scr, bufs = wcache[wname]
            idx = dc * 2 + half
            if idx in bufs:
                P.dma(ldq(), wb.t[:, :, :], scr[idx].rearrange("p (k c) -> p k c", c=128), reads=[bufs[idx]], writes=wb.b)
                return wb
            st = rot(wost, "wo")
            src = w_out[half * 22 * 128:(half + 1) * 22 * 128, dc * 128:(dc + 1) * 128].rearrange("(j p) c -> p j c", p=128)
            P.dma("sp", st.t[:, :, :], src, reads=[in_buf], writes=st.b)
            cast_op(wb.t[:, :, :], st.t[:, :, :], st.b, wb.b)
            bufs[idx] = Buf("wc")
            P.dma("pool", scr[idx].rearrange("p (k c) -> p k c", c=128), wb.t[:, :, :], reads=wb.b, writes=[bufs[idx]])
            return wb

        bg = {"st": [], "bf": [], "i": 0}

        def bg_alloc():
            bg["st"] = [sb("bgst%d" % i, [128, KC * 128], F32) for i in range(2)]
            bg["bf"] = [sb("bgbf%d" % i, [128, KC * 128], BF16) for i in range(2)]

        def bg_jobs_in(w_ap, wname):
            for idx in range(w_ap.shape[1] // 128):
                yield ("in", w_ap, wname, idx, 0)

        def bg_jobs_out(w_out, wname):
            for idx in range(32):
                for piece in range(2):
                    yield ("out", w_out, wname, idx, piece)

        def bg_step(job, cast_eng, ld_q, st_q):
            kind, w_ap, wname, idx, piece = job
            i = bg["i"]
            bg["i"] += 1
            st, wb = bg["st"][i % 2], bg["bf"][i % 2]
            if wname not in wcache:
                shape = [w_ap.shape[1] // 128, 128, KC * 128] if kind == "in" else [32, 128, 22 * 128]
                wcache[wname] = (dram_scr("wc_" + wname, shape, BF16), {})
            scr, bufs = wcache[wname]
            if kind == "in":
                n = KC * 128
                src = w_ap[:, idx * 128:(idx + 1) * 128].rearrange("(kc p) c -> p kc c", p=128)
                dst = scr[idx].rearrange("p (k c) -> p k c", c=128)
            else:
                n = 11 * 128
                dc, half = idx // 2, idx % 2
                r0_ = (half * 22 + piece * 11) * 128
                src = w_ap[r0_:r0_ + 11 * 128, dc * 128:(dc + 1) * 128].rearrange("(j p) c -> p j c", p=128)
                dst = scr[idx][:, piece * n:(piece + 1) * n].rearrange("p (k c) -> p k c", c=128)
            sv = st.t[:, 0:n].rearrange("p (k c) -> p k c", c=128)
            bv = wb.t[:, 0:n].rearrange("p (k c) -> p k c", c=128)
            P.dma(ld_q, sv, src, reads=[in_buf], writes=st.b)
            if cast_eng == "act":
                P.op("act", lambda e: e.activation(wb.t[:, 0:n], st.t[:, 0:n], AF.Copy), reads=st.b, writes=wb.b)
            else:
                P.op(cast_eng, lambda e: e.tensor_copy(wb.t[:, 0:n], st.t[:, 0:n]), reads=st.b, writes=wb.b)
            if idx not in bufs:
                bufs[idx] = Buf("wc")
            P.dma(st_q, dst, bv, reads=wb.b, writes=[bufs[idx]])

        def bg_chain(*gens):
            for g in gens:
                for job in g:
                    yield job

        def ffn(l, s, fi, first=False, last=False, bg_next=None):
            w_in = ffnwi_in['%d%d' % (l, fi)]
            w_out = ffnwo_in['%d%d' % (l, fi)]
            sc = Scope()
            sc.__enter__()
            alloc_ffn_tiles(first or last)
            gT = gTh[0]
            if bg_next is not None:
                bg_alloc()
            for tt in range(NTT):
                load_xT_tile(tt, first)
                norm_mod(l, s)
                for j in range(FC):
                    if bg_next is not None and tt >= 1:
                        for _ in range(2):
                            job = next(bg_next, None)
                            if job is not None:
                                bg_step(job, ("dve", "pool", "act")[bg["i"] % 3], "act", "pool")
                    wa = load_w_chunk(w_in, j * 128, "fi%d%d" % (l, fi))
                    wb_ = load_w_chunk(w_in, DFF + j * 128, "fi%d%d" % (l, fi))
                    pa = rot(mm_banks, "pb")
                    pb_ = rot(mm_banks, "pb")
                    for kc in range(KC):
                        P.op("pe", lambda e, wa=wa, kc=kc, pa=pa: e.matmul(
                            pa.t[:, :], wa.t[:, kc, :], hT.t[:, kc, :], start=(kc == 0), stop=(kc == KC - 1)),
                            reads=wa.b + hT.b, writes=pa.b)
                    for kc in range(KC):
                        P.op("pe", lambda e, wb_=wb_, kc=kc, pb_=pb_: e.matmul(
                            pb_.t[:, :], wb_.t[:, kc, :], hT.t[:, kc, :], start=(kc == 0), stop=(kc == KC - 1)),
                            reads=wb_.b + hT.b, writes=pb_.b)
                    tf = rot(tmpf, "tmpf")
                    P.op("act", lambda e, tf=tf, pa=pa: e.activation(tf.t[:, :], pa.t[:, :], AF.Silu),
                         reads=pa.b, writes=tf.b)
                    P.op("dve", lambda e, tf=tf, pb_=pb_, j=j: e.tensor_tensor(
                        gT.t[:, j, :], tf.t[:, :], pb_.t[:, :], ALU.mult),
                        reads=tf.b + pb_.b, writes=[gT.b[j]])
                for dc in range(KC):
                    po = rot(mm_banks, "pb")
                    for half in range(2):
                        wb = load_wo_half(w_out, dc, half, "fo%d%d" % (l, fi))
                        for jj in range(22):
                            j = half * 22 + jj
                            P.op("pe", lambda e, wb=wb, jj=jj, j=j, po=po: e.matmul(
                                po.t[:, :], wb.t[:, jj, :], gT.t[:, j, :], start=(j == 0), stop=(j == FC - 1)),
                                reads=wb.b + [gT.b[j]], writes=po.b)
                    P.op("dve", lambda e, dc=dc, po=po: e.scalar_tensor_tensor(
                        xT.t[:, dc, :], po.t[:, :], modG[l].t[:, s, dc:dc + 1], xT.t[:, dc, :], ALU.mult, ALU.add),
                        reads=po.b + modG[l].b + xT.b, writes=xT.b)
                store_xT_tile(tt, last)
            if bg_next is not None:
                for job in bg_next:
                    bg_step(job, ("dve", "pool", "act")[bg["i"] % 3], "act", "pool")
            sc.__exit__(None, None, None)


        def attention(l, first=False, last=False):
            s = 1
            lambda_init = 0.8 - 0.6 * math.exp(-0.3 * l)
            scale = HD ** -0.5
            w_in = attnwi_in
            w_out = attnwo_in
            NQB = T // 256
            NKT = T // 128
            NQC = NHL * 2
            sc = Scope()
            sc.__enter__()
            if pair:
                scH = Scope()
                scH.__enter__()
                if first:
                    xtok[:] = [sb("xtok%d" % i, [128, D], F32) for i in range(2)]
                for tt in range(NTT):
                    load_xT_tile(tt, first)
                    if first:
                        store_xT_tile(tt, False)
                    norm_mod(l, s)
                    P.dma("pool", cg_h[tt].i[:, :].rearrange("(kc p) t -> p kc t", p=128), hT.t[:, :, :],
                          reads=hT.b, writes=[cg_h[tt].ib])
                    cg_h[tt].go()
                scH.__exit__(None, None, None)
            ropec = sb("ropec", [32, 4], F32)
            rotm = sb("rotm", [32, 32], F32)
            qkg = sb("qkg", [128, 2], F32)
            negmb = sb("negmb", [128, 1], F32)
            nlam = sb("nlam", [128, 1], F32)
            subg = sb("subg", [128, 256], F32)
            P.dma("sp", ropec.t[:, :], ropec_in[:, :], reads=[in_buf], writes=ropec.b)
            P.dma("sp", rotm.t[:, :], rotm_in[:, :], reads=[in_buf], writes=rotm.b)
            P.dma("sp", qkg.t[:, :], qkg_in[:, :], reads=[in_buf], writes=qkg.b)
            P.dma("sp", subg.t[:, :], subg_in[:, :], reads=[in_buf], writes=subg.b)
            P.op("dve", lambda e: e.tensor_scalar(subg.t[:, :], subg.t[:, :], 1.0 - lambda_init, None, ALU.mult),
                 reads=subg.b, writes=subg.b)
            scA = Scope()
            scA.__enter__()
            wst[:] = [sb("wst%d" % i, [128, KC, 128], F32) for i in range(2)]
            wbf[:] = [sb("wbf%d" % i, [128, KC, 128], BF16) for i in range(5)]
            cosT = sb("cosT", [32, T], F32)
            sinT = sb("sinT", [32, T], F32)
            if first:
                xtok[:] = [sb("xtok%d" % i, [128, D], F32) for i in range(2)]
            sc2 = Scope()
            sc2.__enter__()
            grep_ = sb("grep", [128, 2, 128], F32)
            lrep = sb("lrep", [128, 4, 128], F32)
            sm = sb("sm", [128, 8], F32)
            posi = sb("posi", [32, T], I32)
            ang = sb("ang", [32, T], F32)
            ang2 = sb("ang2", [32, T], F32)
            P.dma("sp", grep_.t[:, :, :], qkgrep_in[:, :, :], reads=[in_buf], writes=grep_.b)
            P.dma("sp", lrep.t[:, :, :], lamrep_in[:, :, :], reads=[in_buf], writes=lrep.b)
            P.dma("sp", posi.t[:, :], pos_in[:, :], reads=[in_buf], writes=posi.b)
            for i in range(2):
                P.op("dve", lambda e, i=i: e.reduce_max(sm.t[:, i:i + 1], grep_.t[:, i, :], AX.X, apply_absolute_value=True),
                     reads=grep_.b, writes=sm.b)
            P.op("dve", lambda e: e.tensor_tensor(sm.t[:, 2:3], sm.t[:, 0:1], sm.t[:, 1:2], ALU.mult), reads=sm.b, writes=sm.b)
            P.op("dve", lambda e: e.tensor_scalar(negmb.t[:, :], sm.t[:, 2:3], -(HD * scale), None, ALU.mult),
                 reads=sm.b, writes=negmb.b)
            for i in range(2):
                P.op("dve", lambda e, i=i: e.tensor_tensor(grep_.t[:, i, :], lrep.t[:, 2 * i, :], lrep.t[:, 2 * i + 1, :], ALU.mult),
                     reads=lrep.b + grep_.b, writes=grep_.b)
                P.op("dve", lambda e, i=i: e.reduce_sum(sm.t[:, 3 + i:4 + i], grep_.t[:, i, :], AX.X), reads=grep_.b, writes=sm.b)
                P.op("act", lambda e, i=i: e.activation(sm.t[:, 5 + i:6 + i], sm.t[:, 3 + i:4 + i], AF.Exp), reads=sm.b, writes=sm.b)
            P.op("dve", lambda e: e.scalar_tensor_tensor(nlam.t[:, :], sm.t[:, 6:7], -lambda_init, sm.t[:, 5:6], ALU.add, ALU.subtract),
                 reads=sm.b, writes=nlam.b)
            P.op("dve", lambda e: e.tensor_copy(ang.t[:, :], posi.t[:, :]), reads=posi.b, writes=ang.b)
            P.op("dve", lambda e: e.tensor_scalar(ang.t[:, :], ang.t[:, :], ropec.t[:, 0:1], None, ALU.mult),
                 reads=ang.b + ropec.b, writes=ang.b)
            P.op("dve", lambda e: e.tensor_scalar(ang2.t[:, :], ang.t[:, :], 1.0 / TWO_PI, None, ALU.mult), reads=ang.b, writes=ang2.b)
            P.op("dve", lambda e: e.tensor_copy(posi.t[:, :], ang2.t[:, :]), reads=ang2.b, writes=posi.b)
            P.op("dve", lambda e: e.tensor_copy(ang2.t[:, :], posi.t[:, :]), reads=posi.b, writes=ang2.b)
            P.op("dve", lambda e: e.scalar_tensor_tensor(ang.t[:, :], ang2.t[:, :], -TWO_PI, ang.t[:, :], ALU.mult, ALU.add),
                 reads=ang2.b + ang.b, writes=ang.b)

            def wrap_pi():
                P.op("dve", lambda e: e.tensor_scalar(ang2.t[:, :], ang.t[:, :], math.pi, TWO_PI, ALU.is_gt, ALU.mult),
                     reads=ang.b, writes=ang2.b)
                P.op("dve", lambda e: e.tensor_tensor(ang.t[:, :], ang.t[:, :], ang2.t[:, :], ALU.subtract),
                     reads=ang.b + ang2.b, writes=ang.b)
            wrap_pi()
            P.op("act", lambda e: e.activation(sinT.t[:, :], ang.t[:, :], AF.Sin, scale=ropec.t[:, 1:2]),
                 reads=ang.b + ropec.b, writes=sinT.b)
            P.op("dve", lambda e: e.tensor_scalar(ang.t[:, :], ang.t[:, :], 0.5 * math.pi, None, ALU.add), reads=ang.b + sinT.b, writes=ang.b)
            wrap_pi()
            P.op("act", lambda e: e.activation(cosT.t[:, :], ang.t[:, :], AF.Sin), reads=ang.b, writes=cosT.b)
            sc2.__exit__(None, None, None)

            qn = [sb("qn%d" % i, [128, NT], F32) for i in range(6)]
            rr = [sb("rr%d" % i, [128, NT], F32) for i in range(6)]
            rt1s = [sb("rt1_%d" % i, [32, NT], F32) for i in range(4)]
            rt2s = [sb("rt2_%d" % i, [32, NT], F32) for i in range(2)]
            qkb = [sb("qkb%d" % i, [128, NT], BF16) for i in range(4)]
            vtile = sb("vtile", [128, 4, NHL * 256], BF16)
            actr = {"qn": 0, "qkb": 0}
            qk_bufs = [Buf("qk%d" % i) for i in range(4 * NHL)]
            bg_a = None
            if pair and BG_ENABLE and "ffn10" in phases:
                bg_alloc()
                bg_a = bg_chain(bg_jobs_in(ffnwi_in["10"], "fi10"), bg_jobs_out(ffnwo_in["10"], "fo10"))
            for tt in range(NTG):
                t0 = tt * NT
                if pair:
                    cgx = cg_h[tt % NTT]
                    src = cgx.o[(tt // NTT) * D:(tt // NTT + 1) * D, :].rearrange("(kc p) t -> p kc t", p=128)
                    P.dma("sp", hT.t[:, :, :], src, reads=[cgx.ob], writes=hT.b)
                else:
                    load_xT_tile(tt, first)
                    if first:
                        store_xT_tile(tt, False)
                    norm_mod(l, s)
                def stA(ch):
                    w_ = load_w_chunk(w_in, ch * 128, "awi")
                    p_ = rot(mm_banks, "pb")
                    for kc in range(KC):
                        P.op("pe", lambda e, w_=w_, kc=kc, p_=p_: e.matmul(
                            p_.t[:, :], w_.t[:, kc, :], hT.t[:, kc, :], start=(kc == 0), stop=(kc == KC - 1)),
                            reads=w_.b + hT.b, writes=p_.b)
                    return {"ch": ch, "p": p_}

                NB_ = 6

                def stB1(st):
                    ch, pq = st["ch"], st["p"]
                    if ch >= 2 * NQC:
                        vb = qkb[actr["qkb"] % len(qkb)]
                        actr["qkb"] += 1
                        st["vb"] = vb
                        P.op("act", lambda e, vb=vb, pq=pq: e.activation(vb.t[:, :], pq.t[:, :], AF.Copy), reads=pq.b, writes=vb.b)
                        return
                    sq = rot(tmpb, "tmpb")
                    P.op("act", lambda e, sq=sq, pq=pq: e.activation(sq.t[:, :], pq.t[:, :], AF.Square), reads=pq.b, writes=sq.b)
                    P.op("pe", lambda e, sq=sq: e.matmul(ssq_bank.t[:, :], onesb.t[:, :], sq.t[:, :], start=True, stop=True),
                         reads=sq.b + onesb.b, writes=ssq_bank.b)
                    r = rr[ch % NB_]
                    st["r"] = r
                    P.op("dve", lambda e, r=r: e.tensor_scalar(r.t[:, :], ssq_bank.t[:, :], 1.0 / HD, EPS, ALU.mult, ALU.add),
                         reads=ssq_bank.b, writes=r.b)

                def stB2(st):
                    ch = st["ch"]
                    if ch >= 2 * NQC:
                        vb = st["vb"]
                        for sub in range(4):
                            P.op("pe", lambda e, vb=vb, sub=sub: e.transpose(
                                pbf.t[:, sub * 128:(sub + 1) * 128], vb.t[:, sub * 128:(sub + 1) * 128], identb.t[:, :]),
                                reads=vb.b + identb.b, writes=pbf.b)
                        vc = ch - 2 * NQC
                        P.op("dve", lambda e, vc=vc: e.tensor_copy(
                            vtile.t[:, :, vc * 128:(vc + 1) * 128], pbf.t[:, 0:512].rearrange("p (s c) -> p s c", c=128)),
                            reads=pbf.b, writes=vtile.b)
                        return
                    r = st["r"]
                    P.op("act", lambda e, r=r: e.activation(r.t[:, :], r.t[:, :], AF.Sqrt), reads=r.b, writes=r.b)

                def stB3(st):
                    ch, pq = st["ch"], st["p"]
                    if ch >= 2 * NQC:
                        return
                    r = st["r"]
                    q = qn[ch % NB_]
                    st["q"] = q
                    P.op("dve", lambda e, r=r: e.reciprocal(r.t[:, :], r.t[:, :]), reads=r.b, writes=r.b)
                    gi = 0 if ch < NQC else 1
                    P.op("dve", lambda e, q=q, r=r, pq=pq, gi=gi: e.scalar_tensor_tensor(
                        q.t[:, :], pq.t[:, :], qkg.t[:, gi:gi + 1], r.t[:, :], ALU.mult, ALU.mult),
                        reads=pq.b + qkg.b + r.b, writes=q.b)

                def stC1(st):
                    ch = st["ch"]
                    if ch >= 2 * NQC:
                        return
                    q = st["q"]
                    rb = tr_banks[ch % 2]
                    st["rb"] = rb
                    P.op("pe", lambda e, q=q, rb=rb: e.matmul(rb.t[0:32, :], rotm.t[:, :], q.t[0:32, :], start=True, stop=True),
                         reads=q.b + rotm.b, writes=rb.b)
                    r1 = rt1s[ch % 4]
                    st["r1"] = r1
                    P.op("pool", lambda e, q=q, t0=t0, r1=r1: e.tensor_tensor(r1.t[:, :], q.t[0:32, :], cosT.t[:, t0:t0 + NT], ALU.mult),
                         reads=q.b + cosT.b, writes=r1.b)

                def stC2(st):
                    ch = st["ch"]
                    if ch >= 2 * NQC:
                        return
                    q, rb, r1 = st["q"], st["rb"], st["r1"]
                    r2 = rt2s[ch % 2]
                    st["r2"] = r2
                    P.op("dve", lambda e, rb=rb, t0=t0, r2=r2: e.tensor_tensor(r2.t[:, :], rb.t[0:32, :], sinT.t[:, t0:t0 + NT], ALU.mult),
                         reads=rb.b + sinT.b, writes=r2.b)

                def stC3(st):
                    ch = st["ch"]
                    if ch >= 2 * NQC:
                        return
                    q, r1, r2 = st["q"], st["r1"], st["r2"]
                    P.op("pool", lambda e, q=q, r1=r1, r2=r2: e.tensor_tensor(q.t[0:32, :], r1.t[:, :], r2.t[:, :], ALU.add),
                         reads=r1.b + r2.b, writes=q.b)

                def stC4(st):
                    ch = st["ch"]
                    if ch >= 2 * NQC:
                        return
                    q = st["q"]
                    qb_ = qkb[actr["qkb"] % len(qkb)]
                    actr["qkb"] += 1
                    P.op("act", lambda e, q=q, qb_=qb_: e.activation(qb_.t[:, :], q.t[:, :], AF.Copy), reads=q.b, writes=qb_.b)
                    P.dma("act", qkT_d[ch, :, t0:t0 + NT], qb_.t[:, :], reads=qb_.b, writes=[qk_bufs[ch]])

                stages = [stA, stB1, stB2, stB3, stC1, stC2, stC3, stC4]
                sts = []
                nch = 3 * NQC
                for i in range(nch + len(stages) - 1):
                    if bg_a is not None:
                        job = next(bg_a, None)
                        if job is not None:
                            bg_step(job, ("pool", "act", "pool")[bg["i"] % 3], "sp", "pool")
                    for lag, fn in enumerate(stages):
                        j = i - lag
                        if 0 <= j < nch:
                            if lag == 0:
                                sts.append(fn(j))
                            else:
                                fn(sts[j])
                P.dma("act", V_d[t0:t0 + NT, :].rearrange("(s p) c -> p s c", p=128), vtile.t[:, :, :],
                      reads=vtile.b, writes=[v_buf])
            if bg_a is not None:
                for job in bg_a:
                    bg_step(job, ("pool", "act", "pool")[bg["i"] % 3], "sp", "pool")
            scA.__exit__(None, None, None)

            scB = Scope()
            scB.__enter__()
            qk_t = [[sb("qk_%d_%d" % (i, j), [128, T], BF16) for j in range(4)] for i in range(2)]
            v_t = [sb("v_%d" % i, [128, NKT, 257], BF16) for i in range(2)]
            maskt = sb("maskt", [128, 2, 256], BF16)
            pT = [sb("pT%d" % i, [128, 256], BF16) for i in range(5)]
            oacc = [[sb("oacc%d_%d" % (c, sub), [128, 257], F32) for sub in range(2)] for c in range(2)]
            sm2 = [sb("sm2_%d" % i, [128, 8], F32) for i in range(2)]
            ot = [sb("ot%d" % i, [128, 256], F32) for i in range(2)]
            osq = sb("osq", [128, 256], F32)
            onb = [sb("onb%d" % i, [128, 256], BF16) for i in range(2)]
            oTt = [sb("oTt%d" % i, [128, 2, 128], BF16) for i in range(2)]
            P.dma("sp", maskt.t[:, :, :], mask_in[:, :, :], reads=[in_buf], writes=maskt.b)
            for i in range(2):
                P.op("dve", lambda e, i=i: e.memset(v_t[i].t[:, :, 256:257], 1.0), writes=v_t[i].b)
            acc_banks = [[pbank[0], pbank[1]], [pbank[2], pbank[3]]]
            sc_banks = [pbank[4], pbank[5], pbank[6]]
            bg_att = None
            if pair and BG_ENABLE and "ffn01" in phases:
                bg_alloc()
                bg_att = bg_chain(bg_jobs_in(attnwo_in, "awo"), bg_jobs_in(ffnwi_in["01"], "fi01"), bg_jobs_out(ffnwo_in["01"], "fo01"))
                nfront = NHL * sum(2 * (2 * qb_ + 2) for qb_ in range(NQB))
                BG_EVERY = max(1, nfront // (16 + 88 + 64 + 8))
            bctr = {"sc": 0, "pT": 0, "o": 0}
            from collections import deque
            LA = 2
            pendq = deque()

            def front(h, qb, c, kt, qkh):
                q0 = qb * 256
                sb_ = sc_banks[bctr["sc"] % 3]
                bctr["sc"] += 1
                P.op("pe", lambda e, sb_=sb_, kt=kt, c=c, q0=q0, qkh=qkh: e.matmul(
                    sb_.t[:, 0:256], qkh[2 + c].t[:, kt * 128:(kt + 1) * 128], qkh[c].t[:, q0:q0 + 256],
                    start=True, stop=True), reads=qkh[2 + c].b + qkh[c].b, writes=sb_.b)
                p_ = pT[bctr["pT"] % len(pT)]
                bctr["pT"] += 1
                P.op("act", lambda e, p_=p_, sb_=sb_: e.activation(
                    p_.t[:, :], sb_.t[:, 0:256], AF.Exp, bias=negmb.t[:, 0:1], scale=scale),
                    reads=sb_.b + negmb.b, writes=p_.b)
                if kt >= 2 * qb:
                    mi = kt - 2 * qb
                    P.op("dve", lambda e, p_=p_, mi=mi: e.tensor_tensor(p_.t[:, :], p_.t[:, :], maskt.t[:, mi, :], ALU.mult),
                         reads=p_.b + maskt.b, writes=p_.b)
                return p_

            def back(h, qb, c, kt, vh, p_):
                q0 = qb * 256
                nkt = 2 * qb + 2
                accs = acc_banks[c]
                for sub in range(2):
                    last_kt = 2 * qb + sub
                    if kt > last_kt:
                        continue
                    P.op("pe", lambda e, p_=p_, sub=sub, kt=kt, vh=vh, accs=accs, last_kt=last_kt: e.matmul(
                        accs[sub].t[:, 0:257], p_.t[:, sub * 128:(sub + 1) * 128], vh.t[:, kt, :],
                        start=(kt == 0), stop=(kt == last_kt)), reads=p_.b + vh.b, writes=accs[sub].b)
                if kt != nkt - 1:
                    return
                for sub in range(2):
                    if sub == 0:
                        P.op("act", lambda e, c=c, sub=sub, accs=accs: e.activation(
                            oacc[c][sub].t[:, :], accs[sub].t[:, 0:257], AF.Copy), reads=accs[sub].b, writes=oacc[c][sub].b)
                    else:
                        P.op("dve", lambda e, c=c, sub=sub, accs=accs: e.tensor_copy(
                            oacc[c][sub].t[:, :], accs[sub].t[:, 0:257]), reads=accs[sub].b, writes=oacc[c][sub].b)
                if c != 1:
                    return
                for sub in range(2):
                    k_ = bctr["o"] % 2
                    bctr["o"] += 1
                    sm_ = sm2[k_]
                    o_ = ot[k_]
                    on_ = onb[k_]
                    oT_ = oTt[k_]
                    o0 = oacc[0][sub]
                    o1 = oacc[1][sub]
                    P.op("dve", lambda e, sm_=sm_, o0=o0: e.reciprocal(sm_.t[:, 0:1], o0.t[:, 256:257]), reads=o0.b, writes=sm_.b)
                    P.op("dve", lambda e, sm_=sm_, o1=o1: e.reciprocal(sm_.t[:, 1:2], o1.t[:, 256:257]), reads=o1.b + sm_.b, writes=sm_.b)
                    P.op("dve", lambda e, sm_=sm_: e.tensor_tensor(sm_.t[:, 2:3], sm_.t[:, 1:2], nlam.t[:, 0:1], ALU.mult),
                         reads=sm_.b + nlam.b, writes=sm_.b)
                    P.op("dve", lambda e, sm_=sm_, o_=o_, o0=o0: e.tensor_scalar(
                        o_.t[:, :], o0.t[:, 0:256], sm_.t[:, 0:1], None, ALU.mult), reads=o0.b + sm_.b, writes=o_.b)
                    P.op("dve", lambda e, sm_=sm_, o_=o_, o1=o1: e.scalar_tensor_tensor(
                        o_.t[:, :], o1.t[:, 0:256], sm_.t[:, 2:3], o_.t[:, :], ALU.mult, ALU.add),
                        reads=o1.b + sm_.b + o_.b, writes=o_.b)
                    P.op("dve", lambda e, o_=o_: e.tensor_tensor(osq.t[:, :], o_.t[:, :], o_.t[:, :], ALU.mult), reads=o_.b, writes=osq.b)
                    P.op("dve", lambda e, sm_=sm_: e.reduce_sum(sm_.t[:, 3:4], osq.t[:, :], AX.X), reads=osq.b + sm_.b, writes=sm_.b)
                    P.op("dve", lambda e, sm_=sm_: e.tensor_scalar(sm_.t[:, 4:5], sm_.t[:, 3:4], 1.0 / 256, EPS, ALU.mult, ALU.add),
                         reads=sm_.b, writes=sm_.b)
                    P.op("act", lambda e, sm_=sm_: e.activation(sm_.t[:, 5:6], sm_.t[:, 4:5], AF.Sqrt), reads=sm_.b, writes=sm_.b)
                    P.op("dve", lambda e, sm_=sm_: e.reciprocal(sm_.t[:, 6:7], sm_.t[:, 5:6]), reads=sm_.b, writes=sm_.b)
                    P.op("dve", lambda e, sm_=sm_, o_=o_, on_=on_: e.scalar_tensor_tensor(
                        on_.t[:, :], o_.t[:, :], sm_.t[:, 6:7], subg.t[:, :], ALU.mult, ALU.mult),
                        reads=o_.b + sm_.b + subg.b, writes=on_.b)
                    for fc in range(2):
                        P.op("pe", lambda e, on_=on_, fc=fc: e.transpose(
                            pbf.t[:, fc * 128:(fc + 1) * 128], on_.t[:, fc * 128:(fc + 1) * 128], identb.t[:, :]),
                            reads=on_.b + identb.b, writes=pbf.b)
                    P.op("act", lambda e, oT_=oT_: e.activation(
                        oT_.t[:, :, :], pbf.t[:, 0:256].rearrange("p (f c) -> p f c", c=128), AF.Copy), reads=pbf.b, writes=oT_.b)
                    qs = q0 + sub * 128
                    if pair:
                        cgx = cg_o[qs // OTC]
                        P.dma("pool", cgx.i[h * 256:(h + 1) * 256, qs % OTC:qs % OTC + 128].rearrange("(f p) q -> p f q", p=128),
                              oT_.t[:, :, :], reads=oT_.b, writes=[cgx.ib])
                    else:
                        P.dma("pool", oT_d[h * 256:(h + 1) * 256, qs:qs + 128].rearrange("(f p) q -> p f q", p=128),
                              oT_.t[:, :, :], reads=oT_.b, writes=[oT_buf])

            for h in range(NHL):
                qkh = qk_t[h % 2]
                vh = v_t[h % 2]
                for c in range(2):
                    P.dma("sp", qkh[c].t[:, :], qkT_d[h * 2 + c, :, :], reads=[qk_bufs[h * 2 + c]], writes=qkh[c].b)
                    P.dma("sp", qkh[2 + c].t[:, :], qkT_d[NQC + h * 2 + c, :, :], reads=[qk_bufs[NQC + h * 2 + c]], writes=qkh[2 + c].b)
                P.dma("act", vh.t[:, :, 0:256], V_d[:, h * 256:(h + 1) * 256].rearrange("(kt p) c -> p kt c", p=128),
                      reads=[v_buf], writes=vh.b)
                for qb in range(NQB):
                    for c in range(2):
                        for kt in range(2 * qb + 2):
                            bctr["bg"] = bctr.get("bg", 0) + 1
                            if bg_att is not None and bctr["bg"] % BG_EVERY == 0:
                                job = next(bg_att, None)
                                if job is not None:
                                    bg_step(job, ("dve", "pool", "dve")[bg["i"] % 3], "sp", "pool")
                            p_ = front(h, qb, c, kt, qkh)
                            pendq.append((h, qb, c, kt, vh, p_))
                            if len(pendq) > LA:
                                back(*pendq.popleft())
            while pendq:
                back(*pendq.popleft())
            if bg_att is not None:
                for job in bg_att:
                    bg_step(job, ("dve", "pool", "act")[bg["i"] % 3], "sp", "pool")
            scB.__exit__(None, None, None)

            if pair:
                for cgx in cg_o:
                    cgx.go()
            scC = Scope()
            scC.__enter__()
            if pair:
                ocand = [sb("ocand%d" % i, [128, KC, NT], BF16) for i in range(2)]
            wst[:] = [sb("wst%d" % i, [128, KC, 128], F32) for i in range(2)]
            wbf[:] = [sb("wbf%d" % i, [128, KC, 128], BF16) for i in range(5)]
            if last:
                xtok[:] = [sb("xtok%d" % i, [128, D], F32) for i in range(2)]
            for tt in range(NTT):
                t0 = tt * NT
                load_xT_tile(tt, False)
                if pair:
                    for r_ in range(2):
                        g0 = r_ * TL + t0
                        cgx = cg_o[g0 // OTC]
                        src = cgx.o[:, g0 % OTC:g0 % OTC + NT].rearrange("(kc p) t -> p kc t", p=128)
                        P.dma(("sp", "act")[r_], ocand[r_].t[:, :, :], src, reads=[cgx.ob], writes=ocand[r_].b)
                    select2(hT.t[:, :, :], ocand[0].t[:, :, :], ocand[1].t[:, :, :], ocand[0].t[:, :, :],
                            ocand[0].b + ocand[1].b, hT.b + ocand[0].b)
                else:
                    src = oT_d[:, t0:t0 + NT].rearrange("(kc p) t -> p kc t", p=128)
                    P.dma("sp", hT.t[:, :, :], src, reads=[oT_buf], writes=hT.b)
                for dc in range(KC):
                    wo = load_w_chunk(w_out, dc * 128, "awo")
                    po = rot(mm_banks, "pb")
                    for kc in range(KC):
                        P.op("pe", lambda e, wo=wo, kc=kc, po=po: e.matmul(
                            po.t[:, :], wo.t[:, kc, :], hT.t[:, kc, :], start=(kc == 0), stop=(kc == KC - 1)),
                            reads=wo.b + hT.b, writes=po.b)
                    P.op("dve", lambda e, dc=dc, po=po: e.scalar_tensor_tensor(
                        xT.t[:, dc, :], po.t[:, :], modG[l].t[:, s, dc:dc + 1], xT.t[:, dc, :], ALU.mult, ALU.add),
                        reads=po.b + modG[l].b + xT.b, writes=xT.b)
                store_xT_tile(tt, last)
            scC.__exit__(None, None, None)
            sc.__exit__(None, None, None)


        def ssm(l, first=False, last=False):
            s = 1
            sc = Scope()
            sc.__enter__()
            rho = sb("rho", [128, NGP], F32)
            fq = sb("fq", [128, NGP], F32)
            c512 = sb("c512", [128, NGP], F32)
            s512 = sb("s512", [128, NGP], F32)
            pm = sb("pm", [128, 2], F32)
            dT = sb("dT", [128, 16], F32)
            iota = sb("iota", [128, NT], F32)
            P.dma("sp", pm.t[:, :], pm_in[:, :], reads=[in_buf], writes=pm.b)
            P.dma("sp", dT.t[:, :], ssmd_in[:, :], reads=[in_buf], writes=dT.b)
            P.dma("sp", iota.t[:, :], iota_in[:, :], reads=[in_buf], writes=iota.b)
            scL = Scope()
            scL.__enter__()
            LB = [sb("LB%d" % i, [128, NGP, 128], BF16) for i in range(2)]
            LC = [sb("LC%d" % i, [128, NGP, 128], BF16) for i in range(2)]
            for i in range(2):
                P.op("pool", lambda e, i=i: e.memset(LB[i].t[:, :, :], 0.0), writes=LB[i].b)
                P.op("pool", lambda e, i=i: e.memset(LC[i].t[:, :, :], 0.0), writes=LC[i].b)

            def frac_wrap(dst, src, W, tmpi, tmpf_):
                P.op("dve", lambda e: e.tensor_copy(tmpi, src), reads=[gen_b], writes=[gen_b])
                P.op("dve", lambda e: e.tensor_copy(tmpf_, tmpi), reads=[gen_b], writes=[gen_b])
                P.op("dve", lambda e: e.tensor_tensor(dst, src, tmpf_, ALU.subtract), reads=[gen_b], writes=[gen_b])
                wrap_half(dst, tmpf_)

            def wrap_half(dst, tmpf_):
                P.op("dve", lambda e: e.tensor_scalar(tmpf_, dst, 0.5, None, ALU.is_gt), reads=[gen_b], writes=[gen_b])
                P.op("dve", lambda e: e.tensor_tensor(dst, dst, tmpf_, ALU.subtract), reads=[gen_b], writes=[gen_b])
                P.op("dve", lambda e: e.tensor_scalar(tmpf_, dst, -0.5, None, ALU.is_lt), reads=[gen_b], writes=[gen_b])
                P.op("dve", lambda e: e.tensor_tensor(dst, dst, tmpf_, ALU.add), reads=[gen_b], writes=[gen_b])

            gen_b = Buf("ssm_setup")

            def G(eng, fn):
                P.op(eng, fn, reads=[gen_b], writes=[gen_b])

            scS = Scope()
            scS.__enter__()
            ap_ = sb("a_pair", [128, 3, NGP], F32)
            P.dma("sp", ap_.t[:, :, :], apair_in[:, :, :], reads=[in_buf], writes=[gen_b])
            t64 = [sb("t64_%d" % i, [128, NGP], F32) for i in range(4)]
            t64i = sb("t64i", [128, NGP], I32)
            dtp, thp, fr64, tm64 = [t.t[:, :] for t in t64]
            G("act", lambda e: e.activation(dtp, ap_.t[:, 2, :], AF.Exp))
            G("dve", lambda e: e.tensor_tensor(thp, dtp, ap_.t[:, 0, :], ALU.mult))
            G("act", lambda e: e.activation(rho.t[:, :], thp, AF.Exp))
            G("dve", lambda e: e.tensor_tensor(thp, dtp, ap_.t[:, 1, :], ALU.mult))
            G("dve", lambda e: e.tensor_scalar(thp, thp, 1.0 / TWO_PI, None, ALU.mult))
            frac_wrap(fq.t[:, :], thp, NGP, t64i.t[:, :], tm64)
            G("dve", lambda e: e.tensor_scalar(thp, fq.t[:, :], float(NT), None, ALU.mult))
            frac_wrap(fr64, thp, NGP, t64i.t[:, :], tm64)
            G("act", lambda e: e.activation(s512.t[:, :], fr64, AF.Sin, scale=TWO_PI))
            G("dve", lambda e: e.tensor_scalar(fr64, fr64, 0.25, None, ALU.add))
            wrap_half(fr64, tm64)
            G("act", lambda e: e.activation(c512.t[:, :], fr64, AF.Sin, scale=TWO_PI))
            W = NKL * 64
            af = sb("a_feat", [128, 3, W], F32)
            bf_ = sb("b_feat", [128, 2, W], F32)
            P.dma("sp", af.t[:, :, :], afeat_in[:, :, :], reads=[in_buf], writes=[gen_b])
            P.dma("act", bf_.t[:, :, :], bfeat_in[:, :, :], reads=[in_buf], writes=[gen_b])
            tw = [sb("tw%d" % i, [128, W], F32) for i in range(8)]
            twi = sb("twi", [128, W], I32)
            dtf, mag, th, cs, sn, t5, t6, t7 = [t.t[:, :] for t in tw]
            ar = af.t[:, 0, :]
            ai = af.t[:, 1, :]
            G("act", lambda e: e.activation(dtf, af.t[:, 2, :], AF.Exp))
            G("dve", lambda e: e.tensor_tensor(th, dtf, ar, ALU.mult))
            G("act", lambda e: e.activation(mag, th, AF.Exp))
            G("dve", lambda e: e.tensor_tensor(th, dtf, ai, ALU.mult))
            G("dve", lambda e: e.tensor_scalar(th, th, 1.0 / TWO_PI, None, ALU.mult))
            frac_wrap(t5, th, W, twi.t[:, :], t6)
            G("act", lambda e: e.activation(sn, t5, AF.Sin, scale=TWO_PI))
            G("dve", lambda e: e.tensor_scalar(t5, t5, 0.25, None, ALU.add))
            wrap_half(t5, t6)
            G("act", lambda e: e.activation(cs, t5, AF.Sin, scale=TWO_PI))
            G("dve", lambda e: e.tensor_tensor(cs, cs, mag, ALU.mult))
            G("dve", lambda e: e.tensor_scalar(cs, cs, -1.0, None, ALU.add))
            G("dve", lambda e: e.tensor_tensor(sn, sn, mag, ALU.mult))
            G("dve", lambda e: e.tensor_tensor(t5, ar, ar, ALU.mult))
            G("dve", lambda e: e.tensor_tensor(t6, ai, ai, ALU.mult))
            G("dve", lambda e: e.tensor_tensor(t5, t5, t6, ALU.add))
            G("dve", lambda e: e.reciprocal(t5, t5))
            G("dve", lambda e: e.tensor_tensor(t6, cs, ar, ALU.mult))
            G("dve", lambda e: e.tensor_tensor(t7, sn, ai, ALU.mult))
            G("dve", lambda e: e.tensor_tensor(t6, t6, t7, ALU.add))
            G("dve", lambda e: e.tensor_tensor(t6, t6, t5, ALU.mult))
            G("dve", lambda e: e.tensor_tensor(t7, sn, ar, ALU.mult))
            G("dve", lambda e: e.tensor_tensor(mag, cs, ai, ALU.mult))
            G("dve", lambda e: e.tensor_tensor(t7, t7, mag, ALU.subtract))
            G("dve", lambda e: e.tensor_tensor(t7, t7, t5, ALU.mult))
            br = bf_.t[:, 0, :]
            bi = bf_.t[:, 1, :]
            G("dve", lambda e: e.tensor_tensor(cs, t6, br, ALU.mult))
            G("dve", lambda e: e.tensor_tensor(mag, t7, bi, ALU.mult))
            G("dve", lambda e: e.tensor_tensor(cs, cs, mag, ALU.subtract))
            G("dve", lambda e: e.tensor_tensor(sn, t6, bi, ALU.mult))
            G("dve", lambda e: e.tensor_tensor(mag, t7, br, ALU.mult))
            G("dve", lambda e: e.tensor_tensor(sn, sn, mag, ALU.add))
            bb = [tw[3].t, tw[4].t]
            for gp in range(NGP):
                kc, j = gp // 4, gp % 4
                r0 = 32 * j
                for ri in range(2):
                    for gi in range(2):
                        eng = "dve" if gi == 0 else "pool"
                        P.op(eng, lambda e, ri=ri, gi=gi, gp=gp, kc=kc, r0=r0: e.tensor_scalar(
                            LB[ri].t[r0:r0 + 32, gp, gi * 64:(gi + 1) * 64], bb[ri][r0:r0 + 32, kc * 64:(kc + 1) * 64],
                            pm.t[r0:r0 + 32, gi:gi + 1], None, ALU.mult),
                            reads=[gen_b] + pm.b, writes=LB[ri].b)
            cp = sb("c_pair", [128, 2, NGP, 16], F32)
            P.dma("sp", cp.t[:, :, :, :], cpair_in[:, :, :, :], reads=[in_buf], writes=[gen_b])
            for gp in range(NGP):
                j = gp % 4
                for ri in range(2):
                    for gi in range(2):
                        c0 = 32 * j + gi * 16
                        eng = "dve" if gi == 0 else "pool"
                        sgn = 1.0 if ri == 0 else -1.0
                        P.op(eng, lambda e, ri=ri, gi=gi, gp=gp, c0=c0, sgn=sgn: e.tensor_scalar(
                            LC[ri].t[gi * 64:(gi + 1) * 64, gp, c0:c0 + 16], cp.t[gi * 64:(gi + 1) * 64, ri, gp, :],
                            sgn, None, ALU.mult), reads=[gen_b], writes=LC[ri].b)
            scS.__exit__(None, None, None)

            scA = Scope()
            scA.__enter__()
            if first:
                xtok[:] = [sb("xtok%d" % i, [128, D], F32) for i in range(2)]
            for tt in range(NTT):
                t0 = tt * NT
                load_xT_tile(tt, first)
                if first:
                    store_xT_tile(tt, False)
                norm_mod(l, s)
                if pair:
                    P.dma("pool", cg_u[tt].i[:, :].rearrange("(kc p) t -> p kc t", p=128), hT.t[:, :, :],
                          reads=hT.b, writes=[cg_u[tt].ib])
                    cg_u[tt].go()
                else:
                    P.dma("pool", uT_d[:, t0:t0 + NT].rearrange("(kc p) t -> p kc t", p=128), hT.t[:, :, :],
                          reads=hT.b, writes=[u_buf])
            if pair:
                ucand = [[sb("ucand%d_%d" % (i, j), [128, NKL, NT], BF16) for j in range(2)] for i in range(2)]
                um_buf = Buf("um")
                it_ = 0
                for tt in range(NTT):
                    for r_ in range(2):
                        cs_ = ucand[it_ % 2]
                        it_ += 1
                        for j in range(2):
                            row0 = r_ * D + j * NKL * 128
                            P.dma(("sp", "act")[j], cs_[j].t[:, :, :],
                                  cg_u[tt].o[row0:row0 + NKL * 128, :].rearrange("(kc p) t -> p kc t", p=128),
                                  reads=[cg_u[tt].ob], writes=cs_[j].b)
                        select2(cs_[0].t[:, :, :], cs_[0].t[:, :, :], cs_[1].t[:, :, :], cs_[0].t[:, :, :], cs_[0].b + cs_[1].b, cs_[0].b)
                        g0 = r_ * TL + tt * NT
                        P.dma("pool", uT_d[:, g0:g0 + NT].rearrange("(kc p) t -> p kc t", p=128), cs_[0].t[:, :, :],
                              reads=cs_[0].b, writes=[um_buf])
                u_rd = um_buf
            else:
                u_rd = u_buf
            scA.__exit__(None, None, None)

            scB = Scope()
            scB.__enter__()
            cosTs = [sb("s_cos%d" % i, [128, NT], F32) for i in range(2)]
            sinTs = [sb("s_sin%d" % i, [128, NT], F32) for i in range(2)]
            rhobs = [sb("s_rhob%d" % i, [128, NT], F32) for i in range(2)]
            tg = [sb("s_tg%d" % i, [128, NT], F32) for i in range(2)]
            tgi = sb("s_tgi", [128, NT], I32)
            uch = [sb("s_u%d" % i, [128, NT], BF16) for i in range(3)]
            bsb = [[sb("s_b%d_%d" % (k, i), [128, NT], F32) for i in range(2)] for k in range(2)]
            fq_ = [[sb("s_f%d_%d" % (k, i), [128, NT], F32) for i in range(4)] for k in range(2)]
            bq_ = [sb("s_q%d" % i, [128, NT], F32) for i in range(4)]
            btr = [sb("s_btr%d" % i, [128, NT], F32) for i in range(2)]
            bti = [sb("s_bti%d" % i, [128, NT], F32) for i in range(2)]
            st_r = [sb("s_str%d" % i, [128, NT], F32) for i in range(2)]
            st_i = [sb("s_sti%d" % i, [128, NT], F32) for i in range(2)]
            sbf_r = [sb("s_sbr%d" % i, [128, NT], BF16) for i in range(2)]
            sbf_i = [sb("s_sbi%d" % i, [128, NT], BF16) for i in range(2)]
            y32 = [sb("s_y%d" % i, [128, NT], F32) for i in range(2)]
            init = [sb("s_init%d" % i, [128, 4], F32) for i in range(2)]
            b_banks = [[pbank[0], pbank[1]], [pbank[2], pbank[3]]]
            y_banks = [pbank[4], pbank[5]]

            def tables(gp):
                cosT, sinT, rhob = cosTs[gp % 2], sinTs[gp % 2], rhobs[gp % 2]
                fr = tg[0].t[:, :]
                tm = tg[1].t[:, :]
                tb = tg[0].b + tg[1].b + tgi.b
                P.op("dve", lambda e, gp=gp: e.tensor_scalar(tm, iota.t[:, :], fq.t[:, gp:gp + 1], None, ALU.mult),
                     reads=iota.b + fq.b + tb, writes=tb)

                def T_(eng, fn, extra_w=()):
                    P.op(eng, fn, reads=tb, writes=tb + list(extra_w))
                T_("dve", lambda e: e.tensor_copy(tgi.t[:, :], tm))
                T_("dve", lambda e: e.tensor_copy(fr, tgi.t[:, :]))
                T_("dve", lambda e: e.tensor_tensor(fr, tm, fr, ALU.subtract))

                def wrapT():
                    T_("dve", lambda e: e.tensor_scalar(tm, fr, 0.5, None, ALU.is_gt))
                    T_("dve", lambda e: e.tensor_tensor(fr, fr, tm, ALU.subtract))
                    T_("dve", lambda e: e.tensor_scalar(tm, fr, -0.5, None, ALU.is_lt))
                    T_("dve", lambda e: e.tensor_tensor(fr, fr, tm, ALU.add))
                wrapT()
                T_("act", lambda e: e.activation(sinT.t[:, :], fr, AF.Sin, scale=TWO_PI), extra_w=sinT.b)
                T_("dve", lambda e: e.tensor_scalar(fr, fr, 0.25, None, ALU.add))
                wrapT()
                T_("act", lambda e: e.activation(cosT.t[:, :], fr, AF.Sin, scale=TWO_PI), extra_w=cosT.b)
                P.op("pool", lambda e, gp=gp: e.tensor_scalar(rhob.t[:, :], iota.t[:, :], 0.0, rho.t[:, gp:gp + 1], ALU.mult, ALU.add),
                     reads=iota.b + rho.b, writes=rhob.b)
                ini = init[gp % 2]
                P.op("pool", lambda e, ini=ini: e.memset(ini.t[:, :], 0.0), writes=ini.b)

            def F1(it, gp, tt):
                k = it % 2
                kc = gp // 4
                t0 = tt * NT
                cosT, sinT = cosTs[gp % 2], sinTs[gp % 2]
                u_ = uch[it % 3]
                P.dma("sp", u_.t[:, :], uT_d[kc * 128:(kc + 1) * 128, t0:t0 + NT], reads=[u_rd], writes=u_.b)
                pre, pim = b_banks[k]
                P.op("pe", lambda e: e.matmul(pre.t[:, :], LB[0].t[:, gp, :], u_.t[:, :], start=True, stop=True),
                     reads=LB[0].b + u_.b, writes=pre.b)
                P.op("pe", lambda e: e.matmul(pim.t[:, :], LB[1].t[:, gp, :], u_.t[:, :], start=True, stop=True),
                     reads=LB[1].b + u_.b, writes=pim.b)
                bre, bim = bsb[k]
                f1, f2, f3, f4 = fq_[k]
                P.op("act", lambda e: e.activation(bre.t[:, :], pre.t[:, :], AF.Copy), reads=pre.b, writes=bre.b)
                P.op("act", lambda e: e.activation(bim.t[:, :], pim.t[:, :], AF.Copy), reads=pim.b, writes=bim.b)
                P.op("dve", lambda e: e.tensor_tensor(f1.t[:, :], bre.t[:, :], cosT.t[:, :], ALU.mult), reads=bre.b + cosT.b, writes=f1.b)
                P.op("dve", lambda e: e.tensor_tensor(f2.t[:, :], bim.t[:, :], sinT.t[:, :], ALU.mult), reads=bim.b + sinT.b, writes=f2.b)
                P.op("pool", lambda e: e.tensor_tensor(f3.t[:, :], bim.t[:, :], cosT.t[:, :], ALU.mult), reads=bim.b + cosT.b, writes=f3.b)
                P.op("pool", lambda e: e.tensor_tensor(f4.t[:, :], bre.t[:, :], sinT.t[:, :], ALU.mult), reads=bre.b + sinT.b, writes=f4.b)

            def F2(it, gp, tt):
                k = it % 2
                f1, f2, f3, f4 = fq_[k]
                br_, bi_ = btr[k], bti[k]
                P.op("dve", lambda e: e.tensor_tensor(br_.t[:, :], f1.t[:, :], f2.t[:, :], ALU.add), reads=f1.b + f2.b, writes=br_.b)
                P.op("pool", lambda e: e.tensor_tensor(bi_.t[:, :], f3.t[:, :], f4.t[:, :], ALU.subtract), reads=f3.b + f4.b, writes=bi_.b)

            def F3(it, gp, tt):
                k = it % 2
                rhob, ini = rhobs[gp % 2], init[gp % 2]
                br_, bi_ = btr[k], bti[k]
                sr, si = st_r[k], st_i[k]
                P.op("dve", lambda e: e.tensor_tensor_scan(sr.t[:, :], rhob.t[:, :], br_.t[:, :], ini.t[:, 0:1], ALU.mult, ALU.add),
                     reads=rhob.b + br_.b + ini.b, writes=sr.b)
                P.op("dve", lambda e: e.tensor_tensor_scan(si.t[:, :], rhob.t[:, :], bi_.t[:, :], ini.t[:, 1:2], ALU.mult, ALU.add),
                     reads=rhob.b + bi_.b + ini.b, writes=si.b)
                P.op("dve", lambda e: e.tensor_tensor(ini.t[:, 3:4], sr.t[:, NT - 1:NT], s512.t[:, gp:gp + 1], ALU.mult),
                     reads=sr.b + s512.b + ini.b, writes=ini.b)
                P.op("dve", lambda e: e.tensor_tensor(ini.t[:, 2:3], si.t[:, NT - 1:NT], s512.t[:, gp:gp + 1], ALU.mult),
                     reads=si.b + s512.b + ini.b, writes=ini.b)
                P.op("dve", lambda e: e.scalar_tensor_tensor(ini.t[:, 1:2], si.t[:, NT - 1:NT], c512.t[:, gp:gp + 1], ini.t[:, 3:4],
                                                             ALU.mult, ALU.add), reads=si.b + c512.b + ini.b, writes=ini.b)
                P.op("dve", lambda e: e.scalar_tensor_tensor(ini.t[:, 0:1], sr.t[:, NT - 1:NT], c512.t[:, gp:gp + 1], ini.t[:, 2:3],
                                                             ALU.mult, ALU.subtract), reads=sr.b + c512.b + ini.b, writes=ini.b)

            def K1(it, gp, tt):
                k = it % 2
                cosT, sinT = cosTs[gp % 2], sinTs[gp % 2]
                sr, si = st_r[k], st_i[k]
                b1, b2, b3, b4 = bq_
                P.op("dve", lambda e: e.tensor_tensor(b1.t[:, :], sr.t[:, :], cosT.t[:, :], ALU.mult), reads=sr.b + cosT.b, writes=b1.b)
                P.op("dve", lambda e: e.tensor_tensor(b2.t[:, :], si.t[:, :], sinT.t[:, :], ALU.mult), reads=si.b + sinT.b, writes=b2.b)
                P.op("pool", lambda e: e.tensor_tensor(b3.t[:, :], si.t[:, :], cosT.t[:, :], ALU.mult), reads=si.b + cosT.b, writes=b3.b)
                P.op("pool", lambda e: e.tensor_tensor(b4.t[:, :], sr.t[:, :], sinT.t[:, :], ALU.mult), reads=sr.b + sinT.b, writes=b4.b)

            def K2(it, gp, tt):
                k = it % 2
                kc, j = gp // 4, gp % 4
                r0 = 32 * j
                t0 = tt * NT
                zr, zi = sbf_r[k], sbf_i[k]
                b1, b2, b3, b4 = bq_
                P.op("dve", lambda e: e.tensor_tensor(zr.t[:, :], b1.t[:, :], b2.t[:, :], ALU.subtract), reads=b1.b + b2.b, writes=zr.b)
                P.op("pool", lambda e: e.tensor_tensor(zi.t[:, :], b3.t[:, :], b4.t[:, :], ALU.add), reads=b3.b + b4.b, writes=zi.b)
                yb = y_banks[k]
                P.op("pe", lambda e: e.matmul(yb.t[:, :], LC[0].t[:, gp, :], zr.t[:, :], start=True, stop=False),
                     reads=LC[0].b + zr.b, writes=yb.b)
                P.op("pe", lambda e: e.matmul(yb.t[:, :], LC[1].t[:, gp, :], zi.t[:, :], start=False, stop=True),
                     reads=LC[1].b + zi.b, writes=yb.b)
                y_ = y32[k]
                P.op("act", lambda e: e.activation(y_.t[r0:r0 + 32, :], yb.t[r0:r0 + 32, :], AF.Copy), reads=yb.b, writes=y_.b)
                if pair:
                    P.dma("act", cg_y[tt].i[kc * 128 + r0:kc * 128 + r0 + 32, :], y_.t[r0:r0 + 32, :], reads=y_.b, writes=[cg_y[tt].ib])
                else:
                    P.dma("act", y_d[kc * 128 + r0:kc * 128 + r0 + 32, t0:t0 + NT], y_.t[r0:r0 + 32, :], reads=y_.b, writes=[y_buf])

            units = []
            for gp in range(NGP):
                for tt in range(NTG):
                    units.append((len(units), gp, tt))
            bg_ssm = None
            if pair and BG_ENABLE and "ffn11" in phases:
                bg_alloc()
                bg_ssm = bg_chain(bg_jobs_in(wglu_in, "glu"), bg_jobs_in(ffnwi_in["11"], "fi11"), bg_jobs_out(ffnwo_in["11"], "fo11"))
            for i, un in enumerate(units):
                if bg_ssm is not None:
                    job = next(bg_ssm, None)
                    if job is not None:
                        bg_step(job, "act", "sp", "pool")
                if un[2] == 0:
                    tables(un[1])
                F1(*un)
                if i >= 1:
                    K1(*units[i - 1])
                F2(*un)
                if i >= 1:
                    K2(*units[i - 1])
                F3(*un)
            K1(*units[-1])
            K2(*units[-1])
            if bg_ssm is not None:
                for job in bg_ssm:
                    bg_step(job, "act", "sp", "pool")
            scB.__exit__(None, None, None)
            scL.__exit__(None, None, None)
            if dbg == "y2":
                dbg2 = nc.dram_tensor("dbg_y", [D, T], F32, kind="ExternalOutput").ap()
                P.dma("sp", dbg2[:, :], y_d[:, :], reads=[y_buf], writes=[Buf("dbg2")])
                P.barrier()

            if pair:
                for cgx in cg_y:
                    cgx.go()
            scC = Scope()
            scC.__enter__()
            if pair:
                ycand = sb("ycand", [128, KC, NT], F32)
            wst[:] = [sb("wst%d" % i, [128, KC, 128], F32) for i in range(2)]
            wbf[:] = [sb("wbf%d" % i, [128, KC, 128], BF16) for i in range(5)]
            h32 = sb("h32", [128, KC, NT], F32)
            yt = sb("yt", [128, KC, NT], F32)
            g1 = [sb("g1_%d" % i, [128, NT], F32) for i in range(2)]
            if last:
                xtok[:] = [sb("xtok%d" % i, [128, D], F32) for i in range(2)]
            GC = 2.0 * math.sqrt(2.0 / math.pi)
            for tt in range(NTT):
                t0 = tt * NT
                load_xT_tile(tt, False)
                norm_mod(l, s, h32=h32, want_bf=False)
                if pair:
                    src0 = cg_y[tt].o[:, :].rearrange("(kc p) t -> p kc t", p=128)
                    src1 = cg_y[NTT + tt].o[:, :].rearrange("(kc p) t -> p kc t", p=128)
                    P.dma("sp", yt.t[:, :, :], src0, reads=[cg_y[tt].ob], writes=yt.b)
                    P.dma("act", ycand.t[:, :, :], src1, reads=[cg_y[NTT + tt].ob], writes=ycand.b)
                    select2(yt.t[:, :, :], yt.t[:, :, :], ycand.t[:, :, :], yt.t[:, :, :], yt.b + ycand.b, yt.b)
                else:
                    src = y_d[:, t0:t0 + NT].rearrange("(kc p) t -> p kc t", p=128)
                    P.dma("sp", yt.t[:, 0:8, :], src[:, 0:8, :], reads=[y_buf], writes=yt.b)
                    P.dma("act", yt.t[:, 8:16, :], src[:, 8:16, :], reads=[y_buf], writes=yt.b)
                for kc in range(KC):
                    ga = g1[kc % 2]
                    P.op("dve", lambda e, kc=kc: e.scalar_tensor_tensor(
                        yt.t[:, kc, :], h32.t[:, kc, :], dT.t[:, kc:kc + 1], yt.t[:, kc, :], ALU.mult, ALU.add),
                        reads=h32.b + dT.b + yt.b, writes=yt.b)
                    P.op("act", lambda e, kc=kc, ga=ga: e.activation(ga.t[:, :], yt.t[:, kc, :], AF.Square), reads=yt.b, writes=ga.b)
                    P.op("dve", lambda e, ga=ga: e.tensor_scalar(ga.t[:, :], ga.t[:, :], 0.044715, 1.0, ALU.mult, ALU.add), reads=ga.b, writes=ga.b)
                    P.op("dve", lambda e, kc=kc, ga=ga: e.tensor_tensor(ga.t[:, :], ga.t[:, :], yt.t[:, kc, :], ALU.mult), reads=ga.b + yt.b, writes=ga.b)
                    P.op("act", lambda e, ga=ga: e.activation(ga.t[:, :], ga.t[:, :], AF.Sigmoid, scale=GC), reads=ga.b, writes=ga.b)
                    P.op("pool", lambda e, kc=kc, ga=ga: e.tensor_tensor(hT.t[:, kc, :], ga.t[:, :], yt.t[:, kc, :], ALU.mult),
                         reads=ga.b + yt.b, writes=hT.b)
                for dc in range(KC):
                    wv = load_w_chunk(wglu_in, dc * 128, "glu")
                    wg = load_w_chunk(wglu_in, D + dc * 128, "glu")
                    pv = rot(mm_banks, "pb")
                    pg = rot(mm_banks, "pb")
                    for kc in range(KC):
                        P.op("pe", lambda e, wv=wv, kc=kc, pv=pv: e.matmul(
                            pv.t[:, :], wv.t[:, kc, :], hT.t[:, kc, :], start=(kc == 0), stop=(kc == KC - 1)),
                            reads=wv.b + hT.b, writes=pv.b)
                    for kc in range(KC):
                        P.op("pe", lambda e, wg=wg, kc=kc, pg=pg: e.matmul(
                            pg.t[:, :], wg.t[:, kc, :], hT.t[:, kc, :], start=(kc == 0), stop=(kc == KC - 1)),
                            reads=wg.b + hT.b, writes=pg.b)
                    tf = rot(tmpf, "tmpf")
                    P.op("act", lambda e, tf=tf, pg=pg: e.activation(tf.t[:, :], pg.t[:, :], AF.Sigmoid), reads=pg.b, writes=tf.b)
                    P.op("dve", lambda e, tf=tf, pv=pv: e.tensor_tensor(tf.t[:, :], tf.t[:, :], pv.t[:, :], ALU.mult),
                         reads=tf.b + pv.b, writes=tf.b)
                    P.op("dve", lambda e, dc=dc, tf=tf: e.scalar_tensor_tensor(
                        xT.t[:, dc, :], tf.t[:, :], modG[l].t[:, s, dc:dc + 1], xT.t[:, dc, :], ALU.mult, ALU.add),
                        reads=tf.b + modG[l].b + xT.b, writes=xT.b)
                store_xT_tile(tt, last)
            scC.__exit__(None, None, None)
            sc.__exit__(None, None, None)

        for i, ph in enumerate(phases):
            first = (i == 0)
            last = (i == len(phases) - 1)
            if ph.startswith("ffn"):
                l = int(ph[3]); fi = int(ph[4])
                bgn = None
                ffn(l, 0 if fi == 0 else 2, fi, first=first, last=last, bg_next=bgn)
            elif ph.startswith("att"):
                attention(int(ph[3]), first=first, last=last)
            elif ph.startswith("ssm"):
                ssm(int(ph[3]), first=first, last=last)

        P.emit()
    return nc


def make_in_maps(inputs, T=SEQ, phases=None, pair=False, n_cores=N_CORES):
    if phases is None:
        phases = ALL_PHASES
    if pair:
        return make_in_maps_pair(inputs, T, phases, n_cores)
    x = np.asarray(inputs["x"], dtype=np.float32)
    c = np.asarray(inputs["c"], dtype=np.float32)
    pos = np.asarray(inputs["positions"], dtype=np.int32)
    shared = {
        "norm_g": np.ascontiguousarray(np.asarray(inputs["norm_g"], np.float32).reshape(96, 128)),
        "ada_b": np.ascontiguousarray(np.asarray(inputs["ada_b"], np.float32).reshape(2, 144, 128)),
        "ident": np.eye(128, dtype=np.float32),
    }
    for l in sorted(set(int(ph[3]) for ph in phases)):
        shared["ada_w%d" % l] = np.asarray(inputs["ada_w"][l], np.float32)
    for ph in phases:
        if ph.startswith("ffn"):
            l = int(ph[3]); fi = int(ph[4])
            shared["ffn_w_in" + ph[3:]] = np.ascontiguousarray(np.asarray(inputs["ffn_w_in"][l, fi], np.float32))
            shared["ffn_w_out" + ph[3:]] = np.ascontiguousarray(np.asarray(inputs["ffn_w_out"][l, fi], np.float32))
    if any(ph.startswith("ssm") for ph in phases):
        p = np.arange(128)
        shared["pm"] = np.stack([1 - (p // 16) % 2, (p // 16) % 2], axis=1).astype(np.float32)
        shared["ssm_dT"] = np.ascontiguousarray(np.asarray(inputs["ssm_d"][0], np.float32).reshape(16, 128).T)
        shared["iota"] = np.ascontiguousarray(np.broadcast_to(np.arange(NT, dtype=np.float32)[None], (128, NT)))
        a_re = np.asarray(inputs["ssm_a_re"][0], np.float32)
        a_im = np.asarray(inputs["ssm_a_im"][0], np.float32)
        ls = np.asarray(inputs["ssm_log_step"][0], np.float32)
        lsb = np.broadcast_to(ls[:, None], (128, 64))
        def pair(a):
            sh = a.shape
            a = a.reshape((64, 2, 64) + sh[2:])
            a = np.moveaxis(a, 0, 2)
            return np.ascontiguousarray(a.reshape((128, 64) + sh[2:]))
        shared["a_pair"] = np.ascontiguousarray(np.stack([pair(a_re), pair(a_im), pair(lsb)], axis=1))
        def feat(a):
            a = a.reshape(16, 8, 64)
            a = np.broadcast_to(a[:, :, None, :], (16, 8, 16, 64))
            return np.ascontiguousarray(a.transpose(1, 2, 0, 3).reshape(128, 1024))
        shared["a_feat"] = np.ascontiguousarray(np.stack([feat(a_re), feat(a_im), feat(lsb)], axis=1))
        def featb(b):
            b = b.reshape(16, 8, 64, 16)
            return np.ascontiguousarray(b.transpose(1, 3, 0, 2).reshape(128, 1024))
        shared["b_feat"] = np.ascontiguousarray(np.stack(
            [featb(np.asarray(inputs["ssm_b_re"][0], np.float32)), featb(np.asarray(inputs["ssm_b_im"][0], np.float32))], axis=1))
        def pairc(c):
            return pair(np.ascontiguousarray(c.transpose(0, 2, 1)))
        shared["c_pair"] = np.ascontiguousarray(np.stack(
            [pairc(np.asarray(inputs["ssm_c_re"][0], np.float32)), pairc(np.asarray(inputs["ssm_c_im"][0], np.float32))], axis=1))
        shared["w_glu"] = np.ascontiguousarray(np.asarray(inputs["ssm_w_glu"][0], np.float32))
    if any(ph.startswith("att") for ph in phases):
        shared["attn_w_in"] = np.ascontiguousarray(np.asarray(inputs["attn_w_in"][0], np.float32))
        shared["attn_w_out"] = np.ascontiguousarray(np.asarray(inputs["attn_w_out"][0], np.float32))
        invf = (500000.0 ** (-np.arange(0, 32, 2, dtype=np.float32) / np.float32(32))).astype(np.float32)
        ropec = np.zeros((32, 4), np.float32)
        ropec[:, 0] = np.concatenate([invf, invf])
        sign = np.concatenate([-np.ones(16), np.ones(16)]).astype(np.float32)
        ropec[:, 1] = sign
        ropec[:, 2] = -math.pi * sign
        ropec[:, 3] = -math.pi
        shared["ropec"] = ropec
        rotm = np.zeros((32, 32), np.float32)
        for m_ in range(32):
            rotm[(m_ + 16) % 32, m_] = 1.0
        shared["rotm"] = rotm
        qg = np.asarray(inputs["attn_q_norm"][0], np.float32)
        kg = np.asarray(inputs["attn_k_norm"][0], np.float32)
        shared["qkg"] = np.ascontiguousarray(np.stack([qg, kg], axis=1))
        shared["qkgrep"] = np.ascontiguousarray(np.broadcast_to(np.stack([qg, kg], axis=0)[None], (128, 2, 128)))
        shared["subg"] = np.ascontiguousarray(np.broadcast_to(np.asarray(inputs["attn_subln"][0], np.float32)[None], (128, 256)))
        shared["lamrep"] = np.ascontiguousarray(np.broadcast_to(np.asarray(inputs["attn_lambda"][0], np.float32)[None], (128, 4, 128)))
        kk = np.arange(128)[:, None]
        qq = np.arange(256)[None, :]
        mask = np.stack([(kk <= qq), (kk + 128 <= qq)], axis=1).astype(np.float32)
        shared["maskc"] = mask.astype(ml_dtypes.bfloat16)
    maps = []
    for core in range(N_CORES):
        b = core % 4
        m = dict(shared)
        m["x"] = np.ascontiguousarray(x[b, :T])
        m["c"] = np.ascontiguousarray(c[b].reshape(16, 128))
        m["pos"] = np.ascontiguousarray(np.broadcast_to(pos[b, :T].reshape(1, T), (32, T)))
        maps.append(m)
    return maps


def make_in_maps_pair(inputs, T, phases, n_cores):
    base = make_in_maps(inputs, T, phases, pair=False)
    TL = T // 2
    x = np.asarray(inputs["x"], np.float32)
    maps = []
    for core in range(n_cores):
        b, r = core // 2, core % 2
        m = dict(base[b])
        m["x"] = np.ascontiguousarray(x[b, r * TL:(r + 1) * TL])
        f = np.zeros((128, 2), np.float32)
        f[:, r] = 1.0
        m["rankf"] = f
        for l in range(2):
            if "ada_w%d" % l in m:
                m["ada_w%d" % l] = np.ascontiguousarray(m["ada_w%d" % l][:, 9216 * r:9216 * (r + 1)])
        m["ada_b"] = np.ascontiguousarray(m["ada_b"][:, 72 * r:72 * (r + 1), :])
        if "attn_w_in" in m:
            w = np.asarray(inputs["attn_w_in"][0], np.float32)
            cols = np.concatenate([np.arange(o + 1024 * r, o + 1024 * (r + 1)) for o in (0, 2048, 4096)])
            m["attn_w_in"] = np.ascontiguousarray(w[:, cols])
        if "a_pair" in m:
            m["a_pair"] = np.ascontiguousarray(m["a_pair"][:, :, 32 * r:32 * (r + 1)])
            m["c_pair"] = np.ascontiguousarray(m["c_pair"][:, :, 32 * r:32 * (r + 1), :])
            m["a_feat"] = np.ascontiguousarray(m["a_feat"][:, :, 512 * r:512 * (r + 1)])
            m["b_feat"] = np.ascontiguousarray(m["b_feat"][:, :, 512 * r:512 * (r + 1)])
        maps.append(m)
    return maps


def kernel(**inputs):
    nc = build(SEQ, pair=True)
    maps = make_in_maps(inputs, SEQ, pair=True)
    res = run_bass_kernel_spmd(nc, maps, core_ids=list(range(N_CORES)))
    out = np.stack([np.concatenate([res.results[2 * b]["out"], res.results[2 * b + 1]["out"]], axis=0) for b in range(4)], axis=0)
    return out.astype(np.float32)
```
